# Optimizing a Trainium2 kernel written in Bass

```python
import jax, jax.numpy as jnp
from jax import lax
import numpy as np

D_MODEL = 1024
BATCH = 8
SEQ = 4096
DEPTH = 4

N_EVEN = (DEPTH + 1) // 2
N_ODD = DEPTH // 2
D_PLE = 256
D_FF = 4 * D_MODEL
DN_ALPHA = (2.0 * DEPTH) ** 0.25
DN_BETA = (8.0 * DEPTH) ** -0.25
LN_EPS = 1e-5
NEG = -1e30

A_HEADS = 4
A_DH = D_MODEL // 8
A_W = A_HEADS * A_DH
A_CONV = 4
A_CHUNK = 64

B_HEADS = 8
B_DH = 64
B_W = B_HEADS * B_DH
B_KV = 2
B_HPG = B_HEADS // B_KV
B_KVW = B_KV * B_DH
B_CMP_LEN = 32
B_CMP_STRIDE = 16
B_CMP_HID = 128
B_SEL_BLK = 64
B_SEL_N = 16
B_WIN = 512
B_QBLK = 64

C_HEADS = 8
C_DN = 128
C_DR = 64
C_DV = 128
C_QL = 512
C_KVL = 256
C_IDX_HEADS = 8
C_IDX_DH = 64
C_IDX_DR = 32
C_TOPK = 256
C_QBLK = 128
ROPE_BASE = 10000.0

EVEN_SIZES = (A_W, A_W, A_W, A_W, A_HEADS, A_HEADS, B_W, B_KVW, B_KVW, B_KVW, B_KVW, B_KVW, B_KVW, 3 * B_HEADS)
ODD_SIZES = (C_QL, C_KVL, C_DR, C_IDX_DH, C_IDX_HEADS)

kernel_name = 'hybrid_mlstm_nsa_dsa_deepnorm'


def _split(z, sizes):
    out, o = [], 0
    for s in sizes:
        out.append(z[..., o:o + s])
        o += s
    return out


def _layer_norm(x, g, b):
    xf = x.astype(jnp.float32)
    mu = xf.mean(-1, keepdims=True)
    var = jnp.square(xf - mu).mean(-1, keepdims=True)
    return ((xf - mu) * lax.rsqrt(var + LN_EPS) * g + b).astype(x.dtype)


def _rms_norm(x, g):
    xf = x.astype(jnp.float32)
    return (xf * lax.rsqrt(jnp.square(xf).mean(-1, keepdims=True) + LN_EPS) * g).astype(x.dtype)


def _masked_softmax(s, mask):
    s = jnp.where(mask, s.astype(jnp.float32), NEG)
    return jnp.where(mask, jax.nn.softmax(s, axis=-1), 0.0)


def _causal_dwconv(x, w):
    k = w.shape[0]
    return lax.conv_general_dilated(x, w[:, None, :].astype(x.dtype), window_strides=(1,), padding=[(k - 1, 0)],
                                    dimension_numbers=('NWC', 'WIO', 'NWC'), feature_group_count=x.shape[-1])


def _rope_tables(pos, d):
    inv = ROPE_BASE ** (-jnp.arange(0, d, 2, dtype=jnp.float32) / d)
    ang = pos.astype(jnp.float32)[..., None] * inv
    return jnp.cos(ang), jnp.sin(ang)


def _rope(x, cos, sin):
    h = x.shape[-1] // 2
    c, s = cos.astype(x.dtype), sin.astype(x.dtype)
    x1, x2 = x[..., :h], x[..., h:]
    return jnp.concatenate([x1 * c - x2 * s, x1 * s + x2 * c], -1)


def _mlstm(q, k, v, o, ig, fg, conv_w, norm_g):
    B, T, _ = q.shape
    dt = q.dtype
    qk = jax.nn.silu(_causal_dwconv(jnp.concatenate([q, k], -1), conv_w))
    q, k = qk[..., :A_W], qk[..., A_W:]
    N = T // A_CHUNK

    def heads(u):
        return u.astype(jnp.float32).reshape(B, N, A_CHUNK, A_HEADS, A_DH).transpose(1, 0, 3, 2, 4)

    def gate(u):
        return u.astype(jnp.float32).reshape(B, N, A_CHUNK, A_HEADS).transpose(1, 0, 3, 2)

    xs = (heads(q) * A_DH ** -0.5, heads(k), heads(v), gate(ig), jax.nn.log_sigmoid(gate(fg)))
    causal = jnp.tril(jnp.ones((A_CHUNK, A_CHUNK), bool))

    def step(carry, inp):
        C, n, m = carry
        qc, kc, vc, ic, lf = inp
        b = jnp.cumsum(lf, axis=-1)
        D = jnp.where(causal, b[..., :, None] - b[..., None, :] + ic[..., None, :], NEG)
        m_inter = b + m[..., None]
        m_t = jnp.maximum(m_inter, D.max(-1))
        e_inter = jnp.exp(m_inter - m_t)
        S = jnp.einsum('bhld,bhsd->bhls', qc, kc) * jnp.exp(D - m_t[..., None])
        num = e_inter[..., None] * jnp.einsum('bhld,bhvd->bhlv', qc, C) + jnp.einsum('bhls,bhsv->bhlv', S, vc)
        den = e_inter * jnp.einsum('bhld,bhd->bhl', qc, n) + S.sum(-1)
        hc = num / jnp.maximum(jnp.abs(den), jnp.exp(-m_t))[..., None]
        bL = b[..., -1]
        dec = bL[..., None] - b + ic
        m_new = jnp.maximum(bL + m, dec.max(-1))
        w = jnp.exp(dec - m_new[..., None])
        e_st = jnp.exp(bL + m - m_new)
        C_new = e_st[..., None, None] * C + jnp.einsum('bhs,bhsv,bhsd->bhvd', w, vc, kc)
        n_new = e_st[..., None] * n + jnp.einsum('bhs,bhsd->bhd', w, kc)
        return (C_new, n_new, m_new), hc

    init = (jnp.zeros((B, A_HEADS, A_DH, A_DH), jnp.float32), jnp.zeros((B, A_HEADS, A_DH), jnp.float32),
            jnp.zeros((B, A_HEADS), jnp.float32))
    _, hs = lax.scan(step, init, xs)
    hs = hs.transpose(1, 0, 3, 2, 4).reshape(B, T, A_HEADS, A_DH)
    mu = hs.mean(-1, keepdims=True)
    var = jnp.square(hs - mu).mean(-1, keepdims=True)
    hn = ((hs - mu) * lax.rsqrt(var + LN_EPS)).reshape(B, T, A_W) * norm_g
    return (hn * jax.nn.sigmoid(o.astype(jnp.float32))).astype(dt)


def _nsa(q, kc_raw, vc_raw, ks, vs, kw, vw, g_pre, cmp_pos, cmp_w1, cmp_w2):
    B, T, _ = q.shape
    dt = q.dtype
    q = q.reshape(B, T, B_KV, B_HPG, B_DH) * B_DH ** -0.5
    r = lambda u: u.reshape(B, T, B_KV, B_DH)
    M = (T - B_CMP_LEN) // B_CMP_STRIDE + 1
    NSB = T // B_SEL_BLK
    n_sel = min(B_SEL_N, NSB)
    win = jnp.arange(M)[:, None] * B_CMP_STRIDE + jnp.arange(B_CMP_LEN)[None, :]

    def compress(u, j):
        blk = u[:, win] + cmp_pos[j][None, None, :, None, :].astype(dt)
        blk = blk.transpose(0, 1, 3, 2, 4).reshape(B, M, B_KV, B_CMP_LEN * B_DH)
        return jax.nn.gelu(blk @ cmp_w1[j]) @ cmp_w2[j]

    k_cmp = compress(r(kc_raw), 0)
    v_cmp = compress(r(vc_raw), 1)
    cmp_end = jnp.arange(M) * B_CMP_STRIDE + B_CMP_LEN - 1
    mi, jb = jnp.arange(M), jnp.arange(NSB)
    overlap = ((mi[:, None] * B_CMP_STRIDE < (jb[None, :] + 1) * B_SEL_BLK) &
               (mi[:, None] * B_CMP_STRIDE + B_CMP_LEN > jb[None, :] * B_SEL_BLK)).astype(jnp.float32)
    ks_blk = r(ks).reshape(B, NSB, B_SEL_BLK, B_KV, B_DH).transpose(0, 3, 1, 2, 4)
    vs_blk = r(vs).reshape(B, NSB, B_SEL_BLK, B_KV, B_DH).transpose(0, 3, 1, 2, 4)
    kw_pad = jnp.pad(r(kw), ((0, 0), (B_WIN, 0), (0, 0), (0, 0)))
    vw_pad = jnp.pad(r(vw), ((0, 0), (B_WIN, 0), (0, 0), (0, 0)))
    gates = jax.nn.sigmoid(g_pre.astype(jnp.float32)).reshape(B, T, B_KV, B_HPG, 3)
    bi = jnp.arange(B)[:, None, None, None]
    gi = jnp.arange(B_KV)[None, :, None, None]
    wlen = B_WIN + B_QBLK

    def block(c):
        s0 = c * B_QBLK
        tq = s0 + jnp.arange(B_QBLK)
        qb = lax.dynamic_slice_in_dim(q, s0, B_QBLK, 1)
        s = jnp.einsum('bqghd,bmgd->bqghm', qb, k_cmp)
        p_cmp = _masked_softmax(s, (cmp_end[None, :] <= tq[:, None])[None, :, None, None, :])
        o_cmp = jnp.einsum('bqghm,bmgd->bqghd', p_cmp.astype(dt), v_cmp)
        imp = p_cmp.sum(3) @ overlap
        cur = tq[:, None] // B_SEL_BLK
        forced = (jb[None, :] == 0) | (jb[None, :] == cur) | (jb[None, :] == cur - 1)
        valid = jb[None, :] * B_SEL_BLK <= tq[:, None]
        score = jnp.where(forced[None, :, None, :], 1e6, imp)
        score = jnp.where(valid[None, :, None, :], score, NEG)
        _, idx = lax.top_k(score, n_sel)
        idx = idx.transpose(0, 2, 1, 3)
        k_sel = ks_blk[bi, gi, idx].reshape(B, B_KV, B_QBLK, n_sel * B_SEL_BLK, B_DH)
        v_sel = vs_blk[bi, gi, idx].reshape(B, B_KV, B_QBLK, n_sel * B_SEL_BLK, B_DH)
        kpos = (idx[..., None] * B_SEL_BLK + jnp.arange(B_SEL_BLK)).reshape(B, B_KV, B_QBLK, n_sel * B_SEL_BLK)
        s = jnp.einsum('bgqhd,bgqkd->bgqhk', qb.transpose(0, 2, 1, 3, 4), k_sel)
        p = _masked_softmax(s, (kpos <= tq[None, None, :, None])[:, :, :, None, :])
        o_sel = jnp.einsum('bgqhk,bgqkd->bqghd', p.astype(dt), v_sel)
        kwb = lax.dynamic_slice_in_dim(kw_pad, s0, wlen, 1)
        vwb = lax.dynamic_slice_in_dim(vw_pad, s0, wlen, 1)
        wpos = s0 - B_WIN + jnp.arange(wlen)
        wmask = (wpos[None, :] <= tq[:, None]) & (wpos[None, :] > tq[:, None] - B_WIN) & (wpos[None, :] >= 0)
        s = jnp.einsum('bqghd,bkgd->bqghk', qb, kwb)
        p = _masked_softmax(s, wmask[None, :, None, None, :])
        o_win = jnp.einsum('bqghk,bkgd->bqghd', p.astype(dt), vwb)
        gb = lax.dynamic_slice_in_dim(gates, s0, B_QBLK, 1).astype(dt)
        o = gb[..., 0:1] * o_cmp + gb[..., 1:2] * o_sel + gb[..., 2:3] * o_win
        return o.reshape(B, B_QBLK, B_W)

    out = lax.map(block, jnp.arange(T // B_QBLK))
    return out.transpose(1, 0, 2, 3).reshape(B, T, B_W)


def _even_mixer(h, w_in, a_conv, a_i_b, a_f_b, a_norm, b_cmp_pos, b_cmp_w1, b_cmp_w2, b_g_b, w_out):
    aq, ak, av, ao, ai, af, bq, bkc, bvc, bks, bvs, bkw, bvw, bg = _split(h @ w_in, EVEN_SIZES)
    ya = _mlstm(aq, ak, av, ao, ai + a_i_b, af + a_f_b, a_conv, a_norm)
    yb = _nsa(bq, bkc, bvc, bks, bvs, bkw, bvw, bg + b_g_b, b_cmp_pos, b_cmp_w1, b_cmp_w2)
    return jnp.concatenate([ya, yb], -1) @ w_out


def _odd_mixer(h, pos, w_in, q_norm, kv_norm, w_qb, w_uk, w_uv, w_iq, ik_g, ik_b, w_out):
    B, T, _ = h.shape
    dt = h.dtype
    cq, ckv, kr, ik, iw = _split(h @ w_in, ODD_SIZES)
    cq = _rms_norm(cq, q_norm)
    ckv = _rms_norm(ckv, kv_norm)
    cos, sin = _rope_tables(pos, C_DR)
    qf = (cq @ w_qb).reshape(B, T, C_HEADS, C_DN + C_DR)
    q_rope = _rope(qf[..., C_DN:], cos[:, :, None], sin[:, :, None])
    k_rope = _rope(kr, cos, sin)
    q_abs = jnp.einsum('bthd,chd->bthc', qf[..., :C_DN], w_uk)
    qa = jnp.concatenate([q_abs, q_rope], -1) * (C_DN + C_DR) ** -0.5
    kv_cat = jnp.concatenate([ckv, k_rope], -1)
    icos, isin = _rope_tables(pos, C_IDX_DR)
    qi = (cq @ w_iq).reshape(B, T, C_IDX_HEADS, C_IDX_DH)
    qi = jnp.concatenate([_rope(qi[..., :C_IDX_DR], icos[:, :, None], isin[:, :, None]), qi[..., C_IDX_DR:]], -1)
    ki = _layer_norm(ik, ik_g, ik_b)
    ki = jnp.concatenate([_rope(ki[..., :C_IDX_DR], icos, isin), ki[..., C_IDX_DR:]], -1)
    wi = iw * (C_IDX_HEADS ** -0.5 * C_IDX_DH ** -0.5)
    k_sel = min(C_TOPK, T // 4)
    kpos = jnp.arange(T)
    bidx = jnp.arange(B)[:, None, None]

    def block(c):
        s0 = c * C_QBLK
        tq = s0 + jnp.arange(C_QBLK)
        qib = lax.dynamic_slice_in_dim(qi, s0, C_QBLK, 1)
        wib = lax.dynamic_slice_in_dim(wi, s0, C_QBLK, 1)
        isc = jnp.einsum('bqh,bqhs->bqs', wib, jax.nn.relu(jnp.einsum('bqhd,bsd->bqhs', qib, ki))).astype(jnp.float32)
        isc = jnp.where(kpos[None, None, :] <= tq[None, :, None], isc, NEG)
        _, idx = lax.top_k(isc, k_sel)
        g = kv_cat[bidx, idx]
        qab = lax.dynamic_slice_in_dim(qa, s0, C_QBLK, 1)
        s = jnp.einsum('bqhc,bqkc->bqhk', qab, g)
        pr = _masked_softmax(s, (idx <= tq[None, :, None])[:, :, None, :])
        return jnp.einsum('bqhk,bqkc->bqhc', pr.astype(dt), g[..., :C_KVL])

    o_lat = lax.map(block, jnp.arange(T // C_QBLK))
    o_lat = o_lat.transpose(1, 0, 2, 3, 4).reshape(B, T, C_HEADS, C_KVL)
    o = jnp.einsum('bthc,chv->bthv', o_lat, w_uv).reshape(B, T, C_HEADS * C_DV)
    return o @ w_out


def setup_inputs(seed: int = 0) -> dict:
    key = jax.random.key(seed)
    keys = iter(jax.random.split(key, 64))

    def nrm(shape, scale):
        return jax.random.normal(next(keys), shape, jnp.float32) * scale

    E, O, L, D = N_EVEN, N_ODD, DEPTH, D_MODEL
    sD = D ** -0.5
    x = nrm((BATCH, SEQ, D), 1.0)
    p = nrm((DEPTH, BATCH, SEQ, D_PLE), 1.0)
    positions = jnp.broadcast_to(jnp.arange(SEQ, dtype=jnp.int32), (BATCH, SEQ))
    seg = lambda n, sc=1.0: nrm((E, D, n), sD * sc)
    e_w_in = jnp.concatenate([seg(A_W), seg(A_W), seg(A_W, DN_BETA), seg(A_W), seg(A_HEADS), seg(A_HEADS),
                              seg(B_W), seg(B_KVW), seg(B_KVW, DN_BETA), seg(B_KVW), seg(B_KVW, DN_BETA),
                              seg(B_KVW), seg(B_KVW, DN_BETA), seg(3 * B_HEADS)], axis=-1)
    e_a_conv = nrm((E, A_CONV, 2 * A_W), A_CONV ** -0.5)
    e_a_i_b = nrm((E, A_HEADS), 0.1)
    e_a_f_b = jnp.linspace(3.0, 6.0, A_HEADS, dtype=jnp.float32)[None] + nrm((E, A_HEADS), 0.1)
    e_a_norm = 1.0 + nrm((E, A_W), 0.02)
    e_b_cmp_pos = nrm((E, 2, B_CMP_LEN, B_DH), 0.1)
    e_b_cmp_w1 = nrm((E, 2, B_CMP_LEN * B_DH, B_CMP_HID), (B_CMP_LEN * B_DH) ** -0.5)
    e_b_cmp_w2 = nrm((E, 2, B_CMP_HID, B_DH), B_CMP_HID ** -0.5) * jnp.array([1.0, DN_BETA], jnp.float32)[None, :, None, None]
    e_b_g_b = nrm((E, 3 * B_HEADS), 0.1)
    e_w_out = nrm((E, A_W + B_W, D), (A_W + B_W) ** -0.5 * DN_BETA)
    o_w_in = jnp.concatenate([nrm((O, D, n), sD) for n in ODD_SIZES], axis=-1)
    o_q_norm = 1.0 + nrm((O, C_QL), 0.02)
    o_kv_norm = 1.0 + nrm((O, C_KVL), 0.02)
    o_w_qb = nrm((O, C_QL, C_HEADS * (C_DN + C_DR)), C_QL ** -0.5)
    o_w_uk = nrm((O, C_KVL, C_HEADS, C_DN), C_KVL ** -0.5)
    o_w_uv = nrm((O, C_KVL, C_HEADS, C_DV), C_KVL ** -0.5 * DN_BETA)
    o_w_iq = nrm((O, C_QL, C_IDX_HEADS * C_IDX_DH), C_QL ** -0.5)
    o_ik_g = 1.0 + nrm((O, C_IDX_DH), 0.02)
    o_ik_b = nrm((O, C_IDX_DH), 0.02)
    o_w_out = nrm((O, C_HEADS * C_DV, D), (C_HEADS * C_DV) ** -0.5 * DN_BETA)
    ln1_g = 1.0 + nrm((L, D), 0.02)
    ln1_b = nrm((L, D), 0.02)
    ln2_g = 1.0 + nrm((L, D), 0.02)
    ln2_b = nrm((L, D), 0.02)
    mlp_w1 = nrm((L, D, D_FF), sD * DN_BETA)
    mlp_w2 = nrm((L, D_FF, D), D_FF ** -0.5 * DN_BETA)
    ple_gate_w = nrm((L, D, D), sD)
    ple_w = nrm((L, D_PLE, D), D_PLE ** -0.5)
    return {'x': x, 'p': p, 'positions': positions,
            'e_w_in': e_w_in, 'e_a_conv': e_a_conv, 'e_a_i_b': e_a_i_b, 'e_a_f_b': e_a_f_b, 'e_a_norm': e_a_norm,
            'e_b_cmp_pos': e_b_cmp_pos, 'e_b_cmp_w1': e_b_cmp_w1, 'e_b_cmp_w2': e_b_cmp_w2, 'e_b_g_b': e_b_g_b,
            'e_w_out': e_w_out,
            'o_w_in': o_w_in, 'o_q_norm': o_q_norm, 'o_kv_norm': o_kv_norm, 'o_w_qb': o_w_qb, 'o_w_uk': o_w_uk,
            'o_w_uv': o_w_uv, 'o_w_iq': o_w_iq, 'o_ik_g': o_ik_g, 'o_ik_b': o_ik_b, 'o_w_out': o_w_out,
            'ln1_g': ln1_g, 'ln1_b': ln1_b, 'ln2_g': ln2_g, 'ln2_b': ln2_b, 'mlp_w1': mlp_w1, 'mlp_w2': mlp_w2,
            'ple_gate_w': ple_gate_w, 'ple_w': ple_w}


def reference(x, p, positions,
              e_w_in, e_a_conv, e_a_i_b, e_a_f_b, e_a_norm, e_b_cmp_pos, e_b_cmp_w1, e_b_cmp_w2, e_b_g_b, e_w_out,
              o_w_in, o_q_norm, o_kv_norm, o_w_qb, o_w_uk, o_w_uv, o_w_iq, o_ik_g, o_ik_b, o_w_out,
              ln1_g, ln1_b, ln2_g, ln2_b, mlp_w1, mlp_w2, ple_gate_w, ple_w):
    h = x
    for i in range(DEPTH):
        j = i // 2
        if i % 2 == 0:
            y = _even_mixer(h, e_w_in[j], e_a_conv[j], e_a_i_b[j], e_a_f_b[j], e_a_norm[j], e_b_cmp_pos[j],
                            e_b_cmp_w1[j], e_b_cmp_w2[j], e_b_g_b[j], e_w_out[j])
        else:
            y = _odd_mixer(h, positions, o_w_in[j], o_q_norm[j], o_kv_norm[j], o_w_qb[j], o_w_uk[j], o_w_uv[j],
                           o_w_iq[j], o_ik_g[j], o_ik_b[j], o_w_out[j])
        h = _layer_norm(DN_ALPHA * h + y, ln1_g[i], ln1_b[i])
        u = jnp.square(jax.nn.relu(h @ mlp_w1[i])) @ mlp_w2[i]
        h = _layer_norm(DN_ALPHA * h + u, ln2_g[i], ln2_b[i])
        h = h + jax.nn.sigmoid(h @ ple_gate_w[i]) * (p[i] @ ple_w[i])
    return h
```

```python
from contextlib import ExitStack
import numpy as np
import concourse.bass as bass
import concourse.mybir as mybir
from concourse.bass_utils import run_bass_kernel_spmd

F32 = mybir.dt.float32
BF16 = mybir.dt.bfloat16
I32 = mybir.dt.int32
ALU = mybir.AluOpType
AF = mybir.ActivationFunctionType
AX = mybir.AxisListType

T = 4096
D = 1024
NT = T // 128
DEPTH = 4
DFF = 4096
DN_ALPHA = (2.0 * DEPTH) ** 0.25
LN_EPS = 1e-5
NEG = -1e30
N_DMA_SEMS = 8


class Prog:
    ENGS = ("tensor", "vector", "scalar", "gpsimd", "sync")

    def __init__(self, nc):
        self.nc = nc
        self.ops = {k: [] for k in self.ENGS}
        self.count = {k: 0 for k in self.ENGS}
        self.waited = {k: {} for k in self.ENGS}
        self.sems = {}
        self.writers = {}
        self.readers = {}
        self.dma_val = [0] * N_DMA_SEMS
        self.dma_rr = 0
        self.out_tokens = []

    def setup(self, stack):
        for k in self.ENGS:
            self.sems[k] = stack.enter_context(self.nc.semaphore("s_" + k))
        for i in range(N_DMA_SEMS):
            self.sems["d%d" % i] = stack.enter_context(self.nc.semaphore("d_%d" % i))

    @staticmethod
    def _key(k):
        if isinstance(k, (str, tuple)):
            return k
        t = getattr(k, "tensor", k)
        return t.name

    def _wait(self, eng, s, v):
        w = self.waited[eng]
        if w.get(s, 0) < v:
            w[s] = v
            self.ops[eng].append(("wait", self.sems[s], v))

    def _deps(self, eng, reads, writes):
        need = {}
        for k in reads:
            for s, v in self.writers.get(k, {}).items():
                if need.get(s, 0) < v:
                    need[s] = v
        for k in writes:
            for d in (self.writers.get(k, {}), self.readers.get(k, {})):
                for s, v in d.items():
                    if need.get(s, 0) < v:
                        need[s] = v
        for s, v in need.items():
            if eng == "tensor" and s == "tensor":
                continue
            self._wait(eng, s, v)

    def _record(self, tok, reads, writes):
        s, v = tok
        for k in reads:
            self.readers.setdefault(k, {})[s] = v
        for k in writes:
            self.writers[k] = {s: v}
            self.readers[k] = {}

    def op(self, eng, meth, reads, writes, *args, **kw):
        reads = [self._key(k) for k in reads]
        writes = [self._key(k) for k in writes]
        self._deps(eng, reads, writes)
        self.count[eng] += 1
        self.ops[eng].append(("op", (meth, args, kw), self.sems[eng], 1))
        self._record((eng, self.count[eng]), reads, writes)

    def mm(self, reads, writes, out, lhsT, rhs, start=True, stop=True):
        self.op("tensor", "matmul", reads, writes, out, lhsT=lhsT, rhs=rhs, start=start, stop=stop)

    def tr(self, reads, writes, out, in_, ident):
        self.op("tensor", "transpose", reads + [ident], writes, out=out, in_=in_, identity=ident[:])

    def v(self, meth, reads, writes, **kw):
        self.op("vector", meth, reads, writes, **kw)

    def s(self, meth, reads, writes, **kw):
        self.op("scalar", meth, reads, writes, **kw)

    def g(self, meth, reads, writes, *args, **kw):
        self.op("gpsimd", meth, reads, writes, *args, **kw)

    def dma(self, eng, reads, writes, out, in_, is_output=False, **kw):
        fn = ("dma_start", (), dict(out=out, in_=in_, **kw))
        reads = [self._key(k) for k in reads]
        writes = [self._key(k) for k in writes]
        i = self.dma_rr
        self.dma_rr = (self.dma_rr + 1) % N_DMA_SEMS
        sname = "d%d" % i
        self._deps(eng, reads, writes)
        if self.dma_val[i]:
            self._wait(eng, sname, self.dma_val[i])
        self.dma_val[i] += 16
        self.ops[eng].append(("op", fn, self.sems[sname], 16))
        self._record((sname, self.dma_val[i]), reads, writes)
        if is_output:
            self.out_tokens.append((sname, self.dma_val[i]))

    def barrier(self):
        for e in self.ENGS:
            for s in self.ENGS:
                if s != e and self.count[s]:
                    self._wait(e, s, self.count[s])
            for i in range(N_DMA_SEMS):
                if self.dma_val[i]:
                    self._wait(e, "d%d" % i, self.dma_val[i])
        for e in self.ENGS:
            if self.count[e]:
                self._wait(e, e, self.count[e])
        self.writers.clear()
        self.readers.clear()

    def finish(self):
        for s, v in self.out_tokens:
            self._wait("sync", s, v)
        ops = self.ops

        def replay(e, lst):
            for o in lst:
                if o[0] == "wait":
                    e.wait_ge(o[1], o[2])
                else:
                    meth, args, kw = o[1]
                    try:
                        ins = getattr(e, meth)(*args, **kw)
                    except Exception:
                        print("FAILED OP", meth, args, kw)
                        raise
                    ins.then_inc(o[2], o[3])

        with self.nc.Block() as block:
            @block.tensor
            def _(e):
                replay(e, ops["tensor"])

            @block.vector
            def _(e):
                replay(e, ops["vector"])

            @block.scalar
            def _(e):
                replay(e, ops["scalar"])

            @block.gpsimd
            def _(e):
                replay(e, ops["gpsimd"])

            @block.sync
            def _(e):
                replay(e, ops["sync"])


class Rot:
    def __init__(self, bufs):
        self.bufs = bufs
        self.i = 0

    def next(self):
        b = self.bufs[self.i % len(self.bufs)]
        self.i += 1
        return b


class Ctx:
    uid = 0

    def __init__(self, nc, P):
        self.nc = nc
        self.P = P
        self.st = ExitStack()

    def __enter__(self):
        self.st.__enter__()
        return self

    def __exit__(self, *a):
        self.P.barrier()
        return self.st.__exit__(*a)

    def sb(self, name, shape, dt):
        Ctx.uid += 1
        return self.st.enter_context(self.nc.sbuf_tensor("%s_%d" % (name, Ctx.uid), shape, dt))

    def ps(self, name, shape, dt=F32):
        Ctx.uid += 1
        return self.st.enter_context(self.nc.psum_tensor("%s_%d" % (name, Ctx.uid), shape, dt))

    def sbrot(self, name, shape, dt, n=2):
        return Rot([self.sb(name + str(i), shape, dt) for i in range(n)])

    def psrot(self, name, shape, dt=F32, n=2):
        return Rot([self.ps(name + str(i), shape, dt) for i in range(n)])


def make_consts(C, P):
    k = {}
    idf = C.sb("identf", [128, 128], F32)
    idb = C.sb("identb", [128, 128], BF16)
    P.g("memset", [], [idf], idf[:], 1.0)
    P.g("affine_select", [idf], [idf], out=idf[:], in_=idf[:], pattern=[[-1, 128]], compare_op=ALU.is_equal,
        fill=0.0, base=0, channel_multiplier=1)
    P.v("tensor_copy", [idf], [idb], out=idb[:], in_=idf[:])
    k["idf"], k["idb"] = idf, idb
    for name, mult, patt, base in (("tri_le", -1, 1, 0), ("tri_ge", 1, -1, 0), ("tri_lt", -1, 1, -1)):
        t = C.sb(name, [128, 128], F32)
        P.g("memset", [], [t], t[:], 1.0)
        P.g("affine_select", [t], [t], out=t[:], in_=t[:], pattern=[[patt, 128]], compare_op=ALU.is_ge, fill=0.0,
            base=base, channel_multiplier=mult)
        k[name] = t
    return k


def load_transposed(P, K, src, xT, nk, key, rot_in, rot_ps, tiles=range(NT), t0=0):
    for i in tiles:
        xt = rot_in.next()
        P.dma("sync", [], [xt], out=xt[:, 0:nk * 128], in_=src[i * 128:(i + 1) * 128, :])
        for half in range((nk + 3) // 4):
            n = min(4, nk - half * 4)
            pt = rot_ps.next()
            for jj in range(n):
                c = half * 4 + jj
                P.tr([xt], [pt], pt[:, jj, :], xt[:, c * 128:(c + 1) * 128], K["idf"])
            col = (i - t0) * 128
            dst = xT[:, half * 4:half * 4 + n, col:col + 128]
            if half % 2 == 0:
                P.v("tensor_copy", [pt], [(key, i)], out=dst, in_=pt[:, 0:n, :])
            else:
                P.s("copy", [pt], [(key, i)], out=dst, in_=pt[:, 0:n, :])


def run_staged(n, body):
    prev = None
    for i in range(n):
        g = body(i)
        next(g, None)
        if prev is not None:
            for _ in prev:
                pass
        prev = g
    if prev is not None:
        for _ in prev:
            pass


def run_interleaved(n, body, k):
    live = []
    nxt = 0
    while live or nxt < n:
        while len(live) < k and nxt < n:
            live.append(body(nxt))
            nxt += 1
        for g in list(live):
            try:
                next(g)
            except StopIteration:
                live.remove(g)


def load_w_bf16(P, dst, src, nk, key=None):
    for kc in range(nk):
        P.dma("gpsimd", [], [key or dst], out=dst[:, kc, :], in_=src[kc * 128:(kc + 1) * 128, :])


def phase_e1(nc, P, K, S, W, j, h_in):
    with Ctx(nc, P) as C:
        hT = C.sb("hT", [128, 8, T], BF16)
        w = C.sb("w_in", [128, 8, 3360], BF16)
        wq = C.sb("w_q", [128, 8, 4, 2, 64], BF16)
        load_w_bf16(P, w, W["e_w_in"][j], 8)
        for kc in range(8):
            for g in range(2):
                P.dma("gpsimd", [], [wq], out=wq[:, kc, :, g, :],
                      in_=W["e_w_in"][j][kc * 128:(kc + 1) * 128, 2056 + g * 256:2056 + (g + 1) * 256].rearrange("p (c d) -> p c d", c=4))
        rps = C.psrot("ps", [128, 512], F32, 3)
        rpt = C.psrot("pt", [128, 4, 128], F32, 2)
        with Ctx(nc, P) as C1:
            rin = C1.sbrot("hin", [128, 1024], F32, 2)
            load_transposed(P, K, h_in, hT, 8, "hT", rin, rpt)

        def feat_mm(ps, lhs_fn, tg, m=128):
            for kc in range(8):
                P.mm([("hT", i2) for i2 in range(tg * 4, tg * 4 + 4)] + [w, wq], [ps], ps[0:m, :], lhs_fn(kc),
                     hT[:, kc, tg * 512:(tg + 1) * 512], kc == 0, kc == 7)

        with Ctx(nc, P) as C1:
            bi = C1.sb("bi", [4, 1], F32)
            bfn = C1.sb("bfn", [4, 1], F32)
            P.dma("sync", [], [bi], out=bi[:], in_=W["e_a_i_b"][j].rearrange("(h o) -> h o", o=1))
            P.dma("sync", [], [bfn], out=bfn[:], in_=W["e_a_f_b"][j].rearrange("(h o) -> h o", o=1))
            P.v("tensor_scalar", [bfn], [bfn], out=bfn[:], in0=bfn[:], scalar1=-1.0, scalar2=None, op0=ALU.mult)
            ig = C1.sb("ig", [4, T], F32)
            sp = C1.sb("sp", [4, T], F32)
            bneg = C1.sb("bneg", [4, T], F32)
            cst = C1.sb("cst", [4, T], F32)
            for tg in range(8):
                cs = slice(tg * 512, (tg + 1) * 512)
                ps = rps.next(); feat_mm(ps, lambda kc: w[:, kc, 2048:2052], tg, 4)
                P.s("activation", [ps, bi], [ig], out=ig[:, cs], in_=ps[0:4, :], func=AF.Identity, bias=bi[:], scale=1.0)
                ps = rps.next(); feat_mm(ps, lambda kc: w[:, kc, 2052:2056], tg, 4)
                P.s("activation", [ps, bfn], [sp], out=sp[:, cs], in_=ps[0:4, :], func=AF.Exp, bias=bfn[:], scale=-1.0)
            P.s("activation", [sp], [sp], out=sp[:], in_=sp[:], func=AF.Ln, bias=1.0, scale=1.0)
            P.g("memset", [], [cst], cst[:], 1.0)
            P.v("tensor_tensor_scan", [cst, sp], [bneg], out=bneg[:], data0=cst[:], data1=sp[:], initial=0.0, op0=ALU.mult, op1=ALU.add)
            P.v("tensor_tensor", [ig, bneg], [ig], out=ig[:], in0=ig[:], in1=bneg[:], op=ALU.add)
            P.g("memset", [cst], [cst], cst[:], 0.0)
            P.v("tensor_tensor_scan", [cst, ig], [sp], out=sp[:], data0=cst[:], data1=ig[:], initial=0.0, op0=ALU.add, op1=ALU.max)
            P.v("tensor_tensor", [bneg, sp], [bneg], out=bneg[:], in0=bneg[:], in1=sp[:], op=ALU.subtract)
            P.v("tensor_scalar", [sp], [sp], out=sp[:], in0=sp[:], scalar1=-1.0, scalar2=None, op0=ALU.mult)
            P.dma("sync", [ig], ["gsc0"], out=S["gsc"][0], in_=ig[:])
            P.dma("sync", [sp], ["gsc1"], out=S["gsc"][1], in_=sp[:])
            P.dma("sync", [bneg], ["gsc2"], out=S["gsc"][2], in_=bneg[:])

        with Ctx(nc, P) as C1:
            convw = C1.sb("convw", [128, 8, 4], F32)
            for kk in range(4):
                P.dma("sync", [], [convw], out=convw[:, :, kk], in_=W["e_a_conv"][j][kk].rearrange("(c p) -> p c", p=128),
                      allow_slow_non_contiguous=True)
            bg = C1.sb("bg", [128, 24], F32)
            P.dma("sync", [], [bg], out=bg[:], in_=W["e_b_g_b"][j].partition_broadcast(128))
            rst = C1.sbrot("stg", [128, 512], F32, 2)
            rstb = C1.sbrot("stgb", [128, 512], BF16, 3)
            for i in range(NT):
                rows = slice(i * 128, (i + 1) * 128)

                def tok_mm(ps, c0, n, pc0=0):
                    for kc in range(8):
                        P.mm([("hT", i), w], [ps], ps[:, pc0:pc0 + n], hT[:, kc, i * 128:(i + 1) * 128], w[:, kc, c0:c0 + n], kc == 0, kc == 7)
                ps = rps.next(); tok_mm(ps, 1024, 512)
                sb_ = rstb.next()
                P.s("copy", [ps], [sb_], out=sb_[:], in_=ps[:])
                P.dma("gpsimd", [sb_], [("v_tok", i)], out=S["v_tok"][rows, :], in_=sb_[:])
                ps = rps.next(); tok_mm(ps, 1536, 512)
                st_ = rst.next()
                P.s("activation", [ps], [st_], out=st_[:], in_=ps[:], func=AF.Sigmoid)
                P.dma("gpsimd", [st_], [("sigo", i)], out=S["sigo"][rows, :], in_=st_[:])
                ps = rps.next(); tok_mm(ps, 2952, 128, 0); tok_mm(ps, 3208, 128, 128); tok_mm(ps, 3336, 24, 256)
                sb_ = rstb.next()
                P.v("tensor_copy", [ps], [sb_], out=sb_[:, 0:256], in_=ps[:, 0:256])
                P.dma("gpsimd", [sb_], [("bv_tok", i)], out=S["bv_tok"][rows, :], in_=sb_[:, 0:256])
                st_ = rst.next()
                P.v("tensor_tensor", [ps, bg], [st_], out=st_[:, 0:24], in0=ps[:, 256:280], in1=bg[:], op=ALU.add)
                P.s("activation", [st_], [st_], out=st_[:, 32:56], in_=st_[:, 0:24], func=AF.Sigmoid)
                P.dma("gpsimd", [st_], [("bgate", i)], out=S["bgate"][rows, :], in_=st_[:, 32:56])

            xpad = C1.sb("xpad", [128, 3 + T], F32)
            P.g("memset", [], [("xpad", -1)], xpad[:, 0:3], 0.0)
            racc = C1.sbrot("acc", [128, 512], F32, 2)
            for c in range(8):
                for tg in range(8):
                    ps = rps.next(); feat_mm(ps, lambda kc: w[:, kc, c * 128:(c + 1) * 128], tg)
                    P.s("copy", [ps], [("xpad", tg)], out=xpad[:, 3 + tg * 512:3 + (tg + 1) * 512], in_=ps[:])
                    acc = racc.next()
                    t0 = tg * 512
                    rd = [("xpad", tg - 1), ("xpad", tg), convw]
                    P.v("tensor_scalar", rd, [acc], out=acc[:], in0=xpad[:, t0:t0 + 512], scalar1=convw[:, c, 0:1], scalar2=None, op0=ALU.mult)
                    for jj in range(1, 4):
                        P.v("scalar_tensor_tensor", rd + [acc], [acc], out=acc[:], in0=xpad[:, t0 + jj:t0 + jj + 512],
                            scalar=convw[:, c, jj:jj + 1], in1=acc[:], op0=ALU.mult, op1=ALU.add)
                    ob = rstb.next()
                    P.s("activation", [acc], [ob], out=ob[:], in_=acc[:], func=AF.Silu)
                    P.dma("gpsimd", [ob], [("qkT", c, tg)], out=S["qkT"][c][:, t0:t0 + 512], in_=ob[:])
            blk = C1.sb("blk", [128, 2], BF16)
            P.g("memset", [], [blk], blk[:], 0.0)
            P.g("memset", [blk], [blk], blk[0:64, 0:1], 1.0)
            P.g("memset", [blk], [blk], blk[64:128, 1:2], 1.0)
            nall = C1.sb("nall", [128, NT, 8], F32)
            kall = C1.sb("kall", [128, 2, NT, 2], F32)
            rsqn = C1.sbrot("sqn", [128, 512], BF16, 2)
            rpn = C1.psrot("pn", [128, 8], F32, 1)

            def norms(ob, dst_fn):
                sq = rsqn.next()
                P.g("tensor_tensor", [ob], [sq], out=sq[:], in0=ob[:], in1=ob[:], op=ALU.mult)
                pn = rpn.next()
                for k in range(4):
                    P.mm([sq, blk], [pn], pn[:, 2 * k:2 * k + 2], sq[:, k * 128:(k + 1) * 128], blk[:])
                dst_fn(pn)
            for c in range(4):
                for tg in range(8):
                    ps = rps.next(); feat_mm(ps, lambda kc: wq[:, kc, c].rearrange("p g d -> p (g d)"), tg)
                    ob = rstb.next()
                    P.s("mul", [ps], [ob], out=ob[:], in_=ps[:], mul=0.125)
                    P.dma("gpsimd", [ob], [("bqT", c, tg)], out=S["bqT"][c][:, tg * 512:(tg + 1) * 512], in_=ob[:])
                    norms(ob, lambda pn: P.v("tensor_copy", [pn], [nall], out=nall[:, tg * 4:(tg + 1) * 4, 2 * c:2 * c + 2],
                                             in_=pn[:].rearrange("p (k g) -> p k g", g=2)))
            for n, c0 in enumerate((2568, 2696, 2824, 3080)):
                for tg in range(8):
                    ps = rps.next(); feat_mm(ps, lambda kc: w[:, kc, c0:c0 + 128], tg)
                    ob = rstb.next()
                    P.v("tensor_copy", [ps], [ob], out=ob[:], in_=ps[:])
                    P.dma("gpsimd", [ob], [("bkT", n, tg)], out=S["bkT"][n][:, tg * 512:(tg + 1) * 512], in_=ob[:])
                    if n >= 2:
                        norms(ob, lambda pn: P.v("tensor_copy", [pn], [kall], out=kall[:, n - 2, tg * 4:(tg + 1) * 4, :],
                                                 in_=pn[:].rearrange("p (k g) -> p k g", g=2)))
            P.dma("sync", [nall], ["nalld"], out=S["nall"], in_=nall[:].rearrange("p a b -> p (a b)"))
            P.dma("sync", [kall], ["kalld"], out=S["kall"], in_=kall[:].rearrange("p a b c -> p (a b c)"))


def phase_e2(nc, P, K, S, W, j):
    NC_ = NT
    with Ctx(nc, P) as C:
        rows = C.sb("rows", [4, 3, T], F32)
        for r in range(3):
            P.dma("sync", ["gsc%d" % r], [rows], out=rows[:, r, :], in_=S["gsc"][r])
        sel = C.sb("sel", [4, 4, 128], F32)
        P.g("memset", [], [sel], sel[:], 1.0)
        P.g("affine_select", [sel], [sel], out=sel[:], in_=sel[:], pattern=[[-1, 4], [0, 128]], compare_op=ALU.is_equal, fill=0.0,
            base=0, channel_multiplier=1)
        gnorm = C.sb("gnorm", [128, 512], F32)
        P.dma("sync", [], [gnorm], out=gnorm[:], in_=W["e_a_norm"][j].partition_broadcast(128))
        maskT = C.sb("maskT", [128, 128], F32)
        P.v("tensor_scalar", [K["tri_le"]], [maskT], out=maskT[:], in0=K["tri_le"][:], scalar1=128.0 ** -0.5, scalar2=None, op0=ALU.mult)

        rps_a = C.psrot("psa", [128, 512], F32, 1)
        rps_s = C.psrot("pss", [128, 128], F32, 2)
        rps_o = C.psrot("pso", [128, 132], F32, 2)
        rps_i = C.psrot("psi", [128, 132], F32, 1)
        rps_k = C.psrot("psk", [128, 128], BF16, 1)
        rET = C.sbrot("ET", [128, 128], F32, 2)
        rETm = C.sbrot("ETm", [128, 128], F32, 2)
        rPT = C.sbrot("PT", [128, 128], BF16, 2)
        rksc = C.sbrot("ksc", [128, 128], BF16, 2)
        rintra = C.sbrot("intra", [128, 132], F32, 2)

        for h in range(4):
          with Ctx(nc, P) as CH:
            qT = CH.sb("qT", [128, T], BF16)
            kT = CH.sb("kT", [128, T], BF16)
            vaug = CH.sb("vaug", [128, NC_, 132], BF16)
            nd = CH.sb("nd", [128, NC_, 132], F32)
            P.dma("sync", [("qkT", h, tg) for tg in range(8)], [qT], out=qT[:], in_=S["qkT"][h])
            P.dma("sync", [("qkT", 4 + h, tg) for tg in range(8)], [kT], out=kT[:], in_=S["qkT"][4 + h])
            P.g("memset", [], [vaug], vaug[:, :, 128:132], 1.0)
            P.dma("sync", [("v_tok", i) for i in range(NT)], [vaug], out=vaug[:, :, 0:128],
                  in_=S["v_tok"][:, h * 128:(h + 1) * 128].rearrange("(c p) d -> p c d", p=128))
            cols = CH.sb("cols", [128, 3, NC_], F32)
            for r in range(3):
                pc = rps_a.next()
                for c in range(NC_):
                    P.mm([rows, sel], [pc], pc[:, c:c + 1], rows[:, r, c * 128:(c + 1) * 128], sel[:, h, 0:1])
                P.v("tensor_copy", [pc], [cols], out=cols[:, r, :], in_=pc[:, 0:NC_])
            ends = CH.sb("ends", [128, 1 + NC_], F32)
            pc = rps_a.next()
            P.mm([rows, sel], [pc], pc[:, 0:NC_], sel[:, h, :], rows[:, 1, 127::128])
            P.g("memset", [], [ends], ends[:, 0:1], 0.0)
            P.v("tensor_copy", [pc, ends], [ends], out=ends[:, 1:1 + NC_], in_=pc[:, 0:NC_])
            wcol = CH.sb("wcol", [128, NC_], F32)
            est = CH.sb("est", [128, NC_], F32)
            eint = CH.sb("eint", [128, NC_], F32)
            enm = CH.sb("enm", [128, NC_], F32)
            P.v("tensor_tensor", [cols, ends], [wcol], out=wcol[:], in0=cols[:, 0, :], in1=ends[:, 1:1 + NC_], op=ALU.add)
            P.s("activation", [wcol], [wcol], out=wcol[:], in_=wcol[:], func=AF.Exp)
            P.v("tensor_tensor", [ends], [est], out=est[:], in0=ends[:, 1:1 + NC_], in1=ends[:, 0:NC_], op=ALU.subtract)
            P.s("activation", [est], [est], out=est[:], in_=est[:], func=AF.Exp)
            P.v("tensor_tensor", [cols, ends], [eint], out=eint[:], in0=cols[:, 1, :], in1=ends[:, 0:NC_], op=ALU.subtract)
            P.s("activation", [eint], [eint], out=eint[:], in_=eint[:], func=AF.Exp)
            P.v("tensor_scalar", [eint], [eint], out=eint[:], in0=eint[:], scalar1=128.0 ** -0.5, scalar2=None, op0=ALU.mult)
            P.s("activation", [cols], [enm], out=enm[:], in_=cols[:, 2, :], func=AF.Exp)

            CT = CH.sb("CT", [128, 132], F32)
            CTb = CH.sb("CTb", [128, 132], BF16)
            P.g("memset", [], [CT], CT[:], 0.0)
            P.g("memset", [], [CTb], CTb[:], 0.0)
            for c in range(NC_):
                cs = slice(c * 128, (c + 1) * 128)
                pg = rps_s.next()
                P.mm([rows, sel], [pg], pg[:], sel[:, h, :], rows[:, 1, cs])
                ET = rET.next()
                P.s("activation", [pg, cols], [ET], out=ET[:], in_=pg[:], func=AF.Exp, bias=cols[:, 0, c:c + 1], scale=1.0)
                ETm = rETm.next()
                P.g("tensor_tensor", [ET, maskT], [ETm], out=ETm[:], in0=ET[:], in1=maskT[:], op=ALU.mult)
                pst = rps_s.next()
                P.mm([kT, qT], [pst], pst[:], kT[:, cs], qT[:, cs])
                PT = rPT.next()
                P.v("tensor_tensor", [pst, ETm], [PT], out=PT[:], in0=pst[:], in1=ETm[:], op=ALU.mult)
                po = rps_o.next()
                P.mm([PT, vaug], [po], po[:, 0:129], PT[:], vaug[:, c, 0:129])
                pi = rps_i.next()
                P.mm([qT, CTb], [pi], pi[:, 0:129], qT[:, cs], CTb[:, 0:129])
                intra = rintra.next()
                P.s("copy", [po], [intra], out=intra[:, 0:129], in_=po[:, 0:129])
                P.v("scalar_tensor_tensor", [pi, intra, eint], [("nd", c)], out=nd[:, c, 0:129], in0=pi[:, 0:129], scalar=eint[:, c:c + 1],
                    in1=intra[:, 0:129], op0=ALU.mult, op1=ALU.add)
                pk = rps_k.next()
                P.tr([kT], [pk], pk[:], kT[:, cs], K["idb"])
                ksc = rksc.next()
                P.s("activation", [pk, wcol], [ksc], out=ksc[:], in_=pk[:], func=AF.Copy, scale=wcol[:, c:c + 1])
                pu = rps_o.next()
                P.mm([ksc, vaug], [pu], pu[:, 0:129], ksc[:], vaug[:, c, 0:129])
                P.v("scalar_tensor_tensor", [CT, pu, est], [CT], out=CT[:, 0:129], in0=CT[:, 0:129], scalar=est[:, c:c + 1],
                    in1=pu[:, 0:129], op0=ALU.mult, op1=ALU.add)
                P.s("copy", [CT], [CTb], out=CTb[:, 0:129], in_=CT[:, 0:129])

            ndk = [("nd", c) for c in range(NC_)]
            dn = CH.sb("dn", [128, NC_], F32)
            bc = lambda t: t[:].unsqueeze(2).to_broadcast([128, NC_, 128])
            P.v("scalar_tensor_tensor", ndk, [dn], out=dn[:], in0=nd[:, :, 128], scalar=-1.0, in1=nd[:, :, 128], op0=ALU.mult, op1=ALU.max)
            P.v("tensor_tensor", [dn, enm], [dn], out=dn[:], in0=dn[:], in1=enm[:], op=ALU.max)
            P.v("reciprocal", [dn], [dn], out=dn[:], in_=dn[:])
            hh = CH.sb("hh", [128, NC_, 128], F32)
            sq = CH.sb("sq", [128, NC_, 128], F32)
            P.v("tensor_tensor", ndk + [dn], [hh], out=hh[:], in0=nd[:, :, 0:128], in1=bc(dn), op=ALU.mult)
            s1 = CH.sb("s1", [128, NC_], F32)
            s2 = CH.sb("s2", [128, NC_], F32)
            m2 = CH.sb("m2", [128, NC_], F32)
            P.v("reduce_sum", [hh], [s1], out=s1[:], in_=hh[:], axis=AX.X)
            P.g("tensor_tensor", [hh], [sq], out=sq[:], in0=hh[:], in1=hh[:], op=ALU.mult)
            P.v("reduce_sum", [sq], [s2], out=s2[:], in_=sq[:], axis=AX.X)
            P.v("tensor_scalar", [s1], [s1], out=s1[:], in0=s1[:], scalar1=1.0 / 128, scalar2=None, op0=ALU.mult)
            P.v("tensor_tensor", [s1], [m2], out=m2[:], in0=s1[:], in1=s1[:], op=ALU.mult)
            P.v("scalar_tensor_tensor", [s2, m2], [s2], out=s2[:], in0=s2[:], scalar=1.0 / 128, in1=m2[:], op0=ALU.mult, op1=ALU.subtract)
            P.v("tensor_scalar", [s2], [s2], out=s2[:], in0=s2[:], scalar1=LN_EPS, scalar2=None, op0=ALU.add)
            P.s("activation", [s2], [s2], out=s2[:], in_=s2[:], func=AF.Ln)
            P.s("activation", [s2], [s2], out=s2[:], in_=s2[:], func=AF.Exp, scale=-0.5)
            P.v("tensor_tensor", [hh, s1], [hh], out=hh[:], in0=hh[:], in1=bc(s1), op=ALU.subtract)
            P.v("tensor_tensor", [hh, s2], [hh], out=hh[:], in0=hh[:], in1=bc(s2), op=ALU.mult)
            P.g("tensor_tensor", [hh, gnorm], [hh], out=hh[:], in0=hh[:],
                in1=gnorm[:, h * 128:(h + 1) * 128].unsqueeze(1).to_broadcast([128, NC_, 128]), op=ALU.mult)
            P.dma("sync", [("sigo", i) for i in range(NT)] + [sq], [sq], out=sq[:],
                  in_=S["sigo"][:, h * 128:(h + 1) * 128].rearrange("(c p) d -> p c d", p=128))
            P.v("tensor_tensor", [hh, sq], [hh], out=hh[:], in0=hh[:], in1=sq[:], op=ALU.mult)
            P.dma("gpsimd", [hh], [("attn", "a", h)], out=S["attn"][:, h * 128:(h + 1) * 128].rearrange("(c p) d -> p c d", p=128), in_=hh[:])


def bcast_row(P, C, name, src_row, n):
    t = C.sb(name, [128, n], F32)
    P.dma("sync", [], [t], out=t[:], in_=src_row.partition_broadcast(128))
    return t


def layer_norm_tile(P, r, cen, sm, g_t, b_t, out_t, n=1024):
    P.v("reduce_sum", [r], [sm], out=sm[:, 0:1], in_=r[:], axis=AX.X)
    P.v("tensor_scalar", [sm], [sm], out=sm[:, 1:2], in0=sm[:, 0:1], scalar1=-1.0 / n, scalar2=None, op0=ALU.mult)
    P.s("activation", [r, sm], [cen], out=cen[:], in_=r[:], func=AF.Identity, bias=sm[:, 1:2], scale=1.0)
    P.s("activation", [cen], [r, sm], out=r[:], in_=cen[:], func=AF.Square, accum_out=sm[:, 2:3])
    P.v("tensor_scalar", [sm], [sm], out=sm[:, 3:4], in0=sm[:, 2:3], scalar1=1.0 / n, scalar2=LN_EPS, op0=ALU.mult, op1=ALU.add)
    P.s("activation", [sm], [sm], out=sm[:, 4:5], in_=sm[:, 3:4], func=AF.Ln)
    P.s("activation", [sm], [sm], out=sm[:, 5:6], in_=sm[:, 4:5], func=AF.Exp, scale=-0.5)
    P.v("scalar_tensor_tensor", [cen, sm, g_t], [cen], out=cen[:], in0=cen[:], scalar=sm[:, 5:6], in1=g_t[:], op0=ALU.mult, op1=ALU.mult)
    P.g("tensor_tensor", [cen, b_t], [out_t], out=out_t[:], in0=cen[:], in1=b_t[:], op=ALU.add)


def phase_tail_a(nc, P, K, S, w_out, ln_g, ln_b, h_in, h1):
    with Ctx(nc, P) as C:
        wo = C.sb("wo", [128, 8, 1024], BF16)
        load_w_bf16(P, wo, w_out, 8)
        g_t = bcast_row(P, C, "g1", ln_g, 1024)
        b_t = bcast_row(P, C, "b1", ln_b, 1024)
        rin = C.sbrot("ain", [128, 1024], F32, 2)
        rh = C.sbrot("hin", [128, 1024], F32, 2)
        rpt = C.psrot("pt", [128, 4, 128], F32, 2)
        rps = C.psrot("ps", [128, 512], F32, 4)
        raT = C.sbrot("aT", [128, 8, 128], BF16, 2)
        rr = C.sbrot("r", [128, 1024], F32, 2)
        rcen = C.sbrot("cen", [128, 1024], F32, 2)
        rout = C.sbrot("o", [128, 1024], F32, 2)
        rsm = C.sbrot("sm", [128, 8], F32, 2)
        def body(i):
            rows = slice(i * 128, (i + 1) * 128)
            aT = raT.next()
            load_transposed(P, K, S["attn"], aT, 8, aT.name, rin, rpt, tiles=[i], t0=i)
            ht = rh.next()
            P.dma("sync", [], [ht], out=ht[:], in_=h_in[rows, :])
            yield
            r = rr.next()
            for n in range(2):
                ps = rps.next()
                for kc in range(8):
                    P.mm([(aT.name, i), wo], [ps], ps[:], aT[:, kc, :], wo[:, kc, n * 512:(n + 1) * 512], kc == 0, kc == 7)
                P.v("scalar_tensor_tensor", [ht, ps], [r], out=r[:, n * 512:(n + 1) * 512], in0=ht[:, n * 512:(n + 1) * 512], scalar=DN_ALPHA,
                    in1=ps[:], op0=ALU.mult, op1=ALU.add)
            cen, sm, o = rcen.next(), rsm.next(), rout.next()
            layer_norm_tile(P, r, cen, sm, g_t, b_t, o)
            P.dma("gpsimd", [o], [("h1", i)], out=h1[rows, :], in_=o[:])
        run_staged(NT, body)


def phase_tail_b(nc, P, K, S, w1, w2, ln_g, ln_b, h1, h2):
    ST = 256
    NS = ST // 128
    with Ctx(nc, P) as C:
        W1 = C.sb("W1", [128, 8, 4096], BF16)
        W2 = C.sb("W2", [128, 32, 1024], BF16)
        load_w_bf16(P, W1, w1, 8)
        load_w_bf16(P, W2, w2, 32)
        g_t = bcast_row(P, C, "g2", ln_g, 1024)
        b_t = bcast_row(P, C, "b2", ln_b, 1024)
        h1s = [C.sb("h1s%d" % k, [128, 1024], F32) for k in range(2 * NS)]
        rpt = C.psrot("pt", [128, 4, 128], F32, 2)
        rps = C.psrot("ps", [128, 512], F32, 4)
        rhT = C.sbrot("h1T", [128, 8, ST], BF16, 2)
        raT = C.sbrot("aT", [128, 32, ST], BF16, 1)
        rtmp = C.sbrot("tmp", [128, ST], F32, 3)
        rr = C.sbrot("r", [128, 1024], F32, 2)
        rcen = C.sbrot("cen", [128, 1024], F32, 1)
        rout = C.sbrot("o", [128, 1024], F32, 2)
        rsm = C.sbrot("sm", [128, 8], F32, 2)
        def body(st):
            hT = rhT.next()
            hts = []
            for k in range(NS):
                i = st * NS + k
                ht = h1s[(st % 2) * NS + k]
                hts.append(ht)
                load_transposed(P, K, h1, hT, 8, hT.name, Rot([ht]), rpt, tiles=[i], t0=st * NS)
            yield
            hk = [(hT.name, st * NS + k) for k in range(NS)]
            aT = raT.next()
            for f in range(32):
                ps = rps.next()
                for kc in range(8):
                    P.mm(hk + [W1], [ps], ps[:, 0:ST], W1[:, kc, f * 128:(f + 1) * 128], hT[:, kc, :], kc == 0, kc == 7)
                tmp = rtmp.next()
                P.s("activation", [ps], [tmp], out=tmp[:], in_=ps[:, 0:ST], func=AF.Relu)
                P.g("tensor_tensor", [tmp], [(aT.name, f)], out=aT[:, f, :], in0=tmp[:], in1=tmp[:], op=ALU.mult)
            ak = [(aT.name, f) for f in range(32)]
            for k in range(NS):
                i = st * NS + k
                r = rr.next()
                for n in range(2):
                    ps = rps.next()
                    for f in range(32):
                        P.mm(ak + [W2], [ps], ps[:], aT[:, f, k * 128:(k + 1) * 128], W2[:, f, n * 512:(n + 1) * 512], f == 0, f == 31)
                    P.v("scalar_tensor_tensor", [hts[k], ps], [r], out=r[:, n * 512:(n + 1) * 512], in0=hts[k][:, n * 512:(n + 1) * 512],
                        scalar=DN_ALPHA, in1=ps[:], op0=ALU.mult, op1=ALU.add)
                cen, sm, o = rcen.next(), rsm.next(), rout.next()
                layer_norm_tile(P, r, cen, sm, g_t, b_t, o)
                P.dma("gpsimd", [o], [("h2", i)], out=h2[i * 128:(i + 1) * 128, :], in_=o[:])
        run_staged(T // ST, body)


def phase_tail_c(nc, P, K, S, wg, wp, p_in, h2, h_out, is_output):
    with Ctx(nc, P) as C:
        Wg = C.sb("Wg", [128, 8, 1024], BF16)
        Wp = C.sb("Wp", [128, 2, 1024], BF16)
        load_w_bf16(P, Wg, wg, 8)
        load_w_bf16(P, Wp, wp, 2)
        rh = C.sbrot("h2t", [128, 1024], F32, 2)
        rp = C.sbrot("pt_", [128, 256], F32, 2)
        rpt = C.psrot("pt", [128, 4, 128], F32, 2)
        rps = C.psrot("ps", [128, 512], F32, 4)
        rhT = C.sbrot("hT", [128, 8, 128], BF16, 2)
        rpT = C.sbrot("pT", [128, 2, 128], BF16, 2)
        rgt = C.sbrot("gt", [128, 512], F32, 2)
        rout = C.sbrot("o", [128, 1024], F32, 2)
        def body(i):
            rows = slice(i * 128, (i + 1) * 128)
            ht = rh.next()
            hT = rhT.next()
            load_transposed(P, K, h2, hT, 8, hT.name, Rot([ht]), rpt, tiles=[i], t0=i)
            pT = rpT.next()
            load_transposed(P, K, p_in, pT, 2, pT.name, rp, rpt, tiles=[i], t0=i)
            yield
            o = rout.next()
            for n in range(2):
                cs = slice(n * 512, (n + 1) * 512)
                psg = rps.next()
                for kc in range(8):
                    P.mm([(hT.name, i), Wg], [psg], psg[:], hT[:, kc, :], Wg[:, kc, cs], kc == 0, kc == 7)
                psp = rps.next()
                for kc in range(2):
                    P.mm([(pT.name, i), Wp], [psp], psp[:], pT[:, kc, :], Wp[:, kc, cs], kc == 0, kc == 1)
                gt = rgt.next()
                P.s("activation", [psg], [gt], out=gt[:], in_=psg[:], func=AF.Sigmoid)
                P.v("tensor_tensor", [gt, psp], [gt], out=gt[:], in0=gt[:], in1=psp[:], op=ALU.mult)
                P.g("tensor_tensor", [gt, ht], [o], out=o[:, cs], in0=gt[:], in1=ht[:, cs], op=ALU.add)
            P.dma("gpsimd", [o], [("hout", i)], out=h_out[rows, :], in_=o[:], is_output=is_output)
        run_staged(NT, body)


TWO_PI = 6.283185307179586
CW1 = 6.28125
CW2 = TWO_PI - CW1


def phase_o1(nc, P, K, S, W, j, h_in):
    SC = 192.0 ** -0.5
    with Ctx(nc, P) as C:
        wi_ = C.sb("w_in_o", [128, 8, 904], BF16)
        load_w_bf16(P, wi_, W["o_w_in"][j], 8)
        wqb = C.sb("wqb", [128, 4, 1536], BF16)
        load_w_bf16(P, wqb, W["o_w_qb"][j], 4)
        wiq = C.sb("wiq", [128, 4, 512], BF16)
        load_w_bf16(P, wiq, W["o_w_iq"][j], 4)
        wqr = C.sb("wqr", [128, 4, 8, 64], BF16)
        P.v("tensor_copy", [wqb], [wqr], out=wqr[:], in_=wqb[:].rearrange("p k (h e) -> p k h e", e=192)[:, :, :, 128:192])
        wuk_f = C.sb("wuk_f", [128, 2, 1024], F32)
        P.dma("sync", [], [wuk_f], out=wuk_f[:], in_=W["o_w_uk"][j].rearrange("(cc p) h d -> p cc (h d)", p=128))
        wukT = C.sb("wukT", [128, 8, 256], BF16)
        rpt = C.psrot("pt", [128, 4, 128], F32, 2)
        for cc in range(2):
            for hq in range(2):
                pt = rpt.next()
                for jj in range(4):
                    h = hq * 4 + jj
                    P.tr([wuk_f], [pt], pt[:, jj, :], wuk_f[:, cc, h * 128:(h + 1) * 128], K["idf"])
                P.v("tensor_copy", [pt], [wukT], out=wukT[:, hq * 4:(hq + 1) * 4, cc * 128:(cc + 1) * 128], in_=pt[:])
        gq = bcast_row(P, C, "gq", W["o_q_norm"][j], 512)
        gkv = bcast_row(P, C, "gkv", W["o_kv_norm"][j], 256)
        ikg = bcast_row(P, C, "ikg", W["o_ik_g"][j], 64)
        ikb = bcast_row(P, C, "ikb", W["o_ik_b"][j], 64)
        inv = bcast_row(P, C, "inv", W["rope_inv"], 48)
        posi = C.sb("posi", [128, NT], I32)
        P.dma("sync", [], [posi], out=posi[:], in_=W["positions"].rearrange("(c p) -> p c", p=128), allow_slow_non_contiguous=True)
        posf = C.sb("posf", [128, NT], F32)
        P.v("tensor_copy", [posi], [posf], out=posf[:], in_=posi[:])

        rh = C.sbrot("hin", [128, 1024], F32, 3)
        rhT = C.sbrot("hT", [128, 8, 128], BF16, 3)
        rps = C.psrot("ps", [128, 512], F32, 3)
        rpk = C.psrot("psk", [128, 512], F32, 1)
        rsm = C.sbrot("sm", [128, 16], F32, 3)
        rcq = C.sbrot("cq", [128, 512], F32, 9)
        rcqT = C.sbrot("cqT", [128, 4, 128], BF16, 3)
        rqn = C.sbrot("qn", [128, 8, 128], BF16, 3)
        rqa = C.sbrot("qa", [128, 16, 128], BF16, 3)
        rang = C.sbrot("ang", [128, 4, 48], F32, 3)
        rki = C.sbrot("ki", [128, 48], I32, 3)
        rtr = C.sbrot("tr", [128, 4, 48], F32, 3)
        rq1 = C.sbrot("q1", [128, 512], F32, 6)
        rq2 = C.sbrot("q2", [128, 512], F32, 6)
        rqb = C.sbrot("qbf", [128, 4, 128], BF16, 9)
        rkv = C.sbrot("kv", [128, 672], F32, 3)
        rkvb = C.sbrot("kvb", [128, 256], BF16, 3)
        rkk = C.sbrot("kk", [128, 2, 128], F32, 3)
        rkkb = C.sbrot("kkb", [128, 2, 128], BF16, 2)
        rwi = C.sbrot("wi", [128, 16], F32, 3)
        kn2 = C.sb("kn2", [128, NT], F32)
        onesb = C.sb("onesb", [128, 1], BF16)
        P.g("memset", [], [onesb], onesb[:], 1.0)
        rsq = C.sbrot("sqa", [128, 16, 128], BF16, 3)
        rqn2 = C.sbrot("qn2", [128, 24], F32, 3)
        rpn = C.psrot("pn", [128, 8], F32, 1)

        def rms_rstd(src, n, sm, col, junk):
            P.s("activation", [src], [junk, sm], out=junk, in_=src, func=AF.Square, accum_out=sm[:, col:col + 1])
            P.v("tensor_scalar", [sm], [sm], out=sm[:, col + 1:col + 2], in0=sm[:, col:col + 1], scalar1=1.0 / n, scalar2=LN_EPS, op0=ALU.mult, op1=ALU.add)
            P.s("activation", [sm], [sm], out=sm[:, col + 1:col + 2], in_=sm[:, col + 1:col + 2], func=AF.Ln)
            P.s("activation", [sm], [sm], out=sm[:, col + 2:col + 3], in_=sm[:, col + 1:col + 2], func=AF.Exp, scale=-0.5)

        def rope(dst, src, cs, sn, nh, half, t1, t2):
            cb = cs.unsqueeze(1).to_broadcast([128, nh, half])
            sb_ = sn.unsqueeze(1).to_broadcast([128, nh, half])
            x1, x2 = src[:, :, 0:half], src[:, :, half:2 * half]
            P.v("tensor_tensor", [src], [t1], out=t1, in0=x1, in1=cb, op=ALU.mult)
            P.g("tensor_tensor", [src], [t2], out=t2, in0=x2, in1=sb_, op=ALU.mult)
            P.v("tensor_tensor", [t1, t2], [dst], out=dst[:, :, 0:half], in0=t1, in1=t2, op=ALU.subtract)
            P.v("tensor_tensor", [src], [t1], out=t1, in0=x1, in1=sb_, op=ALU.mult)
            P.g("tensor_tensor", [src], [t2], out=t2, in0=x2, in1=cb, op=ALU.mult)
            P.v("tensor_tensor", [t1, t2], [dst], out=dst[:, :, half:2 * half], in0=t1, in1=t2, op=ALU.add)

        def body(i):
            rows = slice(i * 128, (i + 1) * 128)
            cols = slice(i * 128, (i + 1) * 128)
            ang, ki, tr = rang.next(), rki.next(), rtr.next()
            P.v("tensor_scalar", [inv, posf], [ang], out=ang[:, 0, :], in0=inv[:], scalar1=posf[:, i:i + 1], scalar2=None, op0=ALU.mult)
            P.v("tensor_scalar", [ang], [ang], out=ang[:, 1, :], in0=ang[:, 0, :], scalar1=1.0 / TWO_PI, scalar2=None, op0=ALU.mult)
            P.v("tensor_copy", [ang], [ki], out=ki[:], in_=ang[:, 1, :])
            P.v("tensor_copy", [ki], [ang], out=ang[:, 1, :], in_=ki[:])
            P.v("scalar_tensor_tensor", [ang], [ang], out=ang[:, 0, :], in0=ang[:, 1, :], scalar=-CW1, in1=ang[:, 0, :], op0=ALU.mult, op1=ALU.add)
            P.v("scalar_tensor_tensor", [ang], [ang], out=ang[:, 0, :], in0=ang[:, 1, :], scalar=-CW2, in1=ang[:, 0, :], op0=ALU.mult, op1=ALU.add)

            def wrap(a):
                P.v("tensor_scalar", [ang], [ang], out=ang[:, 2, :], in0=a, scalar1=float(np.pi), scalar2=-TWO_PI, op0=ALU.is_gt, op1=ALU.mult)
                P.v("tensor_tensor", [ang], [ang], out=a, in0=a, in1=ang[:, 2, :], op=ALU.add)
                P.v("tensor_scalar", [ang], [ang], out=ang[:, 2, :], in0=a, scalar1=-float(np.pi), scalar2=TWO_PI, op0=ALU.is_lt, op1=ALU.mult)
                P.v("tensor_tensor", [ang], [ang], out=a, in0=a, in1=ang[:, 2, :], op=ALU.add)
            wrap(ang[:, 0, :])
            P.v("tensor_scalar", [ang], [ang], out=ang[:, 3, :], in0=ang[:, 0, :], scalar1=float(np.pi / 2), scalar2=None, op0=ALU.add)
            wrap(ang[:, 3, :])
            P.s("activation", [ang], [tr], out=tr[:, 0, :], in_=ang[:, 0, :], func=AF.Sin)
            P.s("activation", [ang], [tr], out=tr[:, 1, :], in_=ang[:, 3, :], func=AF.Sin)
            sin64, cos64, sin32, cos32 = tr[:, 0, 0:32], tr[:, 1, 0:32], tr[:, 0, 32:48], tr[:, 1, 32:48]
            yield

            ht, hT = rh.next(), rhT.next()
            load_transposed(P, K, h_in, hT, 8, hT.name, Rot([ht]), rpt, tiles=[i], t0=i)
            yield
            hk = [(hT.name, i), wi_]
            ps_q, ps_k = rps.next(), rpk.next()
            for kc in range(8):
                P.mm(hk, [ps_q], ps_q[:], hT[:, kc, :], wi_[:, kc, 0:512], kc == 0, kc == 7)
            for kc in range(8):
                P.mm(hk, [ps_k], ps_k[:, 0:392], hT[:, kc, :], wi_[:, kc, 512:904], kc == 0, kc == 7)
            kv = rkv.next()
            P.s("copy", [ps_k], [kv], out=kv[:, 0:392], in_=ps_k[:, 0:392])
            sm = rsm.next()
            cq, q1 = rcq.next(), rq1.next()
            rms_rstd(ps_q[:], 512, sm, 0, q1[:])
            P.v("scalar_tensor_tensor", [ps_q, sm, gq], [cq], out=cq[:], in0=ps_q[:], scalar=sm[:, 2:3], in1=gq[:], op0=ALU.mult, op1=ALU.mult)
            yield
            cqT = rcqT.next()
            pt = rpt.next()
            for jj in range(4):
                P.tr([cq], [pt], pt[:, jj, :], cq[:, jj * 128:(jj + 1) * 128], K["idf"])
            P.s("copy", [pt], [cqT], out=cqT[:], in_=pt[:])
            yield
            qn = rqn.next()
            for hq in range(2):
                ps = rps.next()
                for jj in range(4):
                    h = hq * 4 + jj
                    for kc in range(4):
                        P.mm([cqT, wqb], [ps], ps[:, jj * 128:(jj + 1) * 128], wqb[:, kc, h * 192:h * 192 + 128], cqT[:, kc, :], kc == 0, kc == 3)
                if hq == 0:
                    P.v("tensor_copy", [ps], [qn], out=qn[:, 0:4, :], in_=ps[:].rearrange("p (a b) -> p a b", b=128))
                else:
                    P.s("copy", [ps], [qn], out=qn[:, 4:8, :], in_=ps[:].rearrange("p (a b) -> p a b", b=128))
                yield
            qa = rqa.next()
            for hq in range(4):
                ps = rps.next()
                for jj in range(4):
                    n = hq * 4 + jj
                    h, cc = n // 2, n % 2
                    P.mm([qn, wukT], [ps], ps[:, jj * 128:(jj + 1) * 128], wukT[:, h, cc * 128:(cc + 1) * 128], qn[:, h, :])
                if hq % 2 == 0:
                    P.s("mul", [ps], [qa], out=qa[:, hq * 4:(hq + 1) * 4, :], in_=ps[:].rearrange("p (a b) -> p a b", b=128), mul=SC)
                else:
                    P.v("tensor_scalar", [ps], [qa], out=qa[:, hq * 4:(hq + 1) * 4, :], in0=ps[:].rearrange("p (a b) -> p a b", b=128),
                        scalar1=SC, scalar2=None, op0=ALU.mult)
                yield
            P.dma("gpsimd", [qa], [("qaT", i)], out=S["qaT"][:, :, cols].rearrange("n p t -> p n t"), in_=qa[:])
            sqa = rsq.next()
            P.g("tensor_tensor", [qa], [sqa], out=sqa[:], in0=qa[:], in1=qa[:], op=ALU.mult)
            pn = rpn.next()
            for h in range(8):
                for cc in range(2):
                    P.mm([sqa, onesb], [pn], pn[:, h:h + 1], sqa[:, 2 * h + cc, :], onesb[:], cc == 0, cc == 1)
            qn2 = rqn2.next()
            P.v("tensor_copy", [pn], [qn2], out=qn2[:, 0:8], in_=pn[:])
            yield
            ps = rps.next()
            for kc in range(4):
                P.mm([cqT, wqr], [ps], ps[:], cqT[:, kc, :], wqr[:, kc].rearrange("p h e -> p (h e)"), kc == 0, kc == 3)
            q2 = rq2.next()
            P.s("mul", [ps], [q1], out=q1[:], in_=ps[:], mul=SC)
            yield
            t1 = rcq.next()
            rope(q2[:].rearrange("p (h e) -> p h e", e=64), q1[:].rearrange("p (h e) -> p h e", e=64), cos64, sin64, 8, 32,
                 t1[:, 0:256].rearrange("p (h e) -> p h e", e=32), t1[:, 256:512].rearrange("p (h e) -> p h e", e=32))
            pt = rpt.next()
            for jj in range(4):
                P.tr([q2], [pt], pt[:, jj, :], q2[:, jj * 128:(jj + 1) * 128], K["idf"])
            qb = rqb.next()
            P.s("copy", [pt], [qb], out=qb[:], in_=pt[:])
            P.dma("gpsimd", [qb], [("qrT", i)], out=S["qrT"][:, :, cols].rearrange("n p t -> p n t"), in_=qb[:])
            yield
            P.g("tensor_tensor", [q2], [q1], out=q1[:], in0=q2[:], in1=q2[:], op=ALU.mult)
            P.v("reduce_sum", [q1], [qn2], out=qn2[:, 8:16], in_=q1[:].rearrange("p (h e) -> p h e", e=64), axis=AX.X)
            P.v("tensor_tensor", [qn2], [qn2], out=qn2[:, 16:24], in0=qn2[:, 0:8], in1=qn2[:, 8:16], op=ALU.add)
            P.dma("gpsimd", [qn2], [("qn2", i)], out=S["qn2"][rows, :], in_=qn2[:, 16:24])
            yield
            ps = rps.next()
            for kc in range(4):
                P.mm([cqT, wiq], [ps], ps[:], cqT[:, kc, :], wiq[:, kc, :], kc == 0, kc == 3)
            q1 = rq1.next()
            q2 = rq2.next()
            P.s("copy", [ps], [q1], out=q1[:], in_=ps[:])
            yield
            P.g("tensor_copy", [q1], [q2], out=q2[:], in_=q1[:])
            t1 = rcq.next()
            rope(q2[:].rearrange("p (h e) -> p h e", e=64)[:, :, 0:32], q1[:].rearrange("p (h e) -> p h e", e=64)[:, :, 0:32], cos32, sin32, 8, 16,
                 t1[:, 0:128].rearrange("p (h e) -> p h e", e=16), t1[:, 128:256].rearrange("p (h e) -> p h e", e=16))
            pt = rpt.next()
            for jj in range(4):
                P.tr([q2], [pt], pt[:, jj, :], q2[:, jj * 128:(jj + 1) * 128], K["idf"])
            qb = rqb.next()
            P.v("tensor_copy", [pt], [qb], out=qb[:], in_=pt[:])
            P.dma("gpsimd", [qb], [("qiT", i)], out=S["qiT"][:, :, cols].rearrange("n p t -> p n t"), in_=qb[:])
            yield
            rms_rstd(kv[:, 0:256], 256, sm, 4, kv[:, 400:656])
            P.v("scalar_tensor_tensor", [kv, sm, gkv], [kv], out=kv[:, 0:256], in0=kv[:, 0:256], scalar=sm[:, 6:7], in1=gkv[:], op0=ALU.mult, op1=ALU.mult)
            yield
            kvb = rkvb.next()
            P.g("tensor_copy", [kv], [kvb], out=kvb[:], in_=kv[:, 0:256])
            P.dma("gpsimd", [kvb], [("ckv", i)], out=S["ckv"][rows, :], in_=kvb[:])
            yield
            kk = rkk.next()
            rope(kk[:, 0:1, 0:64], kv[:, 256:320].unsqueeze(1), cos64, sin64, 1, 32, kv[:, 400:432].unsqueeze(1), kv[:, 432:464].unsqueeze(1))
            P.v("tensor_copy", [kk], [kk], out=kk[:, 0, 64:128], in_=kk[:, 0, 0:64])
            P.s("activation", [kv], [kv, sm], out=kv[:, 400:656], in_=kv[:, 0:256], func=AF.Square, accum_out=sm[:, 13:14])
            P.s("activation", [kk], [kv, sm], out=kv[:, 400:464], in_=kk[:, 0, 0:64], func=AF.Square, accum_out=sm[:, 14:15])
            P.v("tensor_tensor", [sm], [kn2], out=kn2[:, i:i + 1], in0=sm[:, 13:14], in1=sm[:, 14:15], op=ALU.add)
            yield
            ik = kv[:, 320:384]
            P.v("reduce_sum", [kv], [sm], out=sm[:, 8:9], in_=ik, axis=AX.X)
            P.v("tensor_scalar", [sm], [sm], out=sm[:, 9:10], in0=sm[:, 8:9], scalar1=-1.0 / 64, scalar2=None, op0=ALU.mult)
            P.s("activation", [kv, sm], [kv], out=kv[:, 464:528], in_=ik, func=AF.Identity, bias=sm[:, 9:10], scale=1.0)
            rms_rstd(kv[:, 464:528], 64, sm, 10, kv[:, 528:592])
            P.v("scalar_tensor_tensor", [kv, sm, ikg], [kv], out=kv[:, 464:528], in0=kv[:, 464:528], scalar=sm[:, 12:13], in1=ikg[:], op0=ALU.mult, op1=ALU.mult)
            P.v("tensor_tensor", [kv, ikb], [kv], out=kv[:, 464:528], in0=kv[:, 464:528], in1=ikb[:], op=ALU.add)
            P.v("tensor_copy", [kv], [kk], out=kk[:, 1, 32:64], in_=kv[:, 496:528])
            rope(kk[:, 1:2, 0:32], kv[:, 464:496].unsqueeze(1), cos32, sin32, 1, 16, kv[:, 592:608].unsqueeze(1), kv[:, 608:624].unsqueeze(1))
            P.v("tensor_copy", [kk], [kk], out=kk[:, 1, 64:128], in_=kk[:, 1, 0:64])
            yield
            wi = rwi.next()
            P.v("tensor_scalar", [kv], [wi], out=wi[:, 0:8], in0=kv[:, 384:392], scalar1=(8.0 ** -0.5) * (64.0 ** -0.5), scalar2=None, op0=ALU.mult)
            P.v("tensor_scalar", [wi], [wi], out=wi[:, 8:16], in0=wi[:, 0:8], scalar1=0.0, scalar2=2.0, op0=ALU.is_ge, op1=ALU.mult)
            P.v("tensor_scalar", [wi], [wi], out=wi[:, 8:16], in0=wi[:, 8:16], scalar1=-1.0, scalar2=None, op0=ALU.add)
            P.v("tensor_tensor", [wi], [wi], out=wi[:, 0:8], in0=wi[:, 0:8], in1=wi[:, 8:16], op=ALU.mult)
            P.dma("gpsimd", [wi], [("wi", i)], out=S["wi"][rows, :], in_=wi[:])
            yield
            pt = rpt.next()
            P.tr([kv], [pt], pt[:, 0, :], kv[:, 0:128], K["idf"])
            P.tr([kv], [pt], pt[:, 1, :], kv[:, 128:256], K["idf"])
            P.tr([kk], [pt], pt[:, 2, :], kk[:, 0, :], K["idf"])
            P.tr([kk], [pt], pt[:, 3, :], kk[:, 1, :], K["idf"])
            kkb = rqb.next()
            P.s("copy", [pt], [kkb], out=kkb[:], in_=pt[:])
            P.dma("gpsimd", [kkb], [("kT", i)], out=S["kT"][:, :, cols].rearrange("n p t -> p n t"), in_=kkb[:])
        run_interleaved(NT, body, 2)
        P.dma("sync", [kn2], ["kn2d"], out=S["kn2"], in_=kn2[:])


MASKV = -30000.0


class AttnRes:
    def __init__(self, C, npT=2):
        self.rps = C.psrot("s_ps", [128, 512], F32, 3)
        self.rpT = C.psrot("pT_ps", [128, 4, 128], BF16, npT)
        self.re = C.sbrot("e", [128, 512], BF16, 4)
        self.rpt = C.sbrot("pT", [128, 4, 128], BF16, 4)
        self.rmx = C.sbrot("mx", [128, 16], F32, 4)
        self.rrs = C.sbrot("rs", [128, 16], F32, 4)
        self.cnt = 0


class Pipe:
    def __init__(self):
        self.hist = []
        self.filler = None
        self.rate = 1

    def push(self, stages):
        self.hist.insert(0, stages)
        self.hist = self.hist[:3]
        for lag, st in enumerate(self.hist):
            if lag < len(st) and st[lag] is not None:
                st[lag]()
        if self.filler is not None:
            for _ in range(self.rate):
                next(self.filler, None)

    def flush(self):
        self.push([])
        self.push([])


def attn_items(P, K, R, terms, mask_term, k0, k1, pv_fn, done_fn, negm_ap=None, negm_reads=()):
    chunks = []
    c = k0
    while c < k1:
        n = min(512, k1 - c)
        chunks.append((c, n))
        c += n
    nc_ = len(chunks)
    nkt = (k1 - k0) // 128
    mx, rs = R.rmx.next(), R.rrs.next()

    def scores(c0, n, tl):
        ps = R.rps.next()
        for ti, (lhsT, rhs_fn, rd) in enumerate(tl):
            P.mm(rd, [ps], ps[:, 0:n], lhsT, rhs_fn(c0, n), ti == 0, ti == len(tl) - 1)
        return ps

    items = []
    for ci, (c0, n) in enumerate(chunks if negm_ap is None else []):
        def A1(ci=ci, c0=c0, n=n):
            ps = scores(c0, n, terms)
            P.v("reduce_max", [ps], [mx], out=mx[:, ci:ci + 1], in_=ps[:, 0:n], axis=AX.X)
            if ci == nc_ - 1:
                if nc_ > 1:
                    P.v("reduce_max", [mx], [mx], out=mx[:, 15:16], in_=mx[:, 0:nc_], axis=AX.X)
                    P.v("tensor_scalar", [mx], [mx], out=mx[:, 14:15], in0=mx[:, 15:16], scalar1=-1.0, scalar2=None, op0=ALU.mult)
                else:
                    P.v("tensor_scalar", [mx], [mx], out=mx[:, 14:15], in0=mx[:, 0:1], scalar1=-1.0, scalar2=None, op0=ALU.mult)
        items.append([A1])
    tl2 = terms + ([mask_term] if mask_term is not None else [])
    kbase = [0]
    for ci, (c0, n) in enumerate(chunks):
        st = {}
        nk = n // 128

        def A2(ci=ci, c0=c0, n=n, st=st):
            ps = scores(c0, n, tl2)
            e = R.re.next()
            if negm_ap is None:
                P.s("activation", [ps, mx], [e, rs], out=e[:, 0:n], in_=ps[:, 0:n], func=AF.Exp, bias=mx[:, 14:15], scale=1.0, accum_out=rs[:, ci:ci + 1])
            else:
                P.s("activation", [ps] + list(negm_reads), [e, rs], out=e[:, 0:n], in_=ps[:, 0:n], func=AF.Exp, bias=negm_ap, scale=1.0, accum_out=rs[:, ci:ci + 1])
            st["e"] = e

        def B2(nk=nk, st=st):
            e = st["e"]
            pTp = R.rpT.next()
            for kk in range(nk):
                P.tr([e], [pTp], pTp[:, kk, :], e[:, kk * 128:(kk + 1) * 128], K["idb"])
            pT = R.rpt.next()
            R.cnt += 1
            if R.cnt % 2 == 0:
                P.s("copy", [pTp], [pT], out=pT[:, 0:nk, :], in_=pTp[:, 0:nk, :])
            else:
                P.v("tensor_copy", [pTp], [pT], out=pT[:, 0:nk, :], in_=pTp[:, 0:nk, :])
            st["pT"] = pT

        def C2(ci=ci, c0=c0, nk=nk, st=st):
            pT = st["pT"]
            for kk in range(nk):
                kt = (c0 - k0) // 128 + kk
                pv_fn(pT[:, kk, :], [pT], kt, kt == 0, kt == nkt - 1)
            if ci == nc_ - 1:
                if nc_ > 1:
                    P.v("reduce_sum", [rs], [rs], out=rs[:, 15:16], in_=rs[:, 0:nc_], axis=AX.X)
                    P.v("tensor_scalar", [rs], [rs], out=rs[:, 14:15], in0=rs[:, 15:16], scalar1=1e-30, scalar2=None, op0=ALU.add)
                else:
                    P.v("tensor_scalar", [rs], [rs], out=rs[:, 14:15], in0=rs[:, 0:1], scalar1=1e-30, scalar2=None, op0=ALU.add)
                P.v("reciprocal", [rs], [rs], out=rs[:, 13:14], in_=rs[:, 14:15])
                done_fn(rs, rs[:, 13:14])
        items.append([A2, B2, C2])
    return items


def phase_o3(nc, P, K, S, W, j):
    NB = 11
    with Ctx(nc, P) as C:
        kT = C.sb("kT", [128, 4, T], BF16)
        for n in range(4):
            P.dma("sync", [("kT", i) for i in range(NT)], [kT], out=kT[:, n, :], in_=S["kT"][n])
        ckv = C.sb("ckv", [128, NT, 256], BF16)
        P.dma("sync", [("ckv", i) for i in range(NT)], [ckv], out=ckv[:], in_=S["ckv"].rearrange("(c p) d -> p c d", p=128))
        wuv = C.sb("wuv", [128, 2, 1024], BF16)
        for cc in range(2):
            P.dma("gpsimd", [], [wuv], out=wuv[:, cc, :], in_=W["o_w_uv"][j][cc * 128:(cc + 1) * 128].rearrange("p h v -> p (h v)"))
        pw = C.sb("pw", [128, NB + 1], F32)
        for k in range(NB + 1):
            P.g("memset", [], [pw], pw[:, k:k + 1], 2.0 ** -(k + 1))
        negtri = C.sb("negtri", [128, 128], F32)
        P.v("tensor_scalar", [K["tri_ge"]], [negtri], out=negtri[:], in0=K["tri_ge"][:], scalar1=-1.0, scalar2=1e30, op0=ALU.add, op1=ALU.mult)
        negtri_b = C.sb("negtri_b", [128, 128], BF16)
        P.v("tensor_scalar", [K["tri_ge"]], [negtri_b], out=negtri_b[:], in0=K["tri_ge"][:], scalar1=-1.0, scalar2=-MASKV, op0=ALU.add, op1=ALU.mult)
        kn2 = C.sb("kn2", [128, NT], F32)
        P.dma("sync", ["kn2d"], [kn2], out=kn2[:], in_=S["kn2"])
        kmx = C.sb("kmx", [128, 8], F32)
        ones1 = C.sb("ones1", [1, 128], F32)
        P.g("memset", [], [ones1], ones1[:], 1.0)
        P.v("reduce_max", [kn2], [kmx], out=kmx[:, 0:1], in_=kn2[:], axis=AX.X)
        with Ctx(nc, P) as Ck:
            pk1 = Ck.ps("pk1", [128, 128], F32)
            P.tr([kmx], [pk1], pk1[0:1, :], kmx[:, 0:1], K["idf"])
            P.v("reduce_max", [pk1], [kmx], out=kmx[0:1, 1:2], in_=pk1[0:1, :], axis=AX.X)
            P.mm([ones1, kmx], [pk1], pk1[:, 0:1], ones1[:], kmx[0:1, 1:2])
            P.v("tensor_copy", [pk1], [kmx], out=kmx[:, 2:3], in_=pk1[:, 0:1])
        isc = C.sb("isc", [128, T], F32)
        negm = [C.sb("negm%d" % k, [128, T], BF16) for k in range(2)]
        qr8s = [C.sb("qr8_%d" % k, [128, 8, 128], BF16) for k in range(2)]
        qi8s = [C.sb("qi8_%d" % k, [128, 8, 128], BF16) for k in range(2)]
        for tq in qr8s + qi8s:
            P.g("memset", [], [tq], tq[:], 0.0)
        rstab = C.sbrot("stab", [128, 24], F32, 2)
        junk = C.sb("junk", [128, T], BF16)
        R = AttnRes(C, npT=1)
        rolat = C.psrot("olat", [128, 2, 128], F32, 2)
        rout = C.psrot("outp", [128, 128], F32, 1)
        rqa = C.sbrot("qa", [128, 16, 128], BF16, 2)
        rqr = C.sbrot("qr", [128, 4, 128], BF16, 2)
        rqi = C.sbrot("qi", [128, 4, 128], BF16, 2)
        rwi = C.sbrot("wi", [128, 16], F32, 2)
        rrl = C.sbrot("rl", [128, 512], BF16, 4)
        rdsg = C.sbrot("dsg", [128, 8, 128], BF16, 2)
        rpacc = C.psrot("pacc", [128, 512], F32, 1)
        rbs = C.sbrot("bs", [128, 32], F32, 2)
        rwh = C.sbrot("wh", [128, NB + 1], F32, 2)
        rol = C.sbrot("ol", [128, 2, 128], BF16, 2)
        rat = C.sbrot("at", [128, 1024], F32, 2)
        tile_in = {}

        def prep(i):
            cols = slice(i * 128, (i + 1) * 128)
            L = (i + 1) * 128
            qa, qr = rqa.next(), qr8s[i % 2]
            nm = negm[i % 2]
            stab = rstab.next()
            tile_in[i] = (qa, qr, nm, stab)
            P.dma("sync", [("qaT", i)], [qa], out=qa[:], in_=S["qaT"][:, :, cols].rearrange("n p t -> p n t"))
            for par in range(2):
                P.dma("sync", [("qrT", i)], [qr], out=qr[par * 64:(par + 1) * 64, par::2, :],
                      in_=S["qrT"][:, par * 64:(par + 1) * 64, cols].rearrange("n p t -> p n t"))
            P.dma("sync", [("qn2", i)], [stab], out=stab[:, 0:8], in_=S["qn2"][cols, :])
            P.v("tensor_scalar", [stab, kmx], [stab], out=stab[:, 0:8], in0=stab[:, 0:8], scalar1=kmx[:, 2:3], scalar2=1e-30, op0=ALU.mult, op1=ALU.add)
            P.s("activation", [stab], [stab], out=stab[:, 8:16], in_=stab[:, 0:8], func=AF.Ln)
            P.s("activation", [stab], [stab], out=stab[:, 8:16], in_=stab[:, 8:16], func=AF.Exp, scale=0.5)
            P.v("tensor_scalar", [stab], [stab], out=stab[:, 16:24], in0=stab[:, 8:16], scalar1=-1.02, scalar2=None, op0=ALU.mult)
            yield
            if i < 2:
                if i == 1:
                    P.g("memset", [], [nm], nm[:, 0:128], 0.0)
                P.g("tensor_copy", [negtri_b, nm], [nm], out=nm[:, L - 128:L], in_=negtri_b[:])
                return
            qi, wi = qi8s[i % 2], rwi.next()
            for par in range(2):
                P.dma("sync", [("qiT", i)], [qi], out=qi[par * 64:(par + 1) * 64, par::2, :],
                      in_=S["qiT"][:, par * 64:(par + 1) * 64, cols].rearrange("n p t -> p n t"))
            P.dma("sync", [("wi", i)], [wi], out=wi[:], in_=S["wi"][cols, :])
            dsg = rdsg.next()
            for h in range(8):
                P.g("tensor_scalar", [K["idb"], wi], [dsg], out=dsg[:, h, :], in0=K["idb"][:], scalar1=wi[:, 8 + h:9 + h], scalar2=0.0, op0=ALU.mult, op1=ALU.add)
            yield
            c0 = 0
            while c0 < L:
                n = min(512, L - c0)
                pacc = rpacc.next()
                for h in range(8):
                    ps = R.rps.next()
                    P.mm([qi, kT], [ps], ps[:, 0:n], qi[:, h, :], kT[:, 3, c0:c0 + n])
                    rl = rrl.next()
                    P.s("activation", [ps, wi], [rl], out=rl[:, 0:n], in_=ps[:, 0:n], func=AF.Relu, scale=wi[:, h:h + 1])
                    P.mm([rl, dsg], [pacc], pacc[:, 0:n], dsg[:, h, :], rl[:, 0:n], h == 0, h == 7)
                    yield
                P.v("tensor_copy", [pacc], [("isc", c0)], out=isc[:, c0:c0 + n], in_=pacc[:, 0:n])
                c0 += n
            ik = [("isc", c) for c in range(0, L, 512)]
            P.g("tensor_tensor", ik + [K["tri_ge"]], ik, out=isc[:, L - 128:L], in0=isc[:, L - 128:L], in1=K["tri_ge"][:], op=ALU.mult)
            P.g("tensor_tensor", ik + [negtri], ik, out=isc[:, L - 128:L], in0=isc[:, L - 128:L], in1=negtri[:], op=ALU.add)
            bs, wh = rbs.next(), rwh.next()
            pcs = [(c, min(1024, L - c)) for c in range(0, L, 1024)]
            npc = len(pcs)
            for pi, (c, n) in enumerate(pcs):
                n2 = min(n, L - 128 - c)
                if n2 > 0:
                    P.v("tensor_reduce", ik, [bs], out=bs[:, 8 + pi:9 + pi], in_=isc[:, c:c + n2], axis=AX.X, op=ALU.min)
                else:
                    P.v("tensor_copy", [bs], [bs], out=bs[:, 8 + pi:9 + pi], in_=bs[:, 8:9])
                P.v("reduce_max", ik, [bs], out=bs[:, 12 + pi:13 + pi], in_=isc[:, c:c + n], axis=AX.X)
                yield
            P.v("tensor_reduce", [bs], [bs], out=bs[:, 0:1], in_=bs[:, 8:8 + npc], axis=AX.X, op=ALU.min)
            P.v("reduce_max", [bs], [bs], out=bs[:, 1:2], in_=bs[:, 12:12 + npc], axis=AX.X)
            P.v("tensor_tensor", [bs], [bs], out=bs[:, 2:3], in0=bs[:, 1:2], in1=bs[:, 0:1], op=ALU.subtract)
            P.v("tensor_scalar", [pw, bs], [wh], out=wh[:], in0=pw[:], scalar1=bs[:, 2:3], scalar2=None, op0=ALU.mult)
            P.v("tensor_tensor", [bs, wh], [bs], out=bs[:, 3:4], in0=bs[:, 0:1], in1=wh[:, 0:1], op=ALU.add)
            yield
            for k in range(NB):
                for pi, (c, n) in enumerate(pcs):
                    P.v("tensor_scalar", ik + [bs], [junk, bs], out=junk[:, c:c + n], in0=isc[:, c:c + n], scalar1=bs[:, 3:4], scalar2=None,
                        op0=ALU.is_ge, op1=ALU.add, accum_out=bs[:, 8 + pi:9 + pi])
                    if pi < npc - 1:
                        yield
                if npc > 1:
                    P.v("reduce_sum", [bs], [bs], out=bs[:, 4:5], in_=bs[:, 8:8 + npc], axis=AX.X)
                    cnt = bs[:, 4:5]
                else:
                    cnt = bs[:, 8:9]
                P.v("tensor_scalar", [bs], [bs], out=bs[:, 5:6], in0=cnt, scalar1=256.0, scalar2=-0.5, op0=ALU.is_ge, op1=ALU.add)
                P.v("scalar_tensor_tensor", [bs, wh], [bs], out=bs[:, 3:4], in0=bs[:, 5:6], scalar=wh[:, k:k + 1], in1=bs[:, 3:4], op0=ALU.mult, op1=ALU.add)
                yield
            P.v("tensor_tensor", [bs, wh], [bs], out=bs[:, 6:7], in0=bs[:, 3:4], in1=wh[:, NB:NB + 1], op=ALU.subtract)
            for pi, (c, n) in enumerate(pcs):
                P.v("tensor_scalar", ik + [bs], [nm], out=nm[:, c:c + n], in0=isc[:, c:c + n], scalar1=bs[:, 6:7], scalar2=MASKV, op0=ALU.is_lt, op1=ALU.mult)
                yield

        pipe = Pipe()
        for _ in prep(0):
            pass
        for i in range(NT):
            rows = slice(i * 128, (i + 1) * 128)
            L = (i + 1) * 128
            nxt = prep(i + 1) if i + 1 < NT else None
            pipe.filler = nxt
            nch_i, nch_n = (L + 511) // 512, (L + 128 + 511) // 512
            npc_n = (L + 128 + 1023) // 1024
            pipe.rate = -(-(8 * nch_n + (NB + 2) * npc_n + 8) // (8 * nch_i))
            qa, qr, nm, stab = tile_in[i]
            at = rat.next()
            for h in range(8):
                po = (h % 2) * 64
                terms = [
                    (qa[:, 2 * h, :], lambda c0, n: kT[:, 0, c0:c0 + n], [qa, kT]),
                    (qa[:, 2 * h + 1, :], lambda c0, n: kT[:, 1, c0:c0 + n], [qa, kT]),
                    (qr[:, h, :], lambda c0, n: kT[:, 2, c0:c0 + n], [qr, kT]),
                ]
                mterm = (K["idb"][:], lambda c0, n, nm=nm: nm[:, c0:c0 + n], [nm, K["idb"]])
                olat = rolat.next()

                def pv(pT, rd, kt, first, last, olat=olat):
                    for cc in range(2):
                        P.mm(rd + [ckv], [olat], olat[:, cc, :], ckv[:, kt, cc * 128:(cc + 1) * 128], pT, first, last)

                def done(rs, rinv, olat=olat, h=h, at=at, rows=rows):
                    ol = rol.next()
                    P.s("copy", [olat], [ol], out=ol[:], in_=olat[:])
                    po_ = rout.next()
                    for cc in range(2):
                        P.mm([ol, wuv], [po_], po_[:], ol[:, cc, :], wuv[:, cc, h * 128:(h + 1) * 128], cc == 0, cc == 1)
                    P.v("tensor_scalar", [po_, rs], [at], out=at[:, h * 128:(h + 1) * 128], in0=po_[:], scalar1=rinv, scalar2=None, op0=ALU.mult)
                    if h == 7:
                        P.dma("gpsimd", [at], [("attn", rows.start)], out=S["attn"][rows, :], in_=at[:])
                for it in attn_items(P, K, R, terms, mterm, 0, L, pv, done, negm_ap=stab[:, 16 + h:17 + h], negm_reads=[stab]):
                    pipe.push(it)
            if nxt is not None:
                for _ in nxt:
                    pass
        pipe.flush()


def phase_e3(nc, P, K, S, W, j):
    with Ctx(nc, P) as C:
        kcmpT = C.sb("kcmpT", [128, 256], BF16)
        vcmp = C.sb("vcmp", [128, 2, 2, 64], BF16)
        P.g("memset", [], [kcmpT], kcmpT[:], 0.0)
        P.g("memset", [], [vcmp], vcmp[:], 0.0)
        with Ctx(nc, P) as C1:
            uT = C1.sb("uT", [128, 2, T], BF16)
            for n in range(2):
                P.dma("sync", [("bkT", n, tg) for tg in range(8)], [uT], out=uT[:, n, :], in_=S["bkT"][n])
            w1 = C1.sb("w1", [128, 2, 32, 128], BF16)
            for kv in range(2):
                for half in range(2):
                    P.dma("gpsimd", [], [w1], out=w1[half * 64:(half + 1) * 64, kv, :, :],
                          in_=W["e_b_cmp_w1"][j][kv].rearrange("(jj d) n -> d jj n", d=64))
            w2f = C1.sb("w2f", [128, 2, 64], F32)
            P.dma("sync", [], [w2f], out=w2f[:], in_=W["e_b_cmp_w2"][j].rearrange("kv n d -> n kv d"))
            w2p = C1.sb("w2p", [128, 2, 128], BF16)
            w2v = C1.sb("w2v", [128, 64], BF16)
            P.g("memset", [], [w2p], w2p[:], 0.0)
            for g in range(2):
                P.v("tensor_copy", [w2f, w2p], [w2p], out=w2p[:, g, g * 64:(g + 1) * 64], in_=w2f[:, 0, :])
            P.v("tensor_copy", [w2f], [w2v], out=w2v[:], in_=w2f[:, 1, :])
            posT = C1.sb("posT", [64, 2, 32], BF16)
            for kv in range(2):
                P.dma("gpsimd", [], [posT], out=posT[:, kv, :], in_=W["e_b_cmp_pos"][j][kv].rearrange("jj d -> d jj"), allow_slow_non_contiguous=True)
            rph = C1.psrot("ph", [128, 256], F32, 2)
            rpb = C1.psrot("pb", [128, 8], F32, 1)
            rpo = C1.psrot("pko", [128, 256], F32, 1)
            rpv = C1.psrot("pvo", [128, 64], F32, 1)
            bias = C1.sb("bias", [128, 2], F32)
            x = C1.sb("x", [128, 256], F32)
            x2 = C1.sb("x2", [128, 256], F32)
            sg = C1.sb("sg", [128, 256], F32)
            gl = [[C1.sb("gl%d%d" % (kv, g), [128, 256], BF16) for g in range(2)] for kv in range(2)]
            pko = rpo.next()
            for kv in range(2):
                pb = rpb.next()
                for jj in range(32):
                    P.mm([w1, posT], [pb], pb[:, 0:1], w1[0:64, kv, jj, :], posT[:, kv, jj:jj + 1], jj == 0, jj == 31)
                P.v("tensor_copy", [pb], [bias], out=bias[:, kv:kv + 1], in_=pb[:, 0:1])
                for g in range(2):
                    ph = rph.next()
                    for jj in range(32):
                        P.mm([w1, uT], [ph], ph[:, 0:255], w1[g * 64:(g + 1) * 64, kv, jj, :], uT[g * 64:(g + 1) * 64, kv, jj:jj + 16 * 254 + 1:16], jj == 0, jj == 31)
                    P.s("activation", [ph, bias], [x], out=x[:, 0:255], in_=ph[:, 0:255], func=AF.Identity, bias=bias[:, kv:kv + 1], scale=1.0)
                    P.v("tensor_tensor", [x], [x2], out=x2[:, 0:255], in0=x[:, 0:255], in1=x[:, 0:255], op=ALU.mult)
                    P.v("tensor_scalar", [x2], [x2], out=x2[:, 0:255], in0=x2[:, 0:255], scalar1=0.044715, scalar2=1.0, op0=ALU.mult, op1=ALU.add)
                    P.v("tensor_tensor", [x2, x], [x2], out=x2[:, 0:255], in0=x2[:, 0:255], in1=x[:, 0:255], op=ALU.mult)
                    P.s("activation", [x2], [sg], out=sg[:, 0:255], in_=x2[:, 0:255], func=AF.Sigmoid, scale=1.5957691216057308)
                    G = gl[kv][g]
                    P.g("memset", [], [G], G[:], 0.0)
                    P.v("tensor_tensor", [x, sg, G], [G], out=G[:, 0:255], in0=x[:, 0:255], in1=sg[:, 0:255], op=ALU.mult)
                    if kv == 0:
                        P.mm([G, w2p], [pko], pko[:, 0:255], w2p[:, g, :], G[:, 0:255], g == 0, g == 1)
                    else:
                        for mc in range(2):
                            nm = 128 if mc == 0 else 127
                            pvo = rpv.next()
                            P.mm([G, w2v], [pvo], pvo[0:nm, :], G[:, mc * 128:mc * 128 + nm], w2v[:])
                            P.v("tensor_copy", [pvo, vcmp], [vcmp], out=vcmp[0:nm, mc, g, :], in_=pvo[0:nm, :])
                if kv == 0:
                    P.v("tensor_copy", [pko, kcmpT], [kcmpT], out=kcmpT[:, 0:255], in_=pko[:, 0:255])

        qT = C.sb("qT", [128, 8, T], BF16)
        for g in range(2):
            P.g("memset", [], [qT], qT[(1 - g) * 64:(2 - g) * 64, g * 4:(g + 1) * 4, :], 0.0)
        for c in range(4):
            for g in range(2):
                P.dma("sync", [("bqT", c, tg) for tg in range(8)] + [qT], [qT], out=qT[g * 64:(g + 1) * 64, g * 4 + c, :],
                      in_=S["bqT"][c][g * 64:(g + 1) * 64, :])
        nall = C.sb("nall", [128, NT, 4, 2], F32)
        kall = C.sb("kall", [128, 2, NT, 2], F32)
        P.dma("sync", ["nalld"], [nall], out=nall[:].rearrange("p a b c -> p (a b c)"), in_=S["nall"])
        P.dma("sync", ["kalld"], [kall], out=kall[:].rearrange("p a b c -> p (a b c)"), in_=S["kall"])
        kmx = C.sb("kmx", [128, 16], F32)
        ones1 = C.sb("ones1", [1, 128], F32)
        P.g("memset", [], [ones1], ones1[:], 1.0)
        for br in range(2):
            P.v("tensor_reduce", [kall], [kmx], out=kmx[:, 2 * br:2 * br + 2], in_=kall[:, br].rearrange("p t g -> p g t"), axis=AX.X, op=ALU.max)
        with Ctx(nc, P) as Ck:
            pk1 = Ck.ps("pk1", [128, 512], F32)
            for jx in range(4):
                P.tr([kmx], [pk1], pk1[0:1, jx * 128:(jx + 1) * 128], kmx[:, jx:jx + 1], K["idf"])
            P.v("reduce_max", [pk1], [kmx], out=kmx[0:1, 4:8], in_=pk1[0:1, :].rearrange("p (j x) -> p j x", x=128), axis=AX.X)
            P.mm([ones1, kmx], [pk1], pk1[:, 0:4], ones1[:], kmx[0:1, 4:8])
            P.v("tensor_copy", [pk1], [kmx], out=kmx[:, 8:12], in_=pk1[:, 0:4])
        rstab = C.sbrot("stab", [128, 2, 4, 2], F32, 3)
        tile_stab = {}
        ksT = C.sb("ksT", [128, T], BF16)
        kwT = C.sb("kwT", [128, T], BF16)
        P.dma("sync", [("bkT", 2, tg) for tg in range(8)], [ksT], out=ksT[:], in_=S["bkT"][2])
        P.dma("sync", [("bkT", 3, tg) for tg in range(8)], [kwT], out=kwT[:], in_=S["bkT"][3])
        vsw = C.sb("vsw", [128, NT, 256], BF16)
        P.dma("sync", [("bv_tok", i) for i in range(NT)], [vsw], out=vsw[:], in_=S["bv_tok"].rearrange("(c p) d -> p c d", p=128))
        ntri_ge = C.sb("ntri_ge", [128, 128], BF16)
        ntri_lt = C.sb("ntri_lt", [128, 128], BF16)
        P.v("tensor_scalar", [K["tri_ge"]], [ntri_ge], out=ntri_ge[:], in0=K["tri_ge"][:], scalar1=-1.0, scalar2=-MASKV, op0=ALU.add, op1=ALU.mult)
        P.v("tensor_scalar", [K["tri_lt"]], [ntri_lt], out=ntri_lt[:], in0=K["tri_lt"][:], scalar1=-1.0, scalar2=-MASKV, op0=ALU.add, op1=ALU.mult)
        negw = C.sb("negw", [128, 640], BF16)
        P.g("memset", [], [negw], negw[:], 0.0)
        P.v("tensor_copy", [ntri_lt, negw], [negw], out=negw[:, 0:128], in_=ntri_lt[:])
        P.v("tensor_copy", [ntri_ge, negw], [negw], out=negw[:, 512:640], in_=ntri_ge[:])
        dltci = C.sb("dltci", [128, 256], I32)
        dltc = C.sb("dltc", [128, 256], F32)
        P.g("iota", [], [dltci], dltci[:], pattern=[[-16, 256]], base=0, channel_multiplier=1)
        P.v("tensor_copy", [dltci], [dltc], out=dltc[:], in_=dltci[:])
        dlti = C.sb("dlti", [128, 64], I32)
        dlt = C.sb("dlt", [128, 64], F32)
        P.g("iota", [], [dlti], dlti[:], pattern=[[-64, 64]], base=0, channel_multiplier=1)
        P.v("tensor_copy", [dlti], [dlt], out=dlt[:], in_=dlti[:])
        negm = [[C.sb("negm%d%d" % (par, g), [128, T], BF16) for g in range(2)] for par in range(2)]
        ycmp = C.sb("ycmp", [128, 2, 8, 64], F32)
        R = AttnRes(C)
        rpo = C.psrot("po", [128, 2, 64], F32, 2)
        rpc = C.psrot("pc", [128, 64], F32, 1)
        rgt = C.sbrot("gt", [128, 24], F32, 3)
        rselc = C.sbrot("selc", [128, 256], F32, 2)
        ryb = C.sbrot("yb", [128, 512], F32, 2)
        rpg = C.sbrot("pg", [128, 256], F32, 2)
        re32 = C.sbrot("e32", [128, 256], F32, 2)
        rp16 = C.sbrot("p16", [128, 256], BF16, 2)
        rsc = C.sbrot("sc", [128, 4, 64], F32, 2)
        rm8 = C.sbrot("m8", [128, 16], F32, 2)
        rcf = C.sbrot("cf", [128, 8], F32, 6)
        rcmx = C.sbrot("cmx", [128, 8], F32, 3)
        tile_gt = {}

        def prep(i, g):
            rows = slice(i * 128, (i + 1) * 128)
            t0 = i * 128
            L = (i + 1) * 128
            if g == 0:
                gt = rgt.next()
                P.dma("sync", [("bgate", i)], [gt], out=gt[:], in_=S["bgate"][rows, :])
                selc = rselc.next()
                P.v("tensor_scalar", [dltc], [selc], out=selc[:], in0=dltc[:], scalar1=float(31 - t0), scalar2=None, op0=ALU.is_ge)
                tile_gt[i] = (gt, selc)
                stab = rstab.next()
                tile_stab[i] = stab
                for br in range(2):
                    P.v("tensor_tensor", [nall, kmx], [stab], out=stab[:, br], in0=nall[:, i],
                        in1=kmx[:, 8 + 2 * br:10 + 2 * br].unsqueeze(1).to_broadcast([128, 4, 2]), op=ALU.mult)
                P.v("tensor_scalar", [stab], [stab], out=stab[:], in0=stab[:], scalar1=1e-30, scalar2=None, op0=ALU.add)
                P.s("activation", [stab], [stab], out=stab[:], in_=stab[:], func=AF.Ln)
                P.s("activation", [stab], [stab], out=stab[:], in_=stab[:], func=AF.Exp, scale=0.5)
                P.v("tensor_scalar", [stab], [stab], out=stab[:], in0=stab[:], scalar1=-1.02, scalar2=None, op0=ALU.mult)
            gt, selc = tile_gt[i]
            pg = rpg.next()
            for hp in range(4):
                h = g * 4 + hp
                qh = qT[:, h, t0:t0 + 128]
                ps = R.rps.next()
                P.mm([qT, kcmpT], [ps], ps[:, 0:256], qh, kcmpT[:, :])
                mx = rcmx.next()
                P.v("reduce_max", [ps], [mx], out=mx[:, 0:1], in_=ps[:, 0:256], axis=AX.X)
                P.v("tensor_scalar", [mx], [mx], out=mx[:, 1:2], in0=mx[:, 0:1], scalar1=-1.0, scalar2=None, op0=ALU.mult)
                e32 = re32.next()
                P.s("activation", [ps, mx], [e32], out=e32[:], in_=ps[:, 0:256], func=AF.Exp, bias=mx[:, 1:2], scale=1.0)
                P.v("scalar_tensor_tensor", [e32, selc], [e32, mx], out=e32[:], in0=e32[:], scalar=1.0, in1=selc[:], op0=ALU.mult, op1=ALU.mult,
                    accum_out=mx[:, 2:3])
                P.v("tensor_scalar", [mx], [mx], out=mx[:, 3:4], in0=mx[:, 2:3], scalar1=1e-30, scalar2=None, op0=ALU.add)
                P.v("reciprocal", [mx], [mx], out=mx[:, 4:5], in_=mx[:, 3:4])
                if hp == 0:
                    P.v("tensor_scalar", [e32, mx], [pg], out=pg[:], in0=e32[:], scalar1=mx[:, 4:5], scalar2=None, op0=ALU.mult)
                else:
                    P.v("scalar_tensor_tensor", [e32, mx, pg], [pg], out=pg[:], in0=e32[:], scalar=mx[:, 4:5], in1=pg[:], op0=ALU.mult, op1=ALU.add)
                p16 = rp16.next()
                P.g("tensor_copy", [e32], [p16], out=p16[:], in_=e32[:])
                yield
                pTp = R.rpT.next()
                for mc in range(2):
                    P.tr([p16], [pTp], pTp[:, mc, :], p16[:, mc * 128:(mc + 1) * 128], K["idb"])
                pT = R.rpt.next()
                P.s("copy", [pTp], [pT], out=pT[:, 0:2, :], in_=pTp[:, 0:2, :])
                yield
                pc = rpc.next()
                for mc in range(2):
                    P.mm([pT, vcmp], [pc], pc[:], pT[:, mc, :], vcmp[:, mc, g, :], mc == 0, mc == 1)
                P.v("tensor_tensor", [mx, gt], [mx], out=mx[:, 5:6], in0=mx[:, 4:5], in1=gt[:, h * 3:h * 3 + 1], op=ALU.mult)
                P.v("tensor_scalar", [pc, mx], [("ycmp", i % 2, h)], out=ycmp[:, i % 2, h, :], in0=pc[:], scalar1=mx[:, 5:6], scalar2=None, op0=ALU.mult)
                yield
            nm = negm[i % 2][g]
            if i >= 8:
                sc = rsc.next()
                m8 = rm8.next()
                imp, s1_, s2_, bm = sc[:, 0, :], sc[:, 1, :], sc[:, 2, :], sc[:, 3, :]
                P.v("reduce_sum", [pg], [sc], out=imp, in_=pg[:].rearrange("p (b f) -> p b f", f=4), axis=AX.X)
                P.v("tensor_tensor", [sc, pg], [sc], out=sc[:, 0, 1:64], in0=sc[:, 0, 1:64], in1=pg[:, 3:255:4], op=ALU.add)
                P.v("tensor_scalar", [dlt], [sc], out=s2_, in0=dlt[:], scalar1=float(128 - t0), scalar2=1e6, op0=ALU.is_lt, op1=ALU.mult)
                P.v("tensor_tensor", [sc], [sc], out=s1_, in0=imp, in1=s2_, op=ALU.max)
                P.v("tensor_scalar", [dlt], [sc], out=s2_, in0=dlt[:], scalar1=float(-t0), scalar2=None, op0=ALU.is_ge)
                P.v("tensor_tensor", [sc], [sc], out=s1_, in0=s1_, in1=s2_, op=ALU.mult)
                P.v("tensor_scalar", [sc], [sc], out=s2_, in0=s2_, scalar1=-1.0, scalar2=1e30, op0=ALU.add, op1=ALU.mult)
                P.v("tensor_tensor", [sc], [sc], out=s1_, in0=s1_, in1=s2_, op=ALU.add)
                P.g("memset", [sc], [sc], sc[:, 1, 0:1], 1e6)
                yield
                P.v("max", [sc], [m8], out=m8[:, 0:8], in_=s1_)
                P.v("match_replace", [sc, m8], [sc], out=s2_, in_to_replace=m8[:, 0:8], in_values=s1_, imm_value=NEG)
                P.v("max", [sc], [m8], out=m8[:, 8:16], in_=s2_)
                P.v("tensor_scalar", [sc, m8], [sc], out=bm, in0=s1_, scalar1=m8[:, 15:16], scalar2=MASKV, op0=ALU.is_lt, op1=ALU.mult)
                nb = L // 64
                P.g("tensor_copy", [sc, nm], [nm], out=nm[:, 0:L].rearrange("p (b f) -> p b f", f=64),
                    in_=sc[:, 3, 0:nb].unsqueeze(2).to_broadcast([128, nb, 64]))
                P.g("tensor_tensor", [nm, ntri_ge], [nm], out=nm[:, L - 128:L], in0=nm[:, L - 128:L], in1=ntri_ge[:], op=ALU.add)
            else:
                if i > 0:
                    P.g("memset", [], [nm], nm[:, 0:L - 128], 0.0)
                P.g("tensor_copy", [ntri_ge, nm], [nm], out=nm[:, L - 128:L], in_=ntri_ge[:])

        pipe = Pipe()
        work = [(i, g) for i in range(NT) for g in range(2)]
        for _ in prep(0, 0):
            pass
        ybs = {}
        for wi_, (i, g) in enumerate(work):
            rows = slice(i * 128, (i + 1) * 128)
            t0 = i * 128
            L = (i + 1) * 128
            nxt = prep(*work[wi_ + 1]) if wi_ + 1 < len(work) else None
            pipe.filler = nxt
            npush = 4 * ((L + 511) // 512 + (min(L, 640) + 511) // 512)
            pipe.rate = -(-16 // npush)
            if g == 0:
                ybs[i] = ryb.next()
            yb = ybs[i]
            gt, _selc = tile_gt[i]
            nm = negm[i % 2][g]
            for hp in range(4):
                h = g * 4 + hp
                qh = qT[:, h, t0:t0 + 128]
                stab = tile_stab[i]
                po = rpo.next()
                cf = rcf.next()
                terms = [(qh, lambda c0, n: ksT[:, c0:c0 + n], [qT, ksT])]
                mterm = (K["idb"][:], lambda c0, n, nm=nm: nm[:, c0:c0 + n], [nm, K["idb"]])

                def pv_s(pT, rd, kt, first, last, po=po, g=g):
                    P.mm(rd + [vsw], [po], po[:, 0, :], pT, vsw[:, kt, g * 64:(g + 1) * 64], first, last)

                def done_s(rs, rinv, cf=cf, gt=gt, h=h):
                    P.v("tensor_tensor", [rs, gt], [cf], out=cf[:, 1:2], in0=rinv, in1=gt[:, h * 3 + 1:h * 3 + 2], op=ALU.mult)
                for it in attn_items(P, K, R, terms, mterm, 0, L, pv_s, done_s, negm_ap=stab[:, 0, hp, g:g + 1], negm_reads=[stab]):
                    pipe.push(it)
                k0 = max(0, (i - 4) * 128)
                woff = 640 - (L - k0)
                terms = [(qh, lambda c0, n: kwT[:, c0:c0 + n], [qT, kwT])]
                mterm = (K["idb"][:], lambda c0, n, k0=k0, woff=woff: negw[:, woff + c0 - k0:woff + c0 - k0 + n], [negw, K["idb"]])

                def pv_w(pT, rd, kt, first, last, po=po, g=g, k0=k0):
                    P.mm(rd + [vsw], [po], po[:, 1, :], pT, vsw[:, k0 // 128 + kt, 128 + g * 64:128 + (g + 1) * 64], first, last)

                def done_w(rs, rinv, cf=cf, gt=gt, h=h, po=po, yb=yb, i=i, rows=rows):
                    P.v("tensor_tensor", [rs, gt], [cf], out=cf[:, 2:3], in0=rinv, in1=gt[:, h * 3 + 2:h * 3 + 3], op=ALU.mult)
                    ys = yb[:, h * 64:(h + 1) * 64]
                    P.v("scalar_tensor_tensor", [po, cf, ("ycmp", i % 2, h)], [yb], out=ys, in0=po[:, 0, :], scalar=cf[:, 1:2], in1=ycmp[:, i % 2, h, :],
                        op0=ALU.mult, op1=ALU.add)
                    P.v("scalar_tensor_tensor", [po, cf, yb], [yb], out=ys, in0=po[:, 1, :], scalar=cf[:, 2:3], in1=ys, op0=ALU.mult, op1=ALU.add)
                    if h == 7:
                        P.dma("gpsimd", [yb], [("attn", "b", i)], out=S["attn"][rows, 512:1024], in_=yb[:])
                for it in attn_items(P, K, R, terms, mterm, k0, L, pv_w, done_w, negm_ap=stab[:, 1, hp, g:g + 1], negm_reads=[stab]):
                    pipe.push(it)
            if nxt is not None:
                for _ in nxt:
                    pass
        pipe.flush()


W_SPECS = dict(
    x=[T, D], p=[DEPTH, T, 256],
    e_w_in=[2, 1024, 3360], e_a_conv=[2, 4, 1024], e_a_i_b=[2, 4], e_a_f_b=[2, 4], e_a_norm=[2, 512],
    e_b_cmp_pos=[2, 2, 32, 64], e_b_cmp_w1=[2, 2, 2048, 128], e_b_cmp_w2=[2, 2, 128, 64], e_b_g_b=[2, 24],
    e_w_out=[2, 1024, 1024], o_w_in=[2, 1024, 904], o_q_norm=[2, 512], o_kv_norm=[2, 256], o_w_qb=[2, 512, 1536],
    o_w_uk=[2, 256, 8, 128], o_w_uv=[2, 256, 8, 128], o_w_iq=[2, 512, 512], o_ik_g=[2, 64], o_ik_b=[2, 64],
    o_w_out=[2, 1024, 1024], ln1_g=[4, 1024], ln1_b=[4, 1024], ln2_g=[4, 1024], ln2_b=[4, 1024],
    mlp_w1=[4, 1024, 4096], mlp_w2=[4, 4096, 1024], ple_gate_w=[4, 1024, 1024], ple_w=[4, 256, 1024], rope_inv=[48])


def rope_inv_table():
    a = (10000.0 ** (-np.arange(0, 64, 2, dtype=np.float32) / np.float32(64))).astype(np.float32)
    b = (10000.0 ** (-np.arange(0, 32, 2, dtype=np.float32) / np.float32(32))).astype(np.float32)
    return np.concatenate([a, b]).astype(np.float32)


def build(debug_outs=(), phases=None, layers=(0, 1, 2, 3)):
    nc = bass.Bass("TRN2", target_bir_lowering=False)
    dbg = set(debug_outs)
    allp = phases is None

    def on(p):
        return allp or p in phases

    def dram(name, shape, dt, kind=None):
        if kind is None:
            kind = "ExternalOutput" if name in dbg else "Internal"
        return nc.dram_tensor(name, shape, dt, kind=kind).ap()

    W = {k: dram(k, s, F32, "ExternalInput") for k, s in W_SPECS.items()}
    W["positions"] = dram("positions", [T], I32, "ExternalInput")
    out = dram("out", [T, D], F32, "ExternalOutput")
    S = dict(
        v_tok=dram("v_tok", [T, 512], BF16), sigo=dram("sigo", [T, 512], F32), bv_tok=dram("bv_tok", [T, 256], BF16),
        bgate=dram("bgate", [T, 24], F32), gsc=dram("gsc", [3, 4, T], F32), qkT=dram("qkT", [8, 128, T], BF16),
        bqT=dram("bqT", [4, 128, T], BF16), bkT=dram("bkT", [4, 128, T], BF16), attn=dram("attn", [T, 1024], F32),
        hA=dram("hA", [T, D], F32), h1=dram("h1", [T, D], F32), h2=dram("h2", [T, D], F32),
        qaT=dram("qaT", [16, 128, T], BF16), qrT=dram("qrT", [4, 128, T], BF16), qiT=dram("qiT", [4, 128, T], BF16),
        kT=dram("kT", [4, 128, T], BF16), ckv=dram("ckv", [T, 256], BF16), wi=dram("wi", [T, 16], F32),
        hB=dram("hB", [T, D], F32), qn2=dram("qn2", [T, 8], F32), kn2=dram("kn2", [128, NT], F32),
        nall=dram("nall", [128, NT * 8], F32), kall=dram("kall", [128, 4 * NT], F32),
    )
    with ExitStack() as st:
        P = Prog(nc)
        P.setup(st)
        with Ctx(nc, P) as C0:
            K = make_consts(C0, P)
            h_in = W["x"]
            wrote_out = False
            for n, li in enumerate(layers):
                j = li // 2
                last = (n == len(layers) - 1)
                if li % 2 == 0:
                    if on("e1"):
                        phase_e1(nc, P, K, S, W, j, h_in)
                    if on("e2"):
                        phase_e2(nc, P, K, S, W, j)
                    if on("e3"):
                        phase_e3(nc, P, K, S, W, j)
                    w_out = W["e_w_out"][j]
                else:
                    if on("o1"):
                        phase_o1(nc, P, K, S, W, j, h_in)
                    if on("o3"):
                        phase_o3(nc, P, K, S, W, j)
                    w_out = W["o_w_out"][j]
                if on("ta"):
                    phase_tail_a(nc, P, K, S, w_out, W["ln1_g"][li], W["ln1_b"][li], h_in, S["h1"])
                if on("tb"):
                    phase_tail_b(nc, P, K, S, W["mlp_w1"][li], W["mlp_w2"][li], W["ln2_g"][li], W["ln2_b"][li], S["h1"], S["h2"])
                if on("tc"):
                    h_out = out if last else (S["hA"] if n % 2 == 0 else S["hB"])
                    phase_tail_c(nc, P, K, S, W["ple_gate_w"][li], W["ple_w"][li], W["p"][li], S["h2"], h_out, last)
                    wrote_out = wrote_out or last
                    h_in = h_out
            if not wrote_out:
                zt = C0.sb("zt", [128, 1024], F32)
                P.g("memset", [], [zt], zt[:], 0.0)
                P.dma("sync", [zt], ["out"], out=out[0:128, :], in_=zt[:], is_output=True)
        P.finish()
    return nc


_NC_CACHE = {}


def kernel(**inputs):
    if "nc" not in _NC_CACHE:
        _NC_CACHE["nc"] = build()
    nc = _NC_CACHE["nc"]
    B = inputs["x"].shape[0]
    rinv = rope_inv_table()
    in_maps = []
    for b in range(B):
        m = {}
        for k in W_SPECS:
            if k == "x":
                m[k] = np.ascontiguousarray(inputs["x"][b], dtype=np.float32)
            elif k == "p":
                m[k] = np.ascontiguousarray(inputs["p"][:, b], dtype=np.float32)
            elif k == "rope_inv":
                m[k] = rinv
            else:
                m[k] = np.ascontiguousarray(inputs[k], dtype=np.float32)
        m["positions"] = np.ascontiguousarray(inputs["positions"][b], dtype=np.int32)
        in_maps.append(m)
    res = run_bass_kernel_spmd(nc, in_maps, core_ids=list(range(B)))
    return np.stack([np.asarray(r["out"], dtype=np.float32) for r in res.results], axis=0)
```

```python
from contextlib import ExitStack
import numpy as np
import concourse.bass as bass
import concourse.mybir as mybir
from concourse.bass_utils import run_bass_kernel_spmd

F32 = mybir.dt.float32
BF16 = mybir.dt.bfloat16
I32 = mybir.dt.int32
ALU = mybir.AluOpType
AF = mybir.ActivationFunctionType
AX = mybir.AxisListType

T = 4096
D = 1024
NT = T // 128
DEPTH = 4
DFF = 4096
DN_ALPHA = (2.0 * DEPTH) ** 0.25
LN_EPS = 1e-5
NEG = -1e30
N_DMA_SEMS = 8


class Prog:
    ENGS = ("tensor", "vector", "scalar", "gpsimd", "sync")

    def __init__(self, nc):
        self.nc = nc
        self.ops = {k: [] for k in self.ENGS}
        self.count = {k: 0 for k in self.ENGS}
        self.waited = {k: {} for k in self.ENGS}
        self.sems = {}
        self.writers = {}
        self.readers = {}
        self.dma_val = [0] * N_DMA_SEMS
        self.dma_rr = 0
        self.out_tokens = []

    def setup(self, stack):
        for k in self.ENGS:
            self.sems[k] = stack.enter_context(self.nc.semaphore("s_" + k))
        for i in range(N_DMA_SEMS):
            self.sems["d%d" % i] = stack.enter_context(self.nc.semaphore("d_%d" % i))

    @staticmethod
    def _key(k):
        if isinstance(k, (str, tuple)):
            return k
        t = getattr(k, "tensor", k)
        return t.name

    def _wait(self, eng, s, v):
        w = self.waited[eng]
        if w.get(s, 0) < v:
            w[s] = v
            self.ops[eng].append(("wait", self.sems[s], v))

    def _deps(self, eng, reads, writes):
        need = {}
        for k in reads:
            for s, v in self.writers.get(k, {}).items():
                if need.get(s, 0) < v:
                    need[s] = v
        for k in writes:
            for d in (self.writers.get(k, {}), self.readers.get(k, {})):
                for s, v in d.items():
                    if need.get(s, 0) < v:
                        need[s] = v
        for s, v in need.items():
            if eng == "tensor" and s == "tensor":
                continue
            self._wait(eng, s, v)

    def _record(self, tok, reads, writes):
        s, v = tok
        for k in reads:
            self.readers.setdefault(k, {})[s] = v
        for k in writes:
            self.writers[k] = {s: v}
            self.readers[k] = {}

    def op(self, eng, meth, reads, writes, *args, **kw):
        reads = [self._key(k) for k in reads]
        writes = [self._key(k) for k in writes]
        self._deps(eng, reads, writes)
        self.count[eng] += 1
        self.ops[eng].append(("op", (meth, args, kw), self.sems[eng], 1))
        self._record((eng, self.count[eng]), reads, writes)

    def mm(self, reads, writes, out, lhsT, rhs, start=True, stop=True):
        self.op("tensor", "matmul", reads, writes, out, lhsT=lhsT, rhs=rhs, start=start, stop=stop)

    def tr(self, reads, writes, out, in_, ident):
        self.op("tensor", "transpose", reads + [ident], writes, out=out, in_=in_, identity=ident[:])

    def v(self, meth, reads, writes, **kw):
        self.op("vector", meth, reads, writes, **kw)

    def s(self, meth, reads, writes, **kw):
        self.op("scalar", meth, reads, writes, **kw)

    def g(self, meth, reads, writes, *args, **kw):
        self.op("gpsimd", meth, reads, writes, *args, **kw)

    def dma(self, eng, reads, writes, out, in_, is_output=False, **kw):
        fn = ("dma_start", (), dict(out=out, in_=in_, **kw))
        reads = [self._key(k) for k in reads]
        writes = [self._key(k) for k in writes]
        i = self.dma_rr
        self.dma_rr = (self.dma_rr + 1) % N_DMA_SEMS
        sname = "d%d" % i
        self._deps(eng, reads, writes)
        if self.dma_val[i]:
            self._wait(eng, sname, self.dma_val[i])
        self.dma_val[i] += 16
        self.ops[eng].append(("op", fn, self.sems[sname], 16))
        self._record((sname, self.dma_val[i]), reads, writes)
        if is_output:
            self.out_tokens.append((sname, self.dma_val[i]))

    def barrier(self):
        for e in self.ENGS:
            for s in self.ENGS:
                if s != e and self.count[s]:
                    self._wait(e, s, self.count[s])
            for i in range(N_DMA_SEMS):
                if self.dma_val[i]:
                    self._wait(e, "d%d" % i, self.dma_val[i])
        for e in self.ENGS:
            if self.count[e]:
                self._wait(e, e, self.count[e])
        self.writers.clear()
        self.readers.clear()

    def finish(self):
        for s, v in self.out_tokens:
            self._wait("sync", s, v)
        ops = self.ops

        def replay(e, lst):
            for o in lst:
                if o[0] == "wait":
                    e.wait_ge(o[1], o[2])
                else:
                    meth, args, kw = o[1]
                    try:
                        ins = getattr(e, meth)(*args, **kw)
                    except Exception:
                        print("FAILED OP", meth, args, kw)
                        raise
                    ins.then_inc(o[2], o[3])

        with self.nc.Block() as block:
            @block.tensor
            def _(e):
                replay(e, ops["tensor"])

            @block.vector
            def _(e):
                replay(e, ops["vector"])

            @block.scalar
            def _(e):
                replay(e, ops["scalar"])

            @block.gpsimd
            def _(e):
                replay(e, ops["gpsimd"])

            @block.sync
            def _(e):
                replay(e, ops["sync"])


class Rot:
    def __init__(self, bufs):
        self.bufs = bufs
        self.i = 0

    def next(self):
        b = self.bufs[self.i % len(self.bufs)]
        self.i += 1
        return b


class Ctx:
    uid = 0

    def __init__(self, nc, P):
        self.nc = nc
        self.P = P
        self.st = ExitStack()

    def __enter__(self):
        self.st.__enter__()
        return self

    def __exit__(self, *a):
        self.P.barrier()
        return self.st.__exit__(*a)

    def sb(self, name, shape, dt):
        Ctx.uid += 1
        return self.st.enter_context(self.nc.sbuf_tensor("%s_%d" % (name, Ctx.uid), shape, dt))

    def ps(self, name, shape, dt=F32):
        Ctx.uid += 1
        return self.st.enter_context(self.nc.psum_tensor("%s_%d" % (name, Ctx.uid), shape, dt))

    def sbrot(self, name, shape, dt, n=2):
        return Rot([self.sb(name + str(i), shape, dt) for i in range(n)])

    def psrot(self, name, shape, dt=F32, n=2):
        return Rot([self.ps(name + str(i), shape, dt) for i in range(n)])


def make_consts(C, P):
    k = {}
    idf = C.sb("identf", [128, 128], F32)
    idb = C.sb("identb", [128, 128], BF16)
    P.g("memset", [], [idf], idf[:], 1.0)
    P.g("affine_select", [idf], [idf], out=idf[:], in_=idf[:], pattern=[[-1, 128]], compare_op=ALU.is_equal,
        fill=0.0, base=0, channel_multiplier=1)
    P.v("tensor_copy", [idf], [idb], out=idb[:], in_=idf[:])
    k["idf"], k["idb"] = idf, idb
    for name, mult, patt, base in (("tri_le", -1, 1, 0), ("tri_ge", 1, -1, 0), ("tri_lt", -1, 1, -1)):
        t = C.sb(name, [128, 128], F32)
        P.g("memset", [], [t], t[:], 1.0)
        P.g("affine_select", [t], [t], out=t[:], in_=t[:], pattern=[[patt, 128]], compare_op=ALU.is_ge, fill=0.0,
            base=base, channel_multiplier=mult)
        k[name] = t
    return k


def load_transposed(P, K, src, xT, nk, key, rot_in, rot_ps, tiles=range(NT), t0=0):
    for i in tiles:
        xt = rot_in.next()
        P.dma("sync", [], [xt], out=xt[:, 0:nk * 128], in_=src[i * 128:(i + 1) * 128, :])
        for half in range((nk + 3) // 4):
            n = min(4, nk - half * 4)
            pt = rot_ps.next()
            for jj in range(n):
                c = half * 4 + jj
                P.tr([xt], [pt], pt[:, jj, :], xt[:, c * 128:(c + 1) * 128], K["idf"])
            col = (i - t0) * 128
            dst = xT[:, half * 4:half * 4 + n, col:col + 128]
            if half % 2 == 0:
                P.v("tensor_copy", [pt], [(key, i)], out=dst, in_=pt[:, 0:n, :])
            else:
                P.s("copy", [pt], [(key, i)], out=dst, in_=pt[:, 0:n, :])


def run_staged(n, body):
    prev = None
    for i in range(n):
        g = body(i)
        next(g, None)
        if prev is not None:
            for _ in prev:
                pass
        prev = g
    if prev is not None:
        for _ in prev:
            pass


def run_interleaved(n, body, k):
    live = []
    nxt = 0
    while live or nxt < n:
        while len(live) < k and nxt < n:
            live.append(body(nxt))
            nxt += 1
        for g in list(live):
            try:
                next(g)
            except StopIteration:
                live.remove(g)


def load_w_bf16(P, dst, src, nk, key=None):
    for kc in range(nk):
        P.dma("gpsimd", [], [key or dst], out=dst[:, kc, :], in_=src[kc * 128:(kc + 1) * 128, :])


def phase_e1(nc, P, K, S, W, j, h_in):
    with Ctx(nc, P) as C:
        hT = C.sb("hT", [128, 8, T], BF16)
        w = C.sb("w_in", [128, 8, 3360], BF16)
        wq = C.sb("w_q", [128, 8, 4, 2, 64], BF16)
        load_w_bf16(P, w, W["e_w_in"][j], 8)
        for kc in range(8):
            for g in range(2):
                P.dma("gpsimd", [], [wq], out=wq[:, kc, :, g, :],
                      in_=W["e_w_in"][j][kc * 128:(kc + 1) * 128, 2056 + g * 256:2056 + (g + 1) * 256].rearrange("p (c d) -> p c d", c=4))
        rps = C.psrot("ps", [128, 512], F32, 3)
        rpt = C.psrot("pt", [128, 4, 128], F32, 2)
        with Ctx(nc, P) as C1:
            rin = C1.sbrot("hin", [128, 1024], F32, 2)
            load_transposed(P, K, h_in, hT, 8, "hT", rin, rpt)

        def feat_mm(ps, lhs_fn, tg, m=128):
            for kc in range(8):
                P.mm([("hT", i2) for i2 in range(tg * 4, tg * 4 + 4)] + [w, wq], [ps], ps[0:m, :], lhs_fn(kc),
                     hT[:, kc, tg * 512:(tg + 1) * 512], kc == 0, kc == 7)

        with Ctx(nc, P) as C1:
            bi = C1.sb("bi", [4, 1], F32)
            bfn = C1.sb("bfn", [4, 1], F32)
            P.dma("sync", [], [bi], out=bi[:], in_=W["e_a_i_b"][j].rearrange("(h o) -> h o", o=1))
            P.dma("sync", [], [bfn], out=bfn[:], in_=W["e_a_f_b"][j].rearrange("(h o) -> h o", o=1))
            P.v("tensor_scalar", [bfn], [bfn], out=bfn[:], in0=bfn[:], scalar1=-1.0, scalar2=None, op0=ALU.mult)
            ig = C1.sb("ig", [4, T], F32)
            sp = C1.sb("sp", [4, T], F32)
            bneg = C1.sb("bneg", [4, T], F32)
            cst = C1.sb("cst", [4, T], F32)
            for tg in range(8):
                cs = slice(tg * 512, (tg + 1) * 512)
                ps = rps.next(); feat_mm(ps, lambda kc: w[:, kc, 2048:2052], tg, 4)
                P.s("activation", [ps, bi], [ig], out=ig[:, cs], in_=ps[0:4, :], func=AF.Identity, bias=bi[:], scale=1.0)
                ps = rps.next(); feat_mm(ps, lambda kc: w[:, kc, 2052:2056], tg, 4)
                P.s("activation", [ps, bfn], [sp], out=sp[:, cs], in_=ps[0:4, :], func=AF.Exp, bias=bfn[:], scale=-1.0)
            P.s("activation", [sp], [sp], out=sp[:], in_=sp[:], func=AF.Ln, bias=1.0, scale=1.0)
            P.g("memset", [], [cst], cst[:], 1.0)
            P.v("tensor_tensor_scan", [cst, sp], [bneg], out=bneg[:], data0=cst[:], data1=sp[:], initial=0.0, op0=ALU.mult, op1=ALU.add)
            P.v("tensor_tensor", [ig, bneg], [ig], out=ig[:], in0=ig[:], in1=bneg[:], op=ALU.add)
            P.g("memset", [cst], [cst], cst[:], 0.0)
            P.v("tensor_tensor_scan", [cst, ig], [sp], out=sp[:], data0=cst[:], data1=ig[:], initial=0.0, op0=ALU.add, op1=ALU.max)
            P.v("tensor_tensor", [bneg, sp], [bneg], out=bneg[:], in0=bneg[:], in1=sp[:], op=ALU.subtract)
            P.v("tensor_scalar", [sp], [sp], out=sp[:], in0=sp[:], scalar1=-1.0, scalar2=None, op0=ALU.mult)
            P.dma("sync", [ig], ["gsc0"], out=S["gsc"][0], in_=ig[:])
            P.dma("sync", [sp], ["gsc1"], out=S["gsc"][1], in_=sp[:])
            P.dma("sync", [bneg], ["gsc2"], out=S["gsc"][2], in_=bneg[:])

        with Ctx(nc, P) as C1:
            convw = C1.sb("convw", [128, 8, 4], F32)
            for kk in range(4):
                P.dma("sync", [], [convw], out=convw[:, :, kk], in_=W["e_a_conv"][j][kk].rearrange("(c p) -> p c", p=128),
                      allow_slow_non_contiguous=True)
            bg = C1.sb("bg", [128, 24], F32)
            P.dma("sync", [], [bg], out=bg[:], in_=W["e_b_g_b"][j].partition_broadcast(128))
            rst = C1.sbrot("stg", [128, 512], F32, 2)
            rstb = C1.sbrot("stgb", [128, 512], BF16, 3)
            for i in range(NT):
                rows = slice(i * 128, (i + 1) * 128)

                def tok_mm(ps, c0, n, pc0=0):
                    for kc in range(8):
                        P.mm([("hT", i), w], [ps], ps[:, pc0:pc0 + n], hT[:, kc, i * 128:(i + 1) * 128], w[:, kc, c0:c0 + n], kc == 0, kc == 7)
                ps = rps.next(); tok_mm(ps, 1024, 512)
                sb_ = rstb.next()
                P.s("copy", [ps], [sb_], out=sb_[:], in_=ps[:])
                P.dma("gpsimd", [sb_], [("v_tok", i)], out=S["v_tok"][rows, :], in_=sb_[:])
                ps = rps.next(); tok_mm(ps, 1536, 512)
                st_ = rst.next()
                P.s("activation", [ps], [st_], out=st_[:], in_=ps[:], func=AF.Sigmoid)
                P.dma("gpsimd", [st_], [("sigo", i)], out=S["sigo"][rows, :], in_=st_[:])
                ps = rps.next(); tok_mm(ps, 2952, 128, 0); tok_mm(ps, 3208, 128, 128); tok_mm(ps, 3336, 24, 256)
                sb_ = rstb.next()
                P.v("tensor_copy", [ps], [sb_], out=sb_[:, 0:256], in_=ps[:, 0:256])
                P.dma("gpsimd", [sb_], [("bv_tok", i)], out=S["bv_tok"][rows, :], in_=sb_[:, 0:256])
                st_ = rst.next()
                P.v("tensor_tensor", [ps, bg], [st_], out=st_[:, 0:24], in0=ps[:, 256:280], in1=bg[:], op=ALU.add)
                P.s("activation", [st_], [st_], out=st_[:, 32:56], in_=st_[:, 0:24], func=AF.Sigmoid)
                P.dma("gpsimd", [st_], [("bgate", i)], out=S["bgate"][rows, :], in_=st_[:, 32:56])

            xpad = C1.sb("xpad", [128, 3 + T], F32)
            P.g("memset", [], [("xpad", -1)], xpad[:, 0:3], 0.0)
            racc = C1.sbrot("acc", [128, 512], F32, 2)
            for c in range(8):
                for tg in range(8):
                    ps = rps.next(); feat_mm(ps, lambda kc: w[:, kc, c * 128:(c + 1) * 128], tg)
                    P.s("copy", [ps], [("xpad", tg)], out=xpad[:, 3 + tg * 512:3 + (tg + 1) * 512], in_=ps[:])
                    acc = racc.next()
                    t0 = tg * 512
                    rd = [("xpad", tg - 1), ("xpad", tg), convw]
                    P.v("tensor_scalar", rd, [acc], out=acc[:], in0=xpad[:, t0:t0 + 512], scalar1=convw[:, c, 0:1], scalar2=None, op0=ALU.mult)
                    for jj in range(1, 4):
                        P.v("scalar_tensor_tensor", rd + [acc], [acc], out=acc[:], in0=xpad[:, t0 + jj:t0 + jj + 512],
                            scalar=convw[:, c, jj:jj + 1], in1=acc[:], op0=ALU.mult, op1=ALU.add)
                    ob = rstb.next()
                    P.s("activation", [acc], [ob], out=ob[:], in_=acc[:], func=AF.Silu)
                    P.dma("gpsimd", [ob], [("qkT", c, tg)], out=S["qkT"][c][:, t0:t0 + 512], in_=ob[:])
            blk = C1.sb("blk", [128, 2], BF16)
            P.g("memset", [], [blk], blk[:], 0.0)
            P.g("memset", [blk], [blk], blk[0:64, 0:1], 1.0)
            P.g("memset", [blk], [blk], blk[64:128, 1:2], 1.0)
            nall = C1.sb("nall", [128, NT, 8], F32)
            kall = C1.sb("kall", [128, 2, NT, 2], F32)
            rsqn = C1.sbrot("sqn", [128, 512], BF16, 2)
            rpn = C1.psrot("pn", [128, 8], F32, 1)

            def norms(ob, dst_fn):
                sq = rsqn.next()
                P.g("tensor_tensor", [ob], [sq], out=sq[:], in0=ob[:], in1=ob[:], op=ALU.mult)
                pn = rpn.next()
                for k in range(4):
                    P.mm([sq, blk], [pn], pn[:, 2 * k:2 * k + 2], sq[:, k * 128:(k + 1) * 128], blk[:])
                dst_fn(pn)
            for c in range(4):
                for tg in range(8):
                    ps = rps.next(); feat_mm(ps, lambda kc: wq[:, kc, c].rearrange("p g d -> p (g d)"), tg)
                    ob = rstb.next()
                    P.s("mul", [ps], [ob], out=ob[:], in_=ps[:], mul=0.125)
                    P.dma("gpsimd", [ob], [("bqT", c, tg)], out=S["bqT"][c][:, tg * 512:(tg + 1) * 512], in_=ob[:])
                    norms(ob, lambda pn: P.v("tensor_copy", [pn], [nall], out=nall[:, tg * 4:(tg + 1) * 4, 2 * c:2 * c + 2],
                                             in_=pn[:].rearrange("p (k g) -> p k g", g=2)))
            for n, c0 in enumerate((2568, 2696, 2824, 3080)):
                for tg in range(8):
                    ps = rps.next(); feat_mm(ps, lambda kc: w[:, kc, c0:c0 + 128], tg)
                    ob = rstb.next()
                    P.v("tensor_copy", [ps], [ob], out=ob[:], in_=ps[:])
                    P.dma("gpsimd", [ob], [("bkT", n, tg)], out=S["bkT"][n][:, tg * 512:(tg + 1) * 512], in_=ob[:])
                    if n >= 2:
                        norms(ob, lambda pn: P.v("tensor_copy", [pn], [kall], out=kall[:, n - 2, tg * 4:(tg + 1) * 4, :],
                                                 in_=pn[:].rearrange("p (k g) -> p k g", g=2)))
            P.dma("sync", [nall], ["nalld"], out=S["nall"], in_=nall[:].rearrange("p a b -> p (a b)"))
            P.dma("sync", [kall], ["kalld"], out=S["kall"], in_=kall[:].rearrange("p a b c -> p (a b c)"))


def phase_e2(nc, P, K, S, W, j):
    NC_ = NT
    with Ctx(nc, P) as C:
        rows = C.sb("rows", [4, 3, T], F32)
        for r in range(3):
            P.dma("sync", ["gsc%d" % r], [rows], out=rows[:, r, :], in_=S["gsc"][r])
        sel = C.sb("sel", [4, 4, 128], F32)
        P.g("memset", [], [sel], sel[:], 1.0)
        P.g("affine_select", [sel], [sel], out=sel[:], in_=sel[:], pattern=[[-1, 4], [0, 128]], compare_op=ALU.is_equal, fill=0.0,
            base=0, channel_multiplier=1)
        gnorm = C.sb("gnorm", [128, 512], F32)
        P.dma("sync", [], [gnorm], out=gnorm[:], in_=W["e_a_norm"][j].partition_broadcast(128))
        maskT = C.sb("maskT", [128, 128], F32)
        P.v("tensor_scalar", [K["tri_le"]], [maskT], out=maskT[:], in0=K["tri_le"][:], scalar1=128.0 ** -0.5, scalar2=None, op0=ALU.mult)

        rps_a = C.psrot("psa", [128, 512], F32, 1)
        rps_s = C.psrot("pss", [128, 128], F32, 2)
        rps_o = C.psrot("pso", [128, 132], F32, 2)
        rps_i = C.psrot("psi", [128, 132], F32, 1)
        rps_k = C.psrot("psk", [128, 128], BF16, 1)
        rET = C.sbrot("ET", [128, 128], F32, 2)
        rETm = C.sbrot("ETm", [128, 128], F32, 2)
        rPT = C.sbrot("PT", [128, 128], BF16, 2)
        rksc = C.sbrot("ksc", [128, 128], BF16, 2)
        rintra = C.sbrot("intra", [128, 132], F32, 2)

        for h in range(4):
          with Ctx(nc, P) as CH:
            qT = CH.sb("qT", [128, T], BF16)
            kT = CH.sb("kT", [128, T], BF16)
            vaug = CH.sb("vaug", [128, NC_, 132], BF16)
            nd = CH.sb("nd", [128, NC_, 132], F32)
            P.dma("sync", [("qkT", h, tg) for tg in range(8)], [qT], out=qT[:], in_=S["qkT"][h])
            P.dma("sync", [("qkT", 4 + h, tg) for tg in range(8)], [kT], out=kT[:], in_=S["qkT"][4 + h])
            P.g("memset", [], [vaug], vaug[:, :, 128:132], 1.0)
            P.dma("sync", [("v_tok", i) for i in range(NT)], [vaug], out=vaug[:, :, 0:128],
                  in_=S["v_tok"][:, h * 128:(h + 1) * 128].rearrange("(c p) d -> p c d", p=128))
            cols = CH.sb("cols", [128, 3, NC_], F32)
            for r in range(3):
                pc = rps_a.next()
                for c in range(NC_):
                    P.mm([rows, sel], [pc], pc[:, c:c + 1], rows[:, r, c * 128:(c + 1) * 128], sel[:, h, 0:1])
                P.v("tensor_copy", [pc], [cols], out=cols[:, r, :], in_=pc[:, 0:NC_])
            ends = CH.sb("ends", [128, 1 + NC_], F32)
            pc = rps_a.next()
            P.mm([rows, sel], [pc], pc[:, 0:NC_], sel[:, h, :], rows[:, 1, 127::128])
            P.g("memset", [], [ends], ends[:, 0:1], 0.0)
            P.v("tensor_copy", [pc, ends], [ends], out=ends[:, 1:1 + NC_], in_=pc[:, 0:NC_])
            wcol = CH.sb("wcol", [128, NC_], F32)
            est = CH.sb("est", [128, NC_], F32)
            eint = CH.sb("eint", [128, NC_], F32)
            enm = CH.sb("enm", [128, NC_], F32)
            P.v("tensor_tensor", [cols, ends], [wcol], out=wcol[:], in0=cols[:, 0, :], in1=ends[:, 1:1 + NC_], op=ALU.add)
            P.s("activation", [wcol], [wcol], out=wcol[:], in_=wcol[:], func=AF.Exp)
            P.v("tensor_tensor", [ends], [est], out=est[:], in0=ends[:, 1:1 + NC_], in1=ends[:, 0:NC_], op=ALU.subtract)
            P.s("activation", [est], [est], out=est[:], in_=est[:], func=AF.Exp)
            P.v("tensor_tensor", [cols, ends], [eint], out=eint[:], in0=cols[:, 1, :], in1=ends[:, 0:NC_], op=ALU.subtract)
            P.s("activation", [eint], [eint], out=eint[:], in_=eint[:], func=AF.Exp)
            P.v("tensor_scalar", [eint], [eint], out=eint[:], in0=eint[:], scalar1=128.0 ** -0.5, scalar2=None, op0=ALU.mult)
            P.s("activation", [cols], [enm], out=enm[:], in_=cols[:, 2, :], func=AF.Exp)

            CT = CH.sb("CT", [128, 132], F32)
            CTb = CH.sb("CTb", [128, 132], BF16)
            P.g("memset", [], [CT], CT[:], 0.0)
            P.g("memset", [], [CTb], CTb[:], 0.0)
            for c in range(NC_):
                cs = slice(c * 128, (c + 1) * 128)
                pg = rps_s.next()
                P.mm([rows, sel], [pg], pg[:], sel[:, h, :], rows[:, 1, cs])
                ET = rET.next()
                P.s("activation", [pg, cols], [ET], out=ET[:], in_=pg[:], func=AF.Exp, bias=cols[:, 0, c:c + 1], scale=1.0)
                ETm = rETm.next()
                P.g("tensor_tensor", [ET, maskT], [ETm], out=ETm[:], in0=ET[:], in1=maskT[:], op=ALU.mult)
                pst = rps_s.next()
                P.mm([kT, qT], [pst], pst[:], kT[:, cs], qT[:, cs])
                PT = rPT.next()
                P.v("tensor_tensor", [pst, ETm], [PT], out=PT[:], in0=pst[:], in1=ETm[:], op=ALU.mult)
                po = rps_o.next()
                P.mm([PT, vaug], [po], po[:, 0:129], PT[:], vaug[:, c, 0:129])
                pi = rps_i.next()
                P.mm([qT, CTb], [pi], pi[:, 0:129], qT[:, cs], CTb[:, 0:129])
                intra = rintra.next()
                P.s("copy", [po], [intra], out=intra[:, 0:129], in_=po[:, 0:129])
                P.v("scalar_tensor_tensor", [pi, intra, eint], [("nd", c)], out=nd[:, c, 0:129], in0=pi[:, 0:129], scalar=eint[:, c:c + 1],
                    in1=intra[:, 0:129], op0=ALU.mult, op1=ALU.add)
                pk = rps_k.next()
                P.tr([kT], [pk], pk[:], kT[:, cs], K["idb"])
                ksc = rksc.next()
                P.s("activation", [pk, wcol], [ksc], out=ksc[:], in_=pk[:], func=AF.Copy, scale=wcol[:, c:c + 1])
                pu = rps_o.next()
                P.mm([ksc, vaug], [pu], pu[:, 0:129], ksc[:], vaug[:, c, 0:129])
                P.v("scalar_tensor_tensor", [CT, pu, est], [CT], out=CT[:, 0:129], in0=CT[:, 0:129], scalar=est[:, c:c + 1],
                    in1=pu[:, 0:129], op0=ALU.mult, op1=ALU.add)
                P.s("copy", [CT], [CTb], out=CTb[:, 0:129], in_=CT[:, 0:129])

            ndk = [("nd", c) for c in range(NC_)]
            dn = CH.sb("dn", [128, NC_], F32)
            bc = lambda t: t[:].unsqueeze(2).to_broadcast([128, NC_, 128])
            P.v("scalar_tensor_tensor", ndk, [dn], out=dn[:], in0=nd[:, :, 128], scalar=-1.0, in1=nd[:, :, 128], op0=ALU.mult, op1=ALU.max)
            P.v("tensor_tensor", [dn, enm], [dn], out=dn[:], in0=dn[:], in1=enm[:], op=ALU.max)
            P.v("reciprocal", [dn], [dn], out=dn[:], in_=dn[:])
            hh = CH.sb("hh", [128, NC_, 128], F32)
            sq = CH.sb("sq", [128, NC_, 128], F32)
            P.v("tensor_tensor", ndk + [dn], [hh], out=hh[:], in0=nd[:, :, 0:128], in1=bc(dn), op=ALU.mult)
            s1 = CH.sb("s1", [128, NC_], F32)
            s2 = CH.sb("s2", [128, NC_], F32)
            m2 = CH.sb("m2", [128, NC_], F32)
            P.v("reduce_sum", [hh], [s1], out=s1[:], in_=hh[:], axis=AX.X)
            P.g("tensor_tensor", [hh], [sq], out=sq[:], in0=hh[:], in1=hh[:], op=ALU.mult)
            P.v("reduce_sum", [sq], [s2], out=s2[:], in_=sq[:], axis=AX.X)
            P.v("tensor_scalar", [s1], [s1], out=s1[:], in0=s1[:], scalar1=1.0 / 128, scalar2=None, op0=ALU.mult)
            P.v("tensor_tensor", [s1], [m2], out=m2[:], in0=s1[:], in1=s1[:], op=ALU.mult)
            P.v("scalar_tensor_tensor", [s2, m2], [s2], out=s2[:], in0=s2[:], scalar=1.0 / 128, in1=m2[:], op0=ALU.mult, op1=ALU.subtract)
            P.v("tensor_scalar", [s2], [s2], out=s2[:], in0=s2[:], scalar1=LN_EPS, scalar2=None, op0=ALU.add)
            P.s("activation", [s2], [s2], out=s2[:], in_=s2[:], func=AF.Ln)
            P.s("activation", [s2], [s2], out=s2[:], in_=s2[:], func=AF.Exp, scale=-0.5)
            P.v("tensor_tensor", [hh, s1], [hh], out=hh[:], in0=hh[:], in1=bc(s1), op=ALU.subtract)
            P.v("tensor_tensor", [hh, s2], [hh], out=hh[:], in0=hh[:], in1=bc(s2), op=ALU.mult)
            P.g("tensor_tensor", [hh, gnorm], [hh], out=hh[:], in0=hh[:],
                in1=gnorm[:, h * 128:(h + 1) * 128].unsqueeze(1).to_broadcast([128, NC_, 128]), op=ALU.mult)
            P.dma("sync", [("sigo", i) for i in range(NT)] + [sq], [sq], out=sq[:],
                  in_=S["sigo"][:, h * 128:(h + 1) * 128].rearrange("(c p) d -> p c d", p=128))
            P.v("tensor_tensor", [hh, sq], [hh], out=hh[:], in0=hh[:], in1=sq[:], op=ALU.mult)
            P.dma("gpsimd", [hh], [("attn", "a", h)], out=S["attn"][:, h * 128:(h + 1) * 128].rearrange("(c p) d -> p c d", p=128), in_=hh[:])


def bcast_row(P, C, name, src_row, n):
    t = C.sb(name, [128, n], F32)
    P.dma("sync", [], [t], out=t[:], in_=src_row.partition_broadcast(128))
    return t


def layer_norm_tile(P, r, cen, sm, g_t, b_t, out_t, n=1024):
    P.v("reduce_sum", [r], [sm], out=sm[:, 0:1], in_=r[:], axis=AX.X)
    P.v("tensor_scalar", [sm], [sm], out=sm[:, 1:2], in0=sm[:, 0:1], scalar1=-1.0 / n, scalar2=None, op0=ALU.mult)
    P.s("activation", [r, sm], [cen], out=cen[:], in_=r[:], func=AF.Identity, bias=sm[:, 1:2], scale=1.0)
    P.s("activation", [cen], [r, sm], out=r[:], in_=cen[:], func=AF.Square, accum_out=sm[:, 2:3])
    P.v("tensor_scalar", [sm], [sm], out=sm[:, 3:4], in0=sm[:, 2:3], scalar1=1.0 / n, scalar2=LN_EPS, op0=ALU.mult, op1=ALU.add)
    P.s("activation", [sm], [sm], out=sm[:, 4:5], in_=sm[:, 3:4], func=AF.Ln)
    P.s("activation", [sm], [sm], out=sm[:, 5:6], in_=sm[:, 4:5], func=AF.Exp, scale=-0.5)
    P.v("scalar_tensor_tensor", [cen, sm, g_t], [cen], out=cen[:], in0=cen[:], scalar=sm[:, 5:6], in1=g_t[:], op0=ALU.mult, op1=ALU.mult)
    P.g("tensor_tensor", [cen, b_t], [out_t], out=out_t[:], in0=cen[:], in1=b_t[:], op=ALU.add)


def phase_tail_a(nc, P, K, S, w_out, ln_g, ln_b, h_in, h1):
    with Ctx(nc, P) as C:
        wo = C.sb("wo", [128, 8, 1024], BF16)
        load_w_bf16(P, wo, w_out, 8)
        g_t = bcast_row(P, C, "g1", ln_g, 1024)
        b_t = bcast_row(P, C, "b1", ln_b, 1024)
        rin = C.sbrot("ain", [128, 1024], F32, 2)
        rh = C.sbrot("hin", [128, 1024], F32, 2)
        rpt = C.psrot("pt", [128, 4, 128], F32, 2)
        rps = C.psrot("ps", [128, 512], F32, 4)
        raT = C.sbrot("aT", [128, 8, 128], BF16, 2)
        rr = C.sbrot("r", [128, 1024], F32, 2)
        rcen = C.sbrot("cen", [128, 1024], F32, 2)
        rout = C.sbrot("o", [128, 1024], F32, 2)
        rsm = C.sbrot("sm", [128, 8], F32, 2)
        def body(i):
            rows = slice(i * 128, (i + 1) * 128)
            aT = raT.next()
            load_transposed(P, K, S["attn"], aT, 8, aT.name, rin, rpt, tiles=[i], t0=i)
            ht = rh.next()
            P.dma("sync", [], [ht], out=ht[:], in_=h_in[rows, :])
            yield
            r = rr.next()
            for n in range(2):
                ps = rps.next()
                for kc in range(8):
                    P.mm([(aT.name, i), wo], [ps], ps[:], aT[:, kc, :], wo[:, kc, n * 512:(n + 1) * 512], kc == 0, kc == 7)
                P.v("scalar_tensor_tensor", [ht, ps], [r], out=r[:, n * 512:(n + 1) * 512], in0=ht[:, n * 512:(n + 1) * 512], scalar=DN_ALPHA,
                    in1=ps[:], op0=ALU.mult, op1=ALU.add)
            cen, sm, o = rcen.next(), rsm.next(), rout.next()
            layer_norm_tile(P, r, cen, sm, g_t, b_t, o)
            P.dma("gpsimd", [o], [("h1", i)], out=h1[rows, :], in_=o[:])
        run_staged(NT, body)


def phase_tail_b(nc, P, K, S, w1, w2, ln_g, ln_b, h1, h2):
    ST = 256
    NS = ST // 128
    with Ctx(nc, P) as C:
        W1 = C.sb("W1", [128, 8, 4096], BF16)
        W2 = C.sb("W2", [128, 32, 1024], BF16)
        load_w_bf16(P, W1, w1, 8)
        load_w_bf16(P, W2, w2, 32)
        g_t = bcast_row(P, C, "g2", ln_g, 1024)
        b_t = bcast_row(P, C, "b2", ln_b, 1024)
        h1s = [C.sb("h1s%d" % k, [128, 1024], F32) for k in range(2 * NS)]
        rpt = C.psrot("pt", [128, 4, 128], F32, 2)
        rps = C.psrot("ps", [128, 512], F32, 4)
        rhT = C.sbrot("h1T", [128, 8, ST], BF16, 2)
        raT = C.sbrot("aT", [128, 32, ST], BF16, 1)
        rtmp = C.sbrot("tmp", [128, ST], F32, 3)
        rr = C.sbrot("r", [128, 1024], F32, 2)
        rcen = C.sbrot("cen", [128, 1024], F32, 1)
        rout = C.sbrot("o", [128, 1024], F32, 2)
        rsm = C.sbrot("sm", [128, 8], F32, 2)
        def body(st):
            hT = rhT.next()
            hts = []
            for k in range(NS):
                i = st * NS + k
                ht = h1s[(st % 2) * NS + k]
                hts.append(ht)
                load_transposed(P, K, h1, hT, 8, hT.name, Rot([ht]), rpt, tiles=[i], t0=st * NS)
            yield
            hk = [(hT.name, st * NS + k) for k in range(NS)]
            aT = raT.next()
            for f in range(32):
                ps = rps.next()
                for kc in range(8):
                    P.mm(hk + [W1], [ps], ps[:, 0:ST], W1[:, kc, f * 128:(f + 1) * 128], hT[:, kc, :], kc == 0, kc == 7)
                tmp = rtmp.next()
                P.s("activation", [ps], [tmp], out=tmp[:], in_=ps[:, 0:ST], func=AF.Relu)
                P.g("tensor_tensor", [tmp], [(aT.name, f)], out=aT[:, f, :], in0=tmp[:], in1=tmp[:], op=ALU.mult)
            ak = [(aT.name, f) for f in range(32)]
            for k in range(NS):
                i = st * NS + k
                r = rr.next()
                for n in range(2):
                    ps = rps.next()
                    for f in range(32):
                        P.mm(ak + [W2], [ps], ps[:], aT[:, f, k * 128:(k + 1) * 128], W2[:, f, n * 512:(n + 1) * 512], f == 0, f == 31)
                    P.v("scalar_tensor_tensor", [hts[k], ps], [r], out=r[:, n * 512:(n + 1) * 512], in0=hts[k][:, n * 512:(n + 1) * 512],
                        scalar=DN_ALPHA, in1=ps[:], op0=ALU.mult, op1=ALU.add)
                cen, sm, o = rcen.next(), rsm.next(), rout.next()
                layer_norm_tile(P, r, cen, sm, g_t, b_t, o)
                P.dma("gpsimd", [o], [("h2", i)], out=h2[i * 128:(i + 1) * 128, :], in_=o[:])
        run_staged(T // ST, body)


def phase_tail_c(nc, P, K, S, wg, wp, p_in, h2, h_out, is_output):
    with Ctx(nc, P) as C:
        Wg = C.sb("Wg", [128, 8, 1024], BF16)
        Wp = C.sb("Wp", [128, 2, 1024], BF16)
        load_w_bf16(P, Wg, wg, 8)
        load_w_bf16(P, Wp, wp, 2)
        rh = C.sbrot("h2t", [128, 1024], F32, 2)
        rp = C.sbrot("pt_", [128, 256], F32, 2)
        rpt = C.psrot("pt", [128, 4, 128], F32, 2)
        rps = C.psrot("ps", [128, 512], F32, 4)
        rhT = C.sbrot("hT", [128, 8, 128], BF16, 2)
        rpT = C.sbrot("pT", [128, 2, 128], BF16, 2)
        rgt = C.sbrot("gt", [128, 512], F32, 2)
        rout = C.sbrot("o", [128, 1024], F32, 2)
        def body(i):
            rows = slice(i * 128, (i + 1) * 128)
            ht = rh.next()
            hT = rhT.next()
            load_transposed(P, K, h2, hT, 8, hT.name, Rot([ht]), rpt, tiles=[i], t0=i)
            pT = rpT.next()
            load_transposed(P, K, p_in, pT, 2, pT.name, rp, rpt, tiles=[i], t0=i)
            yield
            o = rout.next()
            for n in range(2):
                cs = slice(n * 512, (n + 1) * 512)
                psg = rps.next()
                for kc in range(8):
                    P.mm([(hT.name, i), Wg], [psg], psg[:], hT[:, kc, :], Wg[:, kc, cs], kc == 0, kc == 7)
                psp = rps.next()
                for kc in range(2):
                    P.mm([(pT.name, i), Wp], [psp], psp[:], pT[:, kc, :], Wp[:, kc, cs], kc == 0, kc == 1)
                gt = rgt.next()
                P.s("activation", [psg], [gt], out=gt[:], in_=psg[:], func=AF.Sigmoid)
                P.v("tensor_tensor", [gt, psp], [gt], out=gt[:], in0=gt[:], in1=psp[:], op=ALU.mult)
                P.g("tensor_tensor", [gt, ht], [o], out=o[:, cs], in0=gt[:], in1=ht[:, cs], op=ALU.add)
            P.dma("gpsimd", [o], [("hout", i)], out=h_out[rows, :], in_=o[:], is_output=is_output)
        run_staged(NT, body)


TWO_PI = 6.283185307179586
CW1 = 6.28125
CW2 = TWO_PI - CW1


def phase_o1(nc, P, K, S, W, j, h_in):
    SC = 192.0 ** -0.5
    with Ctx(nc, P) as C:
        wi_ = C.sb("w_in_o", [128, 8, 904], BF16)
        load_w_bf16(P, wi_, W["o_w_in"][j], 8)
        wqb = C.sb("wqb", [128, 4, 1536], BF16)
        load_w_bf16(P, wqb, W["o_w_qb"][j], 4)
        wiq = C.sb("wiq", [128, 4, 512], BF16)
        load_w_bf16(P, wiq, W["o_w_iq"][j], 4)
        wqr = C.sb("wqr", [128, 4, 8, 64], BF16)
        P.v("tensor_copy", [wqb], [wqr], out=wqr[:], in_=wqb[:].rearrange("p k (h e) -> p k h e", e=192)[:, :, :, 128:192])
        wuk_f = C.sb("wuk_f", [128, 2, 1024], F32)
        P.dma("sync", [], [wuk_f], out=wuk_f[:], in_=W["o_w_uk"][j].rearrange("(cc p) h d -> p cc (h d)", p=128))
        wukT = C.sb("wukT", [128, 8, 256], BF16)
        rpt = C.psrot("pt", [128, 4, 128], F32, 2)
        for cc in range(2):
            for hq in range(2):
                pt = rpt.next()
                for jj in range(4):
                    h = hq * 4 + jj
                    P.tr([wuk_f], [pt], pt[:, jj, :], wuk_f[:, cc, h * 128:(h + 1) * 128], K["idf"])
                P.v("tensor_copy", [pt], [wukT], out=wukT[:, hq * 4:(hq + 1) * 4, cc * 128:(cc + 1) * 128], in_=pt[:])
        gq = bcast_row(P, C, "gq", W["o_q_norm"][j], 512)
        gkv = bcast_row(P, C, "gkv", W["o_kv_norm"][j], 256)
        ikg = bcast_row(P, C, "ikg", W["o_ik_g"][j], 64)
        ikb = bcast_row(P, C, "ikb", W["o_ik_b"][j], 64)
        inv = bcast_row(P, C, "inv", W["rope_inv"], 48)
        posi = C.sb("posi", [128, NT], I32)
        P.dma("sync", [], [posi], out=posi[:], in_=W["positions"].rearrange("(c p) -> p c", p=128), allow_slow_non_contiguous=True)
        posf = C.sb("posf", [128, NT], F32)
        P.v("tensor_copy", [posi], [posf], out=posf[:], in_=posi[:])

        rh = C.sbrot("hin", [128, 1024], F32, 3)
        rhT = C.sbrot("hT", [128, 8, 128], BF16, 3)
        rps = C.psrot("ps", [128, 512], F32, 3)
        rpk = C.psrot("psk", [128, 512], F32, 1)
        rsm = C.sbrot("sm", [128, 16], F32, 3)
        rcq = C.sbrot("cq", [128, 512], F32, 9)
        rcqT = C.sbrot("cqT", [128, 4, 128], BF16, 3)
        rqn = C.sbrot("qn", [128, 8, 128], BF16, 3)
        rqa = C.sbrot("qa", [128, 16, 128], BF16, 3)
        rang = C.sbrot("ang", [128, 4, 48], F32, 3)
        rki = C.sbrot("ki", [128, 48], I32, 3)
        rtr = C.sbrot("tr", [128, 4, 48], F32, 3)
        rq1 = C.sbrot("q1", [128, 512], F32, 6)
        rq2 = C.sbrot("q2", [128, 512], F32, 6)
        rqb = C.sbrot("qbf", [128, 4, 128], BF16, 9)
        rkv = C.sbrot("kv", [128, 672], F32, 3)
        rkvb = C.sbrot("kvb", [128, 256], BF16, 3)
        rkk = C.sbrot("kk", [128, 2, 128], F32, 3)
        rkkb = C.sbrot("kkb", [128, 2, 128], BF16, 2)
        rwi = C.sbrot("wi", [128, 16], F32, 3)
        kn2 = C.sb("kn2", [128, NT], F32)
        onesb = C.sb("onesb", [128, 1], BF16)
        P.g("memset", [], [onesb], onesb[:], 1.0)
        rsq = C.sbrot("sqa", [128, 16, 128], BF16, 3)
        rqn2 = C.sbrot("qn2", [128, 24], F32, 3)
        rpn = C.psrot("pn", [128, 8], F32, 1)

        def rms_rstd(src, n, sm, col, junk):
            P.s("activation", [src], [junk, sm], out=junk, in_=src, func=AF.Square, accum_out=sm[:, col:col + 1])
            P.v("tensor_scalar", [sm], [sm], out=sm[:, col + 1:col + 2], in0=sm[:, col:col + 1], scalar1=1.0 / n, scalar2=LN_EPS, op0=ALU.mult, op1=ALU.add)
            P.s("activation", [sm], [sm], out=sm[:, col + 1:col + 2], in_=sm[:, col + 1:col + 2], func=AF.Ln)
            P.s("activation", [sm], [sm], out=sm[:, col + 2:col + 3], in_=sm[:, col + 1:col + 2], func=AF.Exp, scale=-0.5)

        def rope(dst, src, cs, sn, nh, half, t1, t2):
            cb = cs.unsqueeze(1).to_broadcast([128, nh, half])
            sb_ = sn.unsqueeze(1).to_broadcast([128, nh, half])
            x1, x2 = src[:, :, 0:half], src[:, :, half:2 * half]
            P.v("tensor_tensor", [src], [t1], out=t1, in0=x1, in1=cb, op=ALU.mult)
            P.g("tensor_tensor", [src], [t2], out=t2, in0=x2, in1=sb_, op=ALU.mult)
            P.v("tensor_tensor", [t1, t2], [dst], out=dst[:, :, 0:half], in0=t1, in1=t2, op=ALU.subtract)
            P.v("tensor_tensor", [src], [t1], out=t1, in0=x1, in1=sb_, op=ALU.mult)
            P.g("tensor_tensor", [src], [t2], out=t2, in0=x2, in1=cb, op=ALU.mult)
            P.v("tensor_tensor", [t1, t2], [dst], out=dst[:, :, half:2 * half], in0=t1, in1=t2, op=ALU.add)

        def body(i):
            rows = slice(i * 128, (i + 1) * 128)
            cols = slice(i * 128, (i + 1) * 128)
            ang, ki, tr = rang.next(), rki.next(), rtr.next()
            P.v("tensor_scalar", [inv, posf], [ang], out=ang[:, 0, :], in0=inv[:], scalar1=posf[:, i:i + 1], scalar2=None, op0=ALU.mult)
            P.v("tensor_scalar", [ang], [ang], out=ang[:, 1, :], in0=ang[:, 0, :], scalar1=1.0 / TWO_PI, scalar2=None, op0=ALU.mult)
            P.v("tensor_copy", [ang], [ki], out=ki[:], in_=ang[:, 1, :])
            P.v("tensor_copy", [ki], [ang], out=ang[:, 1, :], in_=ki[:])
            P.v("scalar_tensor_tensor", [ang], [ang], out=ang[:, 0, :], in0=ang[:, 1, :], scalar=-CW1, in1=ang[:, 0, :], op0=ALU.mult, op1=ALU.add)
            P.v("scalar_tensor_tensor", [ang], [ang], out=ang[:, 0, :], in0=ang[:, 1, :], scalar=-CW2, in1=ang[:, 0, :], op0=ALU.mult, op1=ALU.add)

            def wrap(a):
                P.v("tensor_scalar", [ang], [ang], out=ang[:, 2, :], in0=a, scalar1=float(np.pi), scalar2=-TWO_PI, op0=ALU.is_gt, op1=ALU.mult)
                P.v("tensor_tensor", [ang], [ang], out=a, in0=a, in1=ang[:, 2, :], op=ALU.add)
                P.v("tensor_scalar", [ang], [ang], out=ang[:, 2, :], in0=a, scalar1=-float(np.pi), scalar2=TWO_PI, op0=ALU.is_lt, op1=ALU.mult)
                P.v("tensor_tensor", [ang], [ang], out=a, in0=a, in1=ang[:, 2, :], op=ALU.add)
            wrap(ang[:, 0, :])
            P.v("tensor_scalar", [ang], [ang], out=ang[:, 3, :], in0=ang[:, 0, :], scalar1=float(np.pi / 2), scalar2=None, op0=ALU.add)
            wrap(ang[:, 3, :])
            P.s("activation", [ang], [tr], out=tr[:, 0, :], in_=ang[:, 0, :], func=AF.Sin)
            P.s("activation", [ang], [tr], out=tr[:, 1, :], in_=ang[:, 3, :], func=AF.Sin)
            sin64, cos64, sin32, cos32 = tr[:, 0, 0:32], tr[:, 1, 0:32], tr[:, 0, 32:48], tr[:, 1, 32:48]
            yield

            ht, hT = rh.next(), rhT.next()
            load_transposed(P, K, h_in, hT, 8, hT.name, Rot([ht]), rpt, tiles=[i], t0=i)
            yield
            hk = [(hT.name, i), wi_]
            ps_q, ps_k = rps.next(), rpk.next()
            for kc in range(8):
                P.mm(hk, [ps_q], ps_q[:], hT[:, kc, :], wi_[:, kc, 0:512], kc == 0, kc == 7)
            for kc in range(8):
                P.mm(hk, [ps_k], ps_k[:, 0:392], hT[:, kc, :], wi_[:, kc, 512:904], kc == 0, kc == 7)
            kv = rkv.next()
            P.s("copy", [ps_k], [kv], out=kv[:, 0:392], in_=ps_k[:, 0:392])
            sm = rsm.next()
            cq, q1 = rcq.next(), rq1.next()
            rms_rstd(ps_q[:], 512, sm, 0, q1[:])
            P.v("scalar_tensor_tensor", [ps_q, sm, gq], [cq], out=cq[:], in0=ps_q[:], scalar=sm[:, 2:3], in1=gq[:], op0=ALU.mult, op1=ALU.mult)
            yield
            cqT = rcqT.next()
            pt = rpt.next()
            for jj in range(4):
                P.tr([cq], [pt], pt[:, jj, :], cq[:, jj * 128:(jj + 1) * 128], K["idf"])
            P.s("copy", [pt], [cqT], out=cqT[:], in_=pt[:])
            yield
            qn = rqn.next()
            for hq in range(2):
                ps = rps.next()
                for jj in range(4):
                    h = hq * 4 + jj
                    for kc in range(4):
                        P.mm([cqT, wqb], [ps], ps[:, jj * 128:(jj + 1) * 128], wqb[:, kc, h * 192:h * 192 + 128], cqT[:, kc, :], kc == 0, kc == 3)
                if hq == 0:
                    P.v("tensor_copy", [ps], [qn], out=qn[:, 0:4, :], in_=ps[:].rearrange("p (a b) -> p a b", b=128))
                else:
                    P.s("copy", [ps], [qn], out=qn[:, 4:8, :], in_=ps[:].rearrange("p (a b) -> p a b", b=128))
                yield
            qa = rqa.next()
            for hq in range(4):
                ps = rps.next()
                for jj in range(4):
                    n = hq * 4 + jj
                    h, cc = n // 2, n % 2
                    P.mm([qn, wukT], [ps], ps[:, jj * 128:(jj + 1) * 128], wukT[:, h, cc * 128:(cc + 1) * 128], qn[:, h, :])
                if hq % 2 == 0:
                    P.s("mul", [ps], [qa], out=qa[:, hq * 4:(hq + 1) * 4, :], in_=ps[:].rearrange("p (a b) -> p a b", b=128), mul=SC)
                else:
                    P.v("tensor_scalar", [ps], [qa], out=qa[:, hq * 4:(hq + 1) * 4, :], in0=ps[:].rearrange("p (a b) -> p a b", b=128),
                        scalar1=SC, scalar2=None, op0=ALU.mult)
                yield
            P.dma("gpsimd", [qa], [("qaT", i)], out=S["qaT"][:, :, cols].rearrange("n p t -> p n t"), in_=qa[:])
            sqa = rsq.next()
            P.g("tensor_tensor", [qa], [sqa], out=sqa[:], in0=qa[:], in1=qa[:], op=ALU.mult)
            pn = rpn.next()
            for h in range(8):
                for cc in range(2):
                    P.mm([sqa, onesb], [pn], pn[:, h:h + 1], sqa[:, 2 * h + cc, :], onesb[:], cc == 0, cc == 1)
            qn2 = rqn2.next()
            P.v("tensor_copy", [pn], [qn2], out=qn2[:, 0:8], in_=pn[:])
            yield
            ps = rps.next()
            for kc in range(4):
                P.mm([cqT, wqr], [ps], ps[:], cqT[:, kc, :], wqr[:, kc].rearrange("p h e -> p (h e)"), kc == 0, kc == 3)
            q2 = rq2.next()
            P.s("mul", [ps], [q1], out=q1[:], in_=ps[:], mul=SC)
            yield
            t1 = rcq.next()
            rope(q2[:].rearrange("p (h e) -> p h e", e=64), q1[:].rearrange("p (h e) -> p h e", e=64), cos64, sin64, 8, 32,
                 t1[:, 0:256].rearrange("p (h e) -> p h e", e=32), t1[:, 256:512].rearrange("p (h e) -> p h e", e=32))
            pt = rpt.next()
            for jj in range(4):
                P.tr([q2], [pt], pt[:, jj, :], q2[:, jj * 128:(jj + 1) * 128], K["idf"])
            qb = rqb.next()
            P.s("copy", [pt], [qb], out=qb[:], in_=pt[:])
            P.dma("gpsimd", [qb], [("qrT", i)], out=S["qrT"][:, :, cols].rearrange("n p t -> p n t"), in_=qb[:])
            yield
            P.g("tensor_tensor", [q2], [q1], out=q1[:], in0=q2[:], in1=q2[:], op=ALU.mult)
            P.v("reduce_sum", [q1], [qn2], out=qn2[:, 8:16], in_=q1[:].rearrange("p (h e) -> p h e", e=64), axis=AX.X)
            P.v("tensor_tensor", [qn2], [qn2], out=qn2[:, 16:24], in0=qn2[:, 0:8], in1=qn2[:, 8:16], op=ALU.add)
            P.dma("gpsimd", [qn2], [("qn2", i)], out=S["qn2"][rows, :], in_=qn2[:, 16:24])
            yield
            ps = rps.next()
            for kc in range(4):
                P.mm([cqT, wiq], [ps], ps[:], cqT[:, kc, :], wiq[:, kc, :], kc == 0, kc == 3)
            q1 = rq1.next()
            q2 = rq2.next()
            P.s("copy", [ps], [q1], out=q1[:], in_=ps[:])
            yield
            P.g("tensor_copy", [q1], [q2], out=q2[:], in_=q1[:])
            t1 = rcq.next()
            rope(q2[:].rearrange("p (h e) -> p h e", e=64)[:, :, 0:32], q1[:].rearrange("p (h e) -> p h e", e=64)[:, :, 0:32], cos32, sin32, 8, 16,
                 t1[:, 0:128].rearrange("p (h e) -> p h e", e=16), t1[:, 128:256].rearrange("p (h e) -> p h e", e=16))
            pt = rpt.next()
            for jj in range(4):
                P.tr([q2], [pt], pt[:, jj, :], q2[:, jj * 128:(jj + 1) * 128], K["idf"])
            qb = rqb.next()
            P.v("tensor_copy", [pt], [qb], out=qb[:], in_=pt[:])
            P.dma("gpsimd", [qb], [("qiT", i)], out=S["qiT"][:, :, cols].rearrange("n p t -> p n t"), in_=qb[:])
            yield
            rms_rstd(kv[:, 0:256], 256, sm, 4, kv[:, 400:656])
            P.v("scalar_tensor_tensor", [kv, sm, gkv], [kv], out=kv[:, 0:256], in0=kv[:, 0:256], scalar=sm[:, 6:7], in1=gkv[:], op0=ALU.mult, op1=ALU.mult)
            yield
            kvb = rkvb.next()
            P.g("tensor_copy", [kv], [kvb], out=kvb[:], in_=kv[:, 0:256])
            P.dma("gpsimd", [kvb], [("ckv", i)], out=S["ckv"][rows, :], in_=kvb[:])
            yield
            kk = rkk.next()
            rope(kk[:, 0:1, 0:64], kv[:, 256:320].unsqueeze(1), cos64, sin64, 1, 32, kv[:, 400:432].unsqueeze(1), kv[:, 432:464].unsqueeze(1))
            P.v("tensor_copy", [kk], [kk], out=kk[:, 0, 64:128], in_=kk[:, 0, 0:64])
            P.s("activation", [kv], [kv, sm], out=kv[:, 400:656], in_=kv[:, 0:256], func=AF.Square, accum_out=sm[:, 13:14])
            P.s("activation", [kk], [kv, sm], out=kv[:, 400:464], in_=kk[:, 0, 0:64], func=AF.Square, accum_out=sm[:, 14:15])
            P.v("tensor_tensor", [sm], [kn2], out=kn2[:, i:i + 1], in0=sm[:, 13:14], in1=sm[:, 14:15], op=ALU.add)
            yield
            ik = kv[:, 320:384]
            P.v("reduce_sum", [kv], [sm], out=sm[:, 8:9], in_=ik, axis=AX.X)
            P.v("tensor_scalar", [sm], [sm], out=sm[:, 9:10], in0=sm[:, 8:9], scalar1=-1.0 / 64, scalar2=None, op0=ALU.mult)
            P.s("activation", [kv, sm], [kv], out=kv[:, 464:528], in_=ik, func=AF.Identity, bias=sm[:, 9:10], scale=1.0)
            rms_rstd(kv[:, 464:528], 64, sm, 10, kv[:, 528:592])
            P.v("scalar_tensor_tensor", [kv, sm, ikg], [kv], out=kv[:, 464:528], in0=kv[:, 464:528], scalar=sm[:, 12:13], in1=ikg[:], op0=ALU.mult, op1=ALU.mult)
            P.v("tensor_tensor", [kv, ikb], [kv], out=kv[:, 464:528], in0=kv[:, 464:528], in1=ikb[:], op=ALU.add)
            P.v("tensor_copy", [kv], [kk], out=kk[:, 1, 32:64], in_=kv[:, 496:528])
            rope(kk[:, 1:2, 0:32], kv[:, 464:496].unsqueeze(1), cos32, sin32, 1, 16, kv[:, 592:608].unsqueeze(1), kv[:, 608:624].unsqueeze(1))
            P.v("tensor_copy", [kk], [kk], out=kk[:, 1, 64:128], in_=kk[:, 1, 0:64])
            yield
            wi = rwi.next()
            P.v("tensor_scalar", [kv], [wi], out=wi[:, 0:8], in0=kv[:, 384:392], scalar1=(8.0 ** -0.5) * (64.0 ** -0.5), scalar2=None, op0=ALU.mult)
            P.v("tensor_scalar", [wi], [wi], out=wi[:, 8:16], in0=wi[:, 0:8], scalar1=0.0, scalar2=2.0, op0=ALU.is_ge, op1=ALU.mult)
            P.v("tensor_scalar", [wi], [wi], out=wi[:, 8:16], in0=wi[:, 8:16], scalar1=-1.0, scalar2=None, op0=ALU.add)
            P.v("tensor_tensor", [wi], [wi], out=wi[:, 0:8], in0=wi[:, 0:8], in1=wi[:, 8:16], op=ALU.mult)
            P.dma("gpsimd", [wi], [("wi", i)], out=S["wi"][rows, :], in_=wi[:])
            yield
            pt = rpt.next()
            P.tr([kv], [pt], pt[:, 0, :], kv[:, 0:128], K["idf"])
            P.tr([kv], [pt], pt[:, 1, :], kv[:, 128:256], K["idf"])
            P.tr([kk], [pt], pt[:, 2, :], kk[:, 0, :], K["idf"])
            P.tr([kk], [pt], pt[:, 3, :], kk[:, 1, :], K["idf"])
            kkb = rqb.next()
            P.s("copy", [pt], [kkb], out=kkb[:], in_=pt[:])
            P.dma("gpsimd", [kkb], [("kT", i)], out=S["kT"][:, :, cols].rearrange("n p t -> p n t"), in_=kkb[:])
        run_interleaved(NT, body, 2)
        P.dma("sync", [kn2], ["kn2d"], out=S["kn2"], in_=kn2[:])


MASKV = -30000.0


class AttnRes:
    def __init__(self, C, npT=2):
        self.rps = C.psrot("s_ps", [128, 512], F32, 3)
        self.rpT = C.psrot("pT_ps", [128, 4, 128], BF16, npT)
        self.re = C.sbrot("e", [128, 512], BF16, 4)
        self.rpt = C.sbrot("pT", [128, 4, 128], BF16, 4)
        self.rmx = C.sbrot("mx", [128, 16], F32, 4)
        self.rrs = C.sbrot("rs", [128, 16], F32, 4)
        self.cnt = 0


class Pipe:
    def __init__(self):
        self.hist = []
        self.filler = None
        self.rate = 1

    def push(self, stages):
        self.hist.insert(0, stages)
        self.hist = self.hist[:3]
        for lag, st in enumerate(self.hist):
            if lag < len(st) and st[lag] is not None:
                st[lag]()
        if self.filler is not None:
            for _ in range(self.rate):
                next(self.filler, None)

    def flush(self):
        self.push([])
        self.push([])


def attn_items(P, K, R, terms, mask_term, k0, k1, pv_fn, done_fn, negm_ap=None, negm_reads=()):
    chunks = []
    c = k0
    while c < k1:
        n = min(512, k1 - c)
        chunks.append((c, n))
        c += n
    nc_ = len(chunks)
    nkt = (k1 - k0) // 128
    mx, rs = R.rmx.next(), R.rrs.next()

    def scores(c0, n, tl):
        ps = R.rps.next()
        for ti, (lhsT, rhs_fn, rd) in enumerate(tl):
            P.mm(rd, [ps], ps[:, 0:n], lhsT, rhs_fn(c0, n), ti == 0, ti == len(tl) - 1)
        return ps

    items = []
    for ci, (c0, n) in enumerate(chunks if negm_ap is None else []):
        def A1(ci=ci, c0=c0, n=n):
            ps = scores(c0, n, terms)
            P.v("reduce_max", [ps], [mx], out=mx[:, ci:ci + 1], in_=ps[:, 0:n], axis=AX.X)
            if ci == nc_ - 1:
                if nc_ > 1:
                    P.v("reduce_max", [mx], [mx], out=mx[:, 15:16], in_=mx[:, 0:nc_], axis=AX.X)
                    P.v("tensor_scalar", [mx], [mx], out=mx[:, 14:15], in0=mx[:, 15:16], scalar1=-1.0, scalar2=None, op0=ALU.mult)
                else:
                    P.v("tensor_scalar", [mx], [mx], out=mx[:, 14:15], in0=mx[:, 0:1], scalar1=-1.0, scalar2=None, op0=ALU.mult)
        items.append([A1])
    tl2 = terms + ([mask_term] if mask_term is not None else [])
    kbase = [0]
    for ci, (c0, n) in enumerate(chunks):
        st = {}
        nk = n // 128

        def A2(ci=ci, c0=c0, n=n, st=st):
            ps = scores(c0, n, tl2)
            e = R.re.next()
            if negm_ap is None:
                P.s("activation", [ps, mx], [e, rs], out=e[:, 0:n], in_=ps[:, 0:n], func=AF.Exp, bias=mx[:, 14:15], scale=1.0, accum_out=rs[:, ci:ci + 1])
            else:
                P.s("activation", [ps] + list(negm_reads), [e, rs], out=e[:, 0:n], in_=ps[:, 0:n], func=AF.Exp, bias=negm_ap, scale=1.0, accum_out=rs[:, ci:ci + 1])
            st["e"] = e

        def B2(nk=nk, st=st):
            e = st["e"]
            pTp = R.rpT.next()
            for kk in range(nk):
                P.tr([e], [pTp], pTp[:, kk, :], e[:, kk * 128:(kk + 1) * 128], K["idb"])
            pT = R.rpt.next()
            R.cnt += 1
            if R.cnt % 2 == 0:
                P.s("copy", [pTp], [pT], out=pT[:, 0:nk, :], in_=pTp[:, 0:nk, :])
            else:
                P.v("tensor_copy", [pTp], [pT], out=pT[:, 0:nk, :], in_=pTp[:, 0:nk, :])
            st["pT"] = pT

        def C2(ci=ci, c0=c0, nk=nk, st=st):
            pT = st["pT"]
            for kk in range(nk):
                kt = (c0 - k0) // 128 + kk
                pv_fn(pT[:, kk, :], [pT], kt, kt == 0, kt == nkt - 1)
            if ci == nc_ - 1:
                if nc_ > 1:
                    P.v("reduce_sum", [rs], [rs], out=rs[:, 15:16], in_=rs[:, 0:nc_], axis=AX.X)
                    P.v("tensor_scalar", [rs], [rs], out=rs[:, 14:15], in0=rs[:, 15:16], scalar1=1e-30, scalar2=None, op0=ALU.add)
                else:
                    P.v("tensor_scalar", [rs], [rs], out=rs[:, 14:15], in0=rs[:, 0:1], scalar1=1e-30, scalar2=None, op0=ALU.add)
                P.v("reciprocal", [rs], [rs], out=rs[:, 13:14], in_=rs[:, 14:15])
                done_fn(rs, rs[:, 13:14])
        items.append([A2, B2, C2])
    return items


def phase_o3(nc, P, K, S, W, j):
    NB = 11
    with Ctx(nc, P) as C:
        kT = C.sb("kT", [128, 4, T], BF16)
        for n in range(4):
            P.dma("sync", [("kT", i) for i in range(NT)], [kT], out=kT[:, n, :], in_=S["kT"][n])
        ckv = C.sb("ckv", [128, NT, 256], BF16)
        P.dma("sync", [("ckv", i) for i in range(NT)], [ckv], out=ckv[:], in_=S["ckv"].rearrange("(c p) d -> p c d", p=128))
        wuv = C.sb("wuv", [128, 2, 1024], BF16)
        for cc in range(2):
            P.dma("gpsimd", [], [wuv], out=wuv[:, cc, :], in_=W["o_w_uv"][j][cc * 128:(cc + 1) * 128].rearrange("p h v -> p (h v)"))
        pw = C.sb("pw", [128, NB + 1], F32)
        for k in range(NB + 1):
            P.g("memset", [], [pw], pw[:, k:k + 1], 2.0 ** -(k + 1))
        negtri = C.sb("negtri", [128, 128], F32)
        P.v("tensor_scalar", [K["tri_ge"]], [negtri], out=negtri[:], in0=K["tri_ge"][:], scalar1=-1.0, scalar2=1e30, op0=ALU.add, op1=ALU.mult)
        negtri_b = C.sb("negtri_b", [128, 128], BF16)
        P.v("tensor_scalar", [K["tri_ge"]], [negtri_b], out=negtri_b[:], in0=K["tri_ge"][:], scalar1=-1.0, scalar2=-MASKV, op0=ALU.add, op1=ALU.mult)
        kn2 = C.sb("kn2", [128, NT], F32)
        P.dma("sync", ["kn2d"], [kn2], out=kn2[:], in_=S["kn2"])
        kmx = C.sb("kmx", [128, 8], F32)
        ones1 = C.sb("ones1", [1, 128], F32)
        P.g("memset", [], [ones1], ones1[:], 1.0)
        P.v("reduce_max", [kn2], [kmx], out=kmx[:, 0:1], in_=kn2[:], axis=AX.X)
        with Ctx(nc, P) as Ck:
            pk1 = Ck.ps("pk1", [128, 128], F32)
            P.tr([kmx], [pk1], pk1[0:1, :], kmx[:, 0:1], K["idf"])
            P.v("reduce_max", [pk1], [kmx], out=kmx[0:1, 1:2], in_=pk1[0:1, :], axis=AX.X)
            P.mm([ones1, kmx], [pk1], pk1[:, 0:1], ones1[:], kmx[0:1, 1:2])
            P.v("tensor_copy", [pk1], [kmx], out=kmx[:, 2:3], in_=pk1[:, 0:1])
        isc = C.sb("isc", [128, T], F32)
        negm = [C.sb("negm%d" % k, [128, T], BF16) for k in range(2)]
        qr8s = [C.sb("qr8_%d" % k, [128, 8, 128], BF16) for k in range(2)]
        qi8s = [C.sb("qi8_%d" % k, [128, 8, 128], BF16) for k in range(2)]
        for tq in qr8s + qi8s:
            P.g("memset", [], [tq], tq[:], 0.0)
        rstab = C.sbrot("stab", [128, 24], F32, 2)
        junk = C.sb("junk", [128, T], BF16)
        R = AttnRes(C)
        rolat = C.psrot("olat", [128, 2, 128], F32, 1)
        rout = C.psrot("outp", [128, 128], F32, 1)
        rqa = C.sbrot("qa", [128, 16, 128], BF16, 2)
        rqr = C.sbrot("qr", [128, 4, 128], BF16, 2)
        rqi = C.sbrot("qi", [128, 4, 128], BF16, 2)
        rwi = C.sbrot("wi", [128, 16], F32, 2)
        rrl = C.sbrot("rl", [128, 512], BF16, 4)
        rdsg = C.sbrot("dsg", [128, 8, 128], BF16, 2)
        rpacc = C.psrot("pacc", [128, 512], F32, 1)
        rbs = C.sbrot("bs", [128, 32], F32, 2)
        rwh = C.sbrot("wh", [128, NB + 1], F32, 2)
        rol = C.sbrot("ol", [128, 2, 128], BF16, 2)
        rat = C.sbrot("at", [128, 1024], F32, 2)
        tile_in = {}

        def prep(i):
            cols = slice(i * 128, (i + 1) * 128)
            L = (i + 1) * 128
            qa, qr = rqa.next(), qr8s[i % 2]
            nm = negm[i % 2]
            stab = rstab.next()
            tile_in[i] = (qa, qr, nm, stab)
            P.dma("sync", [("qaT", i)], [qa], out=qa[:], in_=S["qaT"][:, :, cols].rearrange("n p t -> p n t"))
            for par in range(2):
                P.dma("sync", [("qrT", i)], [qr], out=qr[par * 64:(par + 1) * 64, par::2, :],
                      in_=S["qrT"][:, par * 64:(par + 1) * 64, cols].rearrange("n p t -> p n t"))
            P.dma("sync", [("qn2", i)], [stab], out=stab[:, 0:8], in_=S["qn2"][cols, :])
            P.v("tensor_scalar", [stab, kmx], [stab], out=stab[:, 0:8], in0=stab[:, 0:8], scalar1=kmx[:, 2:3], scalar2=1e-30, op0=ALU.mult, op1=ALU.add)
            P.s("activation", [stab], [stab], out=stab[:, 8:16], in_=stab[:, 0:8], func=AF.Ln)
            P.s("activation", [stab], [stab], out=stab[:, 8:16], in_=stab[:, 8:16], func=AF.Exp, scale=0.5)
            P.v("tensor_scalar", [stab], [stab], out=stab[:, 16:24], in0=stab[:, 8:16], scalar1=-1.02, scalar2=None, op0=ALU.mult)
            yield
            if i < 2:
                if i == 1:
                    P.g("memset", [], [nm], nm[:, 0:128], 0.0)
                P.g("tensor_copy", [negtri_b, nm], [nm], out=nm[:, L - 128:L], in_=negtri_b[:])
                return
            qi, wi = qi8s[i % 2], rwi.next()
            for par in range(2):
                P.dma("sync", [("qiT", i)], [qi], out=qi[par * 64:(par + 1) * 64, par::2, :],
                      in_=S["qiT"][:, par * 64:(par + 1) * 64, cols].rearrange("n p t -> p n t"))
            P.dma("sync", [("wi", i)], [wi], out=wi[:], in_=S["wi"][cols, :])
            dsg = rdsg.next()
            for h in range(8):
                P.g("tensor_scalar", [K["idb"], wi], [dsg], out=dsg[:, h, :], in0=K["idb"][:], scalar1=wi[:, 8 + h:9 + h], scalar2=0.0, op0=ALU.mult, op1=ALU.add)
            yield
            c0 = 0
            while c0 < L:
                n = min(512, L - c0)
                pacc = rpacc.next()
                prev = None
                for h in range(8):
                    ps = R.rps.next()
                    P.mm([qi, kT], [ps], ps[:, 0:n], qi[:, h, :], kT[:, 3, c0:c0 + n])
                    rl = rrl.next()
                    P.s("activation", [ps, wi], [rl], out=rl[:, 0:n], in_=ps[:, 0:n], func=AF.Relu, scale=wi[:, h:h + 1])
                    if prev is not None:
                        P.mm([prev, dsg], [pacc], pacc[:, 0:n], dsg[:, h - 1, :], prev[:, 0:n], h == 1, False)
                    prev = rl
                    yield
                P.mm([prev, dsg], [pacc], pacc[:, 0:n], dsg[:, 7, :], prev[:, 0:n], False, True)
                yield
                P.v("tensor_copy", [pacc], [("isc", c0)], out=isc[:, c0:c0 + n], in_=pacc[:, 0:n])
                c0 += n
            ik = [("isc", c) for c in range(0, L, 512)]
            P.g("tensor_tensor", ik + [K["tri_ge"]], ik, out=isc[:, L - 128:L], in0=isc[:, L - 128:L], in1=K["tri_ge"][:], op=ALU.mult)
            P.g("tensor_tensor", ik + [negtri], ik, out=isc[:, L - 128:L], in0=isc[:, L - 128:L], in1=negtri[:], op=ALU.add)
            bs, wh = rbs.next(), rwh.next()
            pcs = [(c, min(1024, L - c)) for c in range(0, L, 1024)]
            npc = len(pcs)
            for pi, (c, n) in enumerate(pcs):
                n2 = min(n, L - 128 - c)
                if n2 > 0:
                    P.v("tensor_reduce", ik, [bs], out=bs[:, 8 + pi:9 + pi], in_=isc[:, c:c + n2], axis=AX.X, op=ALU.min)
                else:
                    P.v("tensor_copy", [bs], [bs], out=bs[:, 8 + pi:9 + pi], in_=bs[:, 8:9])
                P.v("reduce_max", ik, [bs], out=bs[:, 12 + pi:13 + pi], in_=isc[:, c:c + n], axis=AX.X)
                yield
            P.v("tensor_reduce", [bs], [bs], out=bs[:, 0:1], in_=bs[:, 8:8 + npc], axis=AX.X, op=ALU.min)
            P.v("reduce_max", [bs], [bs], out=bs[:, 1:2], in_=bs[:, 12:12 + npc], axis=AX.X)
            P.v("tensor_tensor", [bs], [bs], out=bs[:, 2:3], in0=bs[:, 1:2], in1=bs[:, 0:1], op=ALU.subtract)
            P.v("tensor_scalar", [pw, bs], [wh], out=wh[:], in0=pw[:], scalar1=bs[:, 2:3], scalar2=None, op0=ALU.mult)
            P.v("tensor_tensor", [bs, wh], [bs], out=bs[:, 3:4], in0=bs[:, 0:1], in1=wh[:, 0:1], op=ALU.add)
            yield
            for k in range(NB):
                for pi, (c, n) in enumerate(pcs):
                    P.v("tensor_scalar", ik + [bs], [junk, bs], out=junk[:, c:c + n], in0=isc[:, c:c + n], scalar1=bs[:, 3:4], scalar2=None,
                        op0=ALU.is_ge, op1=ALU.add, accum_out=bs[:, 8 + pi:9 + pi])
                    if pi < npc - 1:
                        yield
                if npc > 1:
                    P.v("reduce_sum", [bs], [bs], out=bs[:, 4:5], in_=bs[:, 8:8 + npc], axis=AX.X)
                    cnt = bs[:, 4:5]
                else:
                    cnt = bs[:, 8:9]
                P.v("tensor_scalar", [bs], [bs], out=bs[:, 5:6], in0=cnt, scalar1=256.0, scalar2=-0.5, op0=ALU.is_ge, op1=ALU.add)
                P.v("scalar_tensor_tensor", [bs, wh], [bs], out=bs[:, 3:4], in0=bs[:, 5:6], scalar=wh[:, k:k + 1], in1=bs[:, 3:4], op0=ALU.mult, op1=ALU.add)
                yield
            P.v("tensor_tensor", [bs, wh], [bs], out=bs[:, 6:7], in0=bs[:, 3:4], in1=wh[:, NB:NB + 1], op=ALU.subtract)
            for pi, (c, n) in enumerate(pcs):
                P.v("tensor_scalar", ik + [bs], [nm], out=nm[:, c:c + n], in0=isc[:, c:c + n], scalar1=bs[:, 6:7], scalar2=MASKV, op0=ALU.is_lt, op1=ALU.mult)
                yield

        pipe = Pipe()
        for _ in prep(0):
            pass
        for i in range(NT):
            rows = slice(i * 128, (i + 1) * 128)
            L = (i + 1) * 128
            nxt = prep(i + 1) if i + 1 < NT else None
            pipe.filler = nxt
            nch_i, nch_n = (L + 511) // 512, (L + 128 + 511) // 512
            npc_n = (L + 128 + 1023) // 1024
            pipe.rate = -(-(8 * nch_n + (NB + 2) * npc_n + 8) // (8 * nch_i))
            qa, qr, nm, stab = tile_in[i]
            at = rat.next()
            for h in range(8):
                po = (h % 2) * 64
                terms = [
                    (qa[:, 2 * h, :], lambda c0, n: kT[:, 0, c0:c0 + n], [qa, kT]),
                    (qa[:, 2 * h + 1, :], lambda c0, n: kT[:, 1, c0:c0 + n], [qa, kT]),
                    (qr[:, h, :], lambda c0, n: kT[:, 2, c0:c0 + n], [qr, kT]),
                ]
                mterm = (K["idb"][:], lambda c0, n, nm=nm: nm[:, c0:c0 + n], [nm, K["idb"]])
                olat = rolat.next()

                def pv(pT, rd, kt, first, last, olat=olat):
                    for cc in range(2):
                        P.mm(rd + [ckv], [olat], olat[:, cc, :], ckv[:, kt, cc * 128:(cc + 1) * 128], pT, first, last)

                def done(rs, rinv, olat=olat, h=h, at=at, rows=rows):
                    ol = rol.next()
                    P.s("copy", [olat], [ol], out=ol[:], in_=olat[:])
                    po_ = rout.next()
                    for cc in range(2):
                        P.mm([ol, wuv], [po_], po_[:], ol[:, cc, :], wuv[:, cc, h * 128:(h + 1) * 128], cc == 0, cc == 1)
                    P.v("tensor_scalar", [po_, rs], [at], out=at[:, h * 128:(h + 1) * 128], in0=po_[:], scalar1=rinv, scalar2=None, op0=ALU.mult)
                    if h == 7:
                        P.dma("gpsimd", [at], [("attn", rows.start)], out=S["attn"][rows, :], in_=at[:])
                for it in attn_items(P, K, R, terms, mterm, 0, L, pv, done, negm_ap=stab[:, 16 + h:17 + h], negm_reads=[stab]):
                    pipe.push(it)
            if nxt is not None:
                for _ in nxt:
                    pass
        pipe.flush()


def phase_e3(nc, P, K, S, W, j):
    with Ctx(nc, P) as C:
        kcmpT = C.sb("kcmpT", [128, 256], BF16)
        vcmp = C.sb("vcmp", [128, 2, 2, 64], BF16)
        P.g("memset", [], [kcmpT], kcmpT[:], 0.0)
        P.g("memset", [], [vcmp], vcmp[:], 0.0)
        with Ctx(nc, P) as C1:
            uT = C1.sb("uT", [128, 2, T], BF16)
            for n in range(2):
                P.dma("sync", [("bkT", n, tg) for tg in range(8)], [uT], out=uT[:, n, :], in_=S["bkT"][n])
            w1 = C1.sb("w1", [128, 2, 32, 128], BF16)
            for kv in range(2):
                for half in range(2):
                    P.dma("gpsimd", [], [w1], out=w1[half * 64:(half + 1) * 64, kv, :, :],
                          in_=W["e_b_cmp_w1"][j][kv].rearrange("(jj d) n -> d jj n", d=64))
            w2f = C1.sb("w2f", [128, 2, 64], F32)
            P.dma("sync", [], [w2f], out=w2f[:], in_=W["e_b_cmp_w2"][j].rearrange("kv n d -> n kv d"))
            w2p = C1.sb("w2p", [128, 2, 128], BF16)
            w2v = C1.sb("w2v", [128, 64], BF16)
            P.g("memset", [], [w2p], w2p[:], 0.0)
            for g in range(2):
                P.v("tensor_copy", [w2f, w2p], [w2p], out=w2p[:, g, g * 64:(g + 1) * 64], in_=w2f[:, 0, :])
            P.v("tensor_copy", [w2f], [w2v], out=w2v[:], in_=w2f[:, 1, :])
            posT = C1.sb("posT", [64, 2, 32], BF16)
            for kv in range(2):
                P.dma("gpsimd", [], [posT], out=posT[:, kv, :], in_=W["e_b_cmp_pos"][j][kv].rearrange("jj d -> d jj"), allow_slow_non_contiguous=True)
            rph = C1.psrot("ph", [128, 256], F32, 2)
            rpb = C1.psrot("pb", [128, 8], F32, 1)
            rpo = C1.psrot("pko", [128, 256], F32, 1)
            rpv = C1.psrot("pvo", [128, 64], F32, 1)
            bias = C1.sb("bias", [128, 2], F32)
            x = C1.sb("x", [128, 256], F32)
            x2 = C1.sb("x2", [128, 256], F32)
            sg = C1.sb("sg", [128, 256], F32)
            gl = [[C1.sb("gl%d%d" % (kv, g), [128, 256], BF16) for g in range(2)] for kv in range(2)]
            pko = rpo.next()
            for kv in range(2):
                pb = rpb.next()
                for jj in range(32):
                    P.mm([w1, posT], [pb], pb[:, 0:1], w1[0:64, kv, jj, :], posT[:, kv, jj:jj + 1], jj == 0, jj == 31)
                P.v("tensor_copy", [pb], [bias], out=bias[:, kv:kv + 1], in_=pb[:, 0:1])
                for g in range(2):
                    ph = rph.next()
                    for jj in range(32):
                        P.mm([w1, uT], [ph], ph[:, 0:255], w1[g * 64:(g + 1) * 64, kv, jj, :], uT[g * 64:(g + 1) * 64, kv, jj:jj + 16 * 254 + 1:16], jj == 0, jj == 31)
                    P.s("activation", [ph, bias], [x], out=x[:, 0:255], in_=ph[:, 0:255], func=AF.Identity, bias=bias[:, kv:kv + 1], scale=1.0)
                    P.v("tensor_tensor", [x], [x2], out=x2[:, 0:255], in0=x[:, 0:255], in1=x[:, 0:255], op=ALU.mult)
                    P.v("tensor_scalar", [x2], [x2], out=x2[:, 0:255], in0=x2[:, 0:255], scalar1=0.044715, scalar2=1.0, op0=ALU.mult, op1=ALU.add)
                    P.v("tensor_tensor", [x2, x], [x2], out=x2[:, 0:255], in0=x2[:, 0:255], in1=x[:, 0:255], op=ALU.mult)
                    P.s("activation", [x2], [sg], out=sg[:, 0:255], in_=x2[:, 0:255], func=AF.Sigmoid, scale=1.5957691216057308)
                    G = gl[kv][g]
                    P.g("memset", [], [G], G[:], 0.0)
                    P.v("tensor_tensor", [x, sg, G], [G], out=G[:, 0:255], in0=x[:, 0:255], in1=sg[:, 0:255], op=ALU.mult)
                    if kv == 0:
                        P.mm([G, w2p], [pko], pko[:, 0:255], w2p[:, g, :], G[:, 0:255], g == 0, g == 1)
                    else:
                        for mc in range(2):
                            nm = 128 if mc == 0 else 127
                            pvo = rpv.next()
                            P.mm([G, w2v], [pvo], pvo[0:nm, :], G[:, mc * 128:mc * 128 + nm], w2v[:])
                            P.v("tensor_copy", [pvo, vcmp], [vcmp], out=vcmp[0:nm, mc, g, :], in_=pvo[0:nm, :])
                if kv == 0:
                    P.v("tensor_copy", [pko, kcmpT], [kcmpT], out=kcmpT[:, 0:255], in_=pko[:, 0:255])

        qT = C.sb("qT", [128, 8, T], BF16)
        for g in range(2):
            P.g("memset", [], [qT], qT[(1 - g) * 64:(2 - g) * 64, g * 4:(g + 1) * 4, :], 0.0)
        for c in range(4):
            for g in range(2):
                P.dma("sync", [("bqT", c, tg) for tg in range(8)] + [qT], [qT], out=qT[g * 64:(g + 1) * 64, g * 4 + c, :],
                      in_=S["bqT"][c][g * 64:(g + 1) * 64, :])
        nall = C.sb("nall", [128, NT, 4, 2], F32)
        kall = C.sb("kall", [128, 2, NT, 2], F32)
        P.dma("sync", ["nalld"], [nall], out=nall[:].rearrange("p a b c -> p (a b c)"), in_=S["nall"])
        P.dma("sync", ["kalld"], [kall], out=kall[:].rearrange("p a b c -> p (a b c)"), in_=S["kall"])
        kmx = C.sb("kmx", [128, 16], F32)
        ones1 = C.sb("ones1", [1, 128], F32)
        P.g("memset", [], [ones1], ones1[:], 1.0)
        for br in range(2):
            P.v("tensor_reduce", [kall], [kmx], out=kmx[:, 2 * br:2 * br + 2], in_=kall[:, br].rearrange("p t g -> p g t"), axis=AX.X, op=ALU.max)
        with Ctx(nc, P) as Ck:
            pk1 = Ck.ps("pk1", [128, 512], F32)
            for jx in range(4):
                P.tr([kmx], [pk1], pk1[0:1, jx * 128:(jx + 1) * 128], kmx[:, jx:jx + 1], K["idf"])
            P.v("reduce_max", [pk1], [kmx], out=kmx[0:1, 4:8], in_=pk1[0:1, :].rearrange("p (j x) -> p j x", x=128), axis=AX.X)
            P.mm([ones1, kmx], [pk1], pk1[:, 0:4], ones1[:], kmx[0:1, 4:8])
            P.v("tensor_copy", [pk1], [kmx], out=kmx[:, 8:12], in_=pk1[:, 0:4])
        rstab = C.sbrot("stab", [128, 2, 4, 2], F32, 3)
        tile_stab = {}
        ksT = C.sb("ksT", [128, T], BF16)
        kwT = C.sb("kwT", [128, T], BF16)
        P.dma("sync", [("bkT", 2, tg) for tg in range(8)], [ksT], out=ksT[:], in_=S["bkT"][2])
        P.dma("sync", [("bkT", 3, tg) for tg in range(8)], [kwT], out=kwT[:], in_=S["bkT"][3])
        vsw = C.sb("vsw", [128, NT, 256], BF16)
        P.dma("sync", [("bv_tok", i) for i in range(NT)], [vsw], out=vsw[:], in_=S["bv_tok"].rearrange("(c p) d -> p c d", p=128))
        ntri_ge = C.sb("ntri_ge", [128, 128], BF16)
        ntri_lt = C.sb("ntri_lt", [128, 128], BF16)
        P.v("tensor_scalar", [K["tri_ge"]], [ntri_ge], out=ntri_ge[:], in0=K["tri_ge"][:], scalar1=-1.0, scalar2=-MASKV, op0=ALU.add, op1=ALU.mult)
        P.v("tensor_scalar", [K["tri_lt"]], [ntri_lt], out=ntri_lt[:], in0=K["tri_lt"][:], scalar1=-1.0, scalar2=-MASKV, op0=ALU.add, op1=ALU.mult)
        negw = C.sb("negw", [128, 640], BF16)
        P.g("memset", [], [negw], negw[:], 0.0)
        P.v("tensor_copy", [ntri_lt, negw], [negw], out=negw[:, 0:128], in_=ntri_lt[:])
        P.v("tensor_copy", [ntri_ge, negw], [negw], out=negw[:, 512:640], in_=ntri_ge[:])
        dltci = C.sb("dltci", [128, 256], I32)
        dltc = C.sb("dltc", [128, 256], F32)
        P.g("iota", [], [dltci], dltci[:], pattern=[[-16, 256]], base=0, channel_multiplier=1)
        P.v("tensor_copy", [dltci], [dltc], out=dltc[:], in_=dltci[:])
        dlti = C.sb("dlti", [128, 64], I32)
        dlt = C.sb("dlt", [128, 64], F32)
        P.g("iota", [], [dlti], dlti[:], pattern=[[-64, 64]], base=0, channel_multiplier=1)
        P.v("tensor_copy", [dlti], [dlt], out=dlt[:], in_=dlti[:])
        negm = [[C.sb("negm%d%d" % (par, g), [128, T], BF16) for g in range(2)] for par in range(2)]
        ycmp = C.sb("ycmp", [128, 2, 8, 64], F32)
        R = AttnRes(C)
        rpo = C.psrot("po", [128, 2, 64], F32, 2)
        rpc = C.psrot("pc", [128, 64], F32, 1)
        rgt = C.sbrot("gt", [128, 24], F32, 3)
        rselc = C.sbrot("selc", [128, 256], F32, 2)
        ryb = C.sbrot("yb", [128, 512], F32, 2)
        rpg = C.sbrot("pg", [128, 256], F32, 2)
        re32 = C.sbrot("e32", [128, 256], F32, 2)
        rp16 = C.sbrot("p16", [128, 256], BF16, 2)
        rsc = C.sbrot("sc", [128, 4, 64], F32, 2)
        rm8 = C.sbrot("m8", [128, 16], F32, 2)
        rcf = C.sbrot("cf", [128, 8], F32, 6)
        rcmx = C.sbrot("cmx", [128, 8], F32, 3)
        tile_gt = {}

        def prep(i, g):
            rows = slice(i * 128, (i + 1) * 128)
            t0 = i * 128
            L = (i + 1) * 128
            if g == 0:
                gt = rgt.next()
                P.dma("sync", [("bgate", i)], [gt], out=gt[:], in_=S["bgate"][rows, :])
                selc = rselc.next()
                P.v("tensor_scalar", [dltc], [selc], out=selc[:], in0=dltc[:], scalar1=float(31 - t0), scalar2=None, op0=ALU.is_ge)
                tile_gt[i] = (gt, selc)
                stab = rstab.next()
                tile_stab[i] = stab
                for br in range(2):
                    P.v("tensor_tensor", [nall, kmx], [stab], out=stab[:, br], in0=nall[:, i],
                        in1=kmx[:, 8 + 2 * br:10 + 2 * br].unsqueeze(1).to_broadcast([128, 4, 2]), op=ALU.mult)
                P.v("tensor_scalar", [stab], [stab], out=stab[:], in0=stab[:], scalar1=1e-30, scalar2=None, op0=ALU.add)
                P.s("activation", [stab], [stab], out=stab[:], in_=stab[:], func=AF.Ln)
                P.s("activation", [stab], [stab], out=stab[:], in_=stab[:], func=AF.Exp, scale=0.5)
                P.v("tensor_scalar", [stab], [stab], out=stab[:], in0=stab[:], scalar1=-1.02, scalar2=None, op0=ALU.mult)
            gt, selc = tile_gt[i]
            pg = rpg.next()
            for hp in range(4):
                h = g * 4 + hp
                qh = qT[:, h, t0:t0 + 128]
                ps = R.rps.next()
                P.mm([qT, kcmpT], [ps], ps[:, 0:256], qh, kcmpT[:, :])
                mx = rcmx.next()
                P.v("reduce_max", [ps], [mx], out=mx[:, 0:1], in_=ps[:, 0:256], axis=AX.X)
                P.v("tensor_scalar", [mx], [mx], out=mx[:, 1:2], in0=mx[:, 0:1], scalar1=-1.0, scalar2=None, op0=ALU.mult)
                e32 = re32.next()
                P.s("activation", [ps, mx], [e32], out=e32[:], in_=ps[:, 0:256], func=AF.Exp, bias=mx[:, 1:2], scale=1.0)
                P.v("scalar_tensor_tensor", [e32, selc], [e32, mx], out=e32[:], in0=e32[:], scalar=1.0, in1=selc[:], op0=ALU.mult, op1=ALU.mult,
                    accum_out=mx[:, 2:3])
                P.v("tensor_scalar", [mx], [mx], out=mx[:, 3:4], in0=mx[:, 2:3], scalar1=1e-30, scalar2=None, op0=ALU.add)
                P.v("reciprocal", [mx], [mx], out=mx[:, 4:5], in_=mx[:, 3:4])
                if hp == 0:
                    P.v("tensor_scalar", [e32, mx], [pg], out=pg[:], in0=e32[:], scalar1=mx[:, 4:5], scalar2=None, op0=ALU.mult)
                else:
                    P.v("scalar_tensor_tensor", [e32, mx, pg], [pg], out=pg[:], in0=e32[:], scalar=mx[:, 4:5], in1=pg[:], op0=ALU.mult, op1=ALU.add)
                p16 = rp16.next()
                P.g("tensor_copy", [e32], [p16], out=p16[:], in_=e32[:])
                yield
                pTp = R.rpT.next()
                for mc in range(2):
                    P.tr([p16], [pTp], pTp[:, mc, :], p16[:, mc * 128:(mc + 1) * 128], K["idb"])
                pT = R.rpt.next()
                P.s("copy", [pTp], [pT], out=pT[:, 0:2, :], in_=pTp[:, 0:2, :])
                yield
                pc = rpc.next()
                for mc in range(2):
                    P.mm([pT, vcmp], [pc], pc[:], pT[:, mc, :], vcmp[:, mc, g, :], mc == 0, mc == 1)
                P.v("tensor_tensor", [mx, gt], [mx], out=mx[:, 5:6], in0=mx[:, 4:5], in1=gt[:, h * 3:h * 3 + 1], op=ALU.mult)
                P.v("tensor_scalar", [pc, mx], [("ycmp", i % 2, h)], out=ycmp[:, i % 2, h, :], in0=pc[:], scalar1=mx[:, 5:6], scalar2=None, op0=ALU.mult)
                yield
            nm = negm[i % 2][g]
            if i >= 8:
                sc = rsc.next()
                m8 = rm8.next()
                imp, s1_, s2_, bm = sc[:, 0, :], sc[:, 1, :], sc[:, 2, :], sc[:, 3, :]
                P.v("reduce_sum", [pg], [sc], out=imp, in_=pg[:].rearrange("p (b f) -> p b f", f=4), axis=AX.X)
                P.v("tensor_tensor", [sc, pg], [sc], out=sc[:, 0, 1:64], in0=sc[:, 0, 1:64], in1=pg[:, 3:255:4], op=ALU.add)
                P.v("tensor_scalar", [dlt], [sc], out=s2_, in0=dlt[:], scalar1=float(128 - t0), scalar2=1e6, op0=ALU.is_lt, op1=ALU.mult)
                P.v("tensor_tensor", [sc], [sc], out=s1_, in0=imp, in1=s2_, op=ALU.max)
                P.v("tensor_scalar", [dlt], [sc], out=s2_, in0=dlt[:], scalar1=float(-t0), scalar2=None, op0=ALU.is_ge)
                P.v("tensor_tensor", [sc], [sc], out=s1_, in0=s1_, in1=s2_, op=ALU.mult)
                P.v("tensor_scalar", [sc], [sc], out=s2_, in0=s2_, scalar1=-1.0, scalar2=1e30, op0=ALU.add, op1=ALU.mult)
                P.v("tensor_tensor", [sc], [sc], out=s1_, in0=s1_, in1=s2_, op=ALU.add)
                P.g("memset", [sc], [sc], sc[:, 1, 0:1], 1e6)
                yield
                P.v("max", [sc], [m8], out=m8[:, 0:8], in_=s1_)
                P.v("match_replace", [sc, m8], [sc], out=s2_, in_to_replace=m8[:, 0:8], in_values=s1_, imm_value=NEG)
                P.v("max", [sc], [m8], out=m8[:, 8:16], in_=s2_)
                P.v("tensor_scalar", [sc, m8], [sc], out=bm, in0=s1_, scalar1=m8[:, 15:16], scalar2=MASKV, op0=ALU.is_lt, op1=ALU.mult)
                nb = L // 64
                P.g("tensor_copy", [sc, nm], [nm], out=nm[:, 0:L].rearrange("p (b f) -> p b f", f=64),
                    in_=sc[:, 3, 0:nb].unsqueeze(2).to_broadcast([128, nb, 64]))
                P.g("tensor_tensor", [nm, ntri_ge], [nm], out=nm[:, L - 128:L], in0=nm[:, L - 128:L], in1=ntri_ge[:], op=ALU.add)
            else:
                if i > 0:
                    P.g("memset", [], [nm], nm[:, 0:L - 128], 0.0)
                P.g("tensor_copy", [ntri_ge, nm], [nm], out=nm[:, L - 128:L], in_=ntri_ge[:])

        pipe = Pipe()
        work = [(i, g) for i in range(NT) for g in range(2)]
        for _ in prep(0, 0):
            pass
        ybs = {}
        for wi_, (i, g) in enumerate(work):
            rows = slice(i * 128, (i + 1) * 128)
            t0 = i * 128
            L = (i + 1) * 128
            nxt = prep(*work[wi_ + 1]) if wi_ + 1 < len(work) else None
            pipe.filler = nxt
            npush = 4 * ((L + 511) // 512 + (min(L, 640) + 511) // 512)
            pipe.rate = -(-16 // npush)
            if g == 0:
                ybs[i] = ryb.next()
            yb = ybs[i]
            gt, _selc = tile_gt[i]
            nm = negm[i % 2][g]
            for hp in range(4):
                h = g * 4 + hp
                qh = qT[:, h, t0:t0 + 128]
                stab = tile_stab[i]
                po = rpo.next()
                cf = rcf.next()
                terms = [(qh, lambda c0, n: ksT[:, c0:c0 + n], [qT, ksT])]
                mterm = (K["idb"][:], lambda c0, n, nm=nm: nm[:, c0:c0 + n], [nm, K["idb"]])

                def pv_s(pT, rd, kt, first, last, po=po, g=g):
                    P.mm(rd + [vsw], [po], po[:, 0, :], pT, vsw[:, kt, g * 64:(g + 1) * 64], first, last)

                def done_s(rs, rinv, cf=cf, gt=gt, h=h):
                    P.v("tensor_tensor", [rs, gt], [cf], out=cf[:, 1:2], in0=rinv, in1=gt[:, h * 3 + 1:h * 3 + 2], op=ALU.mult)
                for it in attn_items(P, K, R, terms, mterm, 0, L, pv_s, done_s, negm_ap=stab[:, 0, hp, g:g + 1], negm_reads=[stab]):
                    pipe.push(it)
                k0 = max(0, (i - 4) * 128)
                woff = 640 - (L - k0)
                terms = [(qh, lambda c0, n: kwT[:, c0:c0 + n], [qT, kwT])]
                mterm = (K["idb"][:], lambda c0, n, k0=k0, woff=woff: negw[:, woff + c0 - k0:woff + c0 - k0 + n], [negw, K["idb"]])

                def pv_w(pT, rd, kt, first, last, po=po, g=g, k0=k0):
                    P.mm(rd + [vsw], [po], po[:, 1, :], pT, vsw[:, k0 // 128 + kt, 128 + g * 64:128 + (g + 1) * 64], first, last)

                def done_w(rs, rinv, cf=cf, gt=gt, h=h, po=po, yb=yb, i=i, rows=rows):
                    P.v("tensor_tensor", [rs, gt], [cf], out=cf[:, 2:3], in0=rinv, in1=gt[:, h * 3 + 2:h * 3 + 3], op=ALU.mult)
                    ys = yb[:, h * 64:(h + 1) * 64]
                    P.v("scalar_tensor_tensor", [po, cf, ("ycmp", i % 2, h)], [yb], out=ys, in0=po[:, 0, :], scalar=cf[:, 1:2], in1=ycmp[:, i % 2, h, :],
                        op0=ALU.mult, op1=ALU.add)
                    P.v("scalar_tensor_tensor", [po, cf, yb], [yb], out=ys, in0=po[:, 1, :], scalar=cf[:, 2:3], in1=ys, op0=ALU.mult, op1=ALU.add)
                    if h == 7:
                        P.dma("gpsimd", [yb], [("attn", "b", i)], out=S["attn"][rows, 512:1024], in_=yb[:])
                for it in attn_items(P, K, R, terms, mterm, k0, L, pv_w, done_w, negm_ap=stab[:, 1, hp, g:g + 1], negm_reads=[stab]):
                    pipe.push(it)
            if nxt is not None:
                for _ in nxt:
                    pass
        pipe.flush()


W_SPECS = dict(
    x=[T, D], p=[DEPTH, T, 256],
    e_w_in=[2, 1024, 3360], e_a_conv=[2, 4, 1024], e_a_i_b=[2, 4], e_a_f_b=[2, 4], e_a_norm=[2, 512],
    e_b_cmp_pos=[2, 2, 32, 64], e_b_cmp_w1=[2, 2, 2048, 128], e_b_cmp_w2=[2, 2, 128, 64], e_b_g_b=[2, 24],
    e_w_out=[2, 1024, 1024], o_w_in=[2, 1024, 904], o_q_norm=[2, 512], o_kv_norm=[2, 256], o_w_qb=[2, 512, 1536],
    o_w_uk=[2, 256, 8, 128], o_w_uv=[2, 256, 8, 128], o_w_iq=[2, 512, 512], o_ik_g=[2, 64], o_ik_b=[2, 64],
    o_w_out=[2, 1024, 1024], ln1_g=[4, 1024], ln1_b=[4, 1024], ln2_g=[4, 1024], ln2_b=[4, 1024],
    mlp_w1=[4, 1024, 4096], mlp_w2=[4, 4096, 1024], ple_gate_w=[4, 1024, 1024], ple_w=[4, 256, 1024], rope_inv=[48])


def rope_inv_table():
    a = (10000.0 ** (-np.arange(0, 64, 2, dtype=np.float32) / np.float32(64))).astype(np.float32)
    b = (10000.0 ** (-np.arange(0, 32, 2, dtype=np.float32) / np.float32(32))).astype(np.float32)
    return np.concatenate([a, b]).astype(np.float32)


def build(debug_outs=(), phases=None, layers=(0, 1, 2, 3)):
    nc = bass.Bass("TRN2", target_bir_lowering=False)
    dbg = set(debug_outs)
    allp = phases is None

    def on(p):
        return allp or p in phases

    def dram(name, shape, dt, kind=None):
        if kind is None:
            kind = "ExternalOutput" if name in dbg else "Internal"
        return nc.dram_tensor(name, shape, dt, kind=kind).ap()

    W = {k: dram(k, s, F32, "ExternalInput") for k, s in W_SPECS.items()}
    W["positions"] = dram("positions", [T], I32, "ExternalInput")
    out = dram("out", [T, D], F32, "ExternalOutput")
    S = dict(
        v_tok=dram("v_tok", [T, 512], BF16), sigo=dram("sigo", [T, 512], F32), bv_tok=dram("bv_tok", [T, 256], BF16),
        bgate=dram("bgate", [T, 24], F32), gsc=dram("gsc", [3, 4, T], F32), qkT=dram("qkT", [8, 128, T], BF16),
        bqT=dram("bqT", [4, 128, T], BF16), bkT=dram("bkT", [4, 128, T], BF16), attn=dram("attn", [T, 1024], F32),
        hA=dram("hA", [T, D], F32), h1=dram("h1", [T, D], F32), h2=dram("h2", [T, D], F32),
        qaT=dram("qaT", [16, 128, T], BF16), qrT=dram("qrT", [4, 128, T], BF16), qiT=dram("qiT", [4, 128, T], BF16),
        kT=dram("kT", [4, 128, T], BF16), ckv=dram("ckv", [T, 256], BF16), wi=dram("wi", [T, 16], F32),
        hB=dram("hB", [T, D], F32), qn2=dram("qn2", [T, 8], F32), kn2=dram("kn2", [128, NT], F32),
        nall=dram("nall", [128, NT * 8], F32), kall=dram("kall", [128, 4 * NT], F32),
    )
    with ExitStack() as st:
        P = Prog(nc)
        P.setup(st)
        with Ctx(nc, P) as C0:
            K = make_consts(C0, P)
            h_in = W["x"]
            wrote_out = False
            for n, li in enumerate(layers):
                j = li // 2
                last = (n == len(layers) - 1)
                if li % 2 == 0:
                    if on("e1"):
                        phase_e1(nc, P, K, S, W, j, h_in)
                    if on("e2"):
                        phase_e2(nc, P, K, S, W, j)
                    if on("e3"):
                        phase_e3(nc, P, K, S, W, j)
                    w_out = W["e_w_out"][j]
                else:
                    if on("o1"):
                        phase_o1(nc, P, K, S, W, j, h_in)
                    if on("o3"):
                        phase_o3(nc, P, K, S, W, j)
                    w_out = W["o_w_out"][j]
                if on("ta"):
                    phase_tail_a(nc, P, K, S, w_out, W["ln1_g"][li], W["ln1_b"][li], h_in, S["h1"])
                if on("tb"):
                    phase_tail_b(nc, P, K, S, W["mlp_w1"][li], W["mlp_w2"][li], W["ln2_g"][li], W["ln2_b"][li], S["h1"], S["h2"])
                if on("tc"):
                    h_out = out if last else (S["hA"] if n % 2 == 0 else S["hB"])
                    phase_tail_c(nc, P, K, S, W["ple_gate_w"][li], W["ple_w"][li], W["p"][li], S["h2"], h_out, last)
                    wrote_out = wrote_out or last
                    h_in = h_out
            if not wrote_out:
                zt = C0.sb("zt", [128, 1024], F32)
                P.g("memset", [], [zt], zt[:], 0.0)
                P.dma("sync", [zt], ["out"], out=out[0:128, :], in_=zt[:], is_output=True)
        P.finish()
    return nc


_NC_CACHE = {}


def kernel(**inputs):
    if "nc" not in _NC_CACHE:
        _NC_CACHE["nc"] = build()
    nc = _NC_CACHE["nc"]
    B = inputs["x"].shape[0]
    rinv = rope_inv_table()
    in_maps = []
    for b in range(B):
        m = {}
        for k in W_SPECS:
            if k == "x":
                m[k] = np.ascontiguousarray(inputs["x"][b], dtype=np.float32)
            elif k == "p":
                m[k] = np.ascontiguousarray(inputs["p"][:, b], dtype=np.float32)
            elif k == "rope_inv":
                m[k] = rinv
            else:
                m[k] = np.ascontiguousarray(inputs[k], dtype=np.float32)
        m["positions"] = np.ascontiguousarray(inputs["positions"][b], dtype=np.int32)
        in_maps.append(m)
    res = run_bass_kernel_spmd(nc, in_maps, core_ids=list(range(B)))
    return np.stack([np.asarray(r["out"], dtype=np.float32) for r in res.results], axis=0)
```

```python
from contextlib import ExitStack
import numpy as np
import concourse.bass as bass
import concourse.mybir as mybir
from concourse.bass_utils import run_bass_kernel_spmd

F32 = mybir.dt.float32
BF16 = mybir.dt.bfloat16
I32 = mybir.dt.int32
ALU = mybir.AluOpType
AF = mybir.ActivationFunctionType
AX = mybir.AxisListType

T = 4096
D = 1024
NT = T // 128
DEPTH = 4
DFF = 4096
DN_ALPHA = (2.0 * DEPTH) ** 0.25
LN_EPS = 1e-5
NEG = -1e30
N_DMA_SEMS = 8


class Prog:
    ENGS = ("tensor", "vector", "scalar", "gpsimd", "sync")

    def __init__(self, nc):
        self.nc = nc
        self.ops = {k: [] for k in self.ENGS}
        self.count = {k: 0 for k in self.ENGS}
        self.waited = {k: {} for k in self.ENGS}
        self.sems = {}
        self.writers = {}
        self.readers = {}
        self.dma_val = [0] * N_DMA_SEMS
        self.dma_rr = 0
        self.out_tokens = []

    def setup(self, stack):
        for k in self.ENGS:
            self.sems[k] = stack.enter_context(self.nc.semaphore("s_" + k))
        for i in range(N_DMA_SEMS):
            self.sems["d%d" % i] = stack.enter_context(self.nc.semaphore("d_%d" % i))

    @staticmethod
    def _key(k):
        if isinstance(k, (str, tuple)):
            return k
        t = getattr(k, "tensor", k)
        return t.name

    def _wait(self, eng, s, v):
        w = self.waited[eng]
        if w.get(s, 0) < v:
            w[s] = v
            self.ops[eng].append(("wait", self.sems[s], v))

    def _deps(self, eng, reads, writes):
        need = {}
        for k in reads:
            for s, v in self.writers.get(k, {}).items():
                if need.get(s, 0) < v:
                    need[s] = v
        for k in writes:
            for d in (self.writers.get(k, {}), self.readers.get(k, {})):
                for s, v in d.items():
                    if need.get(s, 0) < v:
                        need[s] = v
        for s, v in need.items():
            if eng == "tensor" and s == "tensor":
                continue
            self._wait(eng, s, v)

    def _record(self, tok, reads, writes):
        s, v = tok
        for k in reads:
            self.readers.setdefault(k, {})[s] = v
        for k in writes:
            self.writers[k] = {s: v}
            self.readers[k] = {}

    def op(self, eng, meth, reads, writes, *args, **kw):
        reads = [self._key(k) for k in reads]
        writes = [self._key(k) for k in writes]
        self._deps(eng, reads, writes)
        self.count[eng] += 1
        self.ops[eng].append(("op", (meth, args, kw), self.sems[eng], 1))
        self._record((eng, self.count[eng]), reads, writes)

    def mm(self, reads, writes, out, lhsT, rhs, start=True, stop=True):
        self.op("tensor", "matmul", reads, writes, out, lhsT=lhsT, rhs=rhs, start=start, stop=stop)

    def tr(self, reads, writes, out, in_, ident):
        self.op("tensor", "transpose", reads + [ident], writes, out=out, in_=in_, identity=ident[:])

    def v(self, meth, reads, writes, **kw):
        self.op("vector", meth, reads, writes, **kw)

    def s(self, meth, reads, writes, **kw):
        self.op("scalar", meth, reads, writes, **kw)

    def g(self, meth, reads, writes, *args, **kw):
        self.op("gpsimd", meth, reads, writes, *args, **kw)

    def dma(self, eng, reads, writes, out, in_, is_output=False, **kw):
        fn = ("dma_start", (), dict(out=out, in_=in_, **kw))
        reads = [self._key(k) for k in reads]
        writes = [self._key(k) for k in writes]
        i = self.dma_rr
        self.dma_rr = (self.dma_rr + 1) % N_DMA_SEMS
        sname = "d%d" % i
        self._deps(eng, reads, writes)
        if self.dma_val[i]:
            self._wait(eng, sname, self.dma_val[i])
        self.dma_val[i] += 16
        self.ops[eng].append(("op", fn, self.sems[sname], 16))
        self._record((sname, self.dma_val[i]), reads, writes)
        if is_output:
            self.out_tokens.append((sname, self.dma_val[i]))

    def barrier(self):
        for e in self.ENGS:
            for s in self.ENGS:
                if s != e and self.count[s]:
                    self._wait(e, s, self.count[s])
            for i in range(N_DMA_SEMS):
                if self.dma_val[i]:
                    self._wait(e, "d%d" % i, self.dma_val[i])
        for e in self.ENGS:
            if self.count[e]:
                self._wait(e, e, self.count[e])
        self.writers.clear()
        self.readers.clear()

    def finish(self):
        for s, v in self.out_tokens:
            self._wait("sync", s, v)
        ops = self.ops

        def replay(e, lst):
            for o in lst:
                if o[0] == "wait":
                    e.wait_ge(o[1], o[2])
                else:
                    meth, args, kw = o[1]
                    try:
                        ins = getattr(e, meth)(*args, **kw)
                    except Exception:
                        print("FAILED OP", meth, args, kw)
                        raise
                    ins.then_inc(o[2], o[3])

        with self.nc.Block() as block:
            @block.tensor
            def _(e):
                replay(e, ops["tensor"])

            @block.vector
            def _(e):
                replay(e, ops["vector"])

            @block.scalar
            def _(e):
                replay(e, ops["scalar"])

            @block.gpsimd
            def _(e):
                replay(e, ops["gpsimd"])

            @block.sync
            def _(e):
                replay(e, ops["sync"])


class Rot:
    def __init__(self, bufs):
        self.bufs = bufs
        self.i = 0

    def next(self):
        b = self.bufs[self.i % len(self.bufs)]
        self.i += 1
        return b


class Ctx:
    uid = 0

    def __init__(self, nc, P):
        self.nc = nc
        self.P = P
        self.st = ExitStack()

    def __enter__(self):
        self.st.__enter__()
        return self

    def __exit__(self, *a):
        self.P.barrier()
        return self.st.__exit__(*a)

    def sb(self, name, shape, dt):
        Ctx.uid += 1
        return self.st.enter_context(self.nc.sbuf_tensor("%s_%d" % (name, Ctx.uid), shape, dt))

    def ps(self, name, shape, dt=F32):
        Ctx.uid += 1
        return self.st.enter_context(self.nc.psum_tensor("%s_%d" % (name, Ctx.uid), shape, dt))

    def sbrot(self, name, shape, dt, n=2):
        return Rot([self.sb(name + str(i), shape, dt) for i in range(n)])

    def psrot(self, name, shape, dt=F32, n=2):
        return Rot([self.ps(name + str(i), shape, dt) for i in range(n)])


def make_consts(C, P):
    k = {}
    idf = C.sb("identf", [128, 128], F32)
    idb = C.sb("identb", [128, 128], BF16)
    P.g("memset", [], [idf], idf[:], 1.0)
    P.g("affine_select", [idf], [idf], out=idf[:], in_=idf[:], pattern=[[-1, 128]], compare_op=ALU.is_equal,
        fill=0.0, base=0, channel_multiplier=1)
    P.v("tensor_copy", [idf], [idb], out=idb[:], in_=idf[:])
    k["idf"], k["idb"] = idf, idb
    for name, mult, patt, base in (("tri_le", -1, 1, 0), ("tri_ge", 1, -1, 0), ("tri_lt", -1, 1, -1)):
        t = C.sb(name, [128, 128], F32)
        P.g("memset", [], [t], t[:], 1.0)
        P.g("affine_select", [t], [t], out=t[:], in_=t[:], pattern=[[patt, 128]], compare_op=ALU.is_ge, fill=0.0,
            base=base, channel_multiplier=mult)
        k[name] = t
    return k


def load_transposed(P, K, src, xT, nk, key, rot_in, rot_ps, tiles=range(NT), t0=0):
    for i in tiles:
        xt = rot_in.next()
        P.dma("sync", [], [xt], out=xt[:, 0:nk * 128], in_=src[i * 128:(i + 1) * 128, :])
        for half in range((nk + 3) // 4):
            n = min(4, nk - half * 4)
            pt = rot_ps.next()
            for jj in range(n):
                c = half * 4 + jj
                P.tr([xt], [pt], pt[:, jj, :], xt[:, c * 128:(c + 1) * 128], K["idf"])
            col = (i - t0) * 128
            dst = xT[:, half * 4:half * 4 + n, col:col + 128]
            if half % 2 == 0:
                P.v("tensor_copy", [pt], [(key, i)], out=dst, in_=pt[:, 0:n, :])
            else:
                P.s("copy", [pt], [(key, i)], out=dst, in_=pt[:, 0:n, :])


def run_staged(n, body):
    prev = None
    for i in range(n):
        g = body(i)
        next(g, None)
        if prev is not None:
            for _ in prev:
                pass
        prev = g
    if prev is not None:
        for _ in prev:
            pass


def run_interleaved(n, body, k):
    live = []
    nxt = 0
    while live or nxt < n:
        while len(live) < k and nxt < n:
            live.append(body(nxt))
            nxt += 1
        for g in list(live):
            try:
                next(g)
            except StopIteration:
                live.remove(g)


def load_w_bf16(P, dst, src, nk, key=None):
    for kc in range(nk):
        P.dma("gpsimd", [], [key or dst], out=dst[:, kc, :], in_=src[kc * 128:(kc + 1) * 128, :])


def phase_e1(nc, P, K, S, W, j, h_in):
    with Ctx(nc, P) as C:
        hT = C.sb("hT", [128, 8, T], BF16)
        w = C.sb("w_in", [128, 8, 3360], BF16)
        wq = C.sb("w_q", [128, 8, 4, 2, 64], BF16)
        load_w_bf16(P, w, W["e_w_in"][j], 8)
        for kc in range(8):
            for g in range(2):
                P.dma("gpsimd", [], [wq], out=wq[:, kc, :, g, :],
                      in_=W["e_w_in"][j][kc * 128:(kc + 1) * 128, 2056 + g * 256:2056 + (g + 1) * 256].rearrange("p (c d) -> p c d", c=4))
        rps = C.psrot("ps", [128, 512], F32, 3)
        rpt = C.psrot("pt", [128, 4, 128], F32, 2)
        with Ctx(nc, P) as C1:
            rin = C1.sbrot("hin", [128, 1024], F32, 2)
            load_transposed(P, K, h_in, hT, 8, "hT", rin, rpt)

        def feat_mm(ps, lhs_fn, tg, m=128):
            for kc in range(8):
                P.mm([("hT", i2) for i2 in range(tg * 4, tg * 4 + 4)] + [w, wq], [ps], ps[0:m, :], lhs_fn(kc),
                     hT[:, kc, tg * 512:(tg + 1) * 512], kc == 0, kc == 7)

        with Ctx(nc, P) as C1:
            bi = C1.sb("bi", [4, 1], F32)
            bfn = C1.sb("bfn", [4, 1], F32)
            P.dma("sync", [], [bi], out=bi[:], in_=W["e_a_i_b"][j].rearrange("(h o) -> h o", o=1))
            P.dma("sync", [], [bfn], out=bfn[:], in_=W["e_a_f_b"][j].rearrange("(h o) -> h o", o=1))
            P.v("tensor_scalar", [bfn], [bfn], out=bfn[:], in0=bfn[:], scalar1=-1.0, scalar2=None, op0=ALU.mult)
            ig = C1.sb("ig", [4, T], F32)
            sp = C1.sb("sp", [4, T], F32)
            bneg = C1.sb("bneg", [4, T], F32)
            cst = C1.sb("cst", [4, T], F32)
            for tg in range(8):
                cs = slice(tg * 512, (tg + 1) * 512)
                ps = rps.next(); feat_mm(ps, lambda kc: w[:, kc, 2048:2052], tg, 4)
                P.s("activation", [ps, bi], [ig], out=ig[:, cs], in_=ps[0:4, :], func=AF.Identity, bias=bi[:], scale=1.0)
                ps = rps.next(); feat_mm(ps, lambda kc: w[:, kc, 2052:2056], tg, 4)
                P.s("activation", [ps, bfn], [sp], out=sp[:, cs], in_=ps[0:4, :], func=AF.Exp, bias=bfn[:], scale=-1.0)
            P.s("activation", [sp], [sp], out=sp[:], in_=sp[:], func=AF.Ln, bias=1.0, scale=1.0)
            P.g("memset", [], [cst], cst[:], 1.0)
            P.v("tensor_tensor_scan", [cst, sp], [bneg], out=bneg[:], data0=cst[:], data1=sp[:], initial=0.0, op0=ALU.mult, op1=ALU.add)
            P.v("tensor_tensor", [ig, bneg], [ig], out=ig[:], in0=ig[:], in1=bneg[:], op=ALU.add)
            P.g("memset", [cst], [cst], cst[:], 0.0)
            P.v("tensor_tensor_scan", [cst, ig], [sp], out=sp[:], data0=cst[:], data1=ig[:], initial=0.0, op0=ALU.add, op1=ALU.max)
            P.v("tensor_tensor", [bneg, sp], [bneg], out=bneg[:], in0=bneg[:], in1=sp[:], op=ALU.subtract)
            P.v("tensor_scalar", [sp], [sp], out=sp[:], in0=sp[:], scalar1=-1.0, scalar2=None, op0=ALU.mult)
            P.dma("sync", [ig], ["gsc0"], out=S["gsc"][0], in_=ig[:])
            P.dma("sync", [sp], ["gsc1"], out=S["gsc"][1], in_=sp[:])
            P.dma("sync", [bneg], ["gsc2"], out=S["gsc"][2], in_=bneg[:])

        with Ctx(nc, P) as C1:
            convw = C1.sb("convw", [128, 8, 4], F32)
            for kk in range(4):
                P.dma("sync", [], [convw], out=convw[:, :, kk], in_=W["e_a_conv"][j][kk].rearrange("(c p) -> p c", p=128),
                      allow_slow_non_contiguous=True)
            bg = C1.sb("bg", [128, 24], F32)
            P.dma("sync", [], [bg], out=bg[:], in_=W["e_b_g_b"][j].partition_broadcast(128))
            rst = C1.sbrot("stg", [128, 512], F32, 2)
            rstb = C1.sbrot("stgb", [128, 512], BF16, 3)
            for i in range(NT):
                rows = slice(i * 128, (i + 1) * 128)

                def tok_mm(ps, c0, n, pc0=0):
                    for kc in range(8):
                        P.mm([("hT", i), w], [ps], ps[:, pc0:pc0 + n], hT[:, kc, i * 128:(i + 1) * 128], w[:, kc, c0:c0 + n], kc == 0, kc == 7)
                ps = rps.next(); tok_mm(ps, 1024, 512)
                sb_ = rstb.next()
                P.s("copy", [ps], [sb_], out=sb_[:], in_=ps[:])
                P.dma("gpsimd", [sb_], [("v_tok", i)], out=S["v_tok"][rows, :], in_=sb_[:])
                ps = rps.next(); tok_mm(ps, 1536, 512)
                st_ = rst.next()
                P.s("activation", [ps], [st_], out=st_[:], in_=ps[:], func=AF.Sigmoid)
                P.dma("gpsimd", [st_], [("sigo", i)], out=S["sigo"][rows, :], in_=st_[:])
                ps = rps.next(); tok_mm(ps, 2952, 128, 0); tok_mm(ps, 3208, 128, 128); tok_mm(ps, 3336, 24, 256)
                sb_ = rstb.next()
                P.v("tensor_copy", [ps], [sb_], out=sb_[:, 0:256], in_=ps[:, 0:256])
                P.dma("gpsimd", [sb_], [("bv_tok", i)], out=S["bv_tok"][rows, :], in_=sb_[:, 0:256])
                st_ = rst.next()
                P.v("tensor_tensor", [ps, bg], [st_], out=st_[:, 0:24], in0=ps[:, 256:280], in1=bg[:], op=ALU.add)
                P.s("activation", [st_], [st_], out=st_[:, 32:56], in_=st_[:, 0:24], func=AF.Sigmoid)
                P.dma("gpsimd", [st_], [("bgate", i)], out=S["bgate"][rows, :], in_=st_[:, 32:56])

            xpad = C1.sb("xpad", [128, 3 + T], F32)
            P.g("memset", [], [("xpad", -1)], xpad[:, 0:3], 0.0)
            racc = C1.sbrot("acc", [128, 512], F32, 2)
            for c in range(8):
                for tg in range(8):
                    ps = rps.next(); feat_mm(ps, lambda kc: w[:, kc, c * 128:(c + 1) * 128], tg)
                    P.s("copy", [ps], [("xpad", tg)], out=xpad[:, 3 + tg * 512:3 + (tg + 1) * 512], in_=ps[:])
                    acc = racc.next()
                    t0 = tg * 512
                    rd = [("xpad", tg - 1), ("xpad", tg), convw]
                    P.v("tensor_scalar", rd, [acc], out=acc[:], in0=xpad[:, t0:t0 + 512], scalar1=convw[:, c, 0:1], scalar2=None, op0=ALU.mult)
                    for jj in range(1, 4):
                        P.v("scalar_tensor_tensor", rd + [acc], [acc], out=acc[:], in0=xpad[:, t0 + jj:t0 + jj + 512],
                            scalar=convw[:, c, jj:jj + 1], in1=acc[:], op0=ALU.mult, op1=ALU.add)
                    ob = rstb.next()
                    P.s("activation", [acc], [ob], out=ob[:], in_=acc[:], func=AF.Silu)
                    P.dma("gpsimd", [ob], [("qkT", c, tg)], out=S["qkT"][c][:, t0:t0 + 512], in_=ob[:])
            blk = C1.sb("blk", [128, 2], BF16)
            P.g("memset", [], [blk], blk[:], 0.0)
            P.g("memset", [blk], [blk], blk[0:64, 0:1], 1.0)
            P.g("memset", [blk], [blk], blk[64:128, 1:2], 1.0)
            nall = C1.sb("nall", [128, NT, 8], F32)
            kall = C1.sb("kall", [128, 2, NT, 2], F32)
            rsqn = C1.sbrot("sqn", [128, 512], BF16, 2)
            rpn = C1.psrot("pn", [128, 8], F32, 1)

            def norms(ob, dst_fn):
                sq = rsqn.next()
                P.g("tensor_tensor", [ob], [sq], out=sq[:], in0=ob[:], in1=ob[:], op=ALU.mult)
                pn = rpn.next()
                for k in range(4):
                    P.mm([sq, blk], [pn], pn[:, 2 * k:2 * k + 2], sq[:, k * 128:(k + 1) * 128], blk[:])
                dst_fn(pn)
            for c in range(4):
                for tg in range(8):
                    ps = rps.next(); feat_mm(ps, lambda kc: wq[:, kc, c].rearrange("p g d -> p (g d)"), tg)
                    ob = rstb.next()
                    P.s("mul", [ps], [ob], out=ob[:], in_=ps[:], mul=0.125)
                    P.dma("gpsimd", [ob], [("bqT", c, tg)], out=S["bqT"][c][:, tg * 512:(tg + 1) * 512], in_=ob[:])
                    norms(ob, lambda pn: P.v("tensor_copy", [pn], [nall], out=nall[:, tg * 4:(tg + 1) * 4, 2 * c:2 * c + 2],
                                             in_=pn[:].rearrange("p (k g) -> p k g", g=2)))
            for n, c0 in enumerate((2568, 2696, 2824, 3080)):
                for tg in range(8):
                    ps = rps.next(); feat_mm(ps, lambda kc: w[:, kc, c0:c0 + 128], tg)
                    ob = rstb.next()
                    P.v("tensor_copy", [ps], [ob], out=ob[:], in_=ps[:])
                    P.dma("gpsimd", [ob], [("bkT", n, tg)], out=S["bkT"][n][:, tg * 512:(tg + 1) * 512], in_=ob[:])
                    if n >= 2:
                        norms(ob, lambda pn: P.v("tensor_copy", [pn], [kall], out=kall[:, n - 2, tg * 4:(tg + 1) * 4, :],
                                                 in_=pn[:].rearrange("p (k g) -> p k g", g=2)))
            P.dma("sync", [nall], ["nalld"], out=S["nall"], in_=nall[:].rearrange("p a b -> p (a b)"))
            P.dma("sync", [kall], ["kalld"], out=S["kall"], in_=kall[:].rearrange("p a b c -> p (a b c)"))


def phase_e2(nc, P, K, S, W, j):
    NC_ = NT
    with Ctx(nc, P) as C:
        rows = C.sb("rows", [4, 3, T], F32)
        for r in range(3):
            P.dma("sync", ["gsc%d" % r], [rows], out=rows[:, r, :], in_=S["gsc"][r])
        sel = C.sb("sel", [4, 4, 128], F32)
        P.g("memset", [], [sel], sel[:], 1.0)
        P.g("affine_select", [sel], [sel], out=sel[:], in_=sel[:], pattern=[[-1, 4], [0, 128]], compare_op=ALU.is_equal, fill=0.0,
            base=0, channel_multiplier=1)
        gnorm = C.sb("gnorm", [128, 512], F32)
        P.dma("sync", [], [gnorm], out=gnorm[:], in_=W["e_a_norm"][j].partition_broadcast(128))
        maskT = C.sb("maskT", [128, 128], F32)
        P.v("tensor_scalar", [K["tri_le"]], [maskT], out=maskT[:], in0=K["tri_le"][:], scalar1=128.0 ** -0.5, scalar2=None, op0=ALU.mult)

        rps_a = C.psrot("psa", [128, 512], F32, 1)
        rps_s = C.psrot("pss", [128, 128], F32, 2)
        rps_o = C.psrot("pso", [128, 132], F32, 2)
        rps_i = C.psrot("psi", [128, 132], F32, 1)
        rps_k = C.psrot("psk", [128, 128], BF16, 1)
        rET = C.sbrot("ET", [128, 128], F32, 2)
        rETm = C.sbrot("ETm", [128, 128], F32, 2)
        rPT = C.sbrot("PT", [128, 128], BF16, 2)
        rksc = C.sbrot("ksc", [128, 128], BF16, 2)
        rintra = C.sbrot("intra", [128, 132], F32, 2)

        for h in range(4):
          with Ctx(nc, P) as CH:
            qT = CH.sb("qT", [128, T], BF16)
            kT = CH.sb("kT", [128, T], BF16)
            vaug = CH.sb("vaug", [128, NC_, 132], BF16)
            nd = CH.sb("nd", [128, NC_, 132], F32)
            P.dma("sync", [("qkT", h, tg) for tg in range(8)], [qT], out=qT[:], in_=S["qkT"][h])
            P.dma("sync", [("qkT", 4 + h, tg) for tg in range(8)], [kT], out=kT[:], in_=S["qkT"][4 + h])
            P.g("memset", [], [vaug], vaug[:, :, 128:132], 1.0)
            P.dma("sync", [("v_tok", i) for i in range(NT)], [vaug], out=vaug[:, :, 0:128],
                  in_=S["v_tok"][:, h * 128:(h + 1) * 128].rearrange("(c p) d -> p c d", p=128))
            cols = CH.sb("cols", [128, 3, NC_], F32)
            for r in range(3):
                pc = rps_a.next()
                for c in range(NC_):
                    P.mm([rows, sel], [pc], pc[:, c:c + 1], rows[:, r, c * 128:(c + 1) * 128], sel[:, h, 0:1])
                P.v("tensor_copy", [pc], [cols], out=cols[:, r, :], in_=pc[:, 0:NC_])
            ends = CH.sb("ends", [128, 1 + NC_], F32)
            pc = rps_a.next()
            P.mm([rows, sel], [pc], pc[:, 0:NC_], sel[:, h, :], rows[:, 1, 127::128])
            P.g("memset", [], [ends], ends[:, 0:1], 0.0)
            P.v("tensor_copy", [pc, ends], [ends], out=ends[:, 1:1 + NC_], in_=pc[:, 0:NC_])
            wcol = CH.sb("wcol", [128, NC_], F32)
            est = CH.sb("est", [128, NC_], F32)
            eint = CH.sb("eint", [128, NC_], F32)
            enm = CH.sb("enm", [128, NC_], F32)
            P.v("tensor_tensor", [cols, ends], [wcol], out=wcol[:], in0=cols[:, 0, :], in1=ends[:, 1:1 + NC_], op=ALU.add)
            P.s("activation", [wcol], [wcol], out=wcol[:], in_=wcol[:], func=AF.Exp)
            P.v("tensor_tensor", [ends], [est], out=est[:], in0=ends[:, 1:1 + NC_], in1=ends[:, 0:NC_], op=ALU.subtract)
            P.s("activation", [est], [est], out=est[:], in_=est[:], func=AF.Exp)
            P.v("tensor_tensor", [cols, ends], [eint], out=eint[:], in0=cols[:, 1, :], in1=ends[:, 0:NC_], op=ALU.subtract)
            P.s("activation", [eint], [eint], out=eint[:], in_=eint[:], func=AF.Exp)
            P.v("tensor_scalar", [eint], [eint], out=eint[:], in0=eint[:], scalar1=128.0 ** -0.5, scalar2=None, op0=ALU.mult)
            P.s("activation", [cols], [enm], out=enm[:], in_=cols[:, 2, :], func=AF.Exp)

            CT = CH.sb("CT", [128, 132], F32)
            CTb = CH.sb("CTb", [128, 132], BF16)
            P.g("memset", [], [CT], CT[:], 0.0)
            P.g("memset", [], [CTb], CTb[:], 0.0)
            for c in range(NC_):
                cs = slice(c * 128, (c + 1) * 128)
                pg = rps_s.next()
                P.mm([rows, sel], [pg], pg[:], sel[:, h, :], rows[:, 1, cs])
                ET = rET.next()
                P.s("activation", [pg, cols], [ET], out=ET[:], in_=pg[:], func=AF.Exp, bias=cols[:, 0, c:c + 1], scale=1.0)
                ETm = rETm.next()
                P.g("tensor_tensor", [ET, maskT], [ETm], out=ETm[:], in0=ET[:], in1=maskT[:], op=ALU.mult)
                pst = rps_s.next()
                P.mm([kT, qT], [pst], pst[:], kT[:, cs], qT[:, cs])
                PT = rPT.next()
                P.v("tensor_tensor", [pst, ETm], [PT], out=PT[:], in0=pst[:], in1=ETm[:], op=ALU.mult)
                po = rps_o.next()
                P.mm([PT, vaug], [po], po[:, 0:129], PT[:], vaug[:, c, 0:129])
                pi = rps_i.next()
                P.mm([qT, CTb], [pi], pi[:, 0:129], qT[:, cs], CTb[:, 0:129])
                intra = rintra.next()
                P.s("copy", [po], [intra], out=intra[:, 0:129], in_=po[:, 0:129])
                P.v("scalar_tensor_tensor", [pi, intra, eint], [("nd", c)], out=nd[:, c, 0:129], in0=pi[:, 0:129], scalar=eint[:, c:c + 1],
                    in1=intra[:, 0:129], op0=ALU.mult, op1=ALU.add)
                pk = rps_k.next()
                P.tr([kT], [pk], pk[:], kT[:, cs], K["idb"])
                ksc = rksc.next()
                P.s("activation", [pk, wcol], [ksc], out=ksc[:], in_=pk[:], func=AF.Copy, scale=wcol[:, c:c + 1])
                pu = rps_o.next()
                P.mm([ksc, vaug], [pu], pu[:, 0:129], ksc[:], vaug[:, c, 0:129])
                P.v("scalar_tensor_tensor", [CT, pu, est], [CT], out=CT[:, 0:129], in0=CT[:, 0:129], scalar=est[:, c:c + 1],
                    in1=pu[:, 0:129], op0=ALU.mult, op1=ALU.add)
                P.s("copy", [CT], [CTb], out=CTb[:, 0:129], in_=CT[:, 0:129])

            ndk = [("nd", c) for c in range(NC_)]
            dn = CH.sb("dn", [128, NC_], F32)
            bc = lambda t: t[:].unsqueeze(2).to_broadcast([128, NC_, 128])
            P.v("scalar_tensor_tensor", ndk, [dn], out=dn[:], in0=nd[:, :, 128], scalar=-1.0, in1=nd[:, :, 128], op0=ALU.mult, op1=ALU.max)
            P.v("tensor_tensor", [dn, enm], [dn], out=dn[:], in0=dn[:], in1=enm[:], op=ALU.max)
            P.v("reciprocal", [dn], [dn], out=dn[:], in_=dn[:])
            hh = CH.sb("hh", [128, NC_, 128], F32)
            sq = CH.sb("sq", [128, NC_, 128], F32)
            P.v("tensor_tensor", ndk + [dn], [hh], out=hh[:], in0=nd[:, :, 0:128], in1=bc(dn), op=ALU.mult)
            s1 = CH.sb("s1", [128, NC_], F32)
            s2 = CH.sb("s2", [128, NC_], F32)
            m2 = CH.sb("m2", [128, NC_], F32)
            P.v("reduce_sum", [hh], [s1], out=s1[:], in_=hh[:], axis=AX.X)
            P.g("tensor_tensor", [hh], [sq], out=sq[:], in0=hh[:], in1=hh[:], op=ALU.mult)
            P.v("reduce_sum", [sq], [s2], out=s2[:], in_=sq[:], axis=AX.X)
            P.v("tensor_scalar", [s1], [s1], out=s1[:], in0=s1[:], scalar1=1.0 / 128, scalar2=None, op0=ALU.mult)
            P.v("tensor_tensor", [s1], [m2], out=m2[:], in0=s1[:], in1=s1[:], op=ALU.mult)
            P.v("scalar_tensor_tensor", [s2, m2], [s2], out=s2[:], in0=s2[:], scalar=1.0 / 128, in1=m2[:], op0=ALU.mult, op1=ALU.subtract)
            P.v("tensor_scalar", [s2], [s2], out=s2[:], in0=s2[:], scalar1=LN_EPS, scalar2=None, op0=ALU.add)
            P.s("activation", [s2], [s2], out=s2[:], in_=s2[:], func=AF.Ln)
            P.s("activation", [s2], [s2], out=s2[:], in_=s2[:], func=AF.Exp, scale=-0.5)
            P.v("tensor_tensor", [hh, s1], [hh], out=hh[:], in0=hh[:], in1=bc(s1), op=ALU.subtract)
            P.v("tensor_tensor", [hh, s2], [hh], out=hh[:], in0=hh[:], in1=bc(s2), op=ALU.mult)
            P.g("tensor_tensor", [hh, gnorm], [hh], out=hh[:], in0=hh[:],
                in1=gnorm[:, h * 128:(h + 1) * 128].unsqueeze(1).to_broadcast([128, NC_, 128]), op=ALU.mult)
            P.dma("sync", [("sigo", i) for i in range(NT)] + [sq], [sq], out=sq[:],
                  in_=S["sigo"][:, h * 128:(h + 1) * 128].rearrange("(c p) d -> p c d", p=128))
            P.v("tensor_tensor", [hh, sq], [hh], out=hh[:], in0=hh[:], in1=sq[:], op=ALU.mult)
            P.dma("gpsimd", [hh], [("attn", "a", h)], out=S["attn"][:, h * 128:(h + 1) * 128].rearrange("(c p) d -> p c d", p=128), in_=hh[:])


def bcast_row(P, C, name, src_row, n):
    t = C.sb(name, [128, n], F32)
    P.dma("sync", [], [t], out=t[:], in_=src_row.partition_broadcast(128))
    return t


def layer_norm_tile(P, r, cen, sm, g_t, b_t, out_t, n=1024):
    P.v("reduce_sum", [r], [sm], out=sm[:, 0:1], in_=r[:], axis=AX.X)
    P.v("tensor_scalar", [sm], [sm], out=sm[:, 1:2], in0=sm[:, 0:1], scalar1=-1.0 / n, scalar2=None, op0=ALU.mult)
    P.s("activation", [r, sm], [cen], out=cen[:], in_=r[:], func=AF.Identity, bias=sm[:, 1:2], scale=1.0)
    P.s("activation", [cen], [r, sm], out=r[:], in_=cen[:], func=AF.Square, accum_out=sm[:, 2:3])
    P.v("tensor_scalar", [sm], [sm], out=sm[:, 3:4], in0=sm[:, 2:3], scalar1=1.0 / n, scalar2=LN_EPS, op0=ALU.mult, op1=ALU.add)
    P.s("activation", [sm], [sm], out=sm[:, 4:5], in_=sm[:, 3:4], func=AF.Ln)
    P.s("activation", [sm], [sm], out=sm[:, 5:6], in_=sm[:, 4:5], func=AF.Exp, scale=-0.5)
    P.v("scalar_tensor_tensor", [cen, sm, g_t], [cen], out=cen[:], in0=cen[:], scalar=sm[:, 5:6], in1=g_t[:], op0=ALU.mult, op1=ALU.mult)
    P.g("tensor_tensor", [cen, b_t], [out_t], out=out_t[:], in0=cen[:], in1=b_t[:], op=ALU.add)


def phase_tail_a(nc, P, K, S, w_out, ln_g, ln_b, h_in, h1):
    with Ctx(nc, P) as C:
        wo = C.sb("wo", [128, 8, 1024], BF16)
        load_w_bf16(P, wo, w_out, 8)
        g_t = bcast_row(P, C, "g1", ln_g, 1024)
        b_t = bcast_row(P, C, "b1", ln_b, 1024)
        rin = C.sbrot("ain", [128, 1024], F32, 2)
        rh = C.sbrot("hin", [128, 1024], F32, 2)
        rpt = C.psrot("pt", [128, 4, 128], F32, 2)
        rps = C.psrot("ps", [128, 512], F32, 4)
        raT = C.sbrot("aT", [128, 8, 128], BF16, 2)
        rr = C.sbrot("r", [128, 1024], F32, 2)
        rcen = C.sbrot("cen", [128, 1024], F32, 2)
        rout = C.sbrot("o", [128, 1024], F32, 2)
        rsm = C.sbrot("sm", [128, 8], F32, 2)
        def body(i):
            rows = slice(i * 128, (i + 1) * 128)
            aT = raT.next()
            load_transposed(P, K, S["attn"], aT, 8, aT.name, rin, rpt, tiles=[i], t0=i)
            ht = rh.next()
            P.dma("sync", [], [ht], out=ht[:], in_=h_in[rows, :])
            yield
            r = rr.next()
            for n in range(2):
                ps = rps.next()
                for kc in range(8):
                    P.mm([(aT.name, i), wo], [ps], ps[:], aT[:, kc, :], wo[:, kc, n * 512:(n + 1) * 512], kc == 0, kc == 7)
                P.v("scalar_tensor_tensor", [ht, ps], [r], out=r[:, n * 512:(n + 1) * 512], in0=ht[:, n * 512:(n + 1) * 512], scalar=DN_ALPHA,
                    in1=ps[:], op0=ALU.mult, op1=ALU.add)
            cen, sm, o = rcen.next(), rsm.next(), rout.next()
            layer_norm_tile(P, r, cen, sm, g_t, b_t, o)
            P.dma("gpsimd", [o], [("h1", i)], out=h1[rows, :], in_=o[:])
        run_staged(NT, body)


def phase_tail_b(nc, P, K, S, w1, w2, ln_g, ln_b, h1, h2):
    ST = 256
    NS = ST // 128
    with Ctx(nc, P) as C:
        W1 = C.sb("W1", [128, 8, 4096], BF16)
        W2 = C.sb("W2", [128, 32, 1024], BF16)
        load_w_bf16(P, W1, w1, 8)
        load_w_bf16(P, W2, w2, 32)
        g_t = bcast_row(P, C, "g2", ln_g, 1024)
        b_t = bcast_row(P, C, "b2", ln_b, 1024)
        h1s = [C.sb("h1s%d" % k, [128, 1024], F32) for k in range(2 * NS)]
        rpt = C.psrot("pt", [128, 4, 128], F32, 2)
        rps = C.psrot("ps", [128, 512], F32, 4)
        rhT = C.sbrot("h1T", [128, 8, ST], BF16, 2)
        raT = C.sbrot("aT", [128, 32, ST], BF16, 1)
        rtmp = C.sbrot("tmp", [128, ST], F32, 3)
        rr = C.sbrot("r", [128, 1024], F32, 2)
        rcen = C.sbrot("cen", [128, 1024], F32, 1)
        rout = C.sbrot("o", [128, 1024], F32, 2)
        rsm = C.sbrot("sm", [128, 8], F32, 2)
        def body(st):
            hT = rhT.next()
            hts = []
            for k in range(NS):
                i = st * NS + k
                ht = h1s[(st % 2) * NS + k]
                hts.append(ht)
                load_transposed(P, K, h1, hT, 8, hT.name, Rot([ht]), rpt, tiles=[i], t0=st * NS)
            yield
            hk = [(hT.name, st * NS + k) for k in range(NS)]
            aT = raT.next()
            for f in range(32):
                ps = rps.next()
                for kc in range(8):
                    P.mm(hk + [W1], [ps], ps[:, 0:ST], W1[:, kc, f * 128:(f + 1) * 128], hT[:, kc, :], kc == 0, kc == 7)
                tmp = rtmp.next()
                P.s("activation", [ps], [tmp], out=tmp[:], in_=ps[:, 0:ST], func=AF.Relu)
                P.g("tensor_tensor", [tmp], [(aT.name, f)], out=aT[:, f, :], in0=tmp[:], in1=tmp[:], op=ALU.mult)
            ak = [(aT.name, f) for f in range(32)]
            for k in range(NS):
                i = st * NS + k
                r = rr.next()
                for n in range(2):
                    ps = rps.next()
                    for f in range(32):
                        P.mm(ak + [W2], [ps], ps[:], aT[:, f, k * 128:(k + 1) * 128], W2[:, f, n * 512:(n + 1) * 512], f == 0, f == 31)
                    P.v("scalar_tensor_tensor", [hts[k], ps], [r], out=r[:, n * 512:(n + 1) * 512], in0=hts[k][:, n * 512:(n + 1) * 512],
                        scalar=DN_ALPHA, in1=ps[:], op0=ALU.mult, op1=ALU.add)
                cen, sm, o = rcen.next(), rsm.next(), rout.next()
                layer_norm_tile(P, r, cen, sm, g_t, b_t, o)
                P.dma("gpsimd", [o], [("h2", i)], out=h2[i * 128:(i + 1) * 128, :], in_=o[:])
        run_staged(T // ST, body)


def phase_tail_c(nc, P, K, S, wg, wp, p_in, h2, h_out, is_output):
    with Ctx(nc, P) as C:
        Wg = C.sb("Wg", [128, 8, 1024], BF16)
        Wp = C.sb("Wp", [128, 2, 1024], BF16)
        load_w_bf16(P, Wg, wg, 8)
        load_w_bf16(P, Wp, wp, 2)
        rh = C.sbrot("h2t", [128, 1024], F32, 2)
        rp = C.sbrot("pt_", [128, 256], F32, 2)
        rpt = C.psrot("pt", [128, 4, 128], F32, 2)
        rps = C.psrot("ps", [128, 512], F32, 4)
        rhT = C.sbrot("hT", [128, 8, 128], BF16, 2)
        rpT = C.sbrot("pT", [128, 2, 128], BF16, 2)
        rgt = C.sbrot("gt", [128, 512], F32, 2)
        rout = C.sbrot("o", [128, 1024], F32, 2)
        def body(i):
            rows = slice(i * 128, (i + 1) * 128)
            ht = rh.next()
            hT = rhT.next()
            load_transposed(P, K, h2, hT, 8, hT.name, Rot([ht]), rpt, tiles=[i], t0=i)
            pT = rpT.next()
            load_transposed(P, K, p_in, pT, 2, pT.name, rp, rpt, tiles=[i], t0=i)
            yield
            o = rout.next()
            for n in range(2):
                cs = slice(n * 512, (n + 1) * 512)
                psg = rps.next()
                for kc in range(8):
                    P.mm([(hT.name, i), Wg], [psg], psg[:], hT[:, kc, :], Wg[:, kc, cs], kc == 0, kc == 7)
                psp = rps.next()
                for kc in range(2):
                    P.mm([(pT.name, i), Wp], [psp], psp[:], pT[:, kc, :], Wp[:, kc, cs], kc == 0, kc == 1)
                gt = rgt.next()
                P.s("activation", [psg], [gt], out=gt[:], in_=psg[:], func=AF.Sigmoid)
                P.v("tensor_tensor", [gt, psp], [gt], out=gt[:], in0=gt[:], in1=psp[:], op=ALU.mult)
                P.g("tensor_tensor", [gt, ht], [o], out=o[:, cs], in0=gt[:], in1=ht[:, cs], op=ALU.add)
            P.dma("gpsimd", [o], [("hout", i)], out=h_out[rows, :], in_=o[:], is_output=is_output)
        run_staged(NT, body)


TWO_PI = 6.283185307179586
CW1 = 6.28125
CW2 = TWO_PI - CW1


def phase_o1(nc, P, K, S, W, j, h_in):
    SC = 192.0 ** -0.5
    with Ctx(nc, P) as C:
        wi_ = C.sb("w_in_o", [128, 8, 904], BF16)
        load_w_bf16(P, wi_, W["o_w_in"][j], 8)
        wqb = C.sb("wqb", [128, 4, 1536], BF16)
        load_w_bf16(P, wqb, W["o_w_qb"][j], 4)
        wiq = C.sb("wiq", [128, 4, 512], BF16)
        load_w_bf16(P, wiq, W["o_w_iq"][j], 4)
        wqr = C.sb("wqr", [128, 4, 8, 64], BF16)
        P.v("tensor_copy", [wqb], [wqr], out=wqr[:], in_=wqb[:].rearrange("p k (h e) -> p k h e", e=192)[:, :, :, 128:192])
        wuk_f = C.sb("wuk_f", [128, 2, 1024], F32)
        P.dma("sync", [], [wuk_f], out=wuk_f[:], in_=W["o_w_uk"][j].rearrange("(cc p) h d -> p cc (h d)", p=128))
        wukT = C.sb("wukT", [128, 8, 256], BF16)
        rpt = C.psrot("pt", [128, 4, 128], F32, 2)
        for cc in range(2):
            for hq in range(2):
                pt = rpt.next()
                for jj in range(4):
                    h = hq * 4 + jj
                    P.tr([wuk_f], [pt], pt[:, jj, :], wuk_f[:, cc, h * 128:(h + 1) * 128], K["idf"])
                P.v("tensor_copy", [pt], [wukT], out=wukT[:, hq * 4:(hq + 1) * 4, cc * 128:(cc + 1) * 128], in_=pt[:])
        gq = bcast_row(P, C, "gq", W["o_q_norm"][j], 512)
        gkv = bcast_row(P, C, "gkv", W["o_kv_norm"][j], 256)
        ikg = bcast_row(P, C, "ikg", W["o_ik_g"][j], 64)
        ikb = bcast_row(P, C, "ikb", W["o_ik_b"][j], 64)
        inv = bcast_row(P, C, "inv", W["rope_inv"], 48)
        posi = C.sb("posi", [128, NT], I32)
        P.dma("sync", [], [posi], out=posi[:], in_=W["positions"].rearrange("(c p) -> p c", p=128), allow_slow_non_contiguous=True)
        posf = C.sb("posf", [128, NT], F32)
        P.v("tensor_copy", [posi], [posf], out=posf[:], in_=posi[:])

        rh = C.sbrot("hin", [128, 1024], F32, 3)
        rhT = C.sbrot("hT", [128, 8, 128], BF16, 3)
        rps = C.psrot("ps", [128, 512], F32, 3)
        rpk = C.psrot("psk", [128, 512], F32, 1)
        rsm = C.sbrot("sm", [128, 16], F32, 3)
        rcq = C.sbrot("cq", [128, 512], F32, 9)
        rcqT = C.sbrot("cqT", [128, 4, 128], BF16, 3)
        rqn = C.sbrot("qn", [128, 8, 128], BF16, 3)
        rqa = C.sbrot("qa", [128, 16, 128], BF16, 3)
        rang = C.sbrot("ang", [128, 4, 48], F32, 3)
        rki = C.sbrot("ki", [128, 48], I32, 3)
        rtr = C.sbrot("tr", [128, 4, 48], F32, 3)
        rq1 = C.sbrot("q1", [128, 512], F32, 6)
        rq2 = C.sbrot("q2", [128, 512], F32, 6)
        rqb = C.sbrot("qbf", [128, 4, 128], BF16, 9)
        rkv = C.sbrot("kv", [128, 672], F32, 3)
        rkvb = C.sbrot("kvb", [128, 256], BF16, 3)
        rkk = C.sbrot("kk", [128, 2, 128], F32, 3)
        rkkb = C.sbrot("kkb", [128, 2, 128], BF16, 2)
        rwi = C.sbrot("wi", [128, 16], F32, 3)
        kn2 = C.sb("kn2", [128, NT], F32)
        onesb = C.sb("onesb", [128, 1], BF16)
        P.g("memset", [], [onesb], onesb[:], 1.0)
        rsq = C.sbrot("sqa", [128, 16, 128], BF16, 3)
        rqn2 = C.sbrot("qn2", [128, 24], F32, 3)
        rpn = C.psrot("pn", [128, 8], F32, 1)

        def rms_rstd(src, n, sm, col, junk):
            P.s("activation", [src], [junk, sm], out=junk, in_=src, func=AF.Square, accum_out=sm[:, col:col + 1])
            P.v("tensor_scalar", [sm], [sm], out=sm[:, col + 1:col + 2], in0=sm[:, col:col + 1], scalar1=1.0 / n, scalar2=LN_EPS, op0=ALU.mult, op1=ALU.add)
            P.s("activation", [sm], [sm], out=sm[:, col + 1:col + 2], in_=sm[:, col + 1:col + 2], func=AF.Ln)
            P.s("activation", [sm], [sm], out=sm[:, col + 2:col + 3], in_=sm[:, col + 1:col + 2], func=AF.Exp, scale=-0.5)

        def rope(dst, src, cs, sn, nh, half, t1, t2):
            cb = cs.unsqueeze(1).to_broadcast([128, nh, half])
            sb_ = sn.unsqueeze(1).to_broadcast([128, nh, half])
            x1, x2 = src[:, :, 0:half], src[:, :, half:2 * half]
            P.v("tensor_tensor", [src], [t1], out=t1, in0=x1, in1=cb, op=ALU.mult)
            P.g("tensor_tensor", [src], [t2], out=t2, in0=x2, in1=sb_, op=ALU.mult)
            P.v("tensor_tensor", [t1, t2], [dst], out=dst[:, :, 0:half], in0=t1, in1=t2, op=ALU.subtract)
            P.v("tensor_tensor", [src], [t1], out=t1, in0=x1, in1=sb_, op=ALU.mult)
            P.g("tensor_tensor", [src], [t2], out=t2, in0=x2, in1=cb, op=ALU.mult)
            P.v("tensor_tensor", [t1, t2], [dst], out=dst[:, :, half:2 * half], in0=t1, in1=t2, op=ALU.add)

        def body(i):
            rows = slice(i * 128, (i + 1) * 128)
            cols = slice(i * 128, (i + 1) * 128)
            ang, ki, tr = rang.next(), rki.next(), rtr.next()
            P.v("tensor_scalar", [inv, posf], [ang], out=ang[:, 0, :], in0=inv[:], scalar1=posf[:, i:i + 1], scalar2=None, op0=ALU.mult)
            P.v("tensor_scalar", [ang], [ang], out=ang[:, 1, :], in0=ang[:, 0, :], scalar1=1.0 / TWO_PI, scalar2=None, op0=ALU.mult)
            P.v("tensor_copy", [ang], [ki], out=ki[:], in_=ang[:, 1, :])
            P.v("tensor_copy", [ki], [ang], out=ang[:, 1, :], in_=ki[:])
            P.v("scalar_tensor_tensor", [ang], [ang], out=ang[:, 0, :], in0=ang[:, 1, :], scalar=-CW1, in1=ang[:, 0, :], op0=ALU.mult, op1=ALU.add)
            P.v("scalar_tensor_tensor", [ang], [ang], out=ang[:, 0, :], in0=ang[:, 1, :], scalar=-CW2, in1=ang[:, 0, :], op0=ALU.mult, op1=ALU.add)

            def wrap(a):
                P.v("tensor_scalar", [ang], [ang], out=ang[:, 2, :], in0=a, scalar1=float(np.pi), scalar2=-TWO_PI, op0=ALU.is_gt, op1=ALU.mult)
                P.v("tensor_tensor", [ang], [ang], out=a, in0=a, in1=ang[:, 2, :], op=ALU.add)
                P.v("tensor_scalar", [ang], [ang], out=ang[:, 2, :], in0=a, scalar1=-float(np.pi), scalar2=TWO_PI, op0=ALU.is_lt, op1=ALU.mult)
                P.v("tensor_tensor", [ang], [ang], out=a, in0=a, in1=ang[:, 2, :], op=ALU.add)
            wrap(ang[:, 0, :])
            P.v("tensor_scalar", [ang], [ang], out=ang[:, 3, :], in0=ang[:, 0, :], scalar1=float(np.pi / 2), scalar2=None, op0=ALU.add)
            wrap(ang[:, 3, :])
            P.s("activation", [ang], [tr], out=tr[:, 0, :], in_=ang[:, 0, :], func=AF.Sin)
            P.s("activation", [ang], [tr], out=tr[:, 1, :], in_=ang[:, 3, :], func=AF.Sin)
            sin64, cos64, sin32, cos32 = tr[:, 0, 0:32], tr[:, 1, 0:32], tr[:, 0, 32:48], tr[:, 1, 32:48]
            yield

            ht, hT = rh.next(), rhT.next()
            load_transposed(P, K, h_in, hT, 8, hT.name, Rot([ht]), rpt, tiles=[i], t0=i)
            yield
            hk = [(hT.name, i), wi_]
            ps_q, ps_k = rps.next(), rpk.next()
            for kc in range(8):
                P.mm(hk, [ps_q], ps_q[:], hT[:, kc, :], wi_[:, kc, 0:512], kc == 0, kc == 7)
            for kc in range(8):
                P.mm(hk, [ps_k], ps_k[:, 0:392], hT[:, kc, :], wi_[:, kc, 512:904], kc == 0, kc == 7)
            kv = rkv.next()
            P.s("copy", [ps_k], [kv], out=kv[:, 0:392], in_=ps_k[:, 0:392])
            sm = rsm.next()
            cq, q1 = rcq.next(), rq1.next()
            rms_rstd(ps_q[:], 512, sm, 0, q1[:])
            P.v("scalar_tensor_tensor", [ps_q, sm, gq], [cq], out=cq[:], in0=ps_q[:], scalar=sm[:, 2:3], in1=gq[:], op0=ALU.mult, op1=ALU.mult)
            yield
            cqT = rcqT.next()
            pt = rpt.next()
            for jj in range(4):
                P.tr([cq], [pt], pt[:, jj, :], cq[:, jj * 128:(jj + 1) * 128], K["idf"])
            P.s("copy", [pt], [cqT], out=cqT[:], in_=pt[:])
            yield
            qn = rqn.next()
            for hq in range(2):
                ps = rps.next()
                for jj in range(4):
                    h = hq * 4 + jj
                    for kc in range(4):
                        P.mm([cqT, wqb], [ps], ps[:, jj * 128:(jj + 1) * 128], wqb[:, kc, h * 192:h * 192 + 128], cqT[:, kc, :], kc == 0, kc == 3)
                if hq == 0:
                    P.v("tensor_copy", [ps], [qn], out=qn[:, 0:4, :], in_=ps[:].rearrange("p (a b) -> p a b", b=128))
                else:
                    P.s("copy", [ps], [qn], out=qn[:, 4:8, :], in_=ps[:].rearrange("p (a b) -> p a b", b=128))
                yield
            qa = rqa.next()
            for hq in range(4):
                ps = rps.next()
                for jj in range(4):
                    n = hq * 4 + jj
                    h, cc = n // 2, n % 2
                    P.mm([qn, wukT], [ps], ps[:, jj * 128:(jj + 1) * 128], wukT[:, h, cc * 128:(cc + 1) * 128], qn[:, h, :])
                if hq % 2 == 0:
                    P.s("mul", [ps], [qa], out=qa[:, hq * 4:(hq + 1) * 4, :], in_=ps[:].rearrange("p (a b) -> p a b", b=128), mul=SC)
                else:
                    P.v("tensor_scalar", [ps], [qa], out=qa[:, hq * 4:(hq + 1) * 4, :], in0=ps[:].rearrange("p (a b) -> p a b", b=128),
                        scalar1=SC, scalar2=None, op0=ALU.mult)
                yield
            P.dma("gpsimd", [qa], [("qaT", i)], out=S["qaT"][:, :, cols].rearrange("n p t -> p n t"), in_=qa[:])
            sqa = rsq.next()
            P.g("tensor_tensor", [qa], [sqa], out=sqa[:], in0=qa[:], in1=qa[:], op=ALU.mult)
            pn = rpn.next()
            for h in range(8):
                for cc in range(2):
                    P.mm([sqa, onesb], [pn], pn[:, h:h + 1], sqa[:, 2 * h + cc, :], onesb[:], cc == 0, cc == 1)
            qn2 = rqn2.next()
            P.v("tensor_copy", [pn], [qn2], out=qn2[:, 0:8], in_=pn[:])
            yield
            ps = rps.next()
            for kc in range(4):
                P.mm([cqT, wqr], [ps], ps[:], cqT[:, kc, :], wqr[:, kc].rearrange("p h e -> p (h e)"), kc == 0, kc == 3)
            q2 = rq2.next()
            P.s("mul", [ps], [q1], out=q1[:], in_=ps[:], mul=SC)
            yield
            t1 = rcq.next()
            rope(q2[:].rearrange("p (h e) -> p h e", e=64), q1[:].rearrange("p (h e) -> p h e", e=64), cos64, sin64, 8, 32,
                 t1[:, 0:256].rearrange("p (h e) -> p h e", e=32), t1[:, 256:512].rearrange("p (h e) -> p h e", e=32))
            pt = rpt.next()
            for jj in range(4):
                P.tr([q2], [pt], pt[:, jj, :], q2[:, jj * 128:(jj + 1) * 128], K["idf"])
            qb = rqb.next()
            P.s("copy", [pt], [qb], out=qb[:], in_=pt[:])
            P.dma("gpsimd", [qb], [("qrT", i)], out=S["qrT"][:, :, cols].rearrange("n p t -> p n t"), in_=qb[:])
            yield
            P.g("tensor_tensor", [q2], [q1], out=q1[:], in0=q2[:], in1=q2[:], op=ALU.mult)
            P.v("reduce_sum", [q1], [qn2], out=qn2[:, 8:16], in_=q1[:].rearrange("p (h e) -> p h e", e=64), axis=AX.X)
            P.v("tensor_tensor", [qn2], [qn2], out=qn2[:, 16:24], in0=qn2[:, 0:8], in1=qn2[:, 8:16], op=ALU.add)
            P.dma("gpsimd", [qn2], [("qn2", i)], out=S["qn2"][rows, :], in_=qn2[:, 16:24])
            yield
            ps = rps.next()
            for kc in range(4):
                P.mm([cqT, wiq], [ps], ps[:], cqT[:, kc, :], wiq[:, kc, :], kc == 0, kc == 3)
            q1 = rq1.next()
            q2 = rq2.next()
            P.s("copy", [ps], [q1], out=q1[:], in_=ps[:])
            yield
            P.g("tensor_copy", [q1], [q2], out=q2[:], in_=q1[:])
            t1 = rcq.next()
            rope(q2[:].rearrange("p (h e) -> p h e", e=64)[:, :, 0:32], q1[:].rearrange("p (h e) -> p h e", e=64)[:, :, 0:32], cos32, sin32, 8, 16,
                 t1[:, 0:128].rearrange("p (h e) -> p h e", e=16), t1[:, 128:256].rearrange("p (h e) -> p h e", e=16))
            pt = rpt.next()
            for jj in range(4):
                P.tr([q2], [pt], pt[:, jj, :], q2[:, jj * 128:(jj + 1) * 128], K["idf"])
            qb = rqb.next()
            P.v("tensor_copy", [pt], [qb], out=qb[:], in_=pt[:])
            P.dma("gpsimd", [qb], [("qiT", i)], out=S["qiT"][:, :, cols].rearrange("n p t -> p n t"), in_=qb[:])
            yield
            rms_rstd(kv[:, 0:256], 256, sm, 4, kv[:, 400:656])
            P.v("scalar_tensor_tensor", [kv, sm, gkv], [kv], out=kv[:, 0:256], in0=kv[:, 0:256], scalar=sm[:, 6:7], in1=gkv[:], op0=ALU.mult, op1=ALU.mult)
            yield
            kvb = rkvb.next()
            P.g("tensor_copy", [kv], [kvb], out=kvb[:], in_=kv[:, 0:256])
            P.dma("gpsimd", [kvb], [("ckv", i)], out=S["ckv"][rows, :], in_=kvb[:])
            yield
            kk = rkk.next()
            rope(kk[:, 0:1, 0:64], kv[:, 256:320].unsqueeze(1), cos64, sin64, 1, 32, kv[:, 400:432].unsqueeze(1), kv[:, 432:464].unsqueeze(1))
            P.v("tensor_copy", [kk], [kk], out=kk[:, 0, 64:128], in_=kk[:, 0, 0:64])
            P.s("activation", [kv], [kv, sm], out=kv[:, 400:656], in_=kv[:, 0:256], func=AF.Square, accum_out=sm[:, 13:14])
            P.s("activation", [kk], [kv, sm], out=kv[:, 400:464], in_=kk[:, 0, 0:64], func=AF.Square, accum_out=sm[:, 14:15])
            P.v("tensor_tensor", [sm], [kn2], out=kn2[:, i:i + 1], in0=sm[:, 13:14], in1=sm[:, 14:15], op=ALU.add)
            yield
            ik = kv[:, 320:384]
            P.v("reduce_sum", [kv], [sm], out=sm[:, 8:9], in_=ik, axis=AX.X)
            P.v("tensor_scalar", [sm], [sm], out=sm[:, 9:10], in0=sm[:, 8:9], scalar1=-1.0 / 64, scalar2=None, op0=ALU.mult)
            P.s("activation", [kv, sm], [kv], out=kv[:, 464:528], in_=ik, func=AF.Identity, bias=sm[:, 9:10], scale=1.0)
            rms_rstd(kv[:, 464:528], 64, sm, 10, kv[:, 528:592])
            P.v("scalar_tensor_tensor", [kv, sm, ikg], [kv], out=kv[:, 464:528], in0=kv[:, 464:528], scalar=sm[:, 12:13], in1=ikg[:], op0=ALU.mult, op1=ALU.mult)
            P.v("tensor_tensor", [kv, ikb], [kv], out=kv[:, 464:528], in0=kv[:, 464:528], in1=ikb[:], op=ALU.add)
            P.v("tensor_copy", [kv], [kk], out=kk[:, 1, 32:64], in_=kv[:, 496:528])
            rope(kk[:, 1:2, 0:32], kv[:, 464:496].unsqueeze(1), cos32, sin32, 1, 16, kv[:, 592:608].unsqueeze(1), kv[:, 608:624].unsqueeze(1))
            P.v("tensor_copy", [kk], [kk], out=kk[:, 1, 64:128], in_=kk[:, 1, 0:64])
            yield
            wi = rwi.next()
            P.v("tensor_scalar", [kv], [wi], out=wi[:, 0:8], in0=kv[:, 384:392], scalar1=(8.0 ** -0.5) * (64.0 ** -0.5), scalar2=None, op0=ALU.mult)
            P.v("tensor_scalar", [wi], [wi], out=wi[:, 8:16], in0=wi[:, 0:8], scalar1=0.0, scalar2=2.0, op0=ALU.is_ge, op1=ALU.mult)
            P.v("tensor_scalar", [wi], [wi], out=wi[:, 8:16], in0=wi[:, 8:16], scalar1=-1.0, scalar2=None, op0=ALU.add)
            P.v("tensor_tensor", [wi], [wi], out=wi[:, 0:8], in0=wi[:, 0:8], in1=wi[:, 8:16], op=ALU.mult)
            P.dma("gpsimd", [wi], [("wi", i)], out=S["wi"][rows, :], in_=wi[:])
            yield
            pt = rpt.next()
            P.tr([kv], [pt], pt[:, 0, :], kv[:, 0:128], K["idf"])
            P.tr([kv], [pt], pt[:, 1, :], kv[:, 128:256], K["idf"])
            P.tr([kk], [pt], pt[:, 2, :], kk[:, 0, :], K["idf"])
            P.tr([kk], [pt], pt[:, 3, :], kk[:, 1, :], K["idf"])
            kkb = rqb.next()
            P.s("copy", [pt], [kkb], out=kkb[:], in_=pt[:])
            P.dma("gpsimd", [kkb], [("kT", i)], out=S["kT"][:, :, cols].rearrange("n p t -> p n t"), in_=kkb[:])
        run_interleaved(NT, body, 2)
        P.dma("sync", [kn2], ["kn2d"], out=S["kn2"], in_=kn2[:])


MASKV = -30000.0


class AttnRes:
    def __init__(self, C):
        self.rps = C.psrot("s_ps", [128, 512], F32, 3)
        self.rpT = C.psrot("pT_ps", [128, 4, 128], BF16, 2)
        self.re = C.sbrot("e", [128, 512], BF16, 4)
        self.rpt = C.sbrot("pT", [128, 4, 128], BF16, 4)
        self.rmx = C.sbrot("mx", [128, 16], F32, 4)
        self.rrs = C.sbrot("rs", [128, 16], F32, 4)
        self.cnt = 0


class Pipe:
    def __init__(self):
        self.hist = []
        self.filler = None
        self.rate = 1

    def push(self, stages):
        self.hist.insert(0, stages)
        self.hist = self.hist[:3]
        for lag, st in enumerate(self.hist):
            if lag < len(st) and st[lag] is not None:
                st[lag]()
        if self.filler is not None:
            for _ in range(self.rate):
                next(self.filler, None)

    def flush(self):
        self.push([])
        self.push([])


def attn_items(P, K, R, terms, mask_term, k0, k1, pv_fn, done_fn, negm_ap=None, negm_reads=()):
    chunks = []
    c = k0
    while c < k1:
        n = min(512, k1 - c)
        chunks.append((c, n))
        c += n
    nc_ = len(chunks)
    nkt = (k1 - k0) // 128
    mx, rs = R.rmx.next(), R.rrs.next()

    def scores(c0, n, tl):
        ps = R.rps.next()
        for ti, (lhsT, rhs_fn, rd) in enumerate(tl):
            P.mm(rd, [ps], ps[:, 0:n], lhsT, rhs_fn(c0, n), ti == 0, ti == len(tl) - 1)
        return ps

    items = []
    for ci, (c0, n) in enumerate(chunks if negm_ap is None else []):
        def A1(ci=ci, c0=c0, n=n):
            ps = scores(c0, n, terms)
            P.v("reduce_max", [ps], [mx], out=mx[:, ci:ci + 1], in_=ps[:, 0:n], axis=AX.X)
            if ci == nc_ - 1:
                if nc_ > 1:
                    P.v("reduce_max", [mx], [mx], out=mx[:, 15:16], in_=mx[:, 0:nc_], axis=AX.X)
                    P.v("tensor_scalar", [mx], [mx], out=mx[:, 14:15], in0=mx[:, 15:16], scalar1=-1.0, scalar2=None, op0=ALU.mult)
                else:
                    P.v("tensor_scalar", [mx], [mx], out=mx[:, 14:15], in0=mx[:, 0:1], scalar1=-1.0, scalar2=None, op0=ALU.mult)
        items.append([A1])
    tl2 = terms + ([mask_term] if mask_term is not None else [])
    kbase = [0]
    for ci, (c0, n) in enumerate(chunks):
        st = {}
        nk = n // 128

        def A2(ci=ci, c0=c0, n=n, st=st):
            ps = scores(c0, n, tl2)
            e = R.re.next()
            if negm_ap is None:
                P.s("activation", [ps, mx], [e, rs], out=e[:, 0:n], in_=ps[:, 0:n], func=AF.Exp, bias=mx[:, 14:15], scale=1.0, accum_out=rs[:, ci:ci + 1])
            else:
                P.s("activation", [ps] + list(negm_reads), [e, rs], out=e[:, 0:n], in_=ps[:, 0:n], func=AF.Exp, bias=negm_ap, scale=1.0, accum_out=rs[:, ci:ci + 1])
            st["e"] = e

        def B2(nk=nk, st=st):
            e = st["e"]
            pTp = R.rpT.next()
            for kk in range(nk):
                P.tr([e], [pTp], pTp[:, kk, :], e[:, kk * 128:(kk + 1) * 128], K["idb"])
            pT = R.rpt.next()
            R.cnt += 1
            if R.cnt % 2 == 0:
                P.s("copy", [pTp], [pT], out=pT[:, 0:nk, :], in_=pTp[:, 0:nk, :])
            else:
                P.v("tensor_copy", [pTp], [pT], out=pT[:, 0:nk, :], in_=pTp[:, 0:nk, :])
            st["pT"] = pT

        def C2(ci=ci, c0=c0, nk=nk, st=st):
            pT = st["pT"]
            for kk in range(nk):
                kt = (c0 - k0) // 128 + kk
                pv_fn(pT[:, kk, :], [pT], kt, kt == 0, kt == nkt - 1)
            if ci == nc_ - 1:
                if nc_ > 1:
                    P.v("reduce_sum", [rs], [rs], out=rs[:, 15:16], in_=rs[:, 0:nc_], axis=AX.X)
                    P.v("tensor_scalar", [rs], [rs], out=rs[:, 14:15], in0=rs[:, 15:16], scalar1=1e-30, scalar2=None, op0=ALU.add)
                else:
                    P.v("tensor_scalar", [rs], [rs], out=rs[:, 14:15], in0=rs[:, 0:1], scalar1=1e-30, scalar2=None, op0=ALU.add)
                P.v("reciprocal", [rs], [rs], out=rs[:, 13:14], in_=rs[:, 14:15])
                done_fn(rs, rs[:, 13:14])
        items.append([A2, B2, C2])
    return items


def phase_o3(nc, P, K, S, W, j):
    NB = 11
    with Ctx(nc, P) as C:
        kT = C.sb("kT", [128, 4, T], BF16)
        for n in range(4):
            P.dma("sync", [("kT", i) for i in range(NT)], [kT], out=kT[:, n, :], in_=S["kT"][n])
        ckv = C.sb("ckv", [128, NT, 256], BF16)
        P.dma("sync", [("ckv", i) for i in range(NT)], [ckv], out=ckv[:], in_=S["ckv"].rearrange("(c p) d -> p c d", p=128))
        wuv = C.sb("wuv", [128, 2, 1024], BF16)
        for cc in range(2):
            P.dma("gpsimd", [], [wuv], out=wuv[:, cc, :], in_=W["o_w_uv"][j][cc * 128:(cc + 1) * 128].rearrange("p h v -> p (h v)"))
        pw = C.sb("pw", [128, NB + 1], F32)
        for k in range(NB + 1):
            P.g("memset", [], [pw], pw[:, k:k + 1], 2.0 ** -(k + 1))
        negtri = C.sb("negtri", [128, 128], F32)
        P.v("tensor_scalar", [K["tri_ge"]], [negtri], out=negtri[:], in0=K["tri_ge"][:], scalar1=-1.0, scalar2=1e30, op0=ALU.add, op1=ALU.mult)
        negtri_b = C.sb("negtri_b", [128, 128], BF16)
        P.v("tensor_scalar", [K["tri_ge"]], [negtri_b], out=negtri_b[:], in0=K["tri_ge"][:], scalar1=-1.0, scalar2=-MASKV, op0=ALU.add, op1=ALU.mult)
        kn2 = C.sb("kn2", [128, NT], F32)
        P.dma("sync", ["kn2d"], [kn2], out=kn2[:], in_=S["kn2"])
        kmx = C.sb("kmx", [128, 8], F32)
        ones1 = C.sb("ones1", [1, 128], F32)
        P.g("memset", [], [ones1], ones1[:], 1.0)
        P.v("reduce_max", [kn2], [kmx], out=kmx[:, 0:1], in_=kn2[:], axis=AX.X)
        with Ctx(nc, P) as Ck:
            pk1 = Ck.ps("pk1", [128, 128], F32)
            P.tr([kmx], [pk1], pk1[0:1, :], kmx[:, 0:1], K["idf"])
            P.v("reduce_max", [pk1], [kmx], out=kmx[0:1, 1:2], in_=pk1[0:1, :], axis=AX.X)
            P.mm([ones1, kmx], [pk1], pk1[:, 0:1], ones1[:], kmx[0:1, 1:2])
            P.v("tensor_copy", [pk1], [kmx], out=kmx[:, 2:3], in_=pk1[:, 0:1])
        isc = C.sb("isc", [128, T], F32)
        negm = [C.sb("negm%d" % k, [128, T], BF16) for k in range(2)]
        qr8s = [C.sb("qr8_%d" % k, [128, 8, 128], BF16) for k in range(2)]
        qi8s = [C.sb("qi8_%d" % k, [128, 8, 128], BF16) for k in range(2)]
        for tq in qr8s + qi8s:
            P.g("memset", [], [tq], tq[:], 0.0)
        rstab = C.sbrot("stab", [128, 24], F32, 2)
        junk = C.sb("junk", [128, T], BF16)
        R = AttnRes(C)
        rolat = C.psrot("olat", [128, 2, 128], F32, 2)
        rout = C.psrot("outp", [128, 128], F32, 1)
        rqa = C.sbrot("qa", [128, 16, 128], BF16, 2)
        rqr = C.sbrot("qr", [128, 4, 128], BF16, 2)
        rqi = C.sbrot("qi", [128, 4, 128], BF16, 2)
        rwi = C.sbrot("wi", [128, 16], F32, 2)
        rrl = C.sbrot("rl", [128, 512], F32, 4)
        risc2 = C.sbrot("isc2", [128, 512], F32, 2)
        rtmp2 = C.sbrot("tmp2", [128, 512], F32, 2)
        rbs = C.sbrot("bs", [128, 32], F32, 2)
        rwh = C.sbrot("wh", [128, NB + 1], F32, 2)
        rol = C.sbrot("ol", [128, 2, 128], BF16, 2)
        rat = C.sbrot("at", [128, 1024], F32, 2)
        tile_in = {}

        def prep(i):
            cols = slice(i * 128, (i + 1) * 128)
            L = (i + 1) * 128
            qa, qr = rqa.next(), qr8s[i % 2]
            nm = negm[i % 2]
            stab = rstab.next()
            tile_in[i] = (qa, qr, nm, stab)
            P.dma("sync", [("qaT", i)], [qa], out=qa[:], in_=S["qaT"][:, :, cols].rearrange("n p t -> p n t"))
            for par in range(2):
                P.dma("sync", [("qrT", i)], [qr], out=qr[par * 64:(par + 1) * 64, par::2, :],
                      in_=S["qrT"][:, par * 64:(par + 1) * 64, cols].rearrange("n p t -> p n t"))
            P.dma("sync", [("qn2", i)], [stab], out=stab[:, 0:8], in_=S["qn2"][cols, :])
            P.v("tensor_scalar", [stab, kmx], [stab], out=stab[:, 0:8], in0=stab[:, 0:8], scalar1=kmx[:, 2:3], scalar2=1e-30, op0=ALU.mult, op1=ALU.add)
            P.s("activation", [stab], [stab], out=stab[:, 8:16], in_=stab[:, 0:8], func=AF.Ln)
            P.s("activation", [stab], [stab], out=stab[:, 8:16], in_=stab[:, 8:16], func=AF.Exp, scale=0.5)
            P.v("tensor_scalar", [stab], [stab], out=stab[:, 16:24], in0=stab[:, 8:16], scalar1=-1.02, scalar2=None, op0=ALU.mult)
            yield
            if i < 2:
                if i == 1:
                    P.g("memset", [], [nm], nm[:, 0:128], 0.0)
                P.g("tensor_copy", [negtri_b, nm], [nm], out=nm[:, L - 128:L], in_=negtri_b[:])
                return
            qi, wi = qi8s[i % 2], rwi.next()
            for par in range(2):
                P.dma("sync", [("qiT", i)], [qi], out=qi[par * 64:(par + 1) * 64, par::2, :],
                      in_=S["qiT"][:, par * 64:(par + 1) * 64, cols].rearrange("n p t -> p n t"))
            P.dma("sync", [("wi", i)], [wi], out=wi[:], in_=S["wi"][cols, :])
            c0 = 0
            while c0 < L:
                n = min(512, L - c0)
                for h in range(8):
                    po = (h % 2) * 64
                    ps = R.rps.next()
                    P.mm([qi, kT], [ps], ps[:, 0:n], qi[:, h, :], kT[:, 3, c0:c0 + n])
                    rl = rrl.next()
                    P.s("activation", [ps, wi], [rl], out=rl[:, 0:n], in_=ps[:, 0:n], func=AF.Relu, scale=wi[:, h:h + 1])
                    if h == 0:
                        P.v("tensor_scalar", [rl, wi], [("isc", c0)], out=isc[:, c0:c0 + n], in0=rl[:, 0:n], scalar1=wi[:, 8:9], scalar2=None, op0=ALU.mult)
                    elif h < 5:
                        P.v("scalar_tensor_tensor", [rl, wi, ("isc", c0)], [("isc", c0)], out=isc[:, c0:c0 + n], in0=rl[:, 0:n], scalar=wi[:, 8 + h:9 + h],
                            in1=isc[:, c0:c0 + n], op0=ALU.mult, op1=ALU.add)
                    elif h == 5:
                        i2 = risc2.next()
                        P.g("tensor_scalar", [rl, wi], [i2], out=i2[:, 0:n], in0=rl[:, 0:n], scalar1=wi[:, 8 + h:9 + h], scalar2=0.0, op0=ALU.mult, op1=ALU.add)
                    else:
                        tp = rtmp2.next()
                        P.g("tensor_scalar", [rl, wi], [tp], out=tp[:, 0:n], in0=rl[:, 0:n], scalar1=wi[:, 8 + h:9 + h], scalar2=0.0, op0=ALU.mult, op1=ALU.add)
                        P.g("tensor_tensor", [tp, i2], [i2], out=i2[:, 0:n], in0=i2[:, 0:n], in1=tp[:, 0:n], op=ALU.add)
                    yield
                P.g("tensor_tensor", [i2, ("isc", c0)], [("isc", c0)], out=isc[:, c0:c0 + n], in0=isc[:, c0:c0 + n], in1=i2[:, 0:n], op=ALU.add)
                c0 += n
            ik = [("isc", c) for c in range(0, L, 512)]
            P.g("tensor_tensor", ik + [K["tri_ge"]], ik, out=isc[:, L - 128:L], in0=isc[:, L - 128:L], in1=K["tri_ge"][:], op=ALU.mult)
            P.g("tensor_tensor", ik + [negtri], ik, out=isc[:, L - 128:L], in0=isc[:, L - 128:L], in1=negtri[:], op=ALU.add)
            bs, wh = rbs.next(), rwh.next()
            pcs = [(c, min(1024, L - c)) for c in range(0, L, 1024)]
            npc = len(pcs)
            for pi, (c, n) in enumerate(pcs):
                n2 = min(n, L - 128 - c)
                if n2 > 0:
                    P.v("tensor_reduce", ik, [bs], out=bs[:, 8 + pi:9 + pi], in_=isc[:, c:c + n2], axis=AX.X, op=ALU.min)
                else:
                    P.v("tensor_copy", [bs], [bs], out=bs[:, 8 + pi:9 + pi], in_=bs[:, 8:9])
                P.v("reduce_max", ik, [bs], out=bs[:, 12 + pi:13 + pi], in_=isc[:, c:c + n], axis=AX.X)
                yield
            P.v("tensor_reduce", [bs], [bs], out=bs[:, 0:1], in_=bs[:, 8:8 + npc], axis=AX.X, op=ALU.min)
            P.v("reduce_max", [bs], [bs], out=bs[:, 1:2], in_=bs[:, 12:12 + npc], axis=AX.X)
            P.v("tensor_tensor", [bs], [bs], out=bs[:, 2:3], in0=bs[:, 1:2], in1=bs[:, 0:1], op=ALU.subtract)
            P.v("tensor_scalar", [pw, bs], [wh], out=wh[:], in0=pw[:], scalar1=bs[:, 2:3], scalar2=None, op0=ALU.mult)
            P.v("tensor_tensor", [bs, wh], [bs], out=bs[:, 3:4], in0=bs[:, 0:1], in1=wh[:, 0:1], op=ALU.add)
            yield
            for k in range(NB):
                for pi, (c, n) in enumerate(pcs):
                    P.v("tensor_scalar", ik + [bs], [junk, bs], out=junk[:, c:c + n], in0=isc[:, c:c + n], scalar1=bs[:, 3:4], scalar2=None,
                        op0=ALU.is_ge, op1=ALU.add, accum_out=bs[:, 8 + pi:9 + pi])
                    if pi < npc - 1:
                        yield
                if npc > 1:
                    P.v("reduce_sum", [bs], [bs], out=bs[:, 4:5], in_=bs[:, 8:8 + npc], axis=AX.X)
                    cnt = bs[:, 4:5]
                else:
                    cnt = bs[:, 8:9]
                P.v("tensor_scalar", [bs], [bs], out=bs[:, 5:6], in0=cnt, scalar1=256.0, scalar2=-0.5, op0=ALU.is_ge, op1=ALU.add)
                P.v("scalar_tensor_tensor", [bs, wh], [bs], out=bs[:, 3:4], in0=bs[:, 5:6], scalar=wh[:, k:k + 1], in1=bs[:, 3:4], op0=ALU.mult, op1=ALU.add)
                yield
            P.v("tensor_tensor", [bs, wh], [bs], out=bs[:, 6:7], in0=bs[:, 3:4], in1=wh[:, NB:NB + 1], op=ALU.subtract)
            for pi, (c, n) in enumerate(pcs):
                P.v("tensor_scalar", ik + [bs], [nm], out=nm[:, c:c + n], in0=isc[:, c:c + n], scalar1=bs[:, 6:7], scalar2=MASKV, op0=ALU.is_lt, op1=ALU.mult)
                yield

        pipe = Pipe()
        for _ in prep(0):
            pass
        for i in range(NT):
            rows = slice(i * 128, (i + 1) * 128)
            L = (i + 1) * 128
            nxt = prep(i + 1) if i + 1 < NT else None
            pipe.filler = nxt
            nch_i, nch_n = (L + 511) // 512, (L + 128 + 511) // 512
            npc_n = (L + 128 + 1023) // 1024
            pipe.rate = -(-(8 * nch_n + (NB + 2) * npc_n + 8) // (8 * nch_i))
            qa, qr, nm, stab = tile_in[i]
            at = rat.next()
            for h in range(8):
                po = (h % 2) * 64
                terms = [
                    (qa[:, 2 * h, :], lambda c0, n: kT[:, 0, c0:c0 + n], [qa, kT]),
                    (qa[:, 2 * h + 1, :], lambda c0, n: kT[:, 1, c0:c0 + n], [qa, kT]),
                    (qr[:, h, :], lambda c0, n: kT[:, 2, c0:c0 + n], [qr, kT]),
                ]
                mterm = (K["idb"][:], lambda c0, n, nm=nm: nm[:, c0:c0 + n], [nm, K["idb"]])
                olat = rolat.next()

                def pv(pT, rd, kt, first, last, olat=olat):
                    for cc in range(2):
                        P.mm(rd + [ckv], [olat], olat[:, cc, :], ckv[:, kt, cc * 128:(cc + 1) * 128], pT, first, last)

                def done(rs, rinv, olat=olat, h=h, at=at, rows=rows):
                    ol = rol.next()
                    P.s("copy", [olat], [ol], out=ol[:], in_=olat[:])
                    po_ = rout.next()
                    for cc in range(2):
                        P.mm([ol, wuv], [po_], po_[:], ol[:, cc, :], wuv[:, cc, h * 128:(h + 1) * 128], cc == 0, cc == 1)
                    P.v("tensor_scalar", [po_, rs], [at], out=at[:, h * 128:(h + 1) * 128], in0=po_[:], scalar1=rinv, scalar2=None, op0=ALU.mult)
                    if h == 7:
                        P.dma("gpsimd", [at], [("attn", rows.start)], out=S["attn"][rows, :], in_=at[:])
                for it in attn_items(P, K, R, terms, mterm, 0, L, pv, done, negm_ap=stab[:, 16 + h:17 + h], negm_reads=[stab]):
                    pipe.push(it)
            if nxt is not None:
                for _ in nxt:
                    pass
        pipe.flush()


def phase_e3(nc, P, K, S, W, j):
    with Ctx(nc, P) as C:
        kcmpT = C.sb("kcmpT", [128, 256], BF16)
        vcmp = C.sb("vcmp", [128, 2, 2, 64], BF16)
        P.g("memset", [], [kcmpT], kcmpT[:], 0.0)
        P.g("memset", [], [vcmp], vcmp[:], 0.0)
        with Ctx(nc, P) as C1:
            uT = C1.sb("uT", [128, 2, T], BF16)
            for n in range(2):
                P.dma("sync", [("bkT", n, tg) for tg in range(8)], [uT], out=uT[:, n, :], in_=S["bkT"][n])
            w1 = C1.sb("w1", [128, 2, 32, 128], BF16)
            for kv in range(2):
                for half in range(2):
                    P.dma("gpsimd", [], [w1], out=w1[half * 64:(half + 1) * 64, kv, :, :],
                          in_=W["e_b_cmp_w1"][j][kv].rearrange("(jj d) n -> d jj n", d=64))
            w2f = C1.sb("w2f", [128, 2, 64], F32)
            P.dma("sync", [], [w2f], out=w2f[:], in_=W["e_b_cmp_w2"][j].rearrange("kv n d -> n kv d"))
            w2p = C1.sb("w2p", [128, 2, 128], BF16)
            w2v = C1.sb("w2v", [128, 64], BF16)
            P.g("memset", [], [w2p], w2p[:], 0.0)
            for g in range(2):
                P.v("tensor_copy", [w2f, w2p], [w2p], out=w2p[:, g, g * 64:(g + 1) * 64], in_=w2f[:, 0, :])
            P.v("tensor_copy", [w2f], [w2v], out=w2v[:], in_=w2f[:, 1, :])
            posT = C1.sb("posT", [64, 2, 32], BF16)
            for kv in range(2):
                P.dma("gpsimd", [], [posT], out=posT[:, kv, :], in_=W["e_b_cmp_pos"][j][kv].rearrange("jj d -> d jj"), allow_slow_non_contiguous=True)
            rph = C1.psrot("ph", [128, 256], F32, 2)
            rpb = C1.psrot("pb", [128, 8], F32, 1)
            rpo = C1.psrot("pko", [128, 256], F32, 1)
            rpv = C1.psrot("pvo", [128, 64], F32, 1)
            bias = C1.sb("bias", [128, 2], F32)
            x = C1.sb("x", [128, 256], F32)
            x2 = C1.sb("x2", [128, 256], F32)
            sg = C1.sb("sg", [128, 256], F32)
            gl = [[C1.sb("gl%d%d" % (kv, g), [128, 256], BF16) for g in range(2)] for kv in range(2)]
            pko = rpo.next()
            for kv in range(2):
                pb = rpb.next()
                for jj in range(32):
                    P.mm([w1, posT], [pb], pb[:, 0:1], w1[0:64, kv, jj, :], posT[:, kv, jj:jj + 1], jj == 0, jj == 31)
                P.v("tensor_copy", [pb], [bias], out=bias[:, kv:kv + 1], in_=pb[:, 0:1])
                for g in range(2):
                    ph = rph.next()
                    for jj in range(32):
                        P.mm([w1, uT], [ph], ph[:, 0:255], w1[g * 64:(g + 1) * 64, kv, jj, :], uT[g * 64:(g + 1) * 64, kv, jj:jj + 16 * 254 + 1:16], jj == 0, jj == 31)
                    P.s("activation", [ph, bias], [x], out=x[:, 0:255], in_=ph[:, 0:255], func=AF.Identity, bias=bias[:, kv:kv + 1], scale=1.0)
                    P.v("tensor_tensor", [x], [x2], out=x2[:, 0:255], in0=x[:, 0:255], in1=x[:, 0:255], op=ALU.mult)
                    P.v("tensor_scalar", [x2], [x2], out=x2[:, 0:255], in0=x2[:, 0:255], scalar1=0.044715, scalar2=1.0, op0=ALU.mult, op1=ALU.add)
                    P.v("tensor_tensor", [x2, x], [x2], out=x2[:, 0:255], in0=x2[:, 0:255], in1=x[:, 0:255], op=ALU.mult)
                    P.s("activation", [x2], [sg], out=sg[:, 0:255], in_=x2[:, 0:255], func=AF.Sigmoid, scale=1.5957691216057308)
                    G = gl[kv][g]
                    P.g("memset", [], [G], G[:], 0.0)
                    P.v("tensor_tensor", [x, sg, G], [G], out=G[:, 0:255], in0=x[:, 0:255], in1=sg[:, 0:255], op=ALU.mult)
                    if kv == 0:
                        P.mm([G, w2p], [pko], pko[:, 0:255], w2p[:, g, :], G[:, 0:255], g == 0, g == 1)
                    else:
                        for mc in range(2):
                            nm = 128 if mc == 0 else 127
                            pvo = rpv.next()
                            P.mm([G, w2v], [pvo], pvo[0:nm, :], G[:, mc * 128:mc * 128 + nm], w2v[:])
                            P.v("tensor_copy", [pvo, vcmp], [vcmp], out=vcmp[0:nm, mc, g, :], in_=pvo[0:nm, :])
                if kv == 0:
                    P.v("tensor_copy", [pko, kcmpT], [kcmpT], out=kcmpT[:, 0:255], in_=pko[:, 0:255])

        qT = C.sb("qT", [128, 8, T], BF16)
        for g in range(2):
            P.g("memset", [], [qT], qT[(1 - g) * 64:(2 - g) * 64, g * 4:(g + 1) * 4, :], 0.0)
        for c in range(4):
            for g in range(2):
                P.dma("sync", [("bqT", c, tg) for tg in range(8)] + [qT], [qT], out=qT[g * 64:(g + 1) * 64, g * 4 + c, :],
                      in_=S["bqT"][c][g * 64:(g + 1) * 64, :])
        nall = C.sb("nall", [128, NT, 4, 2], F32)
        kall = C.sb("kall", [128, 2, NT, 2], F32)
        P.dma("sync", ["nalld"], [nall], out=nall[:].rearrange("p a b c -> p (a b c)"), in_=S["nall"])
        P.dma("sync", ["kalld"], [kall], out=kall[:].rearrange("p a b c -> p (a b c)"), in_=S["kall"])
        kmx = C.sb("kmx", [128, 16], F32)
        ones1 = C.sb("ones1", [1, 128], F32)
        P.g("memset", [], [ones1], ones1[:], 1.0)
        for br in range(2):
            P.v("tensor_reduce", [kall], [kmx], out=kmx[:, 2 * br:2 * br + 2], in_=kall[:, br].rearrange("p t g -> p g t"), axis=AX.X, op=ALU.max)
        with Ctx(nc, P) as Ck:
            pk1 = Ck.ps("pk1", [128, 512], F32)
            for jx in range(4):
                P.tr([kmx], [pk1], pk1[0:1, jx * 128:(jx + 1) * 128], kmx[:, jx:jx + 1], K["idf"])
            P.v("reduce_max", [pk1], [kmx], out=kmx[0:1, 4:8], in_=pk1[0:1, :].rearrange("p (j x) -> p j x", x=128), axis=AX.X)
            P.mm([ones1, kmx], [pk1], pk1[:, 0:4], ones1[:], kmx[0:1, 4:8])
            P.v("tensor_copy", [pk1], [kmx], out=kmx[:, 8:12], in_=pk1[:, 0:4])
        rstab = C.sbrot("stab", [128, 2, 4, 2], F32, 3)
        tile_stab = {}
        ksT = C.sb("ksT", [128, T], BF16)
        kwT = C.sb("kwT", [128, T], BF16)
        P.dma("sync", [("bkT", 2, tg) for tg in range(8)], [ksT], out=ksT[:], in_=S["bkT"][2])
        P.dma("sync", [("bkT", 3, tg) for tg in range(8)], [kwT], out=kwT[:], in_=S["bkT"][3])
        vsw = C.sb("vsw", [128, NT, 256], BF16)
        P.dma("sync", [("bv_tok", i) for i in range(NT)], [vsw], out=vsw[:], in_=S["bv_tok"].rearrange("(c p) d -> p c d", p=128))
        ntri_ge = C.sb("ntri_ge", [128, 128], BF16)
        ntri_lt = C.sb("ntri_lt", [128, 128], BF16)
        P.v("tensor_scalar", [K["tri_ge"]], [ntri_ge], out=ntri_ge[:], in0=K["tri_ge"][:], scalar1=-1.0, scalar2=-MASKV, op0=ALU.add, op1=ALU.mult)
        P.v("tensor_scalar", [K["tri_lt"]], [ntri_lt], out=ntri_lt[:], in0=K["tri_lt"][:], scalar1=-1.0, scalar2=-MASKV, op0=ALU.add, op1=ALU.mult)
        negw = C.sb("negw", [128, 640], BF16)
        P.g("memset", [], [negw], negw[:], 0.0)
        P.v("tensor_copy", [ntri_lt, negw], [negw], out=negw[:, 0:128], in_=ntri_lt[:])
        P.v("tensor_copy", [ntri_ge, negw], [negw], out=negw[:, 512:640], in_=ntri_ge[:])
        dltci = C.sb("dltci", [128, 256], I32)
        dltc = C.sb("dltc", [128, 256], F32)
        P.g("iota", [], [dltci], dltci[:], pattern=[[-16, 256]], base=0, channel_multiplier=1)
        P.v("tensor_copy", [dltci], [dltc], out=dltc[:], in_=dltci[:])
        dlti = C.sb("dlti", [128, 64], I32)
        dlt = C.sb("dlt", [128, 64], F32)
        P.g("iota", [], [dlti], dlti[:], pattern=[[-64, 64]], base=0, channel_multiplier=1)
        P.v("tensor_copy", [dlti], [dlt], out=dlt[:], in_=dlti[:])
        negm = [[C.sb("negm%d%d" % (par, g), [128, T], BF16) for g in range(2)] for par in range(2)]
        ycmp = C.sb("ycmp", [128, 2, 8, 64], F32)
        R = AttnRes(C)
        rpo = C.psrot("po", [128, 2, 64], F32, 2)
        rpc = C.psrot("pc", [128, 64], F32, 1)
        rgt = C.sbrot("gt", [128, 24], F32, 3)
        rselc = C.sbrot("selc", [128, 256], F32, 2)
        ryb = C.sbrot("yb", [128, 512], F32, 2)
        rpg = C.sbrot("pg", [128, 256], F32, 2)
        re32 = C.sbrot("e32", [128, 256], F32, 2)
        rp16 = C.sbrot("p16", [128, 256], BF16, 2)
        rsc = C.sbrot("sc", [128, 4, 64], F32, 2)
        rm8 = C.sbrot("m8", [128, 16], F32, 2)
        rcf = C.sbrot("cf", [128, 8], F32, 6)
        rcmx = C.sbrot("cmx", [128, 8], F32, 3)
        tile_gt = {}

        def prep(i, g):
            rows = slice(i * 128, (i + 1) * 128)
            t0 = i * 128
            L = (i + 1) * 128
            if g == 0:
                gt = rgt.next()
                P.dma("sync", [("bgate", i)], [gt], out=gt[:], in_=S["bgate"][rows, :])
                selc = rselc.next()
                P.v("tensor_scalar", [dltc], [selc], out=selc[:], in0=dltc[:], scalar1=float(31 - t0), scalar2=None, op0=ALU.is_ge)
                tile_gt[i] = (gt, selc)
                stab = rstab.next()
                tile_stab[i] = stab
                for br in range(2):
                    P.v("tensor_tensor", [nall, kmx], [stab], out=stab[:, br], in0=nall[:, i],
                        in1=kmx[:, 8 + 2 * br:10 + 2 * br].unsqueeze(1).to_broadcast([128, 4, 2]), op=ALU.mult)
                P.v("tensor_scalar", [stab], [stab], out=stab[:], in0=stab[:], scalar1=1e-30, scalar2=None, op0=ALU.add)
                P.s("activation", [stab], [stab], out=stab[:], in_=stab[:], func=AF.Ln)
                P.s("activation", [stab], [stab], out=stab[:], in_=stab[:], func=AF.Exp, scale=0.5)
                P.v("tensor_scalar", [stab], [stab], out=stab[:], in0=stab[:], scalar1=-1.02, scalar2=None, op0=ALU.mult)
            gt, selc = tile_gt[i]
            pg = rpg.next()
            for hp in range(4):
                h = g * 4 + hp
                qh = qT[:, h, t0:t0 + 128]
                ps = R.rps.next()
                P.mm([qT, kcmpT], [ps], ps[:, 0:256], qh, kcmpT[:, :])
                mx = rcmx.next()
                P.v("reduce_max", [ps], [mx], out=mx[:, 0:1], in_=ps[:, 0:256], axis=AX.X)
                P.v("tensor_scalar", [mx], [mx], out=mx[:, 1:2], in0=mx[:, 0:1], scalar1=-1.0, scalar2=None, op0=ALU.mult)
                e32 = re32.next()
                P.s("activation", [ps, mx], [e32], out=e32[:], in_=ps[:, 0:256], func=AF.Exp, bias=mx[:, 1:2], scale=1.0)
                P.v("scalar_tensor_tensor", [e32, selc], [e32, mx], out=e32[:], in0=e32[:], scalar=1.0, in1=selc[:], op0=ALU.mult, op1=ALU.mult,
                    accum_out=mx[:, 2:3])
                P.v("tensor_scalar", [mx], [mx], out=mx[:, 3:4], in0=mx[:, 2:3], scalar1=1e-30, scalar2=None, op0=ALU.add)
                P.v("reciprocal", [mx], [mx], out=mx[:, 4:5], in_=mx[:, 3:4])
                if hp == 0:
                    P.v("tensor_scalar", [e32, mx], [pg], out=pg[:], in0=e32[:], scalar1=mx[:, 4:5], scalar2=None, op0=ALU.mult)
                else:
                    P.v("scalar_tensor_tensor", [e32, mx, pg], [pg], out=pg[:], in0=e32[:], scalar=mx[:, 4:5], in1=pg[:], op0=ALU.mult, op1=ALU.add)
                p16 = rp16.next()
                P.g("tensor_copy", [e32], [p16], out=p16[:], in_=e32[:])
                yield
                pTp = R.rpT.next()
                for mc in range(2):
                    P.tr([p16], [pTp], pTp[:, mc, :], p16[:, mc * 128:(mc + 1) * 128], K["idb"])
                pT = R.rpt.next()
                P.s("copy", [pTp], [pT], out=pT[:, 0:2, :], in_=pTp[:, 0:2, :])
                yield
                pc = rpc.next()
                for mc in range(2):
                    P.mm([pT, vcmp], [pc], pc[:], pT[:, mc, :], vcmp[:, mc, g, :], mc == 0, mc == 1)
                P.v("tensor_tensor", [mx, gt], [mx], out=mx[:, 5:6], in0=mx[:, 4:5], in1=gt[:, h * 3:h * 3 + 1], op=ALU.mult)
                P.v("tensor_scalar", [pc, mx], [("ycmp", i % 2, h)], out=ycmp[:, i % 2, h, :], in0=pc[:], scalar1=mx[:, 5:6], scalar2=None, op0=ALU.mult)
                yield
            nm = negm[i % 2][g]
            if i >= 8:
                sc = rsc.next()
                m8 = rm8.next()
                imp, s1_, s2_, bm = sc[:, 0, :], sc[:, 1, :], sc[:, 2, :], sc[:, 3, :]
                P.v("reduce_sum", [pg], [sc], out=imp, in_=pg[:].rearrange("p (b f) -> p b f", f=4), axis=AX.X)
                P.v("tensor_tensor", [sc, pg], [sc], out=sc[:, 0, 1:64], in0=sc[:, 0, 1:64], in1=pg[:, 3:255:4], op=ALU.add)
                P.v("tensor_scalar", [dlt], [sc], out=s2_, in0=dlt[:], scalar1=float(128 - t0), scalar2=1e6, op0=ALU.is_lt, op1=ALU.mult)
                P.v("tensor_tensor", [sc], [sc], out=s1_, in0=imp, in1=s2_, op=ALU.max)
                P.v("tensor_scalar", [dlt], [sc], out=s2_, in0=dlt[:], scalar1=float(-t0), scalar2=None, op0=ALU.is_ge)
                P.v("tensor_tensor", [sc], [sc], out=s1_, in0=s1_, in1=s2_, op=ALU.mult)
                P.v("tensor_scalar", [sc], [sc], out=s2_, in0=s2_, scalar1=-1.0, scalar2=1e30, op0=ALU.add, op1=ALU.mult)
                P.v("tensor_tensor", [sc], [sc], out=s1_, in0=s1_, in1=s2_, op=ALU.add)
                P.g("memset", [sc], [sc], sc[:, 1, 0:1], 1e6)
                yield
                P.v("max", [sc], [m8], out=m8[:, 0:8], in_=s1_)
                P.v("match_replace", [sc, m8], [sc], out=s2_, in_to_replace=m8[:, 0:8], in_values=s1_, imm_value=NEG)
                P.v("max", [sc], [m8], out=m8[:, 8:16], in_=s2_)
                P.v("tensor_scalar", [sc, m8], [sc], out=bm, in0=s1_, scalar1=m8[:, 15:16], scalar2=MASKV, op0=ALU.is_lt, op1=ALU.mult)
                nb = L // 64
                P.g("tensor_copy", [sc, nm], [nm], out=nm[:, 0:L].rearrange("p (b f) -> p b f", f=64),
                    in_=sc[:, 3, 0:nb].unsqueeze(2).to_broadcast([128, nb, 64]))
                P.g("tensor_tensor", [nm, ntri_ge], [nm], out=nm[:, L - 128:L], in0=nm[:, L - 128:L], in1=ntri_ge[:], op=ALU.add)
            else:
                if i > 0:
                    P.g("memset", [], [nm], nm[:, 0:L - 128], 0.0)
                P.g("tensor_copy", [ntri_ge, nm], [nm], out=nm[:, L - 128:L], in_=ntri_ge[:])

        pipe = Pipe()
        work = [(i, g) for i in range(NT) for g in range(2)]
        for _ in prep(0, 0):
            pass
        ybs = {}
        for wi_, (i, g) in enumerate(work):
            rows = slice(i * 128, (i + 1) * 128)
            t0 = i * 128
            L = (i + 1) * 128
            nxt = prep(*work[wi_ + 1]) if wi_ + 1 < len(work) else None
            pipe.filler = nxt
            npush = 4 * ((L + 511) // 512 + (min(L, 640) + 511) // 512)
            pipe.rate = -(-16 // npush)
            if g == 0:
                ybs[i] = ryb.next()
            yb = ybs[i]
            gt, _selc = tile_gt[i]
            nm = negm[i % 2][g]
            for hp in range(4):
                h = g * 4 + hp
                qh = qT[:, h, t0:t0 + 128]
                stab = tile_stab[i]
                po = rpo.next()
                cf = rcf.next()
                terms = [(qh, lambda c0, n: ksT[:, c0:c0 + n], [qT, ksT])]
                mterm = (K["idb"][:], lambda c0, n, nm=nm: nm[:, c0:c0 + n], [nm, K["idb"]])

                def pv_s(pT, rd, kt, first, last, po=po, g=g):
                    P.mm(rd + [vsw], [po], po[:, 0, :], pT, vsw[:, kt, g * 64:(g + 1) * 64], first, last)

                def done_s(rs, rinv, cf=cf, gt=gt, h=h):
                    P.v("tensor_tensor", [rs, gt], [cf], out=cf[:, 1:2], in0=rinv, in1=gt[:, h * 3 + 1:h * 3 + 2], op=ALU.mult)
                for it in attn_items(P, K, R, terms, mterm, 0, L, pv_s, done_s, negm_ap=stab[:, 0, hp, g:g + 1], negm_reads=[stab]):
                    pipe.push(it)
                k0 = max(0, (i - 4) * 128)
                woff = 640 - (L - k0)
                terms = [(qh, lambda c0, n: kwT[:, c0:c0 + n], [qT, kwT])]
                mterm = (K["idb"][:], lambda c0, n, k0=k0, woff=woff: negw[:, woff + c0 - k0:woff + c0 - k0 + n], [negw, K["idb"]])

                def pv_w(pT, rd, kt, first, last, po=po, g=g, k0=k0):
                    P.mm(rd + [vsw], [po], po[:, 1, :], pT, vsw[:, k0 // 128 + kt, 128 + g * 64:128 + (g + 1) * 64], first, last)

                def done_w(rs, rinv, cf=cf, gt=gt, h=h, po=po, yb=yb, i=i, rows=rows):
                    P.v("tensor_tensor", [rs, gt], [cf], out=cf[:, 2:3], in0=rinv, in1=gt[:, h * 3 + 2:h * 3 + 3], op=ALU.mult)
                    ys = yb[:, h * 64:(h + 1) * 64]
                    P.v("scalar_tensor_tensor", [po, cf, ("ycmp", i % 2, h)], [yb], out=ys, in0=po[:, 0, :], scalar=cf[:, 1:2], in1=ycmp[:, i % 2, h, :],
                        op0=ALU.mult, op1=ALU.add)
                    P.v("scalar_tensor_tensor", [po, cf, yb], [yb], out=ys, in0=po[:, 1, :], scalar=cf[:, 2:3], in1=ys, op0=ALU.mult, op1=ALU.add)
                    if h == 7:
                        P.dma("gpsimd", [yb], [("attn", "b", i)], out=S["attn"][rows, 512:1024], in_=yb[:])
                for it in attn_items(P, K, R, terms, mterm, k0, L, pv_w, done_w, negm_ap=stab[:, 1, hp, g:g + 1], negm_reads=[stab]):
                    pipe.push(it)
            if nxt is not None:
                for _ in nxt:
                    pass
        pipe.flush()


W_SPECS = dict(
    x=[T, D], p=[DEPTH, T, 256],
    e_w_in=[2, 1024, 3360], e_a_conv=[2, 4, 1024], e_a_i_b=[2, 4], e_a_f_b=[2, 4], e_a_norm=[2, 512],
    e_b_cmp_pos=[2, 2, 32, 64], e_b_cmp_w1=[2, 2, 2048, 128], e_b_cmp_w2=[2, 2, 128, 64], e_b_g_b=[2, 24],
    e_w_out=[2, 1024, 1024], o_w_in=[2, 1024, 904], o_q_norm=[2, 512], o_kv_norm=[2, 256], o_w_qb=[2, 512, 1536],
    o_w_uk=[2, 256, 8, 128], o_w_uv=[2, 256, 8, 128], o_w_iq=[2, 512, 512], o_ik_g=[2, 64], o_ik_b=[2, 64],
    o_w_out=[2, 1024, 1024], ln1_g=[4, 1024], ln1_b=[4, 1024], ln2_g=[4, 1024], ln2_b=[4, 1024],
    mlp_w1=[4, 1024, 4096], mlp_w2=[4, 4096, 1024], ple_gate_w=[4, 1024, 1024], ple_w=[4, 256, 1024], rope_inv=[48])


def rope_inv_table():
    a = (10000.0 ** (-np.arange(0, 64, 2, dtype=np.float32) / np.float32(64))).astype(np.float32)
    b = (10000.0 ** (-np.arange(0, 32, 2, dtype=np.float32) / np.float32(32))).astype(np.float32)
    return np.concatenate([a, b]).astype(np.float32)


def build(debug_outs=(), phases=None, layers=(0, 1, 2, 3)):
    nc = bass.Bass("TRN2", target_bir_lowering=False)
    dbg = set(debug_outs)
    allp = phases is None

    def on(p):
        return allp or p in phases

    def dram(name, shape, dt, kind=None):
        if kind is None:
            kind = "ExternalOutput" if name in dbg else "Internal"
        return nc.dram_tensor(name, shape, dt, kind=kind).ap()

    W = {k: dram(k, s, F32, "ExternalInput") for k, s in W_SPECS.items()}
    W["positions"] = dram("positions", [T], I32, "ExternalInput")
    out = dram("out", [T, D], F32, "ExternalOutput")
    S = dict(
        v_tok=dram("v_tok", [T, 512], BF16), sigo=dram("sigo", [T, 512], F32), bv_tok=dram("bv_tok", [T, 256], BF16),
        bgate=dram("bgate", [T, 24], F32), gsc=dram("gsc", [3, 4, T], F32), qkT=dram("qkT", [8, 128, T], BF16),
        bqT=dram("bqT", [4, 128, T], BF16), bkT=dram("bkT", [4, 128, T], BF16), attn=dram("attn", [T, 1024], F32),
        hA=dram("hA", [T, D], F32), h1=dram("h1", [T, D], F32), h2=dram("h2", [T, D], F32),
        qaT=dram("qaT", [16, 128, T], BF16), qrT=dram("qrT", [4, 128, T], BF16), qiT=dram("qiT", [4, 128, T], BF16),
        kT=dram("kT", [4, 128, T], BF16), ckv=dram("ckv", [T, 256], BF16), wi=dram("wi", [T, 16], F32),
        hB=dram("hB", [T, D], F32), qn2=dram("qn2", [T, 8], F32), kn2=dram("kn2", [128, NT], F32),
        nall=dram("nall", [128, NT * 8], F32), kall=dram("kall", [128, 4 * NT], F32),
    )
    with ExitStack() as st:
        P = Prog(nc)
        P.setup(st)
        with Ctx(nc, P) as C0:
            K = make_consts(C0, P)
            h_in = W["x"]
            wrote_out = False
            for n, li in enumerate(layers):
                j = li // 2
                last = (n == len(layers) - 1)
                if li % 2 == 0:
                    if on("e1"):
                        phase_e1(nc, P, K, S, W, j, h_in)
                    if on("e2"):
                        phase_e2(nc, P, K, S, W, j)
                    if on("e3"):
                        phase_e3(nc, P, K, S, W, j)
                    w_out = W["e_w_out"][j]
                else:
                    if on("o1"):
                        phase_o1(nc, P, K, S, W, j, h_in)
                    if on("o3"):
                        phase_o3(nc, P, K, S, W, j)
                    w_out = W["o_w_out"][j]
                if on("ta"):
                    phase_tail_a(nc, P, K, S, w_out, W["ln1_g"][li], W["ln1_b"][li], h_in, S["h1"])
                if on("tb"):
                    phase_tail_b(nc, P, K, S, W["mlp_w1"][li], W["mlp_w2"][li], W["ln2_g"][li], W["ln2_b"][li], S["h1"], S["h2"])
                if on("tc"):
                    h_out = out if last else (S["hA"] if n % 2 == 0 else S["hB"])
                    phase_tail_c(nc, P, K, S, W["ple_gate_w"][li], W["ple_w"][li], W["p"][li], S["h2"], h_out, last)
                    wrote_out = wrote_out or last
                    h_in = h_out
            if not wrote_out:
                zt = C0.sb("zt", [128, 1024], F32)
                P.g("memset", [], [zt], zt[:], 0.0)
                P.dma("sync", [zt], ["out"], out=out[0:128, :], in_=zt[:], is_output=True)
        P.finish()
    return nc


_NC_CACHE = {}


def kernel(**inputs):
    if "nc" not in _NC_CACHE:
        _NC_CACHE["nc"] = build()
    nc = _NC_CACHE["nc"]
    B = inputs["x"].shape[0]
    rinv = rope_inv_table()
    in_maps = []
    for b in range(B):
        m = {}
        for k in W_SPECS:
            if k == "x":
                m[k] = np.ascontiguousarray(inputs["x"][b], dtype=np.float32)
            elif k == "p":
                m[k] = np.ascontiguousarray(inputs["p"][:, b], dtype=np.float32)
            elif k == "rope_inv":
                m[k] = rinv
            else:
                m[k] = np.ascontiguousarray(inputs[k], dtype=np.float32)
        m["positions"] = np.ascontiguousarray(inputs["positions"][b], dtype=np.int32)
        in_maps.append(m)
    res = run_bass_kernel_spmd(nc, in_maps, core_ids=list(range(B)))
    return np.stack([np.asarray(r["out"], dtype=np.float32) for r in res.results], axis=0)
```

```python
from contextlib import ExitStack
import numpy as np
import concourse.bass as bass
import concourse.mybir as mybir
from concourse.bass_utils import run_bass_kernel_spmd

F32 = mybir.dt.float32
BF16 = mybir.dt.bfloat16
I32 = mybir.dt.int32
ALU = mybir.AluOpType
AF = mybir.ActivationFunctionType
AX = mybir.AxisListType

T = 4096
D = 1024
NT = T // 128
DEPTH = 4
DFF = 4096
DN_ALPHA = (2.0 * DEPTH) ** 0.25
LN_EPS = 1e-5
NEG = -1e30
N_DMA_SEMS = 8


class Prog:
    ENGS = ("tensor", "vector", "scalar", "gpsimd", "sync")

    def __init__(self, nc):
        self.nc = nc
        self.ops = {k: [] for k in self.ENGS}
        self.count = {k: 0 for k in self.ENGS}
        self.waited = {k: {} for k in self.ENGS}
        self.sems = {}
        self.writers = {}
        self.readers = {}
        self.dma_val = [0] * N_DMA_SEMS
        self.dma_rr = 0
        self.dma_rr_q = {}
        self.out_tokens = []

    def setup(self, stack):
        for k in self.ENGS:
            self.sems[k] = stack.enter_context(self.nc.semaphore("s_" + k))
        for i in range(N_DMA_SEMS):
            self.sems["d%d" % i] = stack.enter_context(self.nc.semaphore("d_%d" % i))

    @staticmethod
    def _key(k):
        if isinstance(k, (str, tuple)):
            return k
        t = getattr(k, "tensor", k)
        return t.name

    def _wait(self, eng, s, v):
        w = self.waited[eng]
        if w.get(s, 0) < v:
            w[s] = v
            self.ops[eng].append(("wait", self.sems[s], v))

    def _deps(self, eng, reads, writes):
        need = {}
        for k in reads:
            for s, v in self.writers.get(k, {}).items():
                if need.get(s, 0) < v:
                    need[s] = v
        for k in writes:
            for d in (self.writers.get(k, {}), self.readers.get(k, {})):
                for s, v in d.items():
                    if need.get(s, 0) < v:
                        need[s] = v
        for s, v in need.items():
            if eng == "tensor" and s == "tensor":
                continue
            self._wait(eng, s, v)

    def _record(self, tok, reads, writes):
        s, v = tok
        for k in reads:
            self.readers.setdefault(k, {})[s] = v
        for k in writes:
            self.writers[k] = {s: v}
            self.readers[k] = {}

    def op(self, eng, meth, reads, writes, *args, **kw):
        reads = [self._key(k) for k in reads]
        writes = [self._key(k) for k in writes]
        self._deps(eng, reads, writes)
        self.count[eng] += 1
        self.ops[eng].append(("op", (meth, args, kw), self.sems[eng], 1))
        self._record((eng, self.count[eng]), reads, writes)

    def mm(self, reads, writes, out, lhsT, rhs, start=True, stop=True):
        self.op("tensor", "matmul", reads, writes, out, lhsT=lhsT, rhs=rhs, start=start, stop=stop)

    def tr(self, reads, writes, out, in_, ident):
        self.op("tensor", "transpose", reads + [ident], writes, out=out, in_=in_, identity=ident[:])

    def v(self, meth, reads, writes, **kw):
        self.op("vector", meth, reads, writes, **kw)

    def s(self, meth, reads, writes, **kw):
        self.op("scalar", meth, reads, writes, **kw)

    def g(self, meth, reads, writes, *args, **kw):
        self.op("gpsimd", meth, reads, writes, *args, **kw)

    def dma(self, eng, reads, writes, out, in_, is_output=False, **kw):
        fn = ("dma_start", (), dict(out=out, in_=in_, **kw))
        reads = [self._key(k) for k in reads]
        writes = [self._key(k) for k in writes]
        half = N_DMA_SEMS // 2
        base = 0 if eng == "sync" else half
        rr = self.dma_rr_q.get(eng, 0)
        self.dma_rr_q[eng] = (rr + 1) % half
        i = base + rr
        sname = "d%d" % i
        self._deps(eng, reads, writes)
        if self.dma_val[i]:
            self._wait(eng, sname, self.dma_val[i])
        self.dma_val[i] += 16
        self.ops[eng].append(("op", fn, self.sems[sname], 16))
        self._record((sname, self.dma_val[i]), reads, writes)
        if is_output:
            self.out_tokens.append((sname, self.dma_val[i]))

    def barrier(self):
        for e in self.ENGS:
            for s in self.ENGS:
                if s != e and self.count[s]:
                    self._wait(e, s, self.count[s])
            for i in range(N_DMA_SEMS):
                if self.dma_val[i]:
                    self._wait(e, "d%d" % i, self.dma_val[i])
        for e in self.ENGS:
            if self.count[e]:
                self._wait(e, e, self.count[e])
        self.writers.clear()
        self.readers.clear()

    def finish(self):
        for s, v in self.out_tokens:
            self._wait("sync", s, v)
        ops = self.ops

        def replay(e, lst):
            for o in lst:
                if o[0] == "wait":
                    e.wait_ge(o[1], o[2])
                else:
                    meth, args, kw = o[1]
                    try:
                        ins = getattr(e, meth)(*args, **kw)
                    except Exception:
                        print("FAILED OP", meth, args, kw)
                        raise
                    ins.then_inc(o[2], o[3])

        with self.nc.Block() as block:
            @block.tensor
            def _(e):
                replay(e, ops["tensor"])

            @block.vector
            def _(e):
                replay(e, ops["vector"])

            @block.scalar
            def _(e):
                replay(e, ops["scalar"])

            @block.gpsimd
            def _(e):
                replay(e, ops["gpsimd"])

            @block.sync
            def _(e):
                replay(e, ops["sync"])


class Rot:
    def __init__(self, bufs):
        self.bufs = bufs
        self.i = 0

    def next(self):
        b = self.bufs[self.i % len(self.bufs)]
        self.i += 1
        return b


class Ctx:
    uid = 0

    def __init__(self, nc, P):
        self.nc = nc
        self.P = P
        self.st = ExitStack()

    def __enter__(self):
        self.st.__enter__()
        return self

    def __exit__(self, *a):
        self.P.barrier()
        return self.st.__exit__(*a)

    def sb(self, name, shape, dt):
        Ctx.uid += 1
        return self.st.enter_context(self.nc.sbuf_tensor("%s_%d" % (name, Ctx.uid), shape, dt))

    def ps(self, name, shape, dt=F32):
        Ctx.uid += 1
        return self.st.enter_context(self.nc.psum_tensor("%s_%d" % (name, Ctx.uid), shape, dt))

    def sbrot(self, name, shape, dt, n=2):
        return Rot([self.sb(name + str(i), shape, dt) for i in range(n)])

    def psrot(self, name, shape, dt=F32, n=2):
        return Rot([self.ps(name + str(i), shape, dt) for i in range(n)])


def make_consts(C, P):
    k = {}
    idf = C.sb("identf", [128, 128], F32)
    idb = C.sb("identb", [128, 128], BF16)
    P.g("memset", [], [idf], idf[:], 1.0)
    P.g("affine_select", [idf], [idf], out=idf[:], in_=idf[:], pattern=[[-1, 128]], compare_op=ALU.is_equal,
        fill=0.0, base=0, channel_multiplier=1)
    P.v("tensor_copy", [idf], [idb], out=idb[:], in_=idf[:])
    k["idf"], k["idb"] = idf, idb
    for name, mult, patt, base in (("tri_le", -1, 1, 0), ("tri_ge", 1, -1, 0), ("tri_lt", -1, 1, -1)):
        t = C.sb(name, [128, 128], F32)
        P.g("memset", [], [t], t[:], 1.0)
        P.g("affine_select", [t], [t], out=t[:], in_=t[:], pattern=[[patt, 128]], compare_op=ALU.is_ge, fill=0.0,
            base=base, channel_multiplier=mult)
        k[name] = t
    return k


def load_transposed(P, K, src, xT, nk, key, rot_in, rot_ps, tiles=range(NT), t0=0):
    for i in tiles:
        xt = rot_in.next()
        P.dma("sync", [], [xt], out=xt[:, 0:nk * 128], in_=src[i * 128:(i + 1) * 128, :])
        for half in range((nk + 3) // 4):
            n = min(4, nk - half * 4)
            pt = rot_ps.next()
            for jj in range(n):
                c = half * 4 + jj
                P.tr([xt], [pt], pt[:, jj, :], xt[:, c * 128:(c + 1) * 128], K["idf"])
            col = (i - t0) * 128
            dst = xT[:, half * 4:half * 4 + n, col:col + 128]
            if half % 2 == 0:
                P.v("tensor_copy", [pt], [(key, i)], out=dst, in_=pt[:, 0:n, :])
            else:
                P.s("copy", [pt], [(key, i)], out=dst, in_=pt[:, 0:n, :])


def run_staged(n, body):
    prev = None
    for i in range(n):
        g = body(i)
        next(g, None)
        if prev is not None:
            for _ in prev:
                pass
        prev = g
    if prev is not None:
        for _ in prev:
            pass


def run_interleaved(n, body, k):
    live = []
    nxt = 0
    while live or nxt < n:
        while len(live) < k and nxt < n:
            live.append(body(nxt))
            nxt += 1
        for g in list(live):
            try:
                next(g)
            except StopIteration:
                live.remove(g)


def load_w_bf16(P, dst, src, nk, key=None):
    for kc in range(nk):
        P.dma("gpsimd", [], [key or dst], out=dst[:, kc, :], in_=src[kc * 128:(kc + 1) * 128, :])


def phase_e1(nc, P, K, S, W, j, h_in):
    with Ctx(nc, P) as C:
        hT = C.sb("hT", [128, 8, T], BF16)
        w = C.sb("w_in", [128, 8, 3360], BF16)
        wq = C.sb("w_q", [128, 8, 4, 2, 64], BF16)
        load_w_bf16(P, w, W["e_w_in"][j], 8)
        for kc in range(8):
            for g in range(2):
                P.dma("gpsimd", [], [wq], out=wq[:, kc, :, g, :],
                      in_=W["e_w_in"][j][kc * 128:(kc + 1) * 128, 2056 + g * 256:2056 + (g + 1) * 256].rearrange("p (c d) -> p c d", c=4))
        rps = C.psrot("ps", [128, 512], F32, 3)
        rpt = C.psrot("pt", [128, 4, 128], F32, 2)
        with Ctx(nc, P) as C1:
            rin = C1.sbrot("hin", [128, 1024], F32, 2)
            load_transposed(P, K, h_in, hT, 8, "hT", rin, rpt)

        def feat_mm(ps, lhs_fn, tg, m=128):
            for kc in range(8):
                P.mm([("hT", i2) for i2 in range(tg * 4, tg * 4 + 4)] + [w, wq], [ps], ps[0:m, :], lhs_fn(kc),
                     hT[:, kc, tg * 512:(tg + 1) * 512], kc == 0, kc == 7)

        with Ctx(nc, P) as C1:
            bi = C1.sb("bi", [4, 1], F32)
            bfn = C1.sb("bfn", [4, 1], F32)
            P.dma("sync", [], [bi], out=bi[:], in_=W["e_a_i_b"][j].rearrange("(h o) -> h o", o=1))
            P.dma("sync", [], [bfn], out=bfn[:], in_=W["e_a_f_b"][j].rearrange("(h o) -> h o", o=1))
            P.v("tensor_scalar", [bfn], [bfn], out=bfn[:], in0=bfn[:], scalar1=-1.0, scalar2=None, op0=ALU.mult)
            ig = C1.sb("ig", [4, T], F32)
            sp = C1.sb("sp", [4, T], F32)
            bneg = C1.sb("bneg", [4, T], F32)
            cst = C1.sb("cst", [4, T], F32)
            for tg in range(8):
                cs = slice(tg * 512, (tg + 1) * 512)
                ps = rps.next(); feat_mm(ps, lambda kc: w[:, kc, 2048:2052], tg, 4)
                P.s("activation", [ps, bi], [ig], out=ig[:, cs], in_=ps[0:4, :], func=AF.Identity, bias=bi[:], scale=1.0)
                ps = rps.next(); feat_mm(ps, lambda kc: w[:, kc, 2052:2056], tg, 4)
                P.s("activation", [ps, bfn], [sp], out=sp[:, cs], in_=ps[0:4, :], func=AF.Exp, bias=bfn[:], scale=-1.0)
            P.s("activation", [sp], [sp], out=sp[:], in_=sp[:], func=AF.Ln, bias=1.0, scale=1.0)
            P.g("memset", [], [cst], cst[:], 1.0)
            P.v("tensor_tensor_scan", [cst, sp], [bneg], out=bneg[:], data0=cst[:], data1=sp[:], initial=0.0, op0=ALU.mult, op1=ALU.add)
            P.v("tensor_tensor", [ig, bneg], [ig], out=ig[:], in0=ig[:], in1=bneg[:], op=ALU.add)
            P.g("memset", [cst], [cst], cst[:], 0.0)
            P.v("tensor_tensor_scan", [cst, ig], [sp], out=sp[:], data0=cst[:], data1=ig[:], initial=0.0, op0=ALU.add, op1=ALU.max)
            P.v("tensor_tensor", [bneg, sp], [bneg], out=bneg[:], in0=bneg[:], in1=sp[:], op=ALU.subtract)
            P.v("tensor_scalar", [sp], [sp], out=sp[:], in0=sp[:], scalar1=-1.0, scalar2=None, op0=ALU.mult)
            P.dma("sync", [ig], ["gsc0"], out=S["gsc"][0], in_=ig[:])
            P.dma("sync", [sp], ["gsc1"], out=S["gsc"][1], in_=sp[:])
            P.dma("sync", [bneg], ["gsc2"], out=S["gsc"][2], in_=bneg[:])

        with Ctx(nc, P) as C1:
            convw = C1.sb("convw", [128, 8, 4], F32)
            for kk in range(4):
                P.dma("sync", [], [convw], out=convw[:, :, kk], in_=W["e_a_conv"][j][kk].rearrange("(c p) -> p c", p=128),
                      allow_slow_non_contiguous=True)
            bg = C1.sb("bg", [128, 24], F32)
            P.dma("sync", [], [bg], out=bg[:], in_=W["e_b_g_b"][j].partition_broadcast(128))
            rst = C1.sbrot("stg", [128, 512], F32, 2)
            rstb = C1.sbrot("stgb", [128, 512], BF16, 3)
            for i in range(NT):
                rows = slice(i * 128, (i + 1) * 128)

                def tok_mm(ps, c0, n, pc0=0):
                    for kc in range(8):
                        P.mm([("hT", i), w], [ps], ps[:, pc0:pc0 + n], hT[:, kc, i * 128:(i + 1) * 128], w[:, kc, c0:c0 + n], kc == 0, kc == 7)
                ps = rps.next(); tok_mm(ps, 1024, 512)
                sb_ = rstb.next()
                P.s("copy", [ps], [sb_], out=sb_[:], in_=ps[:])
                P.dma("gpsimd", [sb_], [("v_tok", i)], out=S["v_tok"][rows, :], in_=sb_[:])
                ps = rps.next(); tok_mm(ps, 1536, 512)
                st_ = rst.next()
                P.s("activation", [ps], [st_], out=st_[:], in_=ps[:], func=AF.Sigmoid)
                P.dma("gpsimd", [st_], [("sigo", i)], out=S["sigo"][rows, :], in_=st_[:])
                ps = rps.next(); tok_mm(ps, 2952, 128, 0); tok_mm(ps, 3208, 128, 128); tok_mm(ps, 3336, 24, 256)
                sb_ = rstb.next()
                P.v("tensor_copy", [ps], [sb_], out=sb_[:, 0:256], in_=ps[:, 0:256])
                P.dma("gpsimd", [sb_], [("bv_tok", i)], out=S["bv_tok"][rows, :], in_=sb_[:, 0:256])
                st_ = rst.next()
                P.v("tensor_tensor", [ps, bg], [st_], out=st_[:, 0:24], in0=ps[:, 256:280], in1=bg[:], op=ALU.add)
                P.s("activation", [st_], [st_], out=st_[:, 32:56], in_=st_[:, 0:24], func=AF.Sigmoid)
                P.dma("gpsimd", [st_], [("bgate", i)], out=S["bgate"][rows, :], in_=st_[:, 32:56])

            xpad = C1.sb("xpad", [128, 3 + T], F32)
            P.g("memset", [], [("xpad", -1)], xpad[:, 0:3], 0.0)
            racc = C1.sbrot("acc", [128, 512], F32, 2)
            for c in range(8):
                for tg in range(8):
                    ps = rps.next(); feat_mm(ps, lambda kc: w[:, kc, c * 128:(c + 1) * 128], tg)
                    P.s("copy", [ps], [("xpad", tg)], out=xpad[:, 3 + tg * 512:3 + (tg + 1) * 512], in_=ps[:])
                    acc = racc.next()
                    t0 = tg * 512
                    rd = [("xpad", tg - 1), ("xpad", tg), convw]
                    P.v("tensor_scalar", rd, [acc], out=acc[:], in0=xpad[:, t0:t0 + 512], scalar1=convw[:, c, 0:1], scalar2=None, op0=ALU.mult)
                    for jj in range(1, 4):
                        P.v("scalar_tensor_tensor", rd + [acc], [acc], out=acc[:], in0=xpad[:, t0 + jj:t0 + jj + 512],
                            scalar=convw[:, c, jj:jj + 1], in1=acc[:], op0=ALU.mult, op1=ALU.add)
                    ob = rstb.next()
                    P.s("activation", [acc], [ob], out=ob[:], in_=acc[:], func=AF.Silu)
                    P.dma("gpsimd", [ob], [("qkT", c, tg)], out=S["qkT"][c][:, t0:t0 + 512], in_=ob[:])
            blk = C1.sb("blk", [128, 2], BF16)
            P.g("memset", [], [blk], blk[:], 0.0)
            P.g("memset", [blk], [blk], blk[0:64, 0:1], 1.0)
            P.g("memset", [blk], [blk], blk[64:128, 1:2], 1.0)
            nall = C1.sb("nall", [128, NT, 8], F32)
            kall = C1.sb("kall", [128, 2, NT, 2], F32)
            rsqn = C1.sbrot("sqn", [128, 512], BF16, 2)
            rpn = C1.psrot("pn", [128, 8], F32, 1)

            def norms(ob, dst_fn):
                sq = rsqn.next()
                P.g("tensor_tensor", [ob], [sq], out=sq[:], in0=ob[:], in1=ob[:], op=ALU.mult)
                pn = rpn.next()
                for k in range(4):
                    P.mm([sq, blk], [pn], pn[:, 2 * k:2 * k + 2], sq[:, k * 128:(k + 1) * 128], blk[:])
                dst_fn(pn)
            for c in range(4):
                for tg in range(8):
                    ps = rps.next(); feat_mm(ps, lambda kc: wq[:, kc, c].rearrange("p g d -> p (g d)"), tg)
                    ob = rstb.next()
                    P.s("mul", [ps], [ob], out=ob[:], in_=ps[:], mul=0.125)
                    P.dma("gpsimd", [ob], [("bqT", c, tg)], out=S["bqT"][c][:, tg * 512:(tg + 1) * 512], in_=ob[:])
                    norms(ob, lambda pn: P.v("tensor_copy", [pn], [nall], out=nall[:, tg * 4:(tg + 1) * 4, 2 * c:2 * c + 2],
                                             in_=pn[:].rearrange("p (k g) -> p k g", g=2)))
            for n, c0 in enumerate((2568, 2696, 2824, 3080)):
                for tg in range(8):
                    ps = rps.next(); feat_mm(ps, lambda kc: w[:, kc, c0:c0 + 128], tg)
                    ob = rstb.next()
                    P.v("tensor_copy", [ps], [ob], out=ob[:], in_=ps[:])
                    P.dma("gpsimd", [ob], [("bkT", n, tg)], out=S["bkT"][n][:, tg * 512:(tg + 1) * 512], in_=ob[:])
                    if n >= 2:
                        norms(ob, lambda pn: P.v("tensor_copy", [pn], [kall], out=kall[:, n - 2, tg * 4:(tg + 1) * 4, :],
                                                 in_=pn[:].rearrange("p (k g) -> p k g", g=2)))
            P.dma("sync", [nall], ["nalld"], out=S["nall"], in_=nall[:].rearrange("p a b -> p (a b)"))
            P.dma("sync", [kall], ["kalld"], out=S["kall"], in_=kall[:].rearrange("p a b c -> p (a b c)"))


def phase_e2(nc, P, K, S, W, j):
    NC_ = NT
    with Ctx(nc, P) as C:
        rows = C.sb("rows", [4, 3, T], F32)
        for r in range(3):
            P.dma("sync", ["gsc%d" % r], [rows], out=rows[:, r, :], in_=S["gsc"][r])
        sel = C.sb("sel", [4, 4, 128], F32)
        P.g("memset", [], [sel], sel[:], 1.0)
        P.g("affine_select", [sel], [sel], out=sel[:], in_=sel[:], pattern=[[-1, 4], [0, 128]], compare_op=ALU.is_equal, fill=0.0,
            base=0, channel_multiplier=1)
        gnorm = C.sb("gnorm", [128, 512], F32)
        P.dma("sync", [], [gnorm], out=gnorm[:], in_=W["e_a_norm"][j].partition_broadcast(128))
        maskT = C.sb("maskT", [128, 128], F32)
        P.v("tensor_scalar", [K["tri_le"]], [maskT], out=maskT[:], in0=K["tri_le"][:], scalar1=128.0 ** -0.5, scalar2=None, op0=ALU.mult)

        rps_a = C.psrot("psa", [128, 512], F32, 1)
        rps_s = C.psrot("pss", [128, 128], F32, 2)
        rps_o = C.psrot("pso", [128, 132], F32, 2)
        rps_i = C.psrot("psi", [128, 132], F32, 1)
        rps_k = C.psrot("psk", [128, 128], BF16, 1)
        rET = C.sbrot("ET", [128, 128], F32, 2)
        rETm = C.sbrot("ETm", [128, 128], F32, 2)
        rPT = C.sbrot("PT", [128, 128], BF16, 2)
        rksc = C.sbrot("ksc", [128, 128], BF16, 2)
        rintra = C.sbrot("intra", [128, 132], F32, 2)

        for h in range(4):
          with Ctx(nc, P) as CH:
            qT = CH.sb("qT", [128, T], BF16)
            kT = CH.sb("kT", [128, T], BF16)
            vaug = CH.sb("vaug", [128, NC_, 132], BF16)
            nd = CH.sb("nd", [128, NC_, 132], F32)
            P.dma("sync", [("qkT", h, tg) for tg in range(8)], [qT], out=qT[:], in_=S["qkT"][h])
            P.dma("sync", [("qkT", 4 + h, tg) for tg in range(8)], [kT], out=kT[:], in_=S["qkT"][4 + h])
            P.g("memset", [], [vaug], vaug[:, :, 128:132], 1.0)
            P.dma("sync", [("v_tok", i) for i in range(NT)], [vaug], out=vaug[:, :, 0:128],
                  in_=S["v_tok"][:, h * 128:(h + 1) * 128].rearrange("(c p) d -> p c d", p=128))
            cols = CH.sb("cols", [128, 3, NC_], F32)
            for r in range(3):
                pc = rps_a.next()
                for c in range(NC_):
                    P.mm([rows, sel], [pc], pc[:, c:c + 1], rows[:, r, c * 128:(c + 1) * 128], sel[:, h, 0:1])
                P.v("tensor_copy", [pc], [cols], out=cols[:, r, :], in_=pc[:, 0:NC_])
            ends = CH.sb("ends", [128, 1 + NC_], F32)
            pc = rps_a.next()
            P.mm([rows, sel], [pc], pc[:, 0:NC_], sel[:, h, :], rows[:, 1, 127::128])
            P.g("memset", [], [ends], ends[:, 0:1], 0.0)
            P.v("tensor_copy", [pc, ends], [ends], out=ends[:, 1:1 + NC_], in_=pc[:, 0:NC_])
            wcol = CH.sb("wcol", [128, NC_], F32)
            est = CH.sb("est", [128, NC_], F32)
            eint = CH.sb("eint", [128, NC_], F32)
            enm = CH.sb("enm", [128, NC_], F32)
            P.v("tensor_tensor", [cols, ends], [wcol], out=wcol[:], in0=cols[:, 0, :], in1=ends[:, 1:1 + NC_], op=ALU.add)
            P.s("activation", [wcol], [wcol], out=wcol[:], in_=wcol[:], func=AF.Exp)
            P.v("tensor_tensor", [ends], [est], out=est[:], in0=ends[:, 1:1 + NC_], in1=ends[:, 0:NC_], op=ALU.subtract)
            P.s("activation", [est], [est], out=est[:], in_=est[:], func=AF.Exp)
            P.v("tensor_tensor", [cols, ends], [eint], out=eint[:], in0=cols[:, 1, :], in1=ends[:, 0:NC_], op=ALU.subtract)
            P.s("activation", [eint], [eint], out=eint[:], in_=eint[:], func=AF.Exp)
            P.v("tensor_scalar", [eint], [eint], out=eint[:], in0=eint[:], scalar1=128.0 ** -0.5, scalar2=None, op0=ALU.mult)
            P.s("activation", [cols], [enm], out=enm[:], in_=cols[:, 2, :], func=AF.Exp)

            CT = CH.sb("CT", [128, 132], F32)
            CTb = CH.sb("CTb", [128, 132], BF16)
            P.g("memset", [], [CT], CT[:], 0.0)
            P.g("memset", [], [CTb], CTb[:], 0.0)
            for c in range(NC_):
                cs = slice(c * 128, (c + 1) * 128)
                pg = rps_s.next()
                P.mm([rows, sel], [pg], pg[:], sel[:, h, :], rows[:, 1, cs])
                ET = rET.next()
                P.s("activation", [pg, cols], [ET], out=ET[:], in_=pg[:], func=AF.Exp, bias=cols[:, 0, c:c + 1], scale=1.0)
                ETm = rETm.next()
                P.g("tensor_tensor", [ET, maskT], [ETm], out=ETm[:], in0=ET[:], in1=maskT[:], op=ALU.mult)
                pst = rps_s.next()
                P.mm([kT, qT], [pst], pst[:], kT[:, cs], qT[:, cs])
                PT = rPT.next()
                P.v("tensor_tensor", [pst, ETm], [PT], out=PT[:], in0=pst[:], in1=ETm[:], op=ALU.mult)
                po = rps_o.next()
                P.mm([PT, vaug], [po], po[:, 0:129], PT[:], vaug[:, c, 0:129])
                pi = rps_i.next()
                P.mm([qT, CTb], [pi], pi[:, 0:129], qT[:, cs], CTb[:, 0:129])
                intra = rintra.next()
                P.s("copy", [po], [intra], out=intra[:, 0:129], in_=po[:, 0:129])
                P.v("scalar_tensor_tensor", [pi, intra, eint], [("nd", c)], out=nd[:, c, 0:129], in0=pi[:, 0:129], scalar=eint[:, c:c + 1],
                    in1=intra[:, 0:129], op0=ALU.mult, op1=ALU.add)
                pk = rps_k.next()
                P.tr([kT], [pk], pk[:], kT[:, cs], K["idb"])
                ksc = rksc.next()
                P.s("activation", [pk, wcol], [ksc], out=ksc[:], in_=pk[:], func=AF.Copy, scale=wcol[:, c:c + 1])
                pu = rps_o.next()
                P.mm([ksc, vaug], [pu], pu[:, 0:129], ksc[:], vaug[:, c, 0:129])
                P.v("scalar_tensor_tensor", [CT, pu, est], [CT], out=CT[:, 0:129], in0=CT[:, 0:129], scalar=est[:, c:c + 1],
                    in1=pu[:, 0:129], op0=ALU.mult, op1=ALU.add)
                P.s("copy", [CT], [CTb], out=CTb[:, 0:129], in_=CT[:, 0:129])

            ndk = [("nd", c) for c in range(NC_)]
            dn = CH.sb("dn", [128, NC_], F32)
            bc = lambda t: t[:].unsqueeze(2).to_broadcast([128, NC_, 128])
            P.v("scalar_tensor_tensor", ndk, [dn], out=dn[:], in0=nd[:, :, 128], scalar=-1.0, in1=nd[:, :, 128], op0=ALU.mult, op1=ALU.max)
            P.v("tensor_tensor", [dn, enm], [dn], out=dn[:], in0=dn[:], in1=enm[:], op=ALU.max)
            P.v("reciprocal", [dn], [dn], out=dn[:], in_=dn[:])
            hh = CH.sb("hh", [128, NC_, 128], F32)
            sq = CH.sb("sq", [128, NC_, 128], F32)
            P.v("tensor_tensor", ndk + [dn], [hh], out=hh[:], in0=nd[:, :, 0:128], in1=bc(dn), op=ALU.mult)
            s1 = CH.sb("s1", [128, NC_], F32)
            s2 = CH.sb("s2", [128, NC_], F32)
            m2 = CH.sb("m2", [128, NC_], F32)
            P.v("reduce_sum", [hh], [s1], out=s1[:], in_=hh[:], axis=AX.X)
            P.g("tensor_tensor", [hh], [sq], out=sq[:], in0=hh[:], in1=hh[:], op=ALU.mult)
            P.v("reduce_sum", [sq], [s2], out=s2[:], in_=sq[:], axis=AX.X)
            P.v("tensor_scalar", [s1], [s1], out=s1[:], in0=s1[:], scalar1=1.0 / 128, scalar2=None, op0=ALU.mult)
            P.v("tensor_tensor", [s1], [m2], out=m2[:], in0=s1[:], in1=s1[:], op=ALU.mult)
            P.v("scalar_tensor_tensor", [s2, m2], [s2], out=s2[:], in0=s2[:], scalar=1.0 / 128, in1=m2[:], op0=ALU.mult, op1=ALU.subtract)
            P.v("tensor_scalar", [s2], [s2], out=s2[:], in0=s2[:], scalar1=LN_EPS, scalar2=None, op0=ALU.add)
            P.s("activation", [s2], [s2], out=s2[:], in_=s2[:], func=AF.Ln)
            P.s("activation", [s2], [s2], out=s2[:], in_=s2[:], func=AF.Exp, scale=-0.5)
            P.v("tensor_tensor", [hh, s1], [hh], out=hh[:], in0=hh[:], in1=bc(s1), op=ALU.subtract)
            P.v("tensor_tensor", [hh, s2], [hh], out=hh[:], in0=hh[:], in1=bc(s2), op=ALU.mult)
            P.g("tensor_tensor", [hh, gnorm], [hh], out=hh[:], in0=hh[:],
                in1=gnorm[:, h * 128:(h + 1) * 128].unsqueeze(1).to_broadcast([128, NC_, 128]), op=ALU.mult)
            P.dma("sync", [("sigo", i) for i in range(NT)] + [sq], [sq], out=sq[:],
                  in_=S["sigo"][:, h * 128:(h + 1) * 128].rearrange("(c p) d -> p c d", p=128))
            P.v("tensor_tensor", [hh, sq], [hh], out=hh[:], in0=hh[:], in1=sq[:], op=ALU.mult)
            P.dma("gpsimd", [hh], [("attn", "a", h)], out=S["attn"][:, h * 128:(h + 1) * 128].rearrange("(c p) d -> p c d", p=128), in_=hh[:])


def bcast_row(P, C, name, src_row, n):
    t = C.sb(name, [128, n], F32)
    P.dma("sync", [], [t], out=t[:], in_=src_row.partition_broadcast(128))
    return t


def layer_norm_tile(P, r, cen, sm, g_t, b_t, out_t, n=1024):
    P.v("reduce_sum", [r], [sm], out=sm[:, 0:1], in_=r[:], axis=AX.X)
    P.v("tensor_scalar", [sm], [sm], out=sm[:, 1:2], in0=sm[:, 0:1], scalar1=-1.0 / n, scalar2=None, op0=ALU.mult)
    P.s("activation", [r, sm], [cen], out=cen[:], in_=r[:], func=AF.Identity, bias=sm[:, 1:2], scale=1.0)
    P.s("activation", [cen], [r, sm], out=r[:], in_=cen[:], func=AF.Square, accum_out=sm[:, 2:3])
    P.v("tensor_scalar", [sm], [sm], out=sm[:, 3:4], in0=sm[:, 2:3], scalar1=1.0 / n, scalar2=LN_EPS, op0=ALU.mult, op1=ALU.add)
    P.s("activation", [sm], [sm], out=sm[:, 4:5], in_=sm[:, 3:4], func=AF.Ln)
    P.s("activation", [sm], [sm], out=sm[:, 5:6], in_=sm[:, 4:5], func=AF.Exp, scale=-0.5)
    P.v("scalar_tensor_tensor", [cen, sm, g_t], [cen], out=cen[:], in0=cen[:], scalar=sm[:, 5:6], in1=g_t[:], op0=ALU.mult, op1=ALU.mult)
    P.g("tensor_tensor", [cen, b_t], [out_t], out=out_t[:], in0=cen[:], in1=b_t[:], op=ALU.add)


def phase_tail_a(nc, P, K, S, w_out, ln_g, ln_b, h_in, h1):
    with Ctx(nc, P) as C:
        wo = C.sb("wo", [128, 8, 1024], BF16)
        load_w_bf16(P, wo, w_out, 8)
        g_t = bcast_row(P, C, "g1", ln_g, 1024)
        b_t = bcast_row(P, C, "b1", ln_b, 1024)
        rin = C.sbrot("ain", [128, 1024], F32, 2)
        rh = C.sbrot("hin", [128, 1024], F32, 2)
        rpt = C.psrot("pt", [128, 4, 128], F32, 2)
        rps = C.psrot("ps", [128, 512], F32, 4)
        raT = C.sbrot("aT", [128, 8, 128], BF16, 2)
        rr = C.sbrot("r", [128, 1024], F32, 2)
        rcen = C.sbrot("cen", [128, 1024], F32, 2)
        rout = C.sbrot("o", [128, 1024], F32, 2)
        rsm = C.sbrot("sm", [128, 8], F32, 2)
        def body(i):
            rows = slice(i * 128, (i + 1) * 128)
            aT = raT.next()
            load_transposed(P, K, S["attn"], aT, 8, aT.name, rin, rpt, tiles=[i], t0=i)
            ht = rh.next()
            P.dma("sync", [], [ht], out=ht[:], in_=h_in[rows, :])
            yield
            r = rr.next()
            for n in range(2):
                ps = rps.next()
                for kc in range(8):
                    P.mm([(aT.name, i), wo], [ps], ps[:], aT[:, kc, :], wo[:, kc, n * 512:(n + 1) * 512], kc == 0, kc == 7)
                P.v("scalar_tensor_tensor", [ht, ps], [r], out=r[:, n * 512:(n + 1) * 512], in0=ht[:, n * 512:(n + 1) * 512], scalar=DN_ALPHA,
                    in1=ps[:], op0=ALU.mult, op1=ALU.add)
            cen, sm, o = rcen.next(), rsm.next(), rout.next()
            layer_norm_tile(P, r, cen, sm, g_t, b_t, o)
            P.dma("gpsimd", [o], [("h1", i)], out=h1[rows, :], in_=o[:])
        run_staged(NT, body)


def phase_tail_b(nc, P, K, S, w1, w2, ln_g, ln_b, h1, h2):
    ST = 256
    NS = ST // 128
    with Ctx(nc, P) as C:
        W1 = C.sb("W1", [128, 8, 4096], BF16)
        W2 = C.sb("W2", [128, 32, 1024], BF16)
        load_w_bf16(P, W1, w1, 8)
        load_w_bf16(P, W2, w2, 32)
        g_t = bcast_row(P, C, "g2", ln_g, 1024)
        b_t = bcast_row(P, C, "b2", ln_b, 1024)
        h1s = [C.sb("h1s%d" % k, [128, 1024], F32) for k in range(2 * NS)]
        rpt = C.psrot("pt", [128, 4, 128], F32, 2)
        rps = C.psrot("ps", [128, 512], F32, 4)
        rhT = C.sbrot("h1T", [128, 8, ST], BF16, 2)
        raT = C.sbrot("aT", [128, 32, ST], BF16, 1)
        rtmp = C.sbrot("tmp", [128, ST], F32, 3)
        rr = C.sbrot("r", [128, 1024], F32, 2)
        rcen = C.sbrot("cen", [128, 1024], F32, 1)
        rout = C.sbrot("o", [128, 1024], F32, 2)
        rsm = C.sbrot("sm", [128, 8], F32, 2)
        def body(st):
            hT = rhT.next()
            hts = []
            for k in range(NS):
                i = st * NS + k
                ht = h1s[(st % 2) * NS + k]
                hts.append(ht)
                load_transposed(P, K, h1, hT, 8, hT.name, Rot([ht]), rpt, tiles=[i], t0=st * NS)
            yield
            hk = [(hT.name, st * NS + k) for k in range(NS)]
            aT = raT.next()
            for f in range(32):
                ps = rps.next()
                for kc in range(8):
                    P.mm(hk + [W1], [ps], ps[:, 0:ST], W1[:, kc, f * 128:(f + 1) * 128], hT[:, kc, :], kc == 0, kc == 7)
                tmp = rtmp.next()
                P.s("activation", [ps], [tmp], out=tmp[:], in_=ps[:, 0:ST], func=AF.Relu)
                P.g("tensor_tensor", [tmp], [(aT.name, f)], out=aT[:, f, :], in0=tmp[:], in1=tmp[:], op=ALU.mult)
            ak = [(aT.name, f) for f in range(32)]
            for k in range(NS):
                i = st * NS + k
                r = rr.next()
                for n in range(2):
                    ps = rps.next()
                    for f in range(32):
                        P.mm(ak + [W2], [ps], ps[:], aT[:, f, k * 128:(k + 1) * 128], W2[:, f, n * 512:(n + 1) * 512], f == 0, f == 31)
                    P.v("scalar_tensor_tensor", [hts[k], ps], [r], out=r[:, n * 512:(n + 1) * 512], in0=hts[k][:, n * 512:(n + 1) * 512],
                        scalar=DN_ALPHA, in1=ps[:], op0=ALU.mult, op1=ALU.add)
                cen, sm, o = rcen.next(), rsm.next(), rout.next()
                layer_norm_tile(P, r, cen, sm, g_t, b_t, o)
                P.dma("gpsimd", [o], [("h2", i)], out=h2[i * 128:(i + 1) * 128, :], in_=o[:])
        run_staged(T // ST, body)


def phase_tail_c(nc, P, K, S, wg, wp, p_in, h2, h_out, is_output):
    with Ctx(nc, P) as C:
        Wg = C.sb("Wg", [128, 8, 1024], BF16)
        Wp = C.sb("Wp", [128, 2, 1024], BF16)
        load_w_bf16(P, Wg, wg, 8)
        load_w_bf16(P, Wp, wp, 2)
        rh = C.sbrot("h2t", [128, 1024], F32, 2)
        rp = C.sbrot("pt_", [128, 256], F32, 2)
        rpt = C.psrot("pt", [128, 4, 128], F32, 2)
        rps = C.psrot("ps", [128, 512], F32, 4)
        rhT = C.sbrot("hT", [128, 8, 128], BF16, 2)
        rpT = C.sbrot("pT", [128, 2, 128], BF16, 2)
        rgt = C.sbrot("gt", [128, 512], F32, 2)
        rout = C.sbrot("o", [128, 1024], F32, 2)
        def body(i):
            rows = slice(i * 128, (i + 1) * 128)
            ht = rh.next()
            hT = rhT.next()
            load_transposed(P, K, h2, hT, 8, hT.name, Rot([ht]), rpt, tiles=[i], t0=i)
            pT = rpT.next()
            load_transposed(P, K, p_in, pT, 2, pT.name, rp, rpt, tiles=[i], t0=i)
            yield
            o = rout.next()
            for n in range(2):
                cs = slice(n * 512, (n + 1) * 512)
                psg = rps.next()
                for kc in range(8):
                    P.mm([(hT.name, i), Wg], [psg], psg[:], hT[:, kc, :], Wg[:, kc, cs], kc == 0, kc == 7)
                psp = rps.next()
                for kc in range(2):
                    P.mm([(pT.name, i), Wp], [psp], psp[:], pT[:, kc, :], Wp[:, kc, cs], kc == 0, kc == 1)
                gt = rgt.next()
                P.s("activation", [psg], [gt], out=gt[:], in_=psg[:], func=AF.Sigmoid)
                P.v("tensor_tensor", [gt, psp], [gt], out=gt[:], in0=gt[:], in1=psp[:], op=ALU.mult)
                P.g("tensor_tensor", [gt, ht], [o], out=o[:, cs], in0=gt[:], in1=ht[:, cs], op=ALU.add)
            P.dma("gpsimd", [o], [("hout", i)], out=h_out[rows, :], in_=o[:], is_output=is_output)
        run_staged(NT, body)


TWO_PI = 6.283185307179586
CW1 = 6.28125
CW2 = TWO_PI - CW1


def phase_o1(nc, P, K, S, W, j, h_in):
    SC = 192.0 ** -0.5
    with Ctx(nc, P) as C:
        wi_ = C.sb("w_in_o", [128, 8, 904], BF16)
        load_w_bf16(P, wi_, W["o_w_in"][j], 8)
        wqb = C.sb("wqb", [128, 4, 1536], BF16)
        load_w_bf16(P, wqb, W["o_w_qb"][j], 4)
        wiq = C.sb("wiq", [128, 4, 512], BF16)
        load_w_bf16(P, wiq, W["o_w_iq"][j], 4)
        wqr = C.sb("wqr", [128, 4, 8, 64], BF16)
        P.v("tensor_copy", [wqb], [wqr], out=wqr[:], in_=wqb[:].rearrange("p k (h e) -> p k h e", e=192)[:, :, :, 128:192])
        wuk_f = C.sb("wuk_f", [128, 2, 1024], F32)
        P.dma("sync", [], [wuk_f], out=wuk_f[:], in_=W["o_w_uk"][j].rearrange("(cc p) h d -> p cc (h d)", p=128))
        wukT = C.sb("wukT", [128, 8, 256], BF16)
        rpt = C.psrot("pt", [128, 4, 128], F32, 2)
        for cc in range(2):
            for hq in range(2):
                pt = rpt.next()
                for jj in range(4):
                    h = hq * 4 + jj
                    P.tr([wuk_f], [pt], pt[:, jj, :], wuk_f[:, cc, h * 128:(h + 1) * 128], K["idf"])
                P.v("tensor_copy", [pt], [wukT], out=wukT[:, hq * 4:(hq + 1) * 4, cc * 128:(cc + 1) * 128], in_=pt[:])
        gq = bcast_row(P, C, "gq", W["o_q_norm"][j], 512)
        gkv = bcast_row(P, C, "gkv", W["o_kv_norm"][j], 256)
        ikg = bcast_row(P, C, "ikg", W["o_ik_g"][j], 64)
        ikb = bcast_row(P, C, "ikb", W["o_ik_b"][j], 64)
        inv = bcast_row(P, C, "inv", W["rope_inv"], 48)
        posi = C.sb("posi", [128, NT], I32)
        P.dma("sync", [], [posi], out=posi[:], in_=W["positions"].rearrange("(c p) -> p c", p=128), allow_slow_non_contiguous=True)
        posf = C.sb("posf", [128, NT], F32)
        P.v("tensor_copy", [posi], [posf], out=posf[:], in_=posi[:])

        rh = C.sbrot("hin", [128, 1024], F32, 3)
        rhT = C.sbrot("hT", [128, 8, 128], BF16, 3)
        rps = C.psrot("ps", [128, 512], F32, 3)
        rpk = C.psrot("psk", [128, 512], F32, 1)
        rsm = C.sbrot("sm", [128, 16], F32, 3)
        rcq = C.sbrot("cq", [128, 512], F32, 9)
        rcqT = C.sbrot("cqT", [128, 4, 128], BF16, 3)
        rqn = C.sbrot("qn", [128, 8, 128], BF16, 3)
        rqa = C.sbrot("qa", [128, 16, 128], BF16, 3)
        rang = C.sbrot("ang", [128, 4, 48], F32, 3)
        rki = C.sbrot("ki", [128, 48], I32, 3)
        rtr = C.sbrot("tr", [128, 4, 48], F32, 3)
        rq1 = C.sbrot("q1", [128, 512], F32, 6)
        rq2 = C.sbrot("q2", [128, 512], F32, 6)
        rqb = C.sbrot("qbf", [128, 4, 128], BF16, 9)
        rkv = C.sbrot("kv", [128, 672], F32, 3)
        rkvb = C.sbrot("kvb", [128, 256], BF16, 3)
        rkk = C.sbrot("kk", [128, 2, 128], F32, 3)
        rkkb = C.sbrot("kkb", [128, 2, 128], BF16, 2)
        rwi = C.sbrot("wi", [128, 16], F32, 3)
        kn2 = C.sb("kn2", [128, NT], F32)
        onesb = C.sb("onesb", [128, 1], BF16)
        P.g("memset", [], [onesb], onesb[:], 1.0)
        rsq = C.sbrot("sqa", [128, 16, 128], BF16, 3)
        rqn2 = C.sbrot("qn2", [128, 24], F32, 3)
        rpn = C.psrot("pn", [128, 8], F32, 1)

        def rms_rstd(src, n, sm, col, junk):
            P.s("activation", [src], [junk, sm], out=junk, in_=src, func=AF.Square, accum_out=sm[:, col:col + 1])
            P.v("tensor_scalar", [sm], [sm], out=sm[:, col + 1:col + 2], in0=sm[:, col:col + 1], scalar1=1.0 / n, scalar2=LN_EPS, op0=ALU.mult, op1=ALU.add)
            P.s("activation", [sm], [sm], out=sm[:, col + 1:col + 2], in_=sm[:, col + 1:col + 2], func=AF.Ln)
            P.s("activation", [sm], [sm], out=sm[:, col + 2:col + 3], in_=sm[:, col + 1:col + 2], func=AF.Exp, scale=-0.5)

        def rope(dst, src, cs, sn, nh, half, t1, t2):
            cb = cs.unsqueeze(1).to_broadcast([128, nh, half])
            sb_ = sn.unsqueeze(1).to_broadcast([128, nh, half])
            x1, x2 = src[:, :, 0:half], src[:, :, half:2 * half]
            P.v("tensor_tensor", [src], [t1], out=t1, in0=x1, in1=cb, op=ALU.mult)
            P.g("tensor_tensor", [src], [t2], out=t2, in0=x2, in1=sb_, op=ALU.mult)
            P.v("tensor_tensor", [t1, t2], [dst], out=dst[:, :, 0:half], in0=t1, in1=t2, op=ALU.subtract)
            P.v("tensor_tensor", [src], [t1], out=t1, in0=x1, in1=sb_, op=ALU.mult)
            P.g("tensor_tensor", [src], [t2], out=t2, in0=x2, in1=cb, op=ALU.mult)
            P.v("tensor_tensor", [t1, t2], [dst], out=dst[:, :, half:2 * half], in0=t1, in1=t2, op=ALU.add)

        def body(i):
            rows = slice(i * 128, (i + 1) * 128)
            cols = slice(i * 128, (i + 1) * 128)
            ang, ki, tr = rang.next(), rki.next(), rtr.next()
            P.v("tensor_scalar", [inv, posf], [ang], out=ang[:, 0, :], in0=inv[:], scalar1=posf[:, i:i + 1], scalar2=None, op0=ALU.mult)
            P.v("tensor_scalar", [ang], [ang], out=ang[:, 1, :], in0=ang[:, 0, :], scalar1=1.0 / TWO_PI, scalar2=None, op0=ALU.mult)
            P.v("tensor_copy", [ang], [ki], out=ki[:], in_=ang[:, 1, :])
            P.v("tensor_copy", [ki], [ang], out=ang[:, 1, :], in_=ki[:])
            P.v("scalar_tensor_tensor", [ang], [ang], out=ang[:, 0, :], in0=ang[:, 1, :], scalar=-CW1, in1=ang[:, 0, :], op0=ALU.mult, op1=ALU.add)
            P.v("scalar_tensor_tensor", [ang], [ang], out=ang[:, 0, :], in0=ang[:, 1, :], scalar=-CW2, in1=ang[:, 0, :], op0=ALU.mult, op1=ALU.add)

            def wrap(a):
                P.v("tensor_scalar", [ang], [ang], out=ang[:, 2, :], in0=a, scalar1=float(np.pi), scalar2=-TWO_PI, op0=ALU.is_gt, op1=ALU.mult)
                P.v("tensor_tensor", [ang], [ang], out=a, in0=a, in1=ang[:, 2, :], op=ALU.add)
                P.v("tensor_scalar", [ang], [ang], out=ang[:, 2, :], in0=a, scalar1=-float(np.pi), scalar2=TWO_PI, op0=ALU.is_lt, op1=ALU.mult)
                P.v("tensor_tensor", [ang], [ang], out=a, in0=a, in1=ang[:, 2, :], op=ALU.add)
            wrap(ang[:, 0, :])
            P.v("tensor_scalar", [ang], [ang], out=ang[:, 3, :], in0=ang[:, 0, :], scalar1=float(np.pi / 2), scalar2=None, op0=ALU.add)
            wrap(ang[:, 3, :])
            P.s("activation", [ang], [tr], out=tr[:, 0, :], in_=ang[:, 0, :], func=AF.Sin)
            P.s("activation", [ang], [tr], out=tr[:, 1, :], in_=ang[:, 3, :], func=AF.Sin)
            sin64, cos64, sin32, cos32 = tr[:, 0, 0:32], tr[:, 1, 0:32], tr[:, 0, 32:48], tr[:, 1, 32:48]
            yield

            ht, hT = rh.next(), rhT.next()
            load_transposed(P, K, h_in, hT, 8, hT.name, Rot([ht]), rpt, tiles=[i], t0=i)
            yield
            hk = [(hT.name, i), wi_]
            ps_q, ps_k = rps.next(), rpk.next()
            for kc in range(8):
                P.mm(hk, [ps_q], ps_q[:], hT[:, kc, :], wi_[:, kc, 0:512], kc == 0, kc == 7)
            for kc in range(8):
                P.mm(hk, [ps_k], ps_k[:, 0:392], hT[:, kc, :], wi_[:, kc, 512:904], kc == 0, kc == 7)
            kv = rkv.next()
            P.s("copy", [ps_k], [kv], out=kv[:, 0:392], in_=ps_k[:, 0:392])
            sm = rsm.next()
            cq, q1 = rcq.next(), rq1.next()
            rms_rstd(ps_q[:], 512, sm, 0, q1[:])
            P.v("scalar_tensor_tensor", [ps_q, sm, gq], [cq], out=cq[:], in0=ps_q[:], scalar=sm[:, 2:3], in1=gq[:], op0=ALU.mult, op1=ALU.mult)
            yield
            cqT = rcqT.next()
            pt = rpt.next()
            for jj in range(4):
                P.tr([cq], [pt], pt[:, jj, :], cq[:, jj * 128:(jj + 1) * 128], K["idf"])
            P.s("copy", [pt], [cqT], out=cqT[:], in_=pt[:])
            yield
            qn = rqn.next()
            for hq in range(2):
                ps = rps.next()
                for jj in range(4):
                    h = hq * 4 + jj
                    for kc in range(4):
                        P.mm([cqT, wqb], [ps], ps[:, jj * 128:(jj + 1) * 128], wqb[:, kc, h * 192:h * 192 + 128], cqT[:, kc, :], kc == 0, kc == 3)
                if hq == 0:
                    P.v("tensor_copy", [ps], [qn], out=qn[:, 0:4, :], in_=ps[:].rearrange("p (a b) -> p a b", b=128))
                else:
                    P.s("copy", [ps], [qn], out=qn[:, 4:8, :], in_=ps[:].rearrange("p (a b) -> p a b", b=128))
                yield
            qa = rqa.next()
            for hq in range(4):
                ps = rps.next()
                for jj in range(4):
                    n = hq * 4 + jj
                    h, cc = n // 2, n % 2
                    P.mm([qn, wukT], [ps], ps[:, jj * 128:(jj + 1) * 128], wukT[:, h, cc * 128:(cc + 1) * 128], qn[:, h, :])
                if hq % 2 == 0:
                    P.s("mul", [ps], [qa], out=qa[:, hq * 4:(hq + 1) * 4, :], in_=ps[:].rearrange("p (a b) -> p a b", b=128), mul=SC)
                else:
                    P.v("tensor_scalar", [ps], [qa], out=qa[:, hq * 4:(hq + 1) * 4, :], in0=ps[:].rearrange("p (a b) -> p a b", b=128),
                        scalar1=SC, scalar2=None, op0=ALU.mult)
                yield
            P.dma("gpsimd", [qa], [("qaT", i)], out=S["qaT"][:, :, cols].rearrange("n p t -> p n t"), in_=qa[:])
            sqa = rsq.next()
            P.g("tensor_tensor", [qa], [sqa], out=sqa[:], in0=qa[:], in1=qa[:], op=ALU.mult)
            pn = rpn.next()
            for h in range(8):
                for cc in range(2):
                    P.mm([sqa, onesb], [pn], pn[:, h:h + 1], sqa[:, 2 * h + cc, :], onesb[:], cc == 0, cc == 1)
            qn2 = rqn2.next()
            P.v("tensor_copy", [pn], [qn2], out=qn2[:, 0:8], in_=pn[:])
            yield
            ps = rps.next()
            for kc in range(4):
                P.mm([cqT, wqr], [ps], ps[:], cqT[:, kc, :], wqr[:, kc].rearrange("p h e -> p (h e)"), kc == 0, kc == 3)
            q2 = rq2.next()
            P.s("mul", [ps], [q1], out=q1[:], in_=ps[:], mul=SC)
            yield
            t1 = rcq.next()
            rope(q2[:].rearrange("p (h e) -> p h e", e=64), q1[:].rearrange("p (h e) -> p h e", e=64), cos64, sin64, 8, 32,
                 t1[:, 0:256].rearrange("p (h e) -> p h e", e=32), t1[:, 256:512].rearrange("p (h e) -> p h e", e=32))
            pt = rpt.next()
            for jj in range(4):
                P.tr([q2], [pt], pt[:, jj, :], q2[:, jj * 128:(jj + 1) * 128], K["idf"])
            qb = rqb.next()
            P.s("copy", [pt], [qb], out=qb[:], in_=pt[:])
            P.dma("gpsimd", [qb], [("qrT", i)], out=S["qrT"][:, :, cols].rearrange("n p t -> p n t"), in_=qb[:])
            yield
            P.g("tensor_tensor", [q2], [q1], out=q1[:], in0=q2[:], in1=q2[:], op=ALU.mult)
            P.v("reduce_sum", [q1], [qn2], out=qn2[:, 8:16], in_=q1[:].rearrange("p (h e) -> p h e", e=64), axis=AX.X)
            P.v("tensor_tensor", [qn2], [qn2], out=qn2[:, 16:24], in0=qn2[:, 0:8], in1=qn2[:, 8:16], op=ALU.add)
            P.dma("gpsimd", [qn2], [("qn2", i)], out=S["qn2"][rows, :], in_=qn2[:, 16:24])
            yield
            ps = rps.next()
            for kc in range(4):
                P.mm([cqT, wiq], [ps], ps[:], cqT[:, kc, :], wiq[:, kc, :], kc == 0, kc == 3)
            q1 = rq1.next()
            q2 = rq2.next()
            P.s("copy", [ps], [q1], out=q1[:], in_=ps[:])
            yield
            P.g("tensor_copy", [q1], [q2], out=q2[:], in_=q1[:])
            t1 = rcq.next()
            rope(q2[:].rearrange("p (h e) -> p h e", e=64)[:, :, 0:32], q1[:].rearrange("p (h e) -> p h e", e=64)[:, :, 0:32], cos32, sin32, 8, 16,
                 t1[:, 0:128].rearrange("p (h e) -> p h e", e=16), t1[:, 128:256].rearrange("p (h e) -> p h e", e=16))
            pt = rpt.next()
            for jj in range(4):
                P.tr([q2], [pt], pt[:, jj, :], q2[:, jj * 128:(jj + 1) * 128], K["idf"])
            qb = rqb.next()
            P.v("tensor_copy", [pt], [qb], out=qb[:], in_=pt[:])
            P.dma("gpsimd", [qb], [("qiT", i)], out=S["qiT"][:, :, cols].rearrange("n p t -> p n t"), in_=qb[:])
            yield
            rms_rstd(kv[:, 0:256], 256, sm, 4, kv[:, 400:656])
            P.v("scalar_tensor_tensor", [kv, sm, gkv], [kv], out=kv[:, 0:256], in0=kv[:, 0:256], scalar=sm[:, 6:7], in1=gkv[:], op0=ALU.mult, op1=ALU.mult)
            yield
            kvb = rkvb.next()
            P.g("tensor_copy", [kv], [kvb], out=kvb[:], in_=kv[:, 0:256])
            P.dma("gpsimd", [kvb], [("ckv", i)], out=S["ckv"][rows, :], in_=kvb[:])
            yield
            kk = rkk.next()
            rope(kk[:, 0:1, 0:64], kv[:, 256:320].unsqueeze(1), cos64, sin64, 1, 32, kv[:, 400:432].unsqueeze(1), kv[:, 432:464].unsqueeze(1))
            P.v("tensor_copy", [kk], [kk], out=kk[:, 0, 64:128], in_=kk[:, 0, 0:64])
            P.s("activation", [kv], [kv, sm], out=kv[:, 400:656], in_=kv[:, 0:256], func=AF.Square, accum_out=sm[:, 13:14])
            P.s("activation", [kk], [kv, sm], out=kv[:, 400:464], in_=kk[:, 0, 0:64], func=AF.Square, accum_out=sm[:, 14:15])
            P.v("tensor_tensor", [sm], [kn2], out=kn2[:, i:i + 1], in0=sm[:, 13:14], in1=sm[:, 14:15], op=ALU.add)
            yield
            ik = kv[:, 320:384]
            P.v("reduce_sum", [kv], [sm], out=sm[:, 8:9], in_=ik, axis=AX.X)
            P.v("tensor_scalar", [sm], [sm], out=sm[:, 9:10], in0=sm[:, 8:9], scalar1=-1.0 / 64, scalar2=None, op0=ALU.mult)
            P.s("activation", [kv, sm], [kv], out=kv[:, 464:528], in_=ik, func=AF.Identity, bias=sm[:, 9:10], scale=1.0)
            rms_rstd(kv[:, 464:528], 64, sm, 10, kv[:, 528:592])
            P.v("scalar_tensor_tensor", [kv, sm, ikg], [kv], out=kv[:, 464:528], in0=kv[:, 464:528], scalar=sm[:, 12:13], in1=ikg[:], op0=ALU.mult, op1=ALU.mult)
            P.v("tensor_tensor", [kv, ikb], [kv], out=kv[:, 464:528], in0=kv[:, 464:528], in1=ikb[:], op=ALU.add)
            P.v("tensor_copy", [kv], [kk], out=kk[:, 1, 32:64], in_=kv[:, 496:528])
            rope(kk[:, 1:2, 0:32], kv[:, 464:496].unsqueeze(1), cos32, sin32, 1, 16, kv[:, 592:608].unsqueeze(1), kv[:, 608:624].unsqueeze(1))
            P.v("tensor_copy", [kk], [kk], out=kk[:, 1, 64:128], in_=kk[:, 1, 0:64])
            yield
            wi = rwi.next()
            P.v("tensor_scalar", [kv], [wi], out=wi[:, 0:8], in0=kv[:, 384:392], scalar1=(8.0 ** -0.5) * (64.0 ** -0.5), scalar2=None, op0=ALU.mult)
            P.v("tensor_scalar", [wi], [wi], out=wi[:, 8:16], in0=wi[:, 0:8], scalar1=0.0, scalar2=2.0, op0=ALU.is_ge, op1=ALU.mult)
            P.v("tensor_scalar", [wi], [wi], out=wi[:, 8:16], in0=wi[:, 8:16], scalar1=-1.0, scalar2=None, op0=ALU.add)
            P.v("tensor_tensor", [wi], [wi], out=wi[:, 0:8], in0=wi[:, 0:8], in1=wi[:, 8:16], op=ALU.mult)
            P.dma("gpsimd", [wi], [("wi", i)], out=S["wi"][rows, :], in_=wi[:])
            yield
            pt = rpt.next()
            P.tr([kv], [pt], pt[:, 0, :], kv[:, 0:128], K["idf"])
            P.tr([kv], [pt], pt[:, 1, :], kv[:, 128:256], K["idf"])
            P.tr([kk], [pt], pt[:, 2, :], kk[:, 0, :], K["idf"])
            P.tr([kk], [pt], pt[:, 3, :], kk[:, 1, :], K["idf"])
            kkb = rqb.next()
            P.s("copy", [pt], [kkb], out=kkb[:], in_=pt[:])
            P.dma("gpsimd", [kkb], [("kT", i)], out=S["kT"][:, :, cols].rearrange("n p t -> p n t"), in_=kkb[:])
        run_interleaved(NT, body, 2)
        P.dma("sync", [kn2], ["kn2d"], out=S["kn2"], in_=kn2[:])


MASKV = -30000.0


class AttnRes:
    def __init__(self, C):
        self.rps = C.psrot("s_ps", [128, 512], F32, 3)
        self.rpT = C.psrot("pT_ps", [128, 4, 128], BF16, 2)
        self.re = C.sbrot("e", [128, 512], BF16, 4)
        self.rpt = C.sbrot("pT", [128, 4, 128], BF16, 4)
        self.rmx = C.sbrot("mx", [128, 16], F32, 4)
        self.rrs = C.sbrot("rs", [128, 16], F32, 4)
        self.cnt = 0


class Pipe:
    def __init__(self):
        self.hist = []
        self.filler = None
        self.rate = 1

    def push(self, stages):
        self.hist.insert(0, stages)
        self.hist = self.hist[:3]
        for lag, st in enumerate(self.hist):
            if lag < len(st) and st[lag] is not None:
                st[lag]()
        if self.filler is not None:
            for _ in range(self.rate):
                next(self.filler, None)

    def flush(self):
        self.push([])
        self.push([])


def attn_items(P, K, R, terms, mask_term, k0, k1, pv_fn, done_fn, negm_ap=None, negm_reads=()):
    chunks = []
    c = k0
    while c < k1:
        n = min(512, k1 - c)
        chunks.append((c, n))
        c += n
    nc_ = len(chunks)
    nkt = (k1 - k0) // 128
    mx, rs = R.rmx.next(), R.rrs.next()

    def scores(c0, n, tl):
        ps = R.rps.next()
        for ti, (lhsT, rhs_fn, rd) in enumerate(tl):
            P.mm(rd, [ps], ps[:, 0:n], lhsT, rhs_fn(c0, n), ti == 0, ti == len(tl) - 1)
        return ps

    items = []
    for ci, (c0, n) in enumerate(chunks if negm_ap is None else []):
        def A1(ci=ci, c0=c0, n=n):
            ps = scores(c0, n, terms)
            P.v("reduce_max", [ps], [mx], out=mx[:, ci:ci + 1], in_=ps[:, 0:n], axis=AX.X)
            if ci == nc_ - 1:
                if nc_ > 1:
                    P.v("reduce_max", [mx], [mx], out=mx[:, 15:16], in_=mx[:, 0:nc_], axis=AX.X)
                    P.v("tensor_scalar", [mx], [mx], out=mx[:, 14:15], in0=mx[:, 15:16], scalar1=-1.0, scalar2=None, op0=ALU.mult)
                else:
                    P.v("tensor_scalar", [mx], [mx], out=mx[:, 14:15], in0=mx[:, 0:1], scalar1=-1.0, scalar2=None, op0=ALU.mult)
        items.append([A1])
    tl2 = terms + ([mask_term] if mask_term is not None else [])
    kbase = [0]
    for ci, (c0, n) in enumerate(chunks):
        st = {}
        nk = n // 128

        def A2(ci=ci, c0=c0, n=n, st=st):
            ps = scores(c0, n, tl2)
            e = R.re.next()
            if negm_ap is None:
                P.s("activation", [ps, mx], [e, rs], out=e[:, 0:n], in_=ps[:, 0:n], func=AF.Exp, bias=mx[:, 14:15], scale=1.0, accum_out=rs[:, ci:ci + 1])
            else:
                P.s("activation", [ps] + list(negm_reads), [e, rs], out=e[:, 0:n], in_=ps[:, 0:n], func=AF.Exp, bias=negm_ap, scale=1.0, accum_out=rs[:, ci:ci + 1])
            st["e"] = e

        def B2(nk=nk, st=st):
            e = st["e"]
            pTp = R.rpT.next()
            for kk in range(nk):
                P.tr([e], [pTp], pTp[:, kk, :], e[:, kk * 128:(kk + 1) * 128], K["idb"])
            pT = R.rpt.next()
            R.cnt += 1
            if R.cnt % 2 == 0:
                P.s("copy", [pTp], [pT], out=pT[:, 0:nk, :], in_=pTp[:, 0:nk, :])
            else:
                P.v("tensor_copy", [pTp], [pT], out=pT[:, 0:nk, :], in_=pTp[:, 0:nk, :])
            st["pT"] = pT

        def C2(ci=ci, c0=c0, nk=nk, st=st):
            pT = st["pT"]
            for kk in range(nk):
                kt = (c0 - k0) // 128 + kk
                pv_fn(pT[:, kk, :], [pT], kt, kt == 0, kt == nkt - 1)
            if ci == nc_ - 1:
                if nc_ > 1:
                    P.v("reduce_sum", [rs], [rs], out=rs[:, 15:16], in_=rs[:, 0:nc_], axis=AX.X)
                    P.v("tensor_scalar", [rs], [rs], out=rs[:, 14:15], in0=rs[:, 15:16], scalar1=1e-30, scalar2=None, op0=ALU.add)
                else:
                    P.v("tensor_scalar", [rs], [rs], out=rs[:, 14:15], in0=rs[:, 0:1], scalar1=1e-30, scalar2=None, op0=ALU.add)
                P.v("reciprocal", [rs], [rs], out=rs[:, 13:14], in_=rs[:, 14:15])
                done_fn(rs, rs[:, 13:14])
        items.append([A2, B2, C2])
    return items


def phase_o3(nc, P, K, S, W, j):
    NB = 11
    with Ctx(nc, P) as C:
        kT = C.sb("kT", [128, 4, T], BF16)
        for n in range(4):
            P.dma("sync", [("kT", i) for i in range(NT)], [kT], out=kT[:, n, :], in_=S["kT"][n])
        ckv = C.sb("ckv", [128, NT, 256], BF16)
        P.dma("sync", [("ckv", i) for i in range(NT)], [ckv], out=ckv[:], in_=S["ckv"].rearrange("(c p) d -> p c d", p=128))
        wuv = C.sb("wuv", [128, 2, 1024], BF16)
        for cc in range(2):
            P.dma("gpsimd", [], [wuv], out=wuv[:, cc, :], in_=W["o_w_uv"][j][cc * 128:(cc + 1) * 128].rearrange("p h v -> p (h v)"))
        pw = C.sb("pw", [128, NB + 1], F32)
        for k in range(NB + 1):
            P.g("memset", [], [pw], pw[:, k:k + 1], 2.0 ** -(k + 1))
        negtri = C.sb("negtri", [128, 128], F32)
        P.v("tensor_scalar", [K["tri_ge"]], [negtri], out=negtri[:], in0=K["tri_ge"][:], scalar1=-1.0, scalar2=1e30, op0=ALU.add, op1=ALU.mult)
        negtri_b = C.sb("negtri_b", [128, 128], BF16)
        P.v("tensor_scalar", [K["tri_ge"]], [negtri_b], out=negtri_b[:], in0=K["tri_ge"][:], scalar1=-1.0, scalar2=-MASKV, op0=ALU.add, op1=ALU.mult)
        kn2 = C.sb("kn2", [128, NT], F32)
        P.dma("sync", ["kn2d"], [kn2], out=kn2[:], in_=S["kn2"])
        kmx = C.sb("kmx", [128, 8], F32)
        ones1 = C.sb("ones1", [1, 128], F32)
        P.g("memset", [], [ones1], ones1[:], 1.0)
        P.v("reduce_max", [kn2], [kmx], out=kmx[:, 0:1], in_=kn2[:], axis=AX.X)
        with Ctx(nc, P) as Ck:
            pk1 = Ck.ps("pk1", [128, 128], F32)
            P.tr([kmx], [pk1], pk1[0:1, :], kmx[:, 0:1], K["idf"])
            P.v("reduce_max", [pk1], [kmx], out=kmx[0:1, 1:2], in_=pk1[0:1, :], axis=AX.X)
            P.mm([ones1, kmx], [pk1], pk1[:, 0:1], ones1[:], kmx[0:1, 1:2])
            P.v("tensor_copy", [pk1], [kmx], out=kmx[:, 2:3], in_=pk1[:, 0:1])
        isc = C.sb("isc", [128, T], F32)
        negm = [C.sb("negm%d" % k, [128, T], BF16) for k in range(2)]
        qr8s = [C.sb("qr8_%d" % k, [128, 8, 128], BF16) for k in range(2)]
        qi8s = [C.sb("qi8_%d" % k, [128, 8, 128], BF16) for k in range(2)]
        for tq in qr8s + qi8s:
            P.g("memset", [], [tq], tq[:], 0.0)
        rstab = C.sbrot("stab", [128, 24], F32, 2)
        junk = C.sb("junk", [128, T], BF16)
        R = AttnRes(C)
        rolat = C.psrot("olat", [128, 2, 128], F32, 2)
        rout = C.psrot("outp", [128, 128], F32, 1)
        rqa = C.sbrot("qa", [128, 16, 128], BF16, 2)
        rqr = C.sbrot("qr", [128, 4, 128], BF16, 2)
        rqi = C.sbrot("qi", [128, 4, 128], BF16, 2)
        rwi = C.sbrot("wi", [128, 16], F32, 2)
        rrl = C.sbrot("rl", [128, 512], F32, 4)
        risc2 = C.sbrot("isc2", [128, 512], F32, 2)
        rtmp2 = C.sbrot("tmp2", [128, 512], F32, 2)
        rbs = C.sbrot("bs", [128, 32], F32, 2)
        rwh = C.sbrot("wh", [128, NB + 1], F32, 2)
        rol = C.sbrot("ol", [128, 2, 128], BF16, 2)
        rat = C.sbrot("at", [128, 1024], F32, 2)
        tile_in = {}

        def prep(i):
            cols = slice(i * 128, (i + 1) * 128)
            L = (i + 1) * 128
            qa, qr = rqa.next(), qr8s[i % 2]
            nm = negm[i % 2]
            stab = rstab.next()
            tile_in[i] = (qa, qr, nm, stab)
            P.dma("sync", [("qaT", i)], [qa], out=qa[:], in_=S["qaT"][:, :, cols].rearrange("n p t -> p n t"))
            for par in range(2):
                P.dma("sync", [("qrT", i)], [qr], out=qr[par * 64:(par + 1) * 64, par::2, :],
                      in_=S["qrT"][:, par * 64:(par + 1) * 64, cols].rearrange("n p t -> p n t"))
            P.dma("sync", [("qn2", i)], [stab], out=stab[:, 0:8], in_=S["qn2"][cols, :])
            P.v("tensor_scalar", [stab, kmx], [stab], out=stab[:, 0:8], in0=stab[:, 0:8], scalar1=kmx[:, 2:3], scalar2=1e-30, op0=ALU.mult, op1=ALU.add)
            P.s("activation", [stab], [stab], out=stab[:, 8:16], in_=stab[:, 0:8], func=AF.Ln)
            P.s("activation", [stab], [stab], out=stab[:, 8:16], in_=stab[:, 8:16], func=AF.Exp, scale=0.5)
            P.v("tensor_scalar", [stab], [stab], out=stab[:, 16:24], in0=stab[:, 8:16], scalar1=-1.02, scalar2=None, op0=ALU.mult)
            yield
            if i < 2:
                if i == 1:
                    P.g("memset", [], [nm], nm[:, 0:128], 0.0)
                P.g("tensor_copy", [negtri_b, nm], [nm], out=nm[:, L - 128:L], in_=negtri_b[:])
                return
            qi, wi = qi8s[i % 2], rwi.next()
            for par in range(2):
                P.dma("sync", [("qiT", i)], [qi], out=qi[par * 64:(par + 1) * 64, par::2, :],
                      in_=S["qiT"][:, par * 64:(par + 1) * 64, cols].rearrange("n p t -> p n t"))
            P.dma("sync", [("wi", i)], [wi], out=wi[:], in_=S["wi"][cols, :])
            c0 = 0
            while c0 < L:
                n = min(512, L - c0)
                for h in range(8):
                    po = (h % 2) * 64
                    ps = R.rps.next()
                    P.mm([qi, kT], [ps], ps[:, 0:n], qi[:, h, :], kT[:, 3, c0:c0 + n])
                    rl = rrl.next()
                    P.s("activation", [ps, wi], [rl], out=rl[:, 0:n], in_=ps[:, 0:n], func=AF.Relu, scale=wi[:, h:h + 1])
                    if h == 0:
                        P.v("tensor_scalar", [rl, wi], [("isc", c0)], out=isc[:, c0:c0 + n], in0=rl[:, 0:n], scalar1=wi[:, 8:9], scalar2=None, op0=ALU.mult)
                    elif h < 5:
                        P.v("scalar_tensor_tensor", [rl, wi, ("isc", c0)], [("isc", c0)], out=isc[:, c0:c0 + n], in0=rl[:, 0:n], scalar=wi[:, 8 + h:9 + h],
                            in1=isc[:, c0:c0 + n], op0=ALU.mult, op1=ALU.add)
                    elif h == 5:
                        i2 = risc2.next()
                        P.g("tensor_scalar", [rl, wi], [i2], out=i2[:, 0:n], in0=rl[:, 0:n], scalar1=wi[:, 8 + h:9 + h], scalar2=0.0, op0=ALU.mult, op1=ALU.add)
                    else:
                        tp = rtmp2.next()
                        P.g("tensor_scalar", [rl, wi], [tp], out=tp[:, 0:n], in0=rl[:, 0:n], scalar1=wi[:, 8 + h:9 + h], scalar2=0.0, op0=ALU.mult, op1=ALU.add)
                        P.g("tensor_tensor", [tp, i2], [i2], out=i2[:, 0:n], in0=i2[:, 0:n], in1=tp[:, 0:n], op=ALU.add)
                    yield
                P.g("tensor_tensor", [i2, ("isc", c0)], [("isc", c0)], out=isc[:, c0:c0 + n], in0=isc[:, c0:c0 + n], in1=i2[:, 0:n], op=ALU.add)
                c0 += n
            ik = [("isc", c) for c in range(0, L, 512)]
            P.g("tensor_tensor", ik + [K["tri_ge"]], ik, out=isc[:, L - 128:L], in0=isc[:, L - 128:L], in1=K["tri_ge"][:], op=ALU.mult)
            P.g("tensor_tensor", ik + [negtri], ik, out=isc[:, L - 128:L], in0=isc[:, L - 128:L], in1=negtri[:], op=ALU.add)
            bs, wh = rbs.next(), rwh.next()
            pcs = [(c, min(1024, L - c)) for c in range(0, L, 1024)]
            npc = len(pcs)
            for pi, (c, n) in enumerate(pcs):
                n2 = min(n, L - 128 - c)
                if n2 > 0:
                    P.v("tensor_reduce", ik, [bs], out=bs[:, 8 + pi:9 + pi], in_=isc[:, c:c + n2], axis=AX.X, op=ALU.min)
                else:
                    P.v("tensor_copy", [bs], [bs], out=bs[:, 8 + pi:9 + pi], in_=bs[:, 8:9])
                P.v("reduce_max", ik, [bs], out=bs[:, 12 + pi:13 + pi], in_=isc[:, c:c + n], axis=AX.X)
                yield
            P.v("tensor_reduce", [bs], [bs], out=bs[:, 0:1], in_=bs[:, 8:8 + npc], axis=AX.X, op=ALU.min)
            P.v("reduce_max", [bs], [bs], out=bs[:, 1:2], in_=bs[:, 12:12 + npc], axis=AX.X)
            P.v("tensor_tensor", [bs], [bs], out=bs[:, 2:3], in0=bs[:, 1:2], in1=bs[:, 0:1], op=ALU.subtract)
            P.v("tensor_scalar", [pw, bs], [wh], out=wh[:], in0=pw[:], scalar1=bs[:, 2:3], scalar2=None, op0=ALU.mult)
            P.v("tensor_tensor", [bs, wh], [bs], out=bs[:, 3:4], in0=bs[:, 0:1], in1=wh[:, 0:1], op=ALU.add)
            yield
            for k in range(NB):
                for pi, (c, n) in enumerate(pcs):
                    P.v("tensor_scalar", ik + [bs], [junk, bs], out=junk[:, c:c + n], in0=isc[:, c:c + n], scalar1=bs[:, 3:4], scalar2=None,
                        op0=ALU.is_ge, op1=ALU.add, accum_out=bs[:, 8 + pi:9 + pi])
                    if pi < npc - 1:
                        yield
                if npc > 1:
                    P.v("reduce_sum", [bs], [bs], out=bs[:, 4:5], in_=bs[:, 8:8 + npc], axis=AX.X)
                    cnt = bs[:, 4:5]
                else:
                    cnt = bs[:, 8:9]
                P.v("tensor_scalar", [bs], [bs], out=bs[:, 5:6], in0=cnt, scalar1=256.0, scalar2=-0.5, op0=ALU.is_ge, op1=ALU.add)
                P.v("scalar_tensor_tensor", [bs, wh], [bs], out=bs[:, 3:4], in0=bs[:, 5:6], scalar=wh[:, k:k + 1], in1=bs[:, 3:4], op0=ALU.mult, op1=ALU.add)
                yield
            P.v("tensor_tensor", [bs, wh], [bs], out=bs[:, 6:7], in0=bs[:, 3:4], in1=wh[:, NB:NB + 1], op=ALU.subtract)
            for pi, (c, n) in enumerate(pcs):
                P.v("tensor_scalar", ik + [bs], [nm], out=nm[:, c:c + n], in0=isc[:, c:c + n], scalar1=bs[:, 6:7], scalar2=MASKV, op0=ALU.is_lt, op1=ALU.mult)
                yield

        pipe = Pipe()
        for _ in prep(0):
            pass
        for i in range(NT):
            rows = slice(i * 128, (i + 1) * 128)
            L = (i + 1) * 128
            nxt = prep(i + 1) if i + 1 < NT else None
            pipe.filler = nxt
            nch_i, nch_n = (L + 511) // 512, (L + 128 + 511) // 512
            npc_n = (L + 128 + 1023) // 1024
            pipe.rate = -(-(8 * nch_n + (NB + 2) * npc_n + 8) // (8 * nch_i))
            qa, qr, nm, stab = tile_in[i]
            at = rat.next()
            for h in range(8):
                po = (h % 2) * 64
                terms = [
                    (qa[:, 2 * h, :], lambda c0, n: kT[:, 0, c0:c0 + n], [qa, kT]),
                    (qa[:, 2 * h + 1, :], lambda c0, n: kT[:, 1, c0:c0 + n], [qa, kT]),
                    (qr[:, h, :], lambda c0, n: kT[:, 2, c0:c0 + n], [qr, kT]),
                ]
                mterm = (K["idb"][:], lambda c0, n, nm=nm: nm[:, c0:c0 + n], [nm, K["idb"]])
                olat = rolat.next()

                def pv(pT, rd, kt, first, last, olat=olat):
                    for cc in range(2):
                        P.mm(rd + [ckv], [olat], olat[:, cc, :], ckv[:, kt, cc * 128:(cc + 1) * 128], pT, first, last)

                def done(rs, rinv, olat=olat, h=h, at=at, rows=rows):
                    ol = rol.next()
                    P.s("copy", [olat], [ol], out=ol[:], in_=olat[:])
                    po_ = rout.next()
                    for cc in range(2):
                        P.mm([ol, wuv], [po_], po_[:], ol[:, cc, :], wuv[:, cc, h * 128:(h + 1) * 128], cc == 0, cc == 1)
                    P.v("tensor_scalar", [po_, rs], [at], out=at[:, h * 128:(h + 1) * 128], in0=po_[:], scalar1=rinv, scalar2=None, op0=ALU.mult)
                    if h == 7:
                        P.dma("gpsimd", [at], [("attn", rows.start)], out=S["attn"][rows, :], in_=at[:])
                for it in attn_items(P, K, R, terms, mterm, 0, L, pv, done, negm_ap=stab[:, 16 + h:17 + h], negm_reads=[stab]):
                    pipe.push(it)
            if nxt is not None:
                for _ in nxt:
                    pass
        pipe.flush()


def phase_e3(nc, P, K, S, W, j):
    with Ctx(nc, P) as C:
        kcmpT = C.sb("kcmpT", [128, 256], BF16)
        vcmp = C.sb("vcmp", [128, 2, 2, 64], BF16)
        P.g("memset", [], [kcmpT], kcmpT[:], 0.0)
        P.g("memset", [], [vcmp], vcmp[:], 0.0)
        with Ctx(nc, P) as C1:
            uT = C1.sb("uT", [128, 2, T], BF16)
            for n in range(2):
                P.dma("sync", [("bkT", n, tg) for tg in range(8)], [uT], out=uT[:, n, :], in_=S["bkT"][n])
            w1 = C1.sb("w1", [128, 2, 32, 128], BF16)
            for kv in range(2):
                for half in range(2):
                    P.dma("gpsimd", [], [w1], out=w1[half * 64:(half + 1) * 64, kv, :, :],
                          in_=W["e_b_cmp_w1"][j][kv].rearrange("(jj d) n -> d jj n", d=64))
            w2f = C1.sb("w2f", [128, 2, 64], F32)
            P.dma("sync", [], [w2f], out=w2f[:], in_=W["e_b_cmp_w2"][j].rearrange("kv n d -> n kv d"))
            w2p = C1.sb("w2p", [128, 2, 128], BF16)
            w2v = C1.sb("w2v", [128, 64], BF16)
            P.g("memset", [], [w2p], w2p[:], 0.0)
            for g in range(2):
                P.v("tensor_copy", [w2f, w2p], [w2p], out=w2p[:, g, g * 64:(g + 1) * 64], in_=w2f[:, 0, :])
            P.v("tensor_copy", [w2f], [w2v], out=w2v[:], in_=w2f[:, 1, :])
            posT = C1.sb("posT", [64, 2, 32], BF16)
            for kv in range(2):
                P.dma("gpsimd", [], [posT], out=posT[:, kv, :], in_=W["e_b_cmp_pos"][j][kv].rearrange("jj d -> d jj"), allow_slow_non_contiguous=True)
            rph = C1.psrot("ph", [128, 256], F32, 2)
            rpb = C1.psrot("pb", [128, 8], F32, 1)
            rpo = C1.psrot("pko", [128, 256], F32, 1)
            rpv = C1.psrot("pvo", [128, 64], F32, 1)
            bias = C1.sb("bias", [128, 2], F32)
            x = C1.sb("x", [128, 256], F32)
            x2 = C1.sb("x2", [128, 256], F32)
            sg = C1.sb("sg", [128, 256], F32)
            gl = [[C1.sb("gl%d%d" % (kv, g), [128, 256], BF16) for g in range(2)] for kv in range(2)]
            pko = rpo.next()
            for kv in range(2):
                pb = rpb.next()
                for jj in range(32):
                    P.mm([w1, posT], [pb], pb[:, 0:1], w1[0:64, kv, jj, :], posT[:, kv, jj:jj + 1], jj == 0, jj == 31)
                P.v("tensor_copy", [pb], [bias], out=bias[:, kv:kv + 1], in_=pb[:, 0:1])
                for g in range(2):
                    ph = rph.next()
                    for jj in range(32):
                        P.mm([w1, uT], [ph], ph[:, 0:255], w1[g * 64:(g + 1) * 64, kv, jj, :], uT[g * 64:(g + 1) * 64, kv, jj:jj + 16 * 254 + 1:16], jj == 0, jj == 31)
                    P.s("activation", [ph, bias], [x], out=x[:, 0:255], in_=ph[:, 0:255], func=AF.Identity, bias=bias[:, kv:kv + 1], scale=1.0)
                    P.v("tensor_tensor", [x], [x2], out=x2[:, 0:255], in0=x[:, 0:255], in1=x[:, 0:255], op=ALU.mult)
                    P.v("tensor_scalar", [x2], [x2], out=x2[:, 0:255], in0=x2[:, 0:255], scalar1=0.044715, scalar2=1.0, op0=ALU.mult, op1=ALU.add)
                    P.v("tensor_tensor", [x2, x], [x2], out=x2[:, 0:255], in0=x2[:, 0:255], in1=x[:, 0:255], op=ALU.mult)
                    P.s("activation", [x2], [sg], out=sg[:, 0:255], in_=x2[:, 0:255], func=AF.Sigmoid, scale=1.5957691216057308)
                    G = gl[kv][g]
                    P.g("memset", [], [G], G[:], 0.0)
                    P.v("tensor_tensor", [x, sg, G], [G], out=G[:, 0:255], in0=x[:, 0:255], in1=sg[:, 0:255], op=ALU.mult)
                    if kv == 0:
                        P.mm([G, w2p], [pko], pko[:, 0:255], w2p[:, g, :], G[:, 0:255], g == 0, g == 1)
                    else:
                        for mc in range(2):
                            nm = 128 if mc == 0 else 127
                            pvo = rpv.next()
                            P.mm([G, w2v], [pvo], pvo[0:nm, :], G[:, mc * 128:mc * 128 + nm], w2v[:])
                            P.v("tensor_copy", [pvo, vcmp], [vcmp], out=vcmp[0:nm, mc, g, :], in_=pvo[0:nm, :])
                if kv == 0:
                    P.v("tensor_copy", [pko, kcmpT], [kcmpT], out=kcmpT[:, 0:255], in_=pko[:, 0:255])

        qT = C.sb("qT", [128, 8, T], BF16)
        for g in range(2):
            P.g("memset", [], [qT], qT[(1 - g) * 64:(2 - g) * 64, g * 4:(g + 1) * 4, :], 0.0)
        for c in range(4):
            for g in range(2):
                P.dma("sync", [("bqT", c, tg) for tg in range(8)] + [qT], [qT], out=qT[g * 64:(g + 1) * 64, g * 4 + c, :],
                      in_=S["bqT"][c][g * 64:(g + 1) * 64, :])
        nall = C.sb("nall", [128, NT, 4, 2], F32)
        kall = C.sb("kall", [128, 2, NT, 2], F32)
        P.dma("sync", ["nalld"], [nall], out=nall[:].rearrange("p a b c -> p (a b c)"), in_=S["nall"])
        P.dma("sync", ["kalld"], [kall], out=kall[:].rearrange("p a b c -> p (a b c)"), in_=S["kall"])
        kmx = C.sb("kmx", [128, 16], F32)
        ones1 = C.sb("ones1", [1, 128], F32)
        P.g("memset", [], [ones1], ones1[:], 1.0)
        for br in range(2):
            P.v("tensor_reduce", [kall], [kmx], out=kmx[:, 2 * br:2 * br + 2], in_=kall[:, br].rearrange("p t g -> p g t"), axis=AX.X, op=ALU.max)
        with Ctx(nc, P) as Ck:
            pk1 = Ck.ps("pk1", [128, 512], F32)
            for jx in range(4):
                P.tr([kmx], [pk1], pk1[0:1, jx * 128:(jx + 1) * 128], kmx[:, jx:jx + 1], K["idf"])
            P.v("reduce_max", [pk1], [kmx], out=kmx[0:1, 4:8], in_=pk1[0:1, :].rearrange("p (j x) -> p j x", x=128), axis=AX.X)
            P.mm([ones1, kmx], [pk1], pk1[:, 0:4], ones1[:], kmx[0:1, 4:8])
            P.v("tensor_copy", [pk1], [kmx], out=kmx[:, 8:12], in_=pk1[:, 0:4])
        rstab = C.sbrot("stab", [128, 2, 4, 2], F32, 3)
        tile_stab = {}
        ksT = C.sb("ksT", [128, T], BF16)
        kwT = C.sb("kwT", [128, T], BF16)
        P.dma("sync", [("bkT", 2, tg) for tg in range(8)], [ksT], out=ksT[:], in_=S["bkT"][2])
        P.dma("sync", [("bkT", 3, tg) for tg in range(8)], [kwT], out=kwT[:], in_=S["bkT"][3])
        vsw = C.sb("vsw", [128, NT, 256], BF16)
        P.dma("sync", [("bv_tok", i) for i in range(NT)], [vsw], out=vsw[:], in_=S["bv_tok"].rearrange("(c p) d -> p c d", p=128))
        ntri_ge = C.sb("ntri_ge", [128, 128], BF16)
        ntri_lt = C.sb("ntri_lt", [128, 128], BF16)
        P.v("tensor_scalar", [K["tri_ge"]], [ntri_ge], out=ntri_ge[:], in0=K["tri_ge"][:], scalar1=-1.0, scalar2=-MASKV, op0=ALU.add, op1=ALU.mult)
        P.v("tensor_scalar", [K["tri_lt"]], [ntri_lt], out=ntri_lt[:], in0=K["tri_lt"][:], scalar1=-1.0, scalar2=-MASKV, op0=ALU.add, op1=ALU.mult)
        negw = C.sb("negw", [128, 640], BF16)
        P.g("memset", [], [negw], negw[:], 0.0)
        P.v("tensor_copy", [ntri_lt, negw], [negw], out=negw[:, 0:128], in_=ntri_lt[:])
        P.v("tensor_copy", [ntri_ge, negw], [negw], out=negw[:, 512:640], in_=ntri_ge[:])
        dltci = C.sb("dltci", [128, 256], I32)
        dltc = C.sb("dltc", [128, 256], F32)
        P.g("iota", [], [dltci], dltci[:], pattern=[[-16, 256]], base=0, channel_multiplier=1)
        P.v("tensor_copy", [dltci], [dltc], out=dltc[:], in_=dltci[:])
        dlti = C.sb("dlti", [128, 64], I32)
        dlt = C.sb("dlt", [128, 64], F32)
        P.g("iota", [], [dlti], dlti[:], pattern=[[-64, 64]], base=0, channel_multiplier=1)
        P.v("tensor_copy", [dlti], [dlt], out=dlt[:], in_=dlti[:])
        negm = [[C.sb("negm%d%d" % (par, g), [128, T], BF16) for g in range(2)] for par in range(2)]
        ycmp = C.sb("ycmp", [128, 2, 8, 64], F32)
        R = AttnRes(C)
        rpo = C.psrot("po", [128, 2, 64], F32, 2)
        rpc = C.psrot("pc", [128, 64], F32, 1)
        rgt = C.sbrot("gt", [128, 24], F32, 3)
        rselc = C.sbrot("selc", [128, 256], F32, 2)
        ryb = C.sbrot("yb", [128, 512], F32, 2)
        rpg = C.sbrot("pg", [128, 256], F32, 2)
        re32 = C.sbrot("e32", [128, 256], F32, 2)
        rp16 = C.sbrot("p16", [128, 256], BF16, 2)
        rsc = C.sbrot("sc", [128, 4, 64], F32, 2)
        rm8 = C.sbrot("m8", [128, 16], F32, 2)
        rcf = C.sbrot("cf", [128, 8], F32, 6)
        rcmx = C.sbrot("cmx", [128, 8], F32, 3)
        tile_gt = {}

        def prep(i, g):
            rows = slice(i * 128, (i + 1) * 128)
            t0 = i * 128
            L = (i + 1) * 128
            if g == 0:
                gt = rgt.next()
                P.dma("sync", [("bgate", i)], [gt], out=gt[:], in_=S["bgate"][rows, :])
                selc = rselc.next()
                P.v("tensor_scalar", [dltc], [selc], out=selc[:], in0=dltc[:], scalar1=float(31 - t0), scalar2=None, op0=ALU.is_ge)
                tile_gt[i] = (gt, selc)
                stab = rstab.next()
                tile_stab[i] = stab
                for br in range(2):
                    P.v("tensor_tensor", [nall, kmx], [stab], out=stab[:, br], in0=nall[:, i],
                        in1=kmx[:, 8 + 2 * br:10 + 2 * br].unsqueeze(1).to_broadcast([128, 4, 2]), op=ALU.mult)
                P.v("tensor_scalar", [stab], [stab], out=stab[:], in0=stab[:], scalar1=1e-30, scalar2=None, op0=ALU.add)
                P.s("activation", [stab], [stab], out=stab[:], in_=stab[:], func=AF.Ln)
                P.s("activation", [stab], [stab], out=stab[:], in_=stab[:], func=AF.Exp, scale=0.5)
                P.v("tensor_scalar", [stab], [stab], out=stab[:], in0=stab[:], scalar1=-1.02, scalar2=None, op0=ALU.mult)
            gt, selc = tile_gt[i]
            pg = rpg.next()
            for hp in range(4):
                h = g * 4 + hp
                qh = qT[:, h, t0:t0 + 128]
                ps = R.rps.next()
                P.mm([qT, kcmpT], [ps], ps[:, 0:256], qh, kcmpT[:, :])
                mx = rcmx.next()
                P.v("reduce_max", [ps], [mx], out=mx[:, 0:1], in_=ps[:, 0:256], axis=AX.X)
                P.v("tensor_scalar", [mx], [mx], out=mx[:, 1:2], in0=mx[:, 0:1], scalar1=-1.0, scalar2=None, op0=ALU.mult)
                e32 = re32.next()
                P.s("activation", [ps, mx], [e32], out=e32[:], in_=ps[:, 0:256], func=AF.Exp, bias=mx[:, 1:2], scale=1.0)
                P.v("scalar_tensor_tensor", [e32, selc], [e32, mx], out=e32[:], in0=e32[:], scalar=1.0, in1=selc[:], op0=ALU.mult, op1=ALU.mult,
                    accum_out=mx[:, 2:3])
                P.v("tensor_scalar", [mx], [mx], out=mx[:, 3:4], in0=mx[:, 2:3], scalar1=1e-30, scalar2=None, op0=ALU.add)
                P.v("reciprocal", [mx], [mx], out=mx[:, 4:5], in_=mx[:, 3:4])
                if hp == 0:
                    P.v("tensor_scalar", [e32, mx], [pg], out=pg[:], in0=e32[:], scalar1=mx[:, 4:5], scalar2=None, op0=ALU.mult)
                else:
                    P.v("scalar_tensor_tensor", [e32, mx, pg], [pg], out=pg[:], in0=e32[:], scalar=mx[:, 4:5], in1=pg[:], op0=ALU.mult, op1=ALU.add)
                p16 = rp16.next()
                P.g("tensor_copy", [e32], [p16], out=p16[:], in_=e32[:])
                yield
                pTp = R.rpT.next()
                for mc in range(2):
                    P.tr([p16], [pTp], pTp[:, mc, :], p16[:, mc * 128:(mc + 1) * 128], K["idb"])
                pT = R.rpt.next()
                P.s("copy", [pTp], [pT], out=pT[:, 0:2, :], in_=pTp[:, 0:2, :])
                yield
                pc = rpc.next()
                for mc in range(2):
                    P.mm([pT, vcmp], [pc], pc[:], pT[:, mc, :], vcmp[:, mc, g, :], mc == 0, mc == 1)
                P.v("tensor_tensor", [mx, gt], [mx], out=mx[:, 5:6], in0=mx[:, 4:5], in1=gt[:, h * 3:h * 3 + 1], op=ALU.mult)
                P.v("tensor_scalar", [pc, mx], [("ycmp", i % 2, h)], out=ycmp[:, i % 2, h, :], in0=pc[:], scalar1=mx[:, 5:6], scalar2=None, op0=ALU.mult)
                yield
            nm = negm[i % 2][g]
            if i >= 8:
                sc = rsc.next()
                m8 = rm8.next()
                imp, s1_, s2_, bm = sc[:, 0, :], sc[:, 1, :], sc[:, 2, :], sc[:, 3, :]
                P.v("reduce_sum", [pg], [sc], out=imp, in_=pg[:].rearrange("p (b f) -> p b f", f=4), axis=AX.X)
                P.v("tensor_tensor", [sc, pg], [sc], out=sc[:, 0, 1:64], in0=sc[:, 0, 1:64], in1=pg[:, 3:255:4], op=ALU.add)
                P.v("tensor_scalar", [dlt], [sc], out=s2_, in0=dlt[:], scalar1=float(128 - t0), scalar2=1e6, op0=ALU.is_lt, op1=ALU.mult)
                P.v("tensor_tensor", [sc], [sc], out=s1_, in0=imp, in1=s2_, op=ALU.max)
                P.v("tensor_scalar", [dlt], [sc], out=s2_, in0=dlt[:], scalar1=float(-t0), scalar2=None, op0=ALU.is_ge)
                P.v("tensor_tensor", [sc], [sc], out=s1_, in0=s1_, in1=s2_, op=ALU.mult)
                P.v("tensor_scalar", [sc], [sc], out=s2_, in0=s2_, scalar1=-1.0, scalar2=1e30, op0=ALU.add, op1=ALU.mult)
                P.v("tensor_tensor", [sc], [sc], out=s1_, in0=s1_, in1=s2_, op=ALU.add)
                P.g("memset", [sc], [sc], sc[:, 1, 0:1], 1e6)
                yield
                P.v("max", [sc], [m8], out=m8[:, 0:8], in_=s1_)
                P.v("match_replace", [sc, m8], [sc], out=s2_, in_to_replace=m8[:, 0:8], in_values=s1_, imm_value=NEG)
                P.v("max", [sc], [m8], out=m8[:, 8:16], in_=s2_)
                P.v("tensor_scalar", [sc, m8], [sc], out=bm, in0=s1_, scalar1=m8[:, 15:16], scalar2=MASKV, op0=ALU.is_lt, op1=ALU.mult)
                nb = L // 64
                P.g("tensor_copy", [sc, nm], [nm], out=nm[:, 0:L].rearrange("p (b f) -> p b f", f=64),
                    in_=sc[:, 3, 0:nb].unsqueeze(2).to_broadcast([128, nb, 64]))
                P.g("tensor_tensor", [nm, ntri_ge], [nm], out=nm[:, L - 128:L], in0=nm[:, L - 128:L], in1=ntri_ge[:], op=ALU.add)
            else:
                if i > 0:
                    P.g("memset", [], [nm], nm[:, 0:L - 128], 0.0)
                P.g("tensor_copy", [ntri_ge, nm], [nm], out=nm[:, L - 128:L], in_=ntri_ge[:])

        pipe = Pipe()
        work = [(i, g) for i in range(NT) for g in range(2)]
        for _ in prep(0, 0):
            pass
        ybs = {}
        for wi_, (i, g) in enumerate(work):
            rows = slice(i * 128, (i + 1) * 128)
            t0 = i * 128
            L = (i + 1) * 128
            nxt = prep(*work[wi_ + 1]) if wi_ + 1 < len(work) else None
            pipe.filler = nxt
            npush = 4 * ((L + 511) // 512 + (min(L, 640) + 511) // 512)
            pipe.rate = -(-16 // npush)
            if g == 0:
                ybs[i] = ryb.next()
            yb = ybs[i]
            gt, _selc = tile_gt[i]
            nm = negm[i % 2][g]
            for hp in range(4):
                h = g * 4 + hp
                qh = qT[:, h, t0:t0 + 128]
                stab = tile_stab[i]
                po = rpo.next()
                cf = rcf.next()
                terms = [(qh, lambda c0, n: ksT[:, c0:c0 + n], [qT, ksT])]
                mterm = (K["idb"][:], lambda c0, n, nm=nm: nm[:, c0:c0 + n], [nm, K["idb"]])

                def pv_s(pT, rd, kt, first, last, po=po, g=g):
                    P.mm(rd + [vsw], [po], po[:, 0, :], pT, vsw[:, kt, g * 64:(g + 1) * 64], first, last)

                def done_s(rs, rinv, cf=cf, gt=gt, h=h):
                    P.v("tensor_tensor", [rs, gt], [cf], out=cf[:, 1:2], in0=rinv, in1=gt[:, h * 3 + 1:h * 3 + 2], op=ALU.mult)
                for it in attn_items(P, K, R, terms, mterm, 0, L, pv_s, done_s, negm_ap=stab[:, 0, hp, g:g + 1], negm_reads=[stab]):
                    pipe.push(it)
                k0 = max(0, (i - 4) * 128)
                woff = 640 - (L - k0)
                terms = [(qh, lambda c0, n: kwT[:, c0:c0 + n], [qT, kwT])]
                mterm = (K["idb"][:], lambda c0, n, k0=k0, woff=woff: negw[:, woff + c0 - k0:woff + c0 - k0 + n], [negw, K["idb"]])

                def pv_w(pT, rd, kt, first, last, po=po, g=g, k0=k0):
                    P.mm(rd + [vsw], [po], po[:, 1, :], pT, vsw[:, k0 // 128 + kt, 128 + g * 64:128 + (g + 1) * 64], first, last)

                def done_w(rs, rinv, cf=cf, gt=gt, h=h, po=po, yb=yb, i=i, rows=rows):
                    P.v("tensor_tensor", [rs, gt], [cf], out=cf[:, 2:3], in0=rinv, in1=gt[:, h * 3 + 2:h * 3 + 3], op=ALU.mult)
                    ys = yb[:, h * 64:(h + 1) * 64]
                    P.v("scalar_tensor_tensor", [po, cf, ("ycmp", i % 2, h)], [yb], out=ys, in0=po[:, 0, :], scalar=cf[:, 1:2], in1=ycmp[:, i % 2, h, :],
                        op0=ALU.mult, op1=ALU.add)
                    P.v("scalar_tensor_tensor", [po, cf, yb], [yb], out=ys, in0=po[:, 1, :], scalar=cf[:, 2:3], in1=ys, op0=ALU.mult, op1=ALU.add)
                    if h == 7:
                        P.dma("gpsimd", [yb], [("attn", "b", i)], out=S["attn"][rows, 512:1024], in_=yb[:])
                for it in attn_items(P, K, R, terms, mterm, k0, L, pv_w, done_w, negm_ap=stab[:, 1, hp, g:g + 1], negm_reads=[stab]):
                    pipe.push(it)
            if nxt is not None:
                for _ in nxt:
                    pass
        pipe.flush()


W_SPECS = dict(
    x=[T, D], p=[DEPTH, T, 256],
    e_w_in=[2, 1024, 3360], e_a_conv=[2, 4, 1024], e_a_i_b=[2, 4], e_a_f_b=[2, 4], e_a_norm=[2, 512],
    e_b_cmp_pos=[2, 2, 32, 64], e_b_cmp_w1=[2, 2, 2048, 128], e_b_cmp_w2=[2, 2, 128, 64], e_b_g_b=[2, 24],
    e_w_out=[2, 1024, 1024], o_w_in=[2, 1024, 904], o_q_norm=[2, 512], o_kv_norm=[2, 256], o_w_qb=[2, 512, 1536],
    o_w_uk=[2, 256, 8, 128], o_w_uv=[2, 256, 8, 128], o_w_iq=[2, 512, 512], o_ik_g=[2, 64], o_ik_b=[2, 64],
    o_w_out=[2, 1024, 1024], ln1_g=[4, 1024], ln1_b=[4, 1024], ln2_g=[4, 1024], ln2_b=[4, 1024],
    mlp_w1=[4, 1024, 4096], mlp_w2=[4, 4096, 1024], ple_gate_w=[4, 1024, 1024], ple_w=[4, 256, 1024], rope_inv=[48])


def rope_inv_table():
    a = (10000.0 ** (-np.arange(0, 64, 2, dtype=np.float32) / np.float32(64))).astype(np.float32)
    b = (10000.0 ** (-np.arange(0, 32, 2, dtype=np.float32) / np.float32(32))).astype(np.float32)
    return np.concatenate([a, b]).astype(np.float32)


def build(debug_outs=(), phases=None, layers=(0, 1, 2, 3)):
    nc = bass.Bass("TRN2", target_bir_lowering=False)
    dbg = set(debug_outs)
    allp = phases is None

    def on(p):
        return allp or p in phases

    def dram(name, shape, dt, kind=None):
        if kind is None:
            kind = "ExternalOutput" if name in dbg else "Internal"
        return nc.dram_tensor(name, shape, dt, kind=kind).ap()

    W = {k: dram(k, s, F32, "ExternalInput") for k, s in W_SPECS.items()}
    W["positions"] = dram("positions", [T], I32, "ExternalInput")
    out = dram("out", [T, D], F32, "ExternalOutput")
    S = dict(
        v_tok=dram("v_tok", [T, 512], BF16), sigo=dram("sigo", [T, 512], F32), bv_tok=dram("bv_tok", [T, 256], BF16),
        bgate=dram("bgate", [T, 24], F32), gsc=dram("gsc", [3, 4, T], F32), qkT=dram("qkT", [8, 128, T], BF16),
        bqT=dram("bqT", [4, 128, T], BF16), bkT=dram("bkT", [4, 128, T], BF16), attn=dram("attn", [T, 1024], F32),
        hA=dram("hA", [T, D], F32), h1=dram("h1", [T, D], F32), h2=dram("h2", [T, D], F32),
        qaT=dram("qaT", [16, 128, T], BF16), qrT=dram("qrT", [4, 128, T], BF16), qiT=dram("qiT", [4, 128, T], BF16),
        kT=dram("kT", [4, 128, T], BF16), ckv=dram("ckv", [T, 256], BF16), wi=dram("wi", [T, 16], F32),
        hB=dram("hB", [T, D], F32), qn2=dram("qn2", [T, 8], F32), kn2=dram("kn2", [128, NT], F32),
        nall=dram("nall", [128, NT * 8], F32), kall=dram("kall", [128, 4 * NT], F32),
    )
    with ExitStack() as st:
        P = Prog(nc)
        P.setup(st)
        with Ctx(nc, P) as C0:
            K = make_consts(C0, P)
            h_in = W["x"]
            wrote_out = False
            for n, li in enumerate(layers):
                j = li // 2
                last = (n == len(layers) - 1)
                if li % 2 == 0:
                    if on("e1"):
                        phase_e1(nc, P, K, S, W, j, h_in)
                    if on("e2"):
                        phase_e2(nc, P, K, S, W, j)
                    if on("e3"):
                        phase_e3(nc, P, K, S, W, j)
                    w_out = W["e_w_out"][j]
                else:
                    if on("o1"):
                        phase_o1(nc, P, K, S, W, j, h_in)
                    if on("o3"):
                        phase_o3(nc, P, K, S, W, j)
                    w_out = W["o_w_out"][j]
                if on("ta"):
                    phase_tail_a(nc, P, K, S, w_out, W["ln1_g"][li], W["ln1_b"][li], h_in, S["h1"])
                if on("tb"):
                    phase_tail_b(nc, P, K, S, W["mlp_w1"][li], W["mlp_w2"][li], W["ln2_g"][li], W["ln2_b"][li], S["h1"], S["h2"])
                if on("tc"):
                    h_out = out if last else (S["hA"] if n % 2 == 0 else S["hB"])
                    phase_tail_c(nc, P, K, S, W["ple_gate_w"][li], W["ple_w"][li], W["p"][li], S["h2"], h_out, last)
                    wrote_out = wrote_out or last
                    h_in = h_out
            if not wrote_out:
                zt = C0.sb("zt", [128, 1024], F32)
                P.g("memset", [], [zt], zt[:], 0.0)
                P.dma("sync", [zt], ["out"], out=out[0:128, :], in_=zt[:], is_output=True)
        P.finish()
    return nc


_NC_CACHE = {}


def kernel(**inputs):
    if "nc" not in _NC_CACHE:
        _NC_CACHE["nc"] = build()
    nc = _NC_CACHE["nc"]
    B = inputs["x"].shape[0]
    rinv = rope_inv_table()
    in_maps = []
    for b in range(B):
        m = {}
        for k in W_SPECS:
            if k == "x":
                m[k] = np.ascontiguousarray(inputs["x"][b], dtype=np.float32)
            elif k == "p":
                m[k] = np.ascontiguousarray(inputs["p"][:, b], dtype=np.float32)
            elif k == "rope_inv":
                m[k] = rinv
            else:
                m[k] = np.ascontiguousarray(inputs[k], dtype=np.float32)
        m["positions"] = np.ascontiguousarray(inputs["positions"][b], dtype=np.int32)
        in_maps.append(m)
    res = run_bass_kernel_spmd(nc, in_maps, core_ids=list(range(B)))
    return np.stack([np.asarray(r["out"], dtype=np.float32) for r in res.results], axis=0)
```
